# Optimizing a Trainium2 kernel written in Bass

```python
import math
import jax, jax.numpy as jnp
from jax import lax
import numpy as np

D_MODEL = 1024
BATCH = 2
SEQ = 8192
DEPTH = 4
DEC_BATCH = 128
DEC_SEQ = 1
PAST_LEN = 8192
PAGE_SIZE = 128

N_A_LAYERS = DEPTH // 2
N_B_LAYERS = DEPTH - N_A_LAYERS
A_EXPAND = 128
A_HEADS = D_MODEL // A_EXPAND
A_DK = A_EXPAND
A_DV = D_MODEL // A_HEADS
A_CHUNK = 64
F_MIN = 1e-30
B_HEAD_DIM = 64
B_HEADS = D_MODEL // B_HEAD_DIM
B_KV_HEADS = 4
B_GROUPS = B_HEADS // B_KV_HEADS
WINDOW = 128
MASK_VALUE = -1e30
N_BUCKETS = 32
MAX_DISTANCE = 128
D_FF = -(-8 * D_MODEL // (3 * 256)) * 256
EPS = 1e-6

kernel_name = "yoco_hgrn2_swa_sink_adaln_decoder_step"


def rms_norm(x, g):
    x32 = x.astype(jnp.float32)
    y = x32 * lax.rsqrt(jnp.mean(x32 * x32, axis=-1, keepdims=True) + EPS)
    return y * g.astype(jnp.float32)


def modulate(x, g, shift, scale):
    y = rms_norm(x, g) * (1.0 + scale[:, None, :]) + shift[:, None, :]
    return y.astype(x.dtype)


def ada_params(c, w, b):
    return (jax.nn.silu(c) @ w + b).astype(jnp.float32)


def swiglu(x, w_in, w_out):
    gate, up = jnp.split(x @ w_in, 2, axis=-1)
    return (jax.nn.silu(gate) * up) @ w_out


def t5_bucket(dist):
    max_exact = N_BUCKETS // 2
    d = jnp.clip(dist, 0, WINDOW - 1)
    large = max_exact + (jnp.log(jnp.maximum(d, 1).astype(jnp.float32) / max_exact)
                         / math.log(MAX_DISTANCE / max_exact)
                         * (N_BUCKETS - max_exact)).astype(jnp.int32)
    large = jnp.clip(large, 0, N_BUCKETS - 1)
    return jnp.where(d < max_exact, d, large)


def rel_bias_for(dist, rel_bias):
    bias = rel_bias.astype(jnp.float32)[t5_bucket(dist)]
    q_len, k_len = dist.shape
    return bias.transpose(2, 0, 1).reshape(B_KV_HEADS, B_GROUPS, q_len, k_len)


def sink_softmax(logits, sinks):
    sink = sinks.astype(jnp.float32).reshape(B_KV_HEADS, B_GROUPS)[:, :, None, None]
    m = jnp.maximum(jnp.max(logits, axis=-1, keepdims=True), sink)
    p = jnp.exp(logits - m)
    denom = jnp.sum(p, axis=-1, keepdims=True) + jnp.exp(sink - m)
    return p / denom


def hgrn2_chunk_scan(q, k, v, log_f, s0, chunk):
    bsz, t, h, _ = q.shape
    dv = v.shape[-1]
    n = t // chunk

    def blocks(a):
        return a.reshape(bsz, n, chunk, h, a.shape[-1]).transpose(1, 0, 3, 2, 4)

    causal = jnp.tril(jnp.ones((chunk, chunk), dtype=bool))[None, None, :, :, None]

    def step(s, inp):
        qc, kc, vc, gc = inp
        b = jnp.cumsum(gc, axis=2)
        o_inter = jnp.einsum('bhtk,bhkv->bhtv', qc * jnp.exp(b), s)
        diff = b[:, :, :, None, :] - b[:, :, None, :, :]
        decay = jnp.where(causal, jnp.exp(jnp.minimum(diff, 0.0)), 0.0)
        att = jnp.einsum('bhtsk,bhtk,bhsk->bhts', decay, qc, kc)
        o = o_inter + jnp.einsum('bhts,bhsv->bhtv', att, vc)
        b_last = b[:, :, -1:, :]
        s = (jnp.exp(b_last[:, :, 0, :])[..., None] * s
             + jnp.einsum('bhsk,bhsv->bhkv', kc * jnp.exp(b_last - b), vc))
        return s, o

    s_final, o = lax.scan(step, s0, (blocks(q), blocks(k), blocks(v), blocks(log_f)))
    return o.transpose(1, 0, 3, 2, 4).reshape(bsz, t, h, dv), s_final


def hgrn2_mixer(xn, w_in, w_o, gnorm_w, lb, s0, chunk):
    bsz, t, _ = xn.shape
    q, f_raw, i, g = jnp.split(xn @ w_in, 4, axis=-1)
    heads = (bsz, t, A_HEADS, A_DK)
    q = jax.nn.silu(q.astype(jnp.float32)).reshape(heads)
    lb = lb.astype(jnp.float32)
    f = lb + (1.0 - lb) * jax.nn.sigmoid(f_raw.astype(jnp.float32))
    log_f = jnp.log(jnp.maximum(f, F_MIN)).reshape(heads)
    k = (1.0 - f).reshape(heads)
    v = i.astype(jnp.float32).reshape(bsz, t, A_HEADS, A_DV)
    o, s_new = hgrn2_chunk_scan(q, k, v, log_f, s0.astype(jnp.float32), chunk)
    gate = jax.nn.silu(g.astype(jnp.float32)).reshape(bsz, t, A_HEADS, A_DV)
    o = rms_norm(o, gnorm_w) * gate
    return o.reshape(bsz, t, D_MODEL).astype(xn.dtype) @ w_o, s_new


def swa_prompt(q, k, v, sinks, rel_bias):
    bsz, t = q.shape[:2]
    nb = t // WINDOW
    qb = q.reshape(bsz, nb, WINDOW, B_KV_HEADS, B_GROUPS, B_HEAD_DIM)

    def band(a):
        prev = jnp.pad(a, ((0, 0), (WINDOW, 0), (0, 0), (0, 0)))[:, :t]
        shp = (bsz, nb, WINDOW, B_KV_HEADS, B_HEAD_DIM)
        return jnp.concatenate([prev.reshape(shp), a.reshape(shp)], axis=2)

    kb, vb = band(k), band(v)
    scale = 1.0 / math.sqrt(B_HEAD_DIM)
    logits = jnp.einsum('bnqkgd,bnskd->bnkgqs', qb, kb).astype(jnp.float32) * scale
    t_loc = jnp.arange(WINDOW)[:, None]
    s_loc = jnp.arange(2 * WINDOW)[None, :]
    dist = t_loc + WINDOW - s_loc
    band_ok = (dist >= 0) & (dist < WINDOW)
    valid = band_ok[None] & ((jnp.arange(nb) > 0)[:, None, None] | (s_loc >= WINDOW)[None])
    logits = jnp.where(valid[None, :, None, None], logits + rel_bias_for(dist, rel_bias), MASK_VALUE)
    p = sink_softmax(logits, sinks)
    out = jnp.einsum('bnkgqs,bnskd->bnqkgd', p.astype(vb.dtype), vb)
    return out.reshape(bsz, t, B_HEADS * B_HEAD_DIM)


def swa_sample(q, k_buf, v_buf, k_new, v_new, sinks, rel_bias):
    bsz, s_len = q.shape[:2]
    w = k_buf.shape[1]
    kk = jnp.concatenate([k_buf, k_new.astype(k_buf.dtype)], axis=1)
    vv = jnp.concatenate([v_buf, v_new.astype(v_buf.dtype)], axis=1)
    qg = q.reshape(bsz, s_len, B_KV_HEADS, B_GROUPS, B_HEAD_DIM)
    scale = 1.0 / math.sqrt(B_HEAD_DIM)
    logits = jnp.einsum('bqkgd,bskd->bkgqs', qg, kk.astype(qg.dtype)).astype(jnp.float32) * scale
    dist = (w + jnp.arange(s_len))[:, None] - jnp.arange(w + s_len)[None, :]
    valid = (dist >= 0) & (dist < WINDOW)
    logits = jnp.where(valid, logits + rel_bias_for(dist, rel_bias), MASK_VALUE)
    p = sink_softmax(logits, sinks)
    out = jnp.einsum('bkgqs,bskd->bqkgd', p.astype(vv.dtype), vv)
    return out.reshape(bsz, s_len, B_HEADS * B_HEAD_DIM)


def trunk(x, c, hgrn_state0, k_buf, v_buf, w_in_a, w_o_a, gnorm_a, lb_a, w_kv, w_ada_kv, b_ada_kv,
          kv_norm_w, w_q_b, w_o_b, sinks_b, rel_bias, norm_w, w_ada, b_ada, w_ffn_in, w_ffn_out,
          final_norm_w):
    prompt = k_buf is None
    bsz, t, _ = x.shape
    lb_sm = jax.nn.softmax(lb_a.astype(jnp.float32), axis=0)
    lbs = jnp.cumsum(lb_sm, axis=0) - lb_sm[0:1]
    h = x
    new_states = []
    k_sh = v_sh = k_state = v_state = None
    for l in range(DEPTH):
        sh1, sc1, g1, sh2, sc2, g2 = jnp.split(ada_params(c, w_ada[l], b_ada[l]), 6, axis=-1)
        xn = modulate(h, norm_w[l, 0], sh1, sc1)
        if l < N_A_LAYERS:
            if prompt:
                s0 = jnp.zeros((bsz, A_HEADS, A_DK, A_DV), jnp.float32)
                chunk = A_CHUNK
            else:
                s0 = hgrn_state0[:, l]
                chunk = t
            mix, s_new = hgrn2_mixer(xn, w_in_a[l], w_o_a[l], gnorm_a[l], lbs[l], s0, chunk)
            new_states.append(s_new)
        else:
            j = l - N_A_LAYERS
            q = (xn @ w_q_b[j]).reshape(bsz, t, B_HEADS, B_HEAD_DIM)
            if prompt:
                att = swa_prompt(q, k_sh, v_sh, sinks_b[j], rel_bias)
            else:
                att = swa_sample(q, k_buf, v_buf, k_sh, v_sh, sinks_b[j], rel_bias)
            mix = att.astype(x.dtype) @ w_o_b[j]
        h = h + (g1[:, None, :] * mix.astype(jnp.float32)).astype(h.dtype)
        xn = modulate(h, norm_w[l, 1], sh2, sc2)
        h = h + (g2[:, None, :] * swiglu(xn, w_ffn_in[l], w_ffn_out[l]).astype(jnp.float32)).astype(h.dtype)
        if l == N_A_LAYERS - 1:
            sh_kv, sc_kv = jnp.split(ada_params(c, w_ada_kv, b_ada_kv), 2, axis=-1)
            kvn = modulate(h, kv_norm_w, sh_kv, sc_kv)
            k_sh, v_sh = jnp.split(kvn @ w_kv, 2, axis=-1)
            k_sh = k_sh.reshape(bsz, t, B_KV_HEADS, B_HEAD_DIM)
            v_sh = v_sh.reshape(bsz, t, B_KV_HEADS, B_HEAD_DIM)
            if prompt:
                k_state, v_state = k_sh[:, -WINDOW:], v_sh[:, -WINDOW:]
            else:
                k_state = jnp.concatenate([k_buf, k_sh.astype(k_buf.dtype)], axis=1)[:, -WINDOW:]
                v_state = jnp.concatenate([v_buf, v_sh.astype(v_buf.dtype)], axis=1)[:, -WINDOW:]
    y = rms_norm(h, final_norm_w).astype(x.dtype)
    return y, jnp.stack(new_states, axis=1).astype(x.dtype), k_state, v_state


def setup_inputs(seed: int = 0) -> dict:
    key = jax.random.key(seed)
    ks = jax.random.split(key, 25)
    D = D_MODEL
    f32 = jnp.float32

    def nrm(k, shape, scale):
        return jax.random.normal(k, shape, f32) * scale

    return {
        "x_prompt": nrm(ks[0], (BATCH, SEQ, D), 1.0),
        "x_sample": nrm(ks[1], (DEC_BATCH, DEC_SEQ, D), 1.0),
        "state_hgrn": nrm(ks[2], (DEC_BATCH, N_A_LAYERS, A_HEADS, A_DK, A_DV), 0.5),
        "cache_swa_k": nrm(ks[3], (DEC_BATCH, WINDOW, B_KV_HEADS, B_HEAD_DIM), 1.0),
        "cache_swa_v": nrm(ks[4], (DEC_BATCH, WINDOW, B_KV_HEADS, B_HEAD_DIM), 1.0),
        "c_prompt": nrm(ks[5], (BATCH, D), 1.0),
        "c_sample": nrm(ks[6], (DEC_BATCH, D), 1.0),
        "w_in_a": nrm(ks[7], (N_A_LAYERS, D, 4 * D), D ** -0.5),
        "w_o_a": nrm(ks[8], (N_A_LAYERS, D, D), D ** -0.5),
        "gnorm_a": 1.0 + nrm(ks[9], (N_A_LAYERS, A_DV), 0.02),
        "lb_a": nrm(ks[10], (N_A_LAYERS, D), 0.5),
        "w_kv": nrm(ks[11], (D, 2 * B_KV_HEADS * B_HEAD_DIM), D ** -0.5),
        "w_ada_kv": nrm(ks[12], (D, 2 * D), 0.5 * D ** -0.5),
        "b_ada_kv": nrm(ks[13], (2 * D,), 0.01),
        "kv_norm_w": 1.0 + nrm(ks[14], (D,), 0.02),
        "w_q_b": nrm(ks[15], (N_B_LAYERS, D, B_HEADS * B_HEAD_DIM), D ** -0.5),
        "w_o_b": nrm(ks[16], (N_B_LAYERS, B_HEADS * B_HEAD_DIM, D), (B_HEADS * B_HEAD_DIM) ** -0.5),
        "sinks_b": nrm(ks[17], (N_B_LAYERS, B_HEADS), 0.5),
        "rel_bias": nrm(ks[18], (N_BUCKETS, B_HEADS), 0.5),
        "norm_w": 1.0 + nrm(ks[19], (DEPTH, 2, D), 0.02),
        "w_ada": nrm(ks[20], (DEPTH, D, 6 * D), 0.5 * D ** -0.5),
        "b_ada": nrm(ks[21], (DEPTH, 6 * D), 0.01),
        "w_ffn_in": nrm(ks[22], (DEPTH, D, 2 * D_FF), D ** -0.5),
        "w_ffn_out": nrm(ks[23], (DEPTH, D_FF, D), D_FF ** -0.5),
        "final_norm_w": 1.0 + nrm(ks[24], (D,), 0.02),
    }


def reference(x_prompt, x_sample, state_hgrn, cache_swa_k, cache_swa_v, c_prompt, c_sample,
              w_in_a, w_o_a, gnorm_a, lb_a, w_kv, w_ada_kv, b_ada_kv, kv_norm_w, w_q_b, w_o_b,
              sinks_b, rel_bias, norm_w, w_ada, b_ada, w_ffn_in, w_ffn_out, final_norm_w):
    weights = (w_in_a, w_o_a, gnorm_a, lb_a, w_kv, w_ada_kv, b_ada_kv, kv_norm_w, w_q_b, w_o_b,
               sinks_b, rel_bias, norm_w, w_ada, b_ada, w_ffn_in, w_ffn_out, final_norm_w)
    y_prompt, state_hgrn_prompt, cache_swa_k_prompt, cache_swa_v_prompt = trunk(
        x_prompt, c_prompt, None, None, None, *weights)
    y_sample, state_hgrn_sample, cache_swa_k_sample, cache_swa_v_sample = trunk(
        x_sample, c_sample, state_hgrn, cache_swa_k, cache_swa_v, *weights)
    return (y_prompt, y_sample, state_hgrn_prompt, state_hgrn_sample,
            cache_swa_k_prompt, cache_swa_v_prompt, cache_swa_k_sample, cache_swa_v_sample)
```

```python
from contextlib import ExitStack
import numpy as np
import concourse.bass as bass
import concourse.mybir as mybir
from concourse.bass_utils import run_bass_kernel_spmd

F32 = mybir.dt.float32
BF16 = mybir.dt.bfloat16
I32 = mybir.dt.int32
AF = mybir.ActivationFunctionType
ALU = mybir.AluOpType
AX = mybir.AxisListType

SAME_ENGINE_SYNC = True


class KB:
    ENG = ("pe", "act", "dve", "pool", "sp")

    def __init__(self, nc):
        self.nc = nc
        self.stack = ExitStack()
        self.h = {"pe": nc.tensor, "act": nc.scalar, "dve": nc.vector, "pool": nc.gpsimd, "sp": nc.sync}
        self.prog = {e: [] for e in self.ENG}
        self.cnt = {e: 0 for e in self.ENG}
        self.seen = {e: {} for e in self.ENG}
        self.res_w = {}
        self.res_r = {}
        self.sems = {}
        self.n_inst = 0

    def sbuf(self, name, shape, dtype):
        return self.stack.enter_context(self.nc.sbuf_tensor("t_" + name, shape, dtype))

    def psum(self, name, shape, dtype):
        return self.stack.enter_context(self.nc.psum_tensor("p_" + name, shape, dtype))

    def _sem(self, key):
        if key not in self.sems:
            self.sems[key] = self.stack.enter_context(self.nc.semaphore("s_" + key.replace(":", "_")))
            self.cnt.setdefault(key, 0)
        return self.sems[key]

    @staticmethod
    def _is_psum(r):
        return (isinstance(r, str) and r.startswith("ps")) or (isinstance(r, tuple) and str(r[0]).startswith("ps"))

    def _waits(self, e, reads, writes):
        w = {}

        def need(key, val):
            if key == e and (not SAME_ENGINE_SYNC or val > self.cnt[e]):
                return
            if val > w.get(key, 0):
                w[key] = val
        for r in reads:
            lw = self.res_w.get(r)
            if lw:
                need(*lw)
            if self._is_psum(r):
                for k, v in self.res_r.get(r, {}).items():
                    if k != e:
                        need(k, v)
        for x in writes:
            lw = self.res_w.get(x)
            if lw:
                need(*lw)
            for k, v in self.res_r.get(x, {}).items():
                need(k, v)
        out = []
        for k, v in w.items():
            if self.seen[e].get(k, 0) < v:
                self.seen[e][k] = v
                out.append((k, v))
        return out

    def _mark(self, key, val, reads, writes):
        for r in reads:
            d = self.res_r.setdefault(r, {})
            if d.get(key, 0) < val:
                d[key] = val
        for x in writes:
            self.res_w[x] = (key, val)
            self.res_r[x] = {}

    def emit(self, e, fn, reads, writes, inc=True):
        waits = self._waits(e, reads, writes)
        self._sem(e)
        if inc:
            self.cnt[e] += 1
            val = self.cnt[e]
        else:
            val = self.cnt[e] + 1
        self._mark(e, val, reads, writes)
        self.prog[e].append((waits, fn, (e, 1) if inc else None))
        self.n_inst += 1

    def dma(self, q, key, out, in_, reads, writes, **kw):
        key = "d:" + key
        self._sem(key)
        waits = self._waits(q, reads, writes)
        self.cnt[key] += 16
        val = self.cnt[key]
        self._mark(key, val, reads, writes)
        self.prog[q].append((waits, lambda e: e.dma_start(out=out, in_=in_, **kw), (key, 16)))
        self.n_inst += 1

    def collective(self, kind, in_ap, out_ap, reads, writes, key, groups=None):
        key = "c:" + key
        self._sem(key)
        waits = self._waits("pool", reads, writes)
        self.cnt[key] += 1
        val = self.cnt[key]
        self._mark(key, val, reads, writes)
        groups = groups or [list(range(8))]
        self.prog["pool"].append((waits, lambda e: e.collective_compute(
            kind, ALU.bypass, replica_groups=groups, ins=[in_ap.opt()], outs=[out_ap.opt()]), (key, 1)))
        self.n_inst += 1

    def barrier(self):
        for e in self.ENG:
            waits = []
            for k, v in self.cnt.items():
                if v > 0 and k != e and self.seen[e].get(k, 0) < v:
                    self.seen[e][k] = v
                    waits.append((k, v))
            self.prog[e].append((waits, None, None))

    def finish(self, final_res):
        waits = self._waits("sp", final_res, [])
        self.prog["sp"].append((waits, None, None))
        nc = self.nc
        with nc.Block() as block:
            def run(e):
                def body(eng):
                    for waits, fn, inc in self.prog[e]:
                        for k, v in waits:
                            eng.wait_ge(self.sems[k], v)
                        if fn is None:
                            continue
                        ins = fn(eng)
                        if inc is not None:
                            ins.then_inc(self.sems[inc[0]], inc[1])
                return body
            block.tensor(run("pe"))
            block.scalar(run("act"))
            block.vector(run("dve"))
            block.gpsimd(run("pool"))
            block.sync(run("sp"))
        self.stack.close()


def kernel(**inputs):
    raise NotImplementedError


D = 1024
KD = 8
DFF = 2816
NFFC = 22
EPS = 1e-6
MASKV = -1e30
FFG = [(0, 4), (4, 4), (8, 4), (12, 4), (16, 4), (20, 2)]

V_NORM = 0
V_KVN = 64
V_FIN = 72
V_LB = 80
V_BADA = 96
V_BADAKV = 288
V_GN = 304
NV = 306
NCOEF = 208


def slot_heads():
    out = []
    for i in range(8):
        for half in range(2):
            out.append(i + 4 * half if i < 4 else 8 + (i - 4) + 4 * half)
    return out


def t5_bucket_np(d):
    d = np.asarray(d)
    dd = np.clip(d, 0, 127)
    large = 16 + (np.log(np.maximum(dd, 1).astype(np.float32) / np.float32(16)) / np.float32(np.log(8.0))
                  * np.float32(16)).astype(np.int32)
    large = np.clip(large, 0, 31)
    return np.where(dd < 16, dd, large)


class Prog:
    def __init__(self, T=1024, NBLK=8, sample=True, dbg=None, nlayers=4):
        self.T, self.NBLK, self.NB = T, NBLK, T // 512
        self.sample = sample
        self.dbg = dbg
        self.nlayers = nlayers
        self.NTOK = T * NBLK
        nc = self.nc = bass.Bass("TRN2", target_bir_lowering=False)
        kb = self.kb = KB(nc)
        di = lambda n, s: nc.dram_tensor(n, s, F32, kind="ExternalInput")
        do = lambda n, s: nc.dram_tensor(n, s, F32, kind="ExternalOutput")
        NTOK = self.NTOK
        self.xT = di("xT", [128, 8, NTOK])
        self.cT = di("cT", [128, 8, 17])
        self.xsT = di("xsT", [128, 8, 16])
        self.vecs_d = di("vecs", [128, NV])
        self.w_in_h = di("w_in_h", [2, 8, 1024, 512])
        self.w_o_a = di("w_o_a", [2, 1024, 1024])
        self.w_kv = di("w_kv", [1024, 512])
        self.w_ada = di("w_ada", [4, 1024, 6144])
        self.w_ada_kv = di("w_ada_kv", [1024, 2048])
        self.w_q_p = di("w_q_p", [2, 1024, 1024])
        self.w_o_p = di("w_o_p", [2, 1024, 1024])
        self.w_ffn_in = di("w_ffn_in", [4, 1024, 5632])
        self.w_ffn_out = di("w_ffn_out", [4, 2816, 1024])
        self.sinks_bc_d = di("sinks_bc", [128, 32])
        self.sink_s_d = di("sink_s", [16, 2])
        self.rb_ext_d = di("rb_ext", [33, 16])
        self.eline_d = di("eline", [33, 384])
        self.eline_s_d = di("eline_s", [33, 128])
        self.ident_d = di("ident", [128, 128])
        self.trimask_d = di("trimask", [64, 64])
        self.halfmask_d = di("halfmask", [128, 2])
        self.state_d = di("state", [16, 2, 8, 128, 128])
        self.ck_d = di("ck", [16, 128, 256])
        self.cv_d = di("cv", [16, 128, 256])
        self.ckT_d = di("ckT", [16, 256, 127])
        self.yT = do("yT", [128, 8, NTOK])
        self.st_p = do("st_p", [2, 8, 128, 128])
        self.kT_p = do("kT_p", [128, 2, 128])
        self.v_p = do("v_p", [128, 256])
        self.y_s = do("y_s", [128, 8, 16])
        self.st_s = do("st_s", [16, 2, 8, 128, 128])
        self.ck_s = do("ck_s", [16, 128, 256])
        self.cv_s = do("cv_s", [16, 128, 256])
        if dbg is not None:
            self.dbg_d = do("dbg", [128, 8, T])
        self.lines_d = nc.dram_tensor("lines", [16, 128, 384], F32)
        self.rowk_d = nc.dram_tensor("rowk", [16, 256], BF16)
        self.outs = []
        self.alloc()

    def alloc(self):
        kb, T = self.kb, self.T
        sb = kb.sbuf
        W = 8 * T + 3 * 4 * T
        W = max(W, 16384)
        self.arena = sb("arena", [128, W], F32)
        ar = self.arena
        self.hT = ar[:, 0:8 * T].rearrange("p (k t) -> p k t", k=8)
        o = 8 * T
        self.xn = ar[:, o:o + 4 * T].bitcast(BF16).rearrange("p (k t) -> p k t", k=8)
        self.og = ar[:, o + 4 * T:o + 8 * T].bitcast(BF16).rearrange("p (k t) -> p k t", k=8)
        self.qT = ar[:, o + 8 * T:o + 12 * T].bitcast(BF16).rearrange("p (k t) -> p k t", k=8)
        def carve(off, words, dt=F32):
            v = ar[:, off:off + words]
            return v if dt == F32 else v.bitcast(dt)
        o = 0
        self.adaT = carve(o, 208 * 17).rearrange("p (g n) -> p g n", n=17); o += 208 * 17
        self.coefS = carve(o, 208 * 16).rearrange("p (g n) -> p g n", n=16); o += 208 * 16
        self.csb = carve(o, 68, BF16).rearrange("p (k n) -> p k n", k=8); o += 68
        self.hs = carve(o, 128).rearrange("p (k n) -> p k n", k=8); o += 128
        self.xns = carve(o, 64, BF16).rearrange("p (k n) -> p k n", k=8); o += 64
        self.ogs = carve(o, 64, BF16).rearrange("p (k n) -> p k n", k=8); o += 64
        self.as_ = carve(o, 32, BF16).rearrange("p (k n) -> p k n", k=4); o += 32
        self.qTs = carve(o, 64, BF16).rearrange("p (k n) -> p k n", k=8); o += 64
        self.Qb2 = carve(o, 256, BF16).rearrange("p (b c s) -> p b c s", b=16, c=2); o += 256
        self.Sin = carve(o, 2048).rearrange("p (b v) -> p b v", b=16); o += 2048
        self.Sout = carve(o, 2048).rearrange("p (b v) -> p b v", b=16); o += 2048
        self.KTs = carve(o, 2048, BF16).rearrange("p (c b t) -> p c b t", c=2, b=16); o += 2048
        self.Vs = carve(o, 2048, BF16).rearrange("p (b n) -> p b n", b=16); o += 2048
        self.pTs = carve(o, 128, BF16).rearrange("p (b s) -> p b s", b=16); o += 128
        self.bias_s = carve(o, 128); o += 128
        self.sink_s = carve(o, 2); o += 2
        assert o <= W, (o, W)
        self.NSLOT = 5
        self.wsl = [sb("wsl%d" % i, [128, 4096], BF16) for i in range(self.NSLOT)]
        self.NS = 14
        self.scr = sb("scr", [128, self.NS, 512], F32)
        self.NBS = 10
        self.bscr = sb("bscr", [128, self.NBS, 512], BF16)
        self.G = sb("G", [128, 513], F32)
        self.KT = sb("KT", [128, 2, 128 + T], BF16)
        self.V = sb("V", [128, 1 + T // 128, 256], BF16)
        self.bias = sb("bias", [128, 16, 256], F32)
        self.Sst = sb("Sst", [128, 16, 128], F32)
        self.coefP = sb("coefP", [128, NCOEF], F32)
        self.vecs = sb("vecs", [128, NV], F32)
        self.lbs = sb("lbs", [128, 2, 8], F32)
        self.sinks_bc = sb("sinks_bc", [128, 32], F32)
        self.ident = sb("ident", [128, 128], F32)
        self.identb = sb("identb", [128, 128], BF16)
        self.onesb = sb("onesb", [128, 128], BF16)
        self.ones64 = sb("ones64", [128, 512], F32)
        self.trimask = sb("trimask", [64, 64], F32)
        self.small = sb("small", [128, 64], F32)
        self.epsc = sb("epsc", [128, 2], F32)
        self.attm = sb("attm", [64, 8, 64], BF16)
        self.ps = [kb.psum("ps%d" % i, [128, 512], F32) for i in range(8)]
        self.ws_i = 0

    def S(self, i):
        return self.scr[:, i, :]

    def B(self, i):
        return self.bscr[:, i, :]

    def act(self, out, in_, func, reads, writes, **kw):
        self.kb.emit("act", lambda e: e.activation(out=out, in_=in_, func=func, **kw), reads, writes)

    def tt(self, eng, out, a, b, op, reads, writes):
        self.kb.emit(eng, lambda e: e.tensor_tensor(out=out, in0=a, in1=b, op=op), reads, writes)

    def ts(self, eng, out, a, s1, s2, op0, op1, reads, writes):
        self.kb.emit(eng, lambda e: e.tensor_scalar(out=out, in0=a, scalar1=s1, scalar2=s2, op0=op0, op1=op1),
                     reads, writes)

    def stt(self, out, a, s, b, op0, op1, reads, writes):
        self.kb.emit("dve", lambda e: e.scalar_tensor_tensor(out=out, in0=a, scalar=s, in1=b, op0=op0, op1=op1),
                     reads, writes)

    def cp(self, eng, out, in_, reads, writes):
        if eng == "act":
            self.act(out, in_, AF.Copy, reads, writes)
        else:
            self.kb.emit(eng, lambda e: e.tensor_copy(out=out, in_=in_), reads, writes)

    def mm(self, out, pairs, reads, writes):
        n = len(pairs)
        for i, (l, r) in enumerate(pairs):
            self.kb.emit("pe", lambda e, l=l, r=r, i=i: e.matmul(out, lhsT=l, rhs=r, start=(i == 0), stop=(i == n - 1)),
                         reads, writes, inc=(i == n - 1))

    def ld(self, out, in_, writes, key, q="sp", reads=()):
        self.kb.dma(q, key, out, in_, list(reads), writes)

    def slab(self, dram_ap, kind):
        i = self.ws_i % self.NSLOT
        self.ws_i += 1
        t = self.wsl[i]
        res = ("w", i)
        if kind == "K8":
            ncol = dram_ap.shape[1]
            view = t[:, :].rearrange("p (k n) -> p k n", k=8)
            self.kb.dma("pool", "w%d" % i, view[:, :, 0:ncol], dram_ap.rearrange("(k p) n -> p k n", p=128), [], [res])
        else:
            nr = dram_ap.shape[0] // 128
            view = t[:, :].rearrange("p (c n) -> p c n", c=4)
            self.kb.dma("pool", "w%d" % i, view[:, 0:nr, :], dram_ap.rearrange("(c p) n -> p c n", p=128), [], [res])
        return view, res

    def stream(self, items):
        q = []
        it = iter(items)
        for _ in range(2):
            x = next(it, None)
            if x is not None:
                q.append(self.slab(*x))
        while q:
            cur = q.pop(0)
            x = next(it, None)
            if x is not None:
                q.append(self.slab(*x))
            yield cur

    def setup(self):
        kb = self.kb
        self.ld(self.vecs[:, :], self.vecs_d[:, :], ["vecs"], "c11")
        self.ld(self.sinks_bc[:, :], self.sinks_bc_d[:, :], ["sinks"], "c12")
        self.ld(self.ident[:, :], self.ident_d[:, :], ["ident"], "c13")
        self.ld(self.trimask[:, :], self.trimask_d[:, :], ["trimask"], "c14")
        kb.emit("dve", lambda e: e.tensor_copy(out=self.identb[:, :], in_=self.ident[:, :]), ["ident"], ["identb"])
        kb.emit("dve", lambda e: e.memset(self.onesb[:, :], 1.0), [], ["onesb"])
        kb.emit("dve", lambda e: e.memset(self.ones64[:, :], 1.0), [], ["ones64"])
        kb.emit("dve", lambda e: e.memset(self.G[:, 0:1], 0.0), [], ["G0"])
        kb.emit("dve", lambda e: e.memset(self.attm[:, :, :], 0.0), [], ["attm"])
        kb.emit("dve", lambda e: e.memset(self.epsc[:, 0:1], EPS), [], ["epsc"])
        kb.emit("dve", lambda e: e.memset(self.epsc[:, 1:2], 1.0), [], ["epsc"])
        kb.emit("dve", lambda e: e.memset(self.Sst[:, :, :], 0.0), [], ["Sst"])
        kb.emit("dve", lambda e: e.memset(self.lbs[:, 0, :], 0.0), [], ["lbs"])
        t = self.small[:, 0:8]
        self.tt("dve", t, self.vecs[:, V_LB:V_LB + 8], self.vecs[:, V_LB + 8:V_LB + 16], ALU.subtract, ["vecs"], ["small"])
        self.act(t, t, AF.Exp, ["small"], ["small"])
        self.ts("dve", t, t, 1.0, None, ALU.add, ALU.bypass, ["small"], ["small"])
        kb.emit("dve", lambda e: e.reciprocal(out=self.lbs[:, 1, :], in_=t), ["small"], ["lbs"])
        rb = self.S(0)[0:33, 0:16]
        el = self.S(1)[0:33, 0:384]
        self.ld(rb, self.rb_ext_d[:, :], ["s0"], "c15")
        self.ld(el, self.eline_d[:, :], ["s1"], "c16")
        pl = self.ps[0][0:16, 0:384]
        self.mm(pl, [(rb, el)], ["s0", "s1"], ["ps0"])
        ln = self.S(2)[0:16, 0:384]
        self.cp("dve", ln, pl, ["ps0"], ["s2"])
        src = bass.AP(ln.tensor, ln.offset, [list(ln.ap[0]), [0, 128], [1, 384]])
        self.ld(self.lines_d[:, :, :], src, ["lines"], "c1", reads=["s2"])
        tv = bass.AP(self.lines_d, 127, [[383, 128], [128 * 384, 16], [1, 256]])
        self.ld(self.bias[:, :, :], tv, ["bias"], "c1", reads=["lines"])

    def coef(self, l, which):
        if l == 4:
            return 192, 200, None
        base = l * 48 + which * 24
        return base, base + 8, base + 16

    def modulate(self, l, which):
        a0, b0, _ = self.coef(l, which)
        for sb in range(self.NB):
            tok = slice(sb * 512, (sb + 1) * 512)
            hs = ("h", sb)
            sq = self.bscr[:, 0:8, :]
            sqr = ["b%d" % k for k in range(8)]
            self.act(sq, self.hT[:, :, tok], AF.Square, [hs], sqr)
            pss = self.ps[0]
            self.mm(pss[:, :], [(self.onesb[:, :], self.bscr[:, k, :]) for k in range(8)], ["onesb"] + sqr, ["ps0"])
            rstd = self.S(0)
            self.act(rstd, pss[:, :], AF.Ln, ["ps0"], ["s0"], scale=1.0 / D, bias=self.epsc[:, 0:1])
            self.act(rstd, rstd, AF.Exp, ["s0"], ["s0"], scale=-0.5)
            for k in range(8):
                t = self.S(1 + (k % 2))
                r = "s%d" % (1 + (k % 2))
                self.stt(t, self.hT[:, k, tok], self.coefP[:, a0 + k:a0 + k + 1], rstd, ALU.mult, ALU.mult,
                         [hs, "s0", "coefP"], [r])
                self.act(self.xn[:, k, tok], t, AF.Identity, [r, "coefP"], [("xn", sb)],
                         bias=self.coefP[:, b0 + k:b0 + k + 1], scale=1.0)

    def proj_residual(self, slabs, src, src_res, gcol, nk):
        dout = 0
        for view, wres in self.stream(slabs):
            for dd in range(4):
                for sb in range(self.NB):
                    tok = slice(sb * 512, (sb + 1) * 512)
                    pb = 4 + ((dout * self.NB + sb) % 4)
                    p = self.ps[pb]
                    self.mm(p[:, :], [(view[:, k, dd * 128:(dd + 1) * 128], src[:, k, tok]) for k in range(nk)],
                            [wres] + src_res(sb), ["ps%d" % pb])
                    self.stt(self.hT[:, dout, tok], p[:, :], self.coefP[:, gcol + dout:gcol + dout + 1],
                             self.hT[:, dout, tok], ALU.mult, ALU.add, ["ps%d" % pb, ("h", sb), "coefP"], [("h", sb)])
                dout += 1

    def ffn(self, l):
        self.modulate(l, 1)
        _, _, g2 = self.coef(l, 1)
        items = []
        for (c0, n) in FFG:
            items.append((self.w_ffn_in[l, :, c0 * 128:(c0 + n) * 128], "K8"))
            items.append((self.w_ffn_in[l, :, DFF + c0 * 128:DFF + (c0 + n) * 128], "K8"))
            items.append((self.w_ffn_out[l, c0 * 128:(c0 + n) * 128, :], "R4"))
        st = self.stream(items)
        a = self.og
        for (c0, n) in FFG:
            wg, rg = next(st)
            wu, ru = next(st)
            wo, ro = next(st)
            for sb in range(self.NB):
                tok = slice(sb * 512, (sb + 1) * 512)
                for c in range(n):
                    pg, pu = (0, 1) if (c % 2 == 0) else (2, 3)
                    self.mm(self.ps[pg][:, :], [(wg[:, k, c * 128:(c + 1) * 128], self.xn[:, k, tok]) for k in range(8)],
                            [rg, ("xn", sb)], ["ps%d" % pg])
                    self.mm(self.ps[pu][:, :], [(wu[:, k, c * 128:(c + 1) * 128], self.xn[:, k, tok]) for k in range(8)],
                            [ru, ("xn", sb)], ["ps%d" % pu])
                    sg = self.S(c % 2)
                    self.act(sg, self.ps[pg][:, :], AF.Silu, ["ps%d" % pg], ["s%d" % (c % 2)])
                    self.tt("dve", a[:, c, tok], sg, self.ps[pu][:, :], ALU.mult, ["s%d" % (c % 2), "ps%d" % pu],
                            [("og", c, sb)])
            for sb in range(self.NB):
                tok = slice(sb * 512, (sb + 1) * 512)
                for dout in range(8):
                    pb = 4 + (dout % 4)
                    p = self.ps[pb]
                    self.mm(p[:, :], [(wo[:, c, dout * 128:(dout + 1) * 128], a[:, c, tok]) for c in range(n)],
                            [ro] + [("og", c, sb) for c in range(n)], ["ps%d" % pb])
                    self.stt(self.hT[:, dout, tok], p[:, :], self.coefP[:, g2 + dout:g2 + dout + 1],
                             self.hT[:, dout, tok], ALU.mult, ALU.add, ["ps%d" % pb, ("h", sb), "coefP"], [("h", sb)])

    def headA_sub(self, l, j, sb, w, wres):
        kb, ps = self.kb, self.ps
        tok = slice(sb * 512, (sb + 1) * 512)
        xr = ("xn", sb)
        S, Bb = self.S, self.B
        xk = lambda k: self.xn[:, k, tok]
        self.mm(ps[0][:, :], [(w[:, k, 0:128], xk(k)) for k in range(8)], [wres, xr], ["ps0"])
        self.mm(ps[1][:, :], [(w[:, k, 128:256], xk(k)) for k in range(8)], [wres, xr], ["ps1"])
        self.mm(ps[2][:, :], [(w[:, k, 384:512], xk(k)) for k in range(8)], [wres, xr], ["ps2"])
        self.act(S(0), ps[0][:, :], AF.Silu, ["ps0"], ["s0"])
        self.act(Bb(0), ps[2][:, :], AF.Silu, ["ps2"], ["b0"])
        lbc = self.lbs[:, l, j:j + 1]
        self.act(S(1), ps[1][:, :], AF.Exp, ["ps1"], ["s1"], scale=-1.0)
        self.act(S(2), S(1), AF.Ln, ["s1", "lbs"], ["s2"], scale=lbc, bias=self.epsc[:, 1:2])
        self.act(S(3), S(1), AF.Ln, ["s1"], ["s3"], scale=1.0, bias=self.epsc[:, 1:2])
        self.tt("dve", S(2), S(2), S(3), ALU.subtract, ["s2", "s3"], ["s2"])
        Gc = self.G[:, 1:513]
        kb.emit("dve", lambda e: e.tensor_tensor_scan(out=Gc, data0=self.ones64[:, :], data1=S(2), initial=0.0,
                                                      op0=ALU.mult, op1=ALU.add), ["s2", "ones64"], ["G"])
        self.act(S(4), S(2), AF.Exp, ["s2"], ["s4"])
        self.ts("dve", S(3), S(4), -1.0, 1.0, ALU.mult, ALU.add, ["s4"], ["s3"])
        G3 = Gc.rearrange("p (c t) -> p c t", c=8)
        bc = lambda a: a.unsqueeze(2).broadcast_to([128, 8, 64])
        v3 = lambda i: self.scr[:, i, :].rearrange("p (c t) -> p c t", c=8)
        self.tt("dve", v3(5), G3, bc(self.G[:, 32:513:64]), ALU.subtract, ["G", "G0"], ["s5"])
        self.tt("dve", v3(8), G3, bc(self.G[:, 0:512:64]), ALU.subtract, ["G", "G0"], ["s8"])
        self.tt("dve", v3(9), G3, bc(self.G[:, 64:513:64]), ALU.subtract, ["G", "G0"], ["s9"])
        self.act(S(6), S(5), AF.Exp, ["s5"], ["s6"])
        self.act(S(7), S(5), AF.Exp, ["s5"], ["s7"], scale=-1.0)
        self.act(S(8), S(8), AF.Exp, ["s8"], ["s8"])
        self.act(S(9), S(9), AF.Exp, ["s9"], ["s9"], scale=-1.0)
        self.tt("dve", Bb(1), S(0), S(6), ALU.mult, ["s0", "s6"], ["b1"])
        self.tt("dve", S(10), S(0), S(8), ALU.mult, ["s0", "s8"], ["s10"])
        self.tt("dve", Bb(2), S(3), S(7), ALU.mult, ["s3", "s7"], ["b2"])
        self.tt("dve", Bb(3), S(3), S(9), ALU.mult, ["s3", "s9"], ["b3"])
        vtok = self.bscr[0:64, 6:8, :].rearrange("p a (c d) -> p (a c) d", d=128)
        for half in range(2):
            for cc in range(4):
                c = half * 4 + cc
                t0 = sb * 512 + c * 64
                self.mm(ps[3][0:64, cc * 128:(cc + 1) * 128],
                        [(self.xn[:, k, t0:t0 + 64], w[:, k, 256:384]) for k in range(8)], [wres, xr], ["ps3"])
            self.cp("act", self.bscr[0:64, 6 + half, :], ps[3][0:64, :], ["ps3"], ["b%d" % (6 + half)])
        pk = ps[5][:, :].bitcast(BF16)
        for c in range(8):
            kb.emit("pe", lambda e, c=c: e.transpose(out=pk[0:64, c * 128:(c + 1) * 128],
                                                     in_=Bb(3)[:, c * 64:(c + 1) * 64], identity=self.identb[:, :]),
                    ["b3", "identb"], ["ps5"], inc=(c == 7))
        kh = self.bscr[0:64, 8:10, :].rearrange("p a (c d) -> p (a c) d", d=128)
        self.cp("act", self.bscr[0:64, 8:10, :].rearrange("p a n -> p (a n)"), pk[0:64, :], ["ps5"], ["b8", "b9"])
        for c in range(8):
            c0 = c * 64
            kb.emit("pe", lambda e, c0=c0: e.matmul(ps[4][0:64, c0 + 32:c0 + 64], lhsT=Bb(2)[:, c0:c0 + 64],
                                                    rhs=Bb(1)[:, c0 + 32:c0 + 64], start=True, stop=True),
                    ["b1", "b2"], ["ps4"], inc=False)
            kb.emit("pe", lambda e, c0=c0: e.matmul(ps[4][0:32, c0:c0 + 32], lhsT=Bb(2)[:, c0:c0 + 32],
                                                    rhs=Bb(1)[:, c0:c0 + 32], start=True, stop=True),
                    ["b1", "b2"], ["ps4"], inc=(c == 7))
        attm = self.attm[:, :, :]
        p43 = ps[4][0:64, :].rearrange("p (c t) -> p c t", c=8)
        self.tt("dve", self.attm[:, :, 32:64], p43[:, :, 32:64],
                self.trimask[:, 32:64].unsqueeze(1).broadcast_to([64, 8, 32]), ALU.mult, ["ps4", "trimask"], ["attm"])
        self.tt("dve", self.attm[0:32, :, 0:32], p43[0:32, :, 0:32],
                self.trimask[0:32, 0:32].unsqueeze(1).broadcast_to([32, 8, 32]), ALU.mult, ["ps4", "trimask"], ["attm"])
        Sj = self.Sst[:, l * 8 + j, :]
        sres = ("S", l, j)
        for c in range(8):
            cs = slice(c * 64, (c + 1) * 64)
            kb.emit("pe", lambda e, c=c, cs=cs: e.matmul(ps[7][:, cs], lhsT=vtok[:, c, :], rhs=attm[:, c, :], start=True, stop=False),
                    ["b6", "b7", "attm"], ["ps7"], inc=False)
            kb.emit("pe", lambda e, cs=cs: e.matmul(ps[7][:, cs], lhsT=Sj, rhs=S(10)[:, cs], start=False, stop=True),
                    [sres, "s10"], ["ps7"], inc=True)
            sl = c % 4
            pn = ps[6][:, sl * 128:(sl + 1) * 128]
            kb.emit("pe", lambda e, c=c, pn=pn: e.matmul(pn, lhsT=kh[:, c, :], rhs=vtok[:, c, :], start=True, stop=True),
                    ["b8", "b9", "b6", "b7"], ["ps6"], inc=True)
            dcol = self.scr[:, 8, c * 64 + 63:c * 64 + 64]
            self.stt(Sj, Sj, dcol, pn, ALU.mult, ALU.add, [sres, "s8", "ps6"], [sres])
        self.act(Bb(4), ps[7][:, :], AF.Square, ["ps7"], ["b4"])
        self.mm(ps[0][:, :], [(self.onesb[:, :], Bb(4))], ["onesb", "b4"], ["ps0"])
        self.act(S(11), ps[0][:, :], AF.Ln, ["ps0"], ["s11"], scale=1.0 / 128, bias=self.epsc[:, 0:1])
        self.act(S(11), S(11), AF.Exp, ["s11"], ["s11"], scale=-0.5)
        self.stt(S(12), ps[7][:, :], self.vecs[:, V_GN + l:V_GN + l + 1], S(11), ALU.mult, ALU.mult,
                 ["ps7", "s11", "vecs"], ["s12"])
        self.tt("dve", self.og[:, j, tok], S(12), Bb(0), ALU.mult, ["s12", "b0"], [("og", j, sb)])

    def layerA(self, blk, l):
        self.modulate(l, 0)
        items = [(self.w_in_h[l, j, :, :], "K8") for j in range(8)]
        for j, (w, wres) in enumerate(self.stream(items)):
            for sb in range(self.NB):
                self.headA_sub(l, j, sb, w, wres)
        _, _, g1 = self.coef(l, 0)
        self.proj_residual([(self.w_o_a[l, :, 0:512], "K8"), (self.w_o_a[l, :, 512:1024], "K8")], self.og,
                           lambda sb: [("og", j, sb) for j in range(8)], g1, 8)

    def kvproj(self, blk):
        T = self.T
        self.modulate(4, 0)
        (w, wres), = list(self.stream([(self.w_kv[:, :], "K8")]))
        last = (blk == self.NBLK - 1)
        for c in range(2):
            for sb in range(self.NB):
                tok = slice(sb * 512, (sb + 1) * 512)
                pb = c * 2 + (sb % 2)
                self.mm(self.ps[pb][:, :], [(w[:, k, c * 128:(c + 1) * 128], self.xn[:, k, tok]) for k in range(8)],
                        [wres, ("xn", sb)], ["ps%d" % pb])
                self.cp("act", self.KT[:, c, 128 + sb * 512:128 + (sb + 1) * 512], self.ps[pb][:, :], ["ps%d" % pb], ["KT"])
                if last and sb == self.NB - 1:
                    self.cp("dve", self.scr[:, 0, c * 128:(c + 1) * 128], self.ps[pb][:, 384:512], ["ps%d" % pb], ["s0"])
        if last:
            self.ld(self.kT_p[:, :, :], self.scr[:, 0, 0:256].rearrange("p (c t) -> p c t", c=2), ["kT_p"], "ok", reads=["s0"])
            self.outs.append("kT_p")
        for tt_ in range(T // 128):
            pb = 4 + (tt_ % 4)
            self.mm(self.ps[pb][:, 0:256], [(self.xn[:, k, tt_ * 128:(tt_ + 1) * 128], w[:, k, 256:512]) for k in range(8)],
                    [wres, ("xn", tt_ // 4)], ["ps%d" % pb])
            self.cp("act", self.V[:, 1 + tt_, :], self.ps[pb][:, 0:256], ["ps%d" % pb], ["V"])
            if last and tt_ == T // 128 - 1:
                self.cp("dve", self.scr[:, 1, 0:256], self.ps[pb][:, 0:256], ["ps%d" % pb], ["s1"])
                self.ld(self.v_p[:, :], self.scr[:, 1, 0:256], ["v_p"], "ov", reads=["s1"])
                self.outs.append("v_p")

    def layerB(self, blk, jb):
        l = 2 + jb
        kb, ps, T = self.kb, self.ps, self.T
        self.modulate(l, 0)
        items = [(self.w_q_p[jb, :, 0:512], "K8"), (self.w_q_p[jb, :, 512:1024], "K8")]
        i = 0
        for w, wres in self.stream(items):
            for dd in range(4):
                for sb in range(self.NB):
                    tok = slice(sb * 512, (sb + 1) * 512)
                    pb = 6 + ((i * self.NB + sb) % 2)
                    self.mm(ps[pb][:, :], [(w[:, k, dd * 128:(dd + 1) * 128], self.xn[:, k, tok]) for k in range(8)],
                            [wres, ("xn", sb)], ["ps%d" % pb])
                    self.act(self.qT[:, i, tok], ps[pb][:, :], AF.Copy, ["ps%d" % pb], [("qT", i)], scale=0.125)
                i += 1
        it = 0
        for i in range(8):
            c = i // 4
            for qt in range(T // 128):
                qtok = slice(qt * 128, (qt + 1) * 128)
                first = (blk == 0 and qt == 0)
                nk = 128 if first else 256
                koff = qt * 128 + (128 if first else 0)
                for half in range(2):
                    hp = slice(half * 64, (half + 1) * 64)
                    slot = i * 2 + half
                    par = it % 2
                    it += 1
                    pl = ps[par]
                    plr = "ps%d" % par
                    kb.emit("pe", lambda e, pl=pl, hp=hp, i=i, qtok=qtok, c=c, koff=koff, nk=nk: e.matmul(
                        pl[:, 0:nk], lhsT=self.qT[hp, i, qtok], rhs=self.KT[hp, c, koff:koff + nk], start=True, stop=True),
                        [("qT", i), "KT"], [plr])
                    s = self.scr[:, par, 0:nk]
                    sr = "s%d" % par
                    self.tt("dve", s, pl[:, 0:nk], self.bias[:, slot, 256 - nk:256], ALU.add, [plr, "bias"], [sr])
                    sm = self.small[:, par * 8:par * 8 + 8]
                    smr = "sm%d" % par
                    kb.emit("dve", lambda e, sm=sm, s=s: e.tensor_reduce(out=sm[:, 0:1], in_=s, axis=AX.X, op=ALU.max), [sr], [smr])
                    sk = self.sinks_bc[:, jb * 16 + slot:jb * 16 + slot + 1]
                    self.tt("dve", sm[:, 1:2], sm[:, 0:1], sk, ALU.max, [smr, "sinks"], [smr])
                    self.ts("dve", sm[:, 2:3], sm[:, 1:2], -1.0, None, ALU.mult, ALU.bypass, [smr], [smr])
                    self.act(s, s, AF.Exp, [sr, smr], [sr, smr], bias=sm[:, 2:3], scale=1.0, accum_out=sm[:, 3:4])
                    self.act(sm[:, 4:5], sk, AF.Exp, [smr, "sinks"], [smr], bias=sm[:, 2:3], scale=1.0)
                    self.tt("dve", sm[:, 5:6], sm[:, 3:4], sm[:, 4:5], ALU.add, [smr], [smr])
                    kb.emit("dve", lambda e, sm=sm: e.reciprocal(out=sm[:, 6:7], in_=sm[:, 5:6]), [smr], [smr])
                    pn = self.bscr[:, par, 0:nk]
                    pnr = "b%d" % par
                    self.ts("dve", pn, s, sm[:, 6:7], None, ALU.mult, ALU.bypass, [sr, smr], [pnr])
                    ptb = ps[2 + par][:, :].bitcast(BF16)
                    ptr = "ps%d" % (2 + par)
                    nkb = nk // 128
                    for kb2 in range(nkb):
                        kb.emit("pe", lambda e, ptb=ptb, pn=pn, kb2=kb2: e.transpose(
                            out=ptb[:, kb2 * 128:(kb2 + 1) * 128], in_=pn[:, kb2 * 128:(kb2 + 1) * 128], identity=self.identb[:, :]),
                            [pnr, "identb"], [ptr], inc=(kb2 == nkb - 1))
                    pT = self.bscr[:, 2 + par, 0:nk]
                    pTr = "b%d" % (2 + par)
                    self.cp("act", pT, ptb[:, 0:nk], [ptr], [pTr])
                    po = ps[4 + par][hp, 0:128]
                    por = "ps%d" % (4 + par)
                    vcol = (2 * c + half) * 64
                    for kb2 in range(nkb):
                        vt = (qt + kb2) if not first else 1
                        kb.emit("pe", lambda e, po=po, vt=vt, vcol=vcol, pT=pT, kb2=kb2, nkb=nkb: e.matmul(
                            po, lhsT=self.V[:, vt, vcol:vcol + 64], rhs=pT[:, kb2 * 128:(kb2 + 1) * 128],
                            start=(kb2 == 0), stop=(kb2 == nkb - 1)), ["V", pTr], [por], inc=(kb2 == nkb - 1))
                    self.cp("dve", self.og[hp, i, qtok], po, [por], [("og", i, qt // 4)])
        _, _, g1 = self.coef(l, 0)
        self.proj_residual([(self.w_o_p[jb, :, 0:512], "K8"), (self.w_o_p[jb, :, 512:1024], "K8")], self.og,
                           lambda sb: [("og", j, sb) for j in range(8)], g1, 8)

    def final(self, blk):
        T = self.T
        for sb in range(self.NB):
            tok = slice(sb * 512, (sb + 1) * 512)
            hs = ("h", sb)
            sqr = ["b%d" % k for k in range(8)]
            self.act(self.bscr[:, 0:8, :], self.hT[:, :, tok], AF.Square, [hs], sqr)
            self.mm(self.ps[0][:, :], [(self.onesb[:, :], self.bscr[:, k, :]) for k in range(8)], ["onesb"] + sqr, ["ps0"])
            rstd = self.S(0)
            self.act(rstd, self.ps[0][:, :], AF.Ln, ["ps0"], ["s0"], scale=1.0 / D, bias=self.epsc[:, 0:1])
            self.act(rstd, rstd, AF.Exp, ["s0"], ["s0"], scale=-0.5)
            for k in range(8):
                self.stt(self.scr[:, 4 + k, :], self.hT[:, k, tok], self.vecs[:, V_FIN + k:V_FIN + k + 1], rstd,
                         ALU.mult, ALU.mult, [hs, "s0", "vecs"], ["s%d" % (4 + k)])
            self.ld(self.yT[:, :, blk * T + sb * 512:blk * T + (sb + 1) * 512], self.scr[:, 4:12, :], ["yT"], "y%d" % sb,
                    reads=["s%d" % (4 + k) for k in range(8)])
        if "yT" not in self.outs:
            self.outs.append("yT")

    def block(self, blk):
        T = self.T
        for sb in range(self.NB):
            self.ld(self.hT[:, :, sb * 512:(sb + 1) * 512], self.xT[:, :, blk * T + sb * 512:blk * T + (sb + 1) * 512],
                    [("h", sb)], "x%d" % sb)
        for l in range(self.nlayers):
            if l < 2:
                self.layerA(blk, l)
            else:
                self.layerB(blk, l - 2)
            if self.dbg == l + 10 and blk == 0:
                self.ld(self.dbg_d[:, :, :], self.hT[:, :, :], ["dbg"], "o2", reads=[("h", sb) for sb in range(self.NB)])
                self.outs.append("dbg")
            self.ffn(l)
            if l == 1:
                self.kvproj(blk)
            if self.dbg == l and blk == 0:
                self.ld(self.dbg_d[:, :, :], self.hT[:, :, :], ["dbg"], "o2", reads=[("h", sb) for sb in range(self.NB)])
                self.outs.append("dbg")
        if self.nlayers == 4:
            self.final(blk)
        if blk < self.NBLK - 1 and self.nlayers > 1:
            self.cp("dve", self.KT[:, :, 0:128], self.KT[:, :, T:T + 128], ["KT"], ["KT"])
            self.cp("dve", self.V[:, 0, :], self.V[:, T // 128, :], ["V"], ["V"])
        if blk == self.NBLK - 1:
            self.ld(self.st_p.ap().rearrange("l j k v -> k (l j) v"), self.Sst[:, :, :], ["st_p"], "ost", reads=[("S", l, j) for l in range(2) for j in range(8)])
            self.outs.append("st_p")

    def ada_phase(self):
        kb, ps = self.kb, self.ps
        cs = self.scr[:, 0, 0:136].rearrange("p (k n) -> p k n", k=8)
        self.ld(cs, self.cT[:, :, :], ["s0"], "c17")
        csb = self.csb
        self.act(csb[:, :, :], cs, AF.Silu, ["s0"], ["csb"])
        items = []
        for l in range(4):
            for s in range(12):
                items.append((self.w_ada[l, :, s * 512:(s + 1) * 512], "K8"))
        for s in range(4):
            items.append((self.w_ada_kv[:, s * 512:(s + 1) * 512], "K8"))
        n = 0
        for w, wres in self.stream(items):
            l, s = (n // 12, n % 12) if n < 48 else (4, n - 48)
            n += 1
            for dd in range(4):
                ch = s * 4 + dd
                gidx = l * 48 + ch
                bcol = (V_BADA + l * 48 + ch) if l < 4 else (V_BADAKV + ch)
                pb = gidx % 4
                self.mm(ps[pb][:, 0:17], [(w[:, k, dd * 128:(dd + 1) * 128], csb[:, k, :]) for k in range(8)],
                        [wres, "csb"], ["ps%d" % pb])
                self.ts("dve", self.adaT[:, gidx, :], ps[pb][:, 0:17], self.vecs[:, bcol:bcol + 1], None, ALU.add, ALU.bypass,
                        ["ps%d" % pb, "vecs"], ["adaT"])
        aT = self.adaT
        for l in range(5):
            for which in range(2 if l < 4 else 1):
                base = l * 48 + which * 24
                nw = self.vecs[:, V_NORM + (l * 2 + which) * 8:V_NORM + (l * 2 + which) * 8 + 8] if l < 4 else self.vecs[:, V_KVN:V_KVN + 8]
                self.stt(self.coefP[:, base:base + 8], aT[:, base + 8:base + 16, 0], 1.0, nw, ALU.add, ALU.mult,
                         ["adaT", "vecs"], ["coefP"])
                self.cp("dve", self.coefP[:, base + 8:base + 16], aT[:, base:base + 8, 0], ["adaT"], ["coefP"])
                if l < 4:
                    self.cp("dve", self.coefP[:, base + 16:base + 24], aT[:, base + 16:base + 24, 0], ["adaT"], ["coefP"])
                if self.sample:
                    self.stt(self.coefS[:, base:base + 8, :], aT[:, base + 8:base + 16, 1:17], 1.0,
                             nw.unsqueeze(2).broadcast_to([128, 8, 16]), ALU.add, ALU.mult, ["adaT", "vecs"], ["coefS"])
                    self.cp("dve", self.coefS[:, base + 8:base + 16, :], aT[:, base:base + 8, 1:17], ["adaT"], ["coefS"])
                    if l < 4:
                        self.cp("dve", self.coefS[:, base + 16:base + 24, :], aT[:, base + 16:base + 24, 1:17], ["adaT"], ["coefS"])

    def build(self):
        self.setup()
        self.ada_phase()
        if self.sample:
            self.sample_phase()
        self.kb.barrier()
        for blk in range(self.NBLK):
            self.block(blk)
        self.kb.finish(self.outs)
        return self.nc


def _fm(v):
    v = np.asarray(v, np.float32)
    return np.ascontiguousarray(v.reshape(-1, 128).T)


def _consts():
    eline = np.zeros((33, 384), np.float32)
    eline[32, :] = MASKV
    for ip in range(128, 256):
        eline[int(t5_bucket_np(255 - ip)), ip] = 1.0
        eline[32, ip] = 0.0
    eline_s = np.zeros((33, 128), np.float32)
    for r in range(128):
        eline_s[int(t5_bucket_np(127 - r)), r] = 1.0
    ident = np.eye(128, dtype=np.float32)
    tri = (np.arange(64)[:, None] <= np.arange(64)[None, :]).astype(np.float32)
    hm = np.zeros((128, 2), np.float32)
    hm[:64, 0] = 1.0
    hm[64:, 1] = 1.0
    return dict(eline=eline, eline_s=eline_s, ident=ident, trimask=tri, halfmask=hm)


def prep_shared(inp):
    f32 = lambda a: np.ascontiguousarray(np.asarray(a, np.float32))
    sh = slot_heads()
    w_in_a = f32(inp["w_in_a"])
    d = {}
    d["w_in_h"] = np.ascontiguousarray(w_in_a.reshape(2, 1024, 4, 8, 128).transpose(0, 3, 1, 2, 4).reshape(2, 8, 1024, 512))
    d["w_o_a"] = f32(inp["w_o_a"])
    d["w_kv"] = f32(inp["w_kv"])
    d["w_ada"] = f32(inp["w_ada"])
    d["w_ada_kv"] = f32(inp["w_ada_kv"])
    wq = f32(inp["w_q_b"]).reshape(2, 1024, 16, 64)
    d["w_q_p"] = np.ascontiguousarray(wq[:, :, sh, :].reshape(2, 1024, 1024))
    wo = f32(inp["w_o_b"]).reshape(2, 16, 64, 1024)
    d["w_o_p"] = np.ascontiguousarray(wo[:, sh, :, :].reshape(2, 1024, 1024))
    d["w_ffn_in"] = f32(inp["w_ffn_in"])
    d["w_ffn_out"] = f32(inp["w_ffn_out"])
    sk = f32(inp["sinks_b"])[:, sh]
    d["sinks_bc"] = np.ascontiguousarray(np.broadcast_to(sk.reshape(1, 32), (128, 32)))
    d["sink_s"] = np.ascontiguousarray(sk.T)
    rb = f32(inp["rel_bias"])[:, sh]
    d["rb_ext"] = np.ascontiguousarray(np.concatenate([rb, np.ones((1, 16), np.float32)], 0))
    vecs = np.zeros((128, NV), np.float32)
    nw = f32(inp["norm_w"])
    for l in range(4):
        for wh in range(2):
            vecs[:, V_NORM + (l * 2 + wh) * 8:V_NORM + (l * 2 + wh) * 8 + 8] = _fm(nw[l, wh])
    vecs[:, V_KVN:V_KVN + 8] = _fm(inp["kv_norm_w"])
    vecs[:, V_FIN:V_FIN + 8] = _fm(inp["final_norm_w"])
    lb = f32(inp["lb_a"])
    vecs[:, V_LB:V_LB + 8] = _fm(lb[0])
    vecs[:, V_LB + 8:V_LB + 16] = _fm(lb[1])
    ba = f32(inp["b_ada"])
    for l in range(4):
        vecs[:, V_BADA + l * 48:V_BADA + (l + 1) * 48] = _fm(ba[l])
    vecs[:, V_BADAKV:V_BADAKV + 16] = _fm(inp["b_ada_kv"])
    gn = f32(inp["gnorm_a"])
    vecs[:, V_GN] = gn[0]
    vecs[:, V_GN + 1] = gn[1]
    d["vecs"] = vecs
    d.update(_consts())
    return d


def prep_core(inp, shared, core, T, NBLK, seq=None):
    f32 = lambda a: np.ascontiguousarray(np.asarray(a, np.float32))
    NTOK = T * NBLK
    seq = core % 2 if seq is None else seq
    d = dict(shared)
    x = f32(inp["x_prompt"])[seq, :NTOK]
    d["xT"] = np.ascontiguousarray(x.T.reshape(8, 128, NTOK).transpose(1, 0, 2))
    bs = slice(core * 16, (core + 1) * 16)
    c17 = np.concatenate([f32(inp["c_prompt"])[seq][None], f32(inp["c_sample"])[bs]], 0)
    d["cT"] = np.ascontiguousarray(c17.T.reshape(8, 128, 17).transpose(1, 0, 2))
    xs = f32(inp["x_sample"])[bs, 0]
    d["xsT"] = np.ascontiguousarray(xs.T.reshape(8, 128, 16).transpose(1, 0, 2))
    d["state"] = f32(inp["state_hgrn"])[bs]
    ck = f32(inp["cache_swa_k"])[bs].reshape(16, 128, 256)
    d["ck"] = ck
    d["cv"] = f32(inp["cache_swa_v"])[bs].reshape(16, 128, 256)
    d["ckT"] = np.ascontiguousarray(ck[:, 1:128, :].transpose(0, 2, 1))
    return d


def _sample_methods():
    def bc3(a, n):
        return a.unsqueeze(2).broadcast_to([a.shape[0], a.shape[1], n])

    def sample_modulate(self, l, which):
        a0, b0, _ = self.coef(l, which)
        S, ps = self.S, self.ps
        sq = self.B(0)[:, 0:128].rearrange("p (k n) -> p k n", k=8)
        self.act(sq, self.hs[:, :, :], AF.Square, ["hs"], ["b0"])
        self.mm(ps[0][:, 0:16], [(self.onesb[:, :], sq[:, k, :]) for k in range(8)], ["onesb", "b0"], ["ps0"])
        rstd = S(0)[:, 0:16]
        self.act(rstd, ps[0][:, 0:16], AF.Ln, ["ps0"], ["s0"], scale=1.0 / D, bias=self.epsc[:, 0:1])
        self.act(rstd, rstd, AF.Exp, ["s0"], ["s0"], scale=-0.5)
        t = S(1)[:, 0:128].rearrange("p (k n) -> p k n", k=8)
        self.tt("dve", t, self.hs[:, :, :], rstd.unsqueeze(1).broadcast_to([128, 8, 16]), ALU.mult, ["hs", "s0"], ["s1"])
        self.tt("dve", t, t, self.coefS[:, a0:a0 + 8, :], ALU.mult, ["s1", "coefS"], ["s1"])
        self.tt("dve", self.xns[:, :, :], t, self.coefS[:, b0:b0 + 8, :], ALU.add, ["s1", "coefS"], ["xns"])

    def sample_proj(self, slabs, src, src_res, gcol, nk):
        dout = 0
        for view, wres in self.stream(slabs):
            for dd in range(4):
                pb = 4 + (dout % 4)
                p = self.ps[pb]
                self.mm(p[:, 0:16], [(view[:, k, dd * 128:(dd + 1) * 128], src[:, k, :]) for k in range(nk)],
                        [wres, src_res], ["ps%d" % pb])
                tmp = self.S(3)[:, 0:16]
                self.tt("dve", tmp, p[:, 0:16], self.coefS[:, gcol + dout, :], ALU.mult, ["ps%d" % pb, "coefS"], ["s3"])
                self.tt("dve", self.hs[:, dout, :], self.hs[:, dout, :], tmp, ALU.add, ["hs", "s3"], ["hs"])
                dout += 1

    def sample_ffn(self, l):
        self.sample_modulate(l, 1)
        _, _, g2 = self.coef(l, 1)
        items = []
        for (c0, n) in FFG:
            items.append((self.w_ffn_in[l, :, c0 * 128:(c0 + n) * 128], "K8"))
            items.append((self.w_ffn_in[l, :, DFF + c0 * 128:DFF + (c0 + n) * 128], "K8"))
            items.append((self.w_ffn_out[l, c0 * 128:(c0 + n) * 128, :], "R4"))
        st = self.stream(items)
        ps = self.ps
        for (c0, n) in FFG:
            wg, rg = next(st)
            wu, ru = next(st)
            wo, ro = next(st)
            for c in range(n):
                self.mm(ps[0][:, 0:16], [(wg[:, k, c * 128:(c + 1) * 128], self.xns[:, k, :]) for k in range(8)], [rg, "xns"], ["ps0"])
                self.mm(ps[1][:, 0:16], [(wu[:, k, c * 128:(c + 1) * 128], self.xns[:, k, :]) for k in range(8)], [ru, "xns"], ["ps1"])
                sg = self.S(0)[:, 0:16]
                self.act(sg, ps[0][:, 0:16], AF.Silu, ["ps0"], ["s0"])
                self.tt("dve", self.as_[:, c, :], sg, ps[1][:, 0:16], ALU.mult, ["s0", "ps1"], ["as"])
            for dout in range(8):
                pb = 4 + (dout % 4)
                self.mm(ps[pb][:, 0:16], [(wo[:, c, dout * 128:(dout + 1) * 128], self.as_[:, c, :]) for c in range(n)],
                        [ro, "as"], ["ps%d" % pb])
                tmp = self.S(3)[:, 0:16]
                self.tt("dve", tmp, ps[pb][:, 0:16], self.coefS[:, g2 + dout, :], ALU.mult, ["ps%d" % pb, "coefS"], ["s3"])
                self.tt("dve", self.hs[:, dout, :], self.hs[:, dout, :], tmp, ALU.add, ["hs", "s3"], ["hs"])

    def sampleA(self, l):
        kb, ps, S = self.kb, self.ps, self.S
        self.sample_modulate(l, 0)
        items = [(self.w_in_h[l, j, :, :], "K8") for j in range(8)]
        for j, (w, wres) in enumerate(self.stream(items)):
            self.ld(self.Sin[:, :, :], self.state_d[:, l, j].rearrange("b k v -> k b v"), ["Sin"], "si")
            pp = ps[0]
            for part in range(4):
                self.mm(pp[:, part * 16:(part + 1) * 16],
                        [(w[:, k, part * 128:(part + 1) * 128], self.xns[:, k, :]) for k in range(8)], [wres, "xns"], ["ps0"])
            q, gate = S(0)[:, 0:16], S(0)[:, 16:32]
            self.act(q, pp[:, 0:16], AF.Silu, ["ps0"], ["s0"])
            self.act(gate, pp[:, 48:64], AF.Silu, ["ps0"], ["s0"])
            e, L1, L2, lg, f, kk, v = [S(1)[:, i * 16:(i + 1) * 16] for i in range(7)]
            self.act(e, pp[:, 16:32], AF.Exp, ["ps0"], ["s1"], scale=-1.0)
            self.act(L1, e, AF.Ln, ["s1", "lbs"], ["s1"], scale=self.lbs[:, l, j:j + 1], bias=self.epsc[:, 1:2])
            self.act(L2, e, AF.Ln, ["s1"], ["s1"], scale=1.0, bias=self.epsc[:, 1:2])
            self.tt("dve", lg, L1, L2, ALU.subtract, ["s1"], ["s1"])
            self.act(f, lg, AF.Exp, ["s1"], ["s1"])
            self.ts("dve", kk, f, -1.0, 1.0, ALU.mult, ALU.add, ["s1"], ["s1"])
            self.cp("dve", v, pp[:, 32:48], ["ps0"], ["s1"])
            rd = self.scr[:, 4:8, :].rearrange("p a (b v) -> p (a b) v", v=128)
            self.tt("dve", rd, self.ident[:, :].unsqueeze(1).broadcast_to([128, 16, 128]), bc3(v, 128), ALU.mult,
                    ["ident", "s1"], ["s4", "s5", "s6", "s7"])
            for qd in range(4):
                self.mm(ps[4 + qd][:, :], [(self.ones64[:, 0:128], self.scr[:, 4 + qd, :])], ["ones64", "s%d" % (4 + qd)],
                        ["ps%d" % (4 + qd)])
                self.tt("dve", self.scr[:, 8 + qd, :].rearrange("p (b v) -> p b v", v=128),
                        ps[4 + qd][:, :].rearrange("p (b v) -> p b v", v=128), bc3(kk[:, 4 * qd:4 * qd + 4], 128), ALU.mult,
                        ["ps%d" % (4 + qd), "s1"], ["s%d" % (8 + qd)])
            self.tt("dve", self.Sout[:, :, :], self.Sin[:, :, :], bc3(f, 128), ALU.mult, ["Sin", "s1"], ["Sout"])
            self.tt("dve", self.Sout[:, :, :], self.Sout[:, :, :], self.scr[:, 8:12, :].rearrange("p a (b v) -> p (a b) v", v=128),
                    ALU.add, ["Sout", "s8", "s9", "s10", "s11"], ["Sout"])
            self.ld(self.st_s[:, l, j].rearrange("b k v -> k b v"), self.Sout[:, :, :], ["st_s"], "so", reads=["Sout"])
            po2 = ps[1]
            q2 = S(0)[:, 32:64].rearrange("p (b t) -> p b t", t=2)
            self.cp("dve", q2, bc3(q, 2), ["s0"], ["s0"])
            for b in range(16):
                kb.emit("pe", lambda e, b=b: e.matmul(po2[:, 2 * b:2 * b + 2], lhsT=self.Sout[:, b, :], rhs=q2[:, b, :], start=True, stop=True),
                        ["Sout", "s0"], ["ps1"], inc=(b == 15))
            po = po2[:, 0:32:2]
            osq = self.B(1)[:, 0:16]
            self.act(osq, po, AF.Square, ["ps1"], ["b1"])
            self.mm(ps[2][:, 0:16], [(self.onesb[:, :], osq)], ["onesb", "b1"], ["ps2"])
            rstd = S(2)[:, 0:16]
            self.act(rstd, ps[2][:, 0:16], AF.Ln, ["ps2"], ["s2"], scale=1.0 / 128, bias=self.epsc[:, 0:1])
            self.act(rstd, rstd, AF.Exp, ["s2"], ["s2"], scale=-0.5)
            t = S(2)[:, 16:32]
            self.stt(t, po, self.vecs[:, V_GN + l:V_GN + l + 1], rstd, ALU.mult, ALU.mult, ["ps1", "s2", "vecs"], ["s2"])
            self.tt("dve", self.ogs[:, j, :], t, gate, ALU.mult, ["s2", "s0"], ["ogs"])
        if "st_s" not in self.outs:
            self.outs.append("st_s")
        _, _, g1 = self.coef(l, 0)
        self.sample_proj([(self.w_o_a[l, :, 0:512], "K8"), (self.w_o_a[l, :, 512:1024], "K8")], self.ogs, "ogs", g1, 8)

    def sample_kv(self):
        ps, S = self.ps, self.S
        self.sample_modulate(4, 0)
        self.kb.dma("pool", "cv", self.Vs[0:112, :, :], self.cv_d[:, 1:113, :].rearrange("b s n -> s b n"), [], ["Vs"])
        self.kb.dma("pool", "cv2", self.Vs[112:127, :, :], self.cv_d[:, 113:128, :].rearrange("b s n -> s b n"), [], ["Vs"])
        ckv = self.ckT_d.ap().rearrange("b (c p) t -> p c b t", p=128)
        for c in range(2):
            self.kb.dma("pool", "ck%d" % c, self.KTs[:, c, :, 0:127], ckv[:, c, :, :], [], ["KTs"])
        self.ld(self.ck_s[:, 0:127, :], self.ck_d[:, 1:128, :], ["ck_s"], "ock")
        self.ld(self.cv_s[:, 0:127, :], self.cv_d[:, 1:128, :], ["cv_s"], "ocv")
        (w, wres), = list(self.stream([(self.w_kv[:, :], "K8")]))
        for c in range(2):
            self.mm(ps[0][:, c * 16:(c + 1) * 16], [(w[:, k, c * 128:(c + 1) * 128], self.xns[:, k, :]) for k in range(8)],
                    [wres, "xns"], ["ps0"])
        self.cp("dve", self.KTs[:, :, :, 127], ps[0][:, 0:32].rearrange("p (c b) -> p c b", c=2), ["ps0"], ["KTs"])
        self.mm(ps[1][0:16, :], [(self.xns[:, k, :], w[:, k, :]) for k in range(8)], [wres, "xns"], ["ps1"])
        rowf = S(0)[0:16, :]
        self.cp("dve", rowf, ps[1][0:16, :], ["ps1"], ["s0"])
        self.ld(self.ck_s[:, 127, :], rowf[:, 0:256], ["ck_s"], "ock2", reads=["s0"])
        self.ld(self.cv_s[:, 127, :], rowf[:, 256:512], ["cv_s"], "ocv2", reads=["s0"])
        rowb = self.B(0)[0:16, 0:256]
        self.cp("dve", rowb, ps[1][0:16, 256:512], ["ps1"], ["b0"])
        self.ld(self.rowk_d[:, :], rowb, ["rowk"], "rk", reads=["b0"])
        self.ld(self.Vs[127:128, :, :], self.rowk_d.ap().rearrange("(o b) n -> o b n", o=1), ["Vs"], "rk2", reads=["rowk"])
        self.outs += ["ck_s", "cv_s"]

    def sampleB(self, jb):
        kb, ps, S = self.kb, self.ps, self.S
        l = 2 + jb
        self.sample_modulate(l, 0)
        i = 0
        for w, wres in self.stream([(self.w_q_p[jb, :, 0:512], "K8"), (self.w_q_p[jb, :, 512:1024], "K8")]):
            for dd in range(4):
                pb = i % 4
                self.mm(ps[pb][:, 0:16], [(w[:, k, dd * 128:(dd + 1) * 128], self.xns[:, k, :]) for k in range(8)],
                        [wres, "xns"], ["ps%d" % pb])
                self.act(self.qTs[:, i, :], ps[pb][:, 0:16], AF.Copy, ["ps%d" % pb], ["qTs"], scale=0.125)
                i += 1
        kb.emit("dve", lambda e: e.memset(self.Qb2[:, :, :, :], 0.0), [], ["Qb2"])
        for c in range(2):
            for half in range(2):
                hp = slice(half * 64, (half + 1) * 64)
                self.cp("dve", self.Qb2[hp, :, c, c * 8 + half:c * 8 + 8:2],
                        self.qTs[hp, 4 * c:4 * c + 4, :].rearrange("p i b -> p b i"), ["qTs"], ["Qb2"])
        for b in range(16):
            pb = 4 + b // 4
            self.mm(ps[pb][0:16, (b % 4) * 128:(b % 4 + 1) * 128],
                    [(self.Qb2[:, b, c, :], self.KTs[:, c, b, :]) for c in range(2)], ["Qb2", "KTs"], ["ps%d" % pb])
        sv = lambda qd: self.scr[0:16, 4 + qd, :].rearrange("p (b t) -> p b t", t=128)
        for qd in range(4):
            self.tt("dve", sv(qd), ps[4 + qd][0:16, :].rearrange("p (b t) -> p b t", t=128),
                    self.bias_s[0:16, :].unsqueeze(1).broadcast_to([16, 4, 128]), ALU.add, ["ps%d" % (4 + qd), "bias_s"], ["s%d" % (4 + qd)])
        s_all = self.scr[0:16, 4:8, :].rearrange("p a (b t) -> p (a b) t", t=128)
        sr = ["s4", "s5", "s6", "s7"]
        sm = self.small
        mx, rs, es, dn = sm[0:16, 0:16], sm[0:16, 16:32], sm[0:16, 32:48], sm[0:16, 48:64]
        kb.emit("dve", lambda e: e.tensor_reduce(out=mx, in_=s_all, axis=AX.X, op=ALU.max), sr, ["sm0"])
        skc = self.sink_s[0:16, jb:jb + 1]
        self.ts("dve", mx, mx, skc, None, ALU.max, ALU.bypass, ["sm0", "sink_s"], ["sm0"])
        self.tt("dve", s_all, s_all, bc3(mx, 128), ALU.subtract, sr + ["sm0"], sr)
        self.act(self.scr[0:16, 4:8, :], self.scr[0:16, 4:8, :], AF.Exp, sr, sr)
        kb.emit("dve", lambda e: e.tensor_reduce(out=rs, in_=s_all, axis=AX.X, op=ALU.add), sr, ["sm0"])
        self.act(es, mx, AF.Exp, ["sm0", "sink_s"], ["sm0"], scale=-1.0, bias=skc)
        self.tt("dve", dn, rs, es, ALU.add, ["sm0"], ["sm0"])
        kb.emit("dve", lambda e: e.reciprocal(out=dn, in_=dn), ["sm0"], ["sm0"])
        pn = self.bscr[0:16, 4:8, :].rearrange("p a (b t) -> p (a b) t", t=128)
        pnr = ["b4", "b5", "b6", "b7"]
        self.tt("dve", pn, s_all, bc3(dn, 128), ALU.mult, sr + ["sm0"], pnr)
        ptb = ps[0][:, :].bitcast(BF16)
        for b in range(16):
            kb.emit("pe", lambda e, b=b: e.transpose(out=ptb[:, b * 16:(b + 1) * 16], in_=pn[:, b, :], identity=self.identb[0:16, 0:16]),
                    pnr + ["identb"], ["ps0"], inc=(b == 15))
        self.cp("act", self.pTs[:, :, :].rearrange("p b s -> p (b s)"), ptb[:, 0:256], ["ps0"], ["pTs"])
        pv = ps[1]
        for b in range(16):
            for c in range(2):
                o0 = (b * 2 + c) * 16
                kb.emit("pe", lambda e, b=b, c=c, o0=o0: e.matmul(pv[:, o0:o0 + 16], lhsT=self.Vs[:, b, c * 128:(c + 1) * 128],
                                                                  rhs=self.pTs[:, b, :], start=True, stop=True),
                        ["Vs", "pTs"], ["ps1"], inc=(b == 15 and c == 1))
        pvv = pv[:, :].rearrange("p (b c s) -> p b c s", b=16, c=2)
        for c in range(2):
            for half in range(2):
                hp = slice(half * 64, (half + 1) * 64)
                self.cp("dve", self.ogs[hp, 4 * c:4 * c + 4, :].rearrange("p i b -> p b i"),
                        pvv[hp, :, c, c * 8 + half:c * 8 + 8:2], ["ps1"], ["ogs"])
        _, _, g1 = self.coef(l, 0)
        self.sample_proj([(self.w_o_p[jb, :, 0:512], "K8"), (self.w_o_p[jb, :, 512:1024], "K8")], self.ogs, "ogs", g1, 8)

    def sample_phase(self):
        S, ps = self.S, self.ps
        self.ld(self.hs[:, :, :], self.xsT[:, :, :], ["hs"], "c20")
        self.ld(self.sink_s[0:16, 0:2], self.sink_s_d[:, :], ["sink_s"], "c21")
        rb = S(12)[0:33, 0:16]
        el = S(13)[0:33, 0:128]
        self.ld(rb, self.rb_ext_d[:, :], ["s12"], "c22")
        self.ld(el, self.eline_s_d[:, :], ["s13"], "c23")
        self.mm(ps[3][0:16, 0:128], [(rb, el)], ["s12", "s13"], ["ps3"])
        self.cp("dve", self.bias_s[0:16, :], ps[3][0:16, 0:128], ["ps3"], ["bias_s"])
        for l in range(4):
            if l < 2:
                self.sampleA(l)
            else:
                self.sampleB(l - 2)
            self.sample_ffn(l)
            if l == 1:
                self.sample_kv()
        sq = self.B(0)[:, 0:128].rearrange("p (k n) -> p k n", k=8)
        self.act(sq, self.hs[:, :, :], AF.Square, ["hs"], ["b0"])
        self.mm(ps[0][:, 0:16], [(self.onesb[:, :], sq[:, k, :]) for k in range(8)], ["onesb", "b0"], ["ps0"])
        rstd = S(0)[:, 0:16]
        self.act(rstd, ps[0][:, 0:16], AF.Ln, ["ps0"], ["s0"], scale=1.0 / D, bias=self.epsc[:, 0:1])
        self.act(rstd, rstd, AF.Exp, ["s0"], ["s0"], scale=-0.5)
        t = S(1)[:, 0:128].rearrange("p (k n) -> p k n", k=8)
        self.tt("dve", t, self.hs[:, :, :], rstd.unsqueeze(1).broadcast_to([128, 8, 16]), ALU.mult, ["hs", "s0"], ["s1"])
        self.tt("dve", t, t, bc3(self.vecs[:, V_FIN:V_FIN + 8], 16), ALU.mult, ["s1", "vecs"], ["s1"])
        self.ld(self.y_s[:, :, :], t, ["y_s"], "oys", reads=["s1"])
        self.outs.append("y_s")

    for f in (sample_modulate, sample_proj, sample_ffn, sampleA, sample_kv, sampleB, sample_phase):
        setattr(Prog, f.__name__, f)


_sample_methods()


T_BLK = 1024
N_BLK = 8
_CACHE = {}


def kernel(**inputs):
    if "nc" not in _CACHE:
        prog = Prog(T=T_BLK, NBLK=N_BLK, sample=True)
        _CACHE["nc"] = prog.build()
    nc = _CACHE["nc"]
    shared = prep_shared(inputs)
    in_maps = [prep_core(inputs, shared, c, T_BLK, N_BLK) for c in range(8)]
    res = run_bass_kernel_spmd(nc, in_maps, core_ids=list(range(8)))
    r = res.results
    NTOK = T_BLK * N_BLK
    g = lambda c, n, shp: np.asarray(r[c][n], np.float32).reshape(shp)
    y_prompt = np.stack([g(s, "yT", (128, 8, NTOK)).transpose(2, 1, 0).reshape(NTOK, 1024) for s in range(2)])
    y_sample = np.concatenate([g(c, "y_s", (128, 8, 16)).transpose(2, 1, 0).reshape(16, 1, 1024) for c in range(8)], 0)
    st_p = np.stack([g(s, "st_p", (2, 8, 128, 128)) for s in range(2)])
    st_s = np.concatenate([g(c, "st_s", (16, 2, 8, 128, 128)) for c in range(8)], 0)
    k_p = np.stack([g(s, "kT_p", (2, 64, 2, 128)).transpose(3, 2, 0, 1).reshape(128, 4, 64) for s in range(2)])
    v_p = np.stack([g(s, "v_p", (128, 4, 64)) for s in range(2)])
    k_s = np.concatenate([g(c, "ck_s", (16, 128, 4, 64)) for c in range(8)], 0)
    v_s = np.concatenate([g(c, "cv_s", (16, 128, 4, 64)) for c in range(8)], 0)
    f = np.ascontiguousarray
    return (f(y_prompt), f(y_sample), f(st_p), f(st_s), f(k_p), f(v_p), f(k_s), f(v_s))
```

```python
from contextlib import ExitStack
import numpy as np
import concourse.bass as bass
import concourse.mybir as mybir
from concourse.bass_utils import run_bass_kernel_spmd

F32 = mybir.dt.float32
BF16 = mybir.dt.bfloat16
I32 = mybir.dt.int32
AF = mybir.ActivationFunctionType
ALU = mybir.AluOpType
AX = mybir.AxisListType

SAME_ENGINE_SYNC = True


class KB:
    ENG = ("pe", "act", "dve", "pool", "sp")

    def __init__(self, nc):
        self.nc = nc
        self.stack = ExitStack()
        self.h = {"pe": nc.tensor, "act": nc.scalar, "dve": nc.vector, "pool": nc.gpsimd, "sp": nc.sync}
        self.prog = {e: [] for e in self.ENG}
        self.cnt = {e: 0 for e in self.ENG}
        self.seen = {e: {} for e in self.ENG}
        self.res_w = {}
        self.res_r = {}
        self.sems = {}
        self.n_inst = 0

    def sbuf(self, name, shape, dtype):
        return self.stack.enter_context(self.nc.sbuf_tensor("t_" + name, shape, dtype))

    def psum(self, name, shape, dtype):
        return self.stack.enter_context(self.nc.psum_tensor("p_" + name, shape, dtype))

    def _sem(self, key):
        if key not in self.sems:
            self.sems[key] = self.stack.enter_context(self.nc.semaphore("s_" + key.replace(":", "_")))
            self.cnt.setdefault(key, 0)
        return self.sems[key]

    @staticmethod
    def _is_psum(r):
        return (isinstance(r, str) and r.startswith("ps")) or (isinstance(r, tuple) and str(r[0]).startswith("ps"))

    def _waits(self, e, reads, writes):
        w = {}

        def need(key, val):
            if key == e and (not SAME_ENGINE_SYNC or val > self.cnt[e]):
                return
            if val > w.get(key, 0):
                w[key] = val
        for r in reads:
            lw = self.res_w.get(r)
            if lw:
                need(*lw)
            if self._is_psum(r):
                for k, v in self.res_r.get(r, {}).items():
                    if k != e:
                        need(k, v)
        for x in writes:
            lw = self.res_w.get(x)
            if lw:
                need(*lw)
            for k, v in self.res_r.get(x, {}).items():
                need(k, v)
        out = []
        for k, v in w.items():
            if self.seen[e].get(k, 0) < v:
                self.seen[e][k] = v
                out.append((k, v))
        return out

    def _mark(self, key, val, reads, writes):
        for r in reads:
            d = self.res_r.setdefault(r, {})
            if d.get(key, 0) < val:
                d[key] = val
        for x in writes:
            self.res_w[x] = (key, val)
            self.res_r[x] = {}

    def emit(self, e, fn, reads, writes, inc=True):
        waits = self._waits(e, reads, writes)
        self._sem(e)
        if inc:
            self.cnt[e] += 1
            val = self.cnt[e]
        else:
            val = self.cnt[e] + 1
        self._mark(e, val, reads, writes)
        self.prog[e].append((waits, fn, (e, 1) if inc else None))
        self.n_inst += 1

    def dma(self, q, key, out, in_, reads, writes, **kw):
        key = "d:" + key
        self._sem(key)
        waits = self._waits(q, reads, writes)
        self.cnt[key] += 16
        val = self.cnt[key]
        self._mark(key, val, reads, writes)
        self.prog[q].append((waits, lambda e: e.dma_start(out=out, in_=in_, **kw), (key, 16)))
        self.n_inst += 1

    def collective(self, kind, in_ap, out_ap, reads, writes, key, groups=None):
        key = "c:" + key
        self._sem(key)
        waits = self._waits("pool", reads, writes)
        self.cnt[key] += 1
        val = self.cnt[key]
        self._mark(key, val, reads, writes)
        groups = groups or [list(range(8))]
        self.prog["pool"].append((waits, lambda e: e.collective_compute(
            kind, ALU.bypass, replica_groups=groups, ins=[in_ap.opt()], outs=[out_ap.opt()]), (key, 1)))
        self.n_inst += 1

    def barrier(self):
        for e in self.ENG:
            waits = []
            for k, v in self.cnt.items():
                if v > 0 and k != e and self.seen[e].get(k, 0) < v:
                    self.seen[e][k] = v
                    waits.append((k, v))
            self.prog[e].append((waits, None, None))

    def finish(self, final_res):
        waits = self._waits("sp", final_res, [])
        self.prog["sp"].append((waits, None, None))
        nc = self.nc
        with nc.Block() as block:
            def run(e):
                def body(eng):
                    for waits, fn, inc in self.prog[e]:
                        for k, v in waits:
                            eng.wait_ge(self.sems[k], v)
                        if fn is None:
                            continue
                        ins = fn(eng)
                        if inc is not None:
                            ins.then_inc(self.sems[inc[0]], inc[1])
                return body
            block.tensor(run("pe"))
            block.scalar(run("act"))
            block.vector(run("dve"))
            block.gpsimd(run("pool"))
            block.sync(run("sp"))
        self.stack.close()


def kernel(**inputs):
    raise NotImplementedError


D = 1024
KD = 8
DFF = 2816
NFFC = 22
EPS = 1e-6
MASKV = -1e30
FFG = [(0, 4), (4, 4), (8, 4), (12, 4), (16, 4), (20, 2)]

V_NORM = 0
V_KVN = 64
V_FIN = 72
V_LB = 80
V_BADA = 96
V_BADAKV = 288
V_GN = 304
NV = 306
NCOEF = 208


def slot_heads():
    out = []
    for i in range(8):
        for half in range(2):
            out.append(i + 4 * half if i < 4 else 8 + (i - 4) + 4 * half)
    return out


def t5_bucket_np(d):
    d = np.asarray(d)
    dd = np.clip(d, 0, 127)
    large = 16 + (np.log(np.maximum(dd, 1).astype(np.float32) / np.float32(16)) / np.float32(np.log(8.0))
                  * np.float32(16)).astype(np.int32)
    large = np.clip(large, 0, 31)
    return np.where(dd < 16, dd, large)


class Prog:
    def __init__(self, T=1024, NBLK=8, sample=True, dbg=None, nlayers=4, NPRE=None, NOWN=None):
        if NPRE is None:
            self.types = ["own"] * NBLK
        else:
            self.types = ["pre"] * NPRE + ["prekv"] + ["own"] * NOWN
            NBLK = len(self.types)
        self.rmode = NPRE is not None
        self.own_blks = [i for i, t in enumerate(self.types) if t == "own"]
        self.T, self.NBLK, self.NB = T, NBLK, T // 512
        self.sample = sample
        self.dbg = dbg
        self.nlayers = nlayers
        self.NTOK = T * NBLK
        nc = self.nc = bass.Bass("TRN2", target_bir_lowering=False)
        kb = self.kb = KB(nc)
        di = lambda n, s: nc.dram_tensor(n, s, F32, kind="ExternalInput")
        do = lambda n, s: nc.dram_tensor(n, s, F32, kind="ExternalOutput")
        NTOK = self.NTOK
        self.xT = di("xT", [128, 8, NTOK])
        self.cT = di("cT", [128, 8, 17])
        self.xsT = di("xsT", [128, 8, 16])
        self.vecs_d = di("vecs", [128, NV])
        self.w_in_h = di("w_in_h", [2, 8, 1024, 512])
        self.w_o_a = di("w_o_a", [2, 1024, 1024])
        self.w_kv = di("w_kv", [1024, 512])
        self.w_ada = di("w_ada", [4, 1024, 6144])
        self.w_ada_kv = di("w_ada_kv", [1024, 2048])
        self.w_q_p = di("w_q_p", [2, 1024, 1024])
        self.w_o_p = di("w_o_p", [2, 1024, 1024])
        self.w_ffn_in = di("w_ffn_in", [4, 1024, 5632])
        self.w_ffn_out = di("w_ffn_out", [4, 2816, 1024])
        self.sinks_bc_d = di("sinks_bc", [128, 32])
        self.sink_s_d = di("sink_s", [16, 2])
        self.rb_ext_d = di("rb_ext", [33, 16])
        self.eline_d = di("eline", [33, 384])
        self.eline_s_d = di("eline_s", [33, 128])
        self.ident_d = di("ident", [128, 128])
        self.trimask_d = di("trimask", [64, 64])
        self.halfmask_d = di("halfmask", [128, 2])
        self.state_d = di("state", [16, 2, 8, 128, 128])
        self.ck_d = di("ck", [16, 128, 256])
        self.cv_d = di("cv", [16, 128, 256])
        self.ckT_d = di("ckT", [16, 256, 127])
        self.bmask_d = di("bmask", [128, NBLK])
        self.yT = do("yT", [128, 8, T * len(self.own_blks)])
        self.st_p = do("st_p", [2, 8, 128, 128])
        self.kT_p = do("kT_p", [128, 2, 128])
        self.v_p = do("v_p", [128, 256])
        self.y_s = do("y_s", [128, 8, 16])
        self.st_s = do("st_s", [16, 2, 8, 128, 128])
        self.ck_s = do("ck_s", [16, 128, 256])
        self.cv_s = do("cv_s", [16, 128, 256])
        if dbg is not None:
            self.dbg_d = do("dbg", [128, 8, T])
        self.lines_d = nc.dram_tensor("lines", [16, 128, 384], F32)
        self.rowk_d = nc.dram_tensor("rowk", [16, 256], BF16)
        self.outs = []
        self.alloc()

    def alloc(self):
        kb, T = self.kb, self.T
        sb = kb.sbuf
        W = 8 * T + 3 * 4 * T
        W = max(W, 16384)
        self.arena = sb("arena", [128, W], F32)
        ar = self.arena
        self.hT = ar[:, 0:8 * T].rearrange("p (k t) -> p k t", k=8)
        o = 8 * T
        self.xn = ar[:, o:o + 4 * T].bitcast(BF16).rearrange("p (k t) -> p k t", k=8)
        self.og = ar[:, o + 4 * T:o + 8 * T].bitcast(BF16).rearrange("p (k t) -> p k t", k=8)
        self.qT = ar[:, o + 8 * T:o + 12 * T].bitcast(BF16).rearrange("p (k t) -> p k t", k=8)
        def carve(off, words, dt=F32):
            v = ar[:, off:off + words]
            return v if dt == F32 else v.bitcast(dt)
        o = 0
        self.adaT = carve(o, 208 * 17).rearrange("p (g n) -> p g n", n=17); o += 208 * 17
        self.coefS = carve(o, 208 * 16).rearrange("p (g n) -> p g n", n=16); o += 208 * 16
        self.csb = carve(o, 68, BF16).rearrange("p (k n) -> p k n", k=8); o += 68
        self.hs = carve(o, 128).rearrange("p (k n) -> p k n", k=8); o += 128
        self.xns = carve(o, 64, BF16).rearrange("p (k n) -> p k n", k=8); o += 64
        self.ogs = carve(o, 64, BF16).rearrange("p (k n) -> p k n", k=8); o += 64
        self.as_ = carve(o, 32, BF16).rearrange("p (k n) -> p k n", k=4); o += 32
        self.qTs = carve(o, 64, BF16).rearrange("p (k n) -> p k n", k=8); o += 64
        self.Qb2 = carve(o, 256, BF16).rearrange("p (b c s) -> p b c s", b=16, c=2); o += 256
        self.Sin = carve(o, 2048).rearrange("p (b v) -> p b v", b=16); o += 2048
        self.Sout = carve(o, 2048).rearrange("p (b v) -> p b v", b=16); o += 2048
        self.KTs = carve(o, 2048, BF16).rearrange("p (c b t) -> p c b t", c=2, b=16); o += 2048
        self.Vs = carve(o, 2048, BF16).rearrange("p (b n) -> p b n", b=16); o += 2048
        self.pTs = carve(o, 128, BF16).rearrange("p (b s) -> p b s", b=16); o += 128
        self.bias_s = carve(o, 128); o += 128
        self.sink_s = carve(o, 2); o += 2
        assert o <= W, (o, W)
        self.NSLOT = 5
        self.wsl = [sb("wsl%d" % i, [128, 4096], BF16) for i in range(self.NSLOT)]
        self.NS = 14
        self.scr = sb("scr", [128, self.NS, 512], F32)
        self.NBS = 10
        self.bscr = sb("bscr", [128, self.NBS, 512], BF16)
        self.G = sb("G", [128, 513], F32)
        self.KT = sb("KT", [128, 2, 128 + T], BF16)
        self.V = sb("V", [128, 1 + T // 128, 256], BF16)
        self.bias = sb("bias", [128, 16, 256], F32)
        self.Sst = sb("Sst", [128, 16, 128], F32)
        self.coefP = sb("coefP", [128, NCOEF], F32)
        self.vecs = sb("vecs", [128, NV], F32)
        self.lbs = sb("lbs", [128, 2, 8], F32)
        self.sinks_bc = sb("sinks_bc", [128, 32], F32)
        self.ident = sb("ident", [128, 128], F32)
        self.identb = sb("identb", [128, 128], BF16)
        self.onesb = sb("onesb", [128, 128], BF16)
        self.ones64 = sb("ones64", [128, 512], F32)
        self.trimask = sb("trimask", [64, 64], F32)
        self.small = sb("small", [128, 64], F32)
        self.epsc = sb("epsc", [128, 2], F32)
        self.bmask = sb("bmask", [128, self.NBLK], F32)
        self.hmask = sb("hmask", [128, 1], F32)
        self.attm = sb("attm", [64, 8, 64], BF16)
        self.ps = [kb.psum("ps%d" % i, [128, 512], F32) for i in range(8)]
        self.ws_i = 0

    def S(self, i):
        return self.scr[:, i, :]

    def B(self, i):
        return self.bscr[:, i, :]

    def act(self, out, in_, func, reads, writes, **kw):
        self.kb.emit("act", lambda e: e.activation(out=out, in_=in_, func=func, **kw), reads, writes)

    def tt(self, eng, out, a, b, op, reads, writes):
        self.kb.emit(eng, lambda e: e.tensor_tensor(out=out, in0=a, in1=b, op=op), reads, writes)

    def ts(self, eng, out, a, s1, s2, op0, op1, reads, writes):
        self.kb.emit(eng, lambda e: e.tensor_scalar(out=out, in0=a, scalar1=s1, scalar2=s2, op0=op0, op1=op1),
                     reads, writes)

    def stt(self, out, a, s, b, op0, op1, reads, writes):
        self.kb.emit("dve", lambda e: e.scalar_tensor_tensor(out=out, in0=a, scalar=s, in1=b, op0=op0, op1=op1),
                     reads, writes)

    def cp(self, eng, out, in_, reads, writes):
        if eng == "act":
            self.act(out, in_, AF.Copy, reads, writes)
        else:
            self.kb.emit(eng, lambda e: e.tensor_copy(out=out, in_=in_), reads, writes)

    def mm(self, out, pairs, reads, writes):
        n = len(pairs)
        for i, (l, r) in enumerate(pairs):
            self.kb.emit("pe", lambda e, l=l, r=r, i=i: e.matmul(out, lhsT=l, rhs=r, start=(i == 0), stop=(i == n - 1)),
                         reads, writes, inc=(i == n - 1))

    def ld(self, out, in_, writes, key, q="sp", reads=()):
        self.kb.dma(q, key, out, in_, list(reads), writes)

    def slab(self, dram_ap, kind):
        i = self.ws_i % self.NSLOT
        self.ws_i += 1
        t = self.wsl[i]
        res = ("w", i)
        if kind == "K8":
            ncol = dram_ap.shape[1]
            view = t[:, :].rearrange("p (k n) -> p k n", k=8)
            self.kb.dma("pool", "w%d" % i, view[:, :, 0:ncol], dram_ap.rearrange("(k p) n -> p k n", p=128), [], [res])
        else:
            nr = dram_ap.shape[0] // 128
            view = t[:, :].rearrange("p (c n) -> p c n", c=4)
            self.kb.dma("pool", "w%d" % i, view[:, 0:nr, :], dram_ap.rearrange("(c p) n -> p c n", p=128), [], [res])
        return view, res

    def stream(self, items):
        q = []
        it = iter(items)
        for _ in range(2):
            x = next(it, None)
            if x is not None:
                q.append(self.slab(*x))
        while q:
            cur = q.pop(0)
            x = next(it, None)
            if x is not None:
                q.append(self.slab(*x))
            yield cur

    def setup(self):
        kb = self.kb
        self.ld(self.vecs[:, :], self.vecs_d[:, :], ["vecs"], "c11")
        self.ld(self.sinks_bc[:, :], self.sinks_bc_d[:, :], ["sinks"], "c12")
        self.ld(self.ident[:, :], self.ident_d[:, :], ["ident"], "c13")
        self.ld(self.trimask[:, :], self.trimask_d[:, :], ["trimask"], "c14")
        kb.emit("dve", lambda e: e.tensor_copy(out=self.identb[:, :], in_=self.ident[:, :]), ["ident"], ["identb"])
        self.ld(self.bmask[:, :], self.bmask_d[:, :], ["bmask"], "c30")
        if self.rmode:
            pk = self.types.index("prekv")
            self.ts("dve", self.hmask[:, 0:1], self.bmask[:, pk:pk + 1], 1.0, -MASKV, ALU.subtract, ALU.mult, ["bmask"], ["hmask"])
        kb.emit("dve", lambda e: e.memset(self.onesb[:, :], 1.0), [], ["onesb"])
        kb.emit("dve", lambda e: e.memset(self.ones64[:, :], 1.0), [], ["ones64"])
        kb.emit("dve", lambda e: e.memset(self.G[:, 0:1], 0.0), [], ["G0"])
        kb.emit("dve", lambda e: e.memset(self.attm[:, :, :], 0.0), [], ["attm"])
        kb.emit("dve", lambda e: e.memset(self.epsc[:, 0:1], EPS), [], ["epsc"])
        kb.emit("dve", lambda e: e.memset(self.epsc[:, 1:2], 1.0), [], ["epsc"])
        kb.emit("dve", lambda e: e.memset(self.Sst[:, :, :], 0.0), [], ["Sst"])
        kb.emit("dve", lambda e: e.memset(self.lbs[:, 0, :], 0.0), [], ["lbs"])
        t = self.small[:, 0:8]
        self.tt("dve", t, self.vecs[:, V_LB:V_LB + 8], self.vecs[:, V_LB + 8:V_LB + 16], ALU.subtract, ["vecs"], ["small"])
        self.act(t, t, AF.Exp, ["small"], ["small"])
        self.ts("dve", t, t, 1.0, None, ALU.add, ALU.bypass, ["small"], ["small"])
        kb.emit("dve", lambda e: e.reciprocal(out=self.lbs[:, 1, :], in_=t), ["small"], ["lbs"])
        rb = self.S(0)[0:33, 0:16]
        el = self.S(1)[0:33, 0:384]
        self.ld(rb, self.rb_ext_d[:, :], ["s0"], "c15")
        self.ld(el, self.eline_d[:, :], ["s1"], "c16")
        pl = self.ps[0][0:16, 0:384]
        self.mm(pl, [(rb, el)], ["s0", "s1"], ["ps0"])
        ln = self.S(2)[0:16, 0:384]
        self.cp("dve", ln, pl, ["ps0"], ["s2"])
        src = bass.AP(ln.tensor, ln.offset, [list(ln.ap[0]), [0, 128], [1, 384]])
        self.ld(self.lines_d[:, :, :], src, ["lines"], "c1", reads=["s2"])
        tv = bass.AP(self.lines_d, 127, [[383, 128], [128 * 384, 16], [1, 256]])
        self.ld(self.bias[:, :, :], tv, ["bias"], "c1", reads=["lines"])

    def coef(self, l, which):
        if l == 4:
            return 192, 200, None
        base = l * 48 + which * 24
        return base, base + 8, base + 16

    def modulate(self, l, which):
        a0, b0, _ = self.coef(l, which)
        for sb in range(self.NB):
            tok = slice(sb * 512, (sb + 1) * 512)
            hs = ("h", sb)
            sq = self.bscr[:, 0:8, :]
            sqr = ["b%d" % k for k in range(8)]
            self.act(sq, self.hT[:, :, tok], AF.Square, [hs], sqr)
            pss = self.ps[0]
            self.mm(pss[:, :], [(self.onesb[:, :], self.bscr[:, k, :]) for k in range(8)], ["onesb"] + sqr, ["ps0"])
            rstd = self.S(0)
            self.act(rstd, pss[:, :], AF.Ln, ["ps0"], ["s0"], scale=1.0 / D, bias=self.epsc[:, 0:1])
            self.act(rstd, rstd, AF.Exp, ["s0"], ["s0"], scale=-0.5)
            for k in range(8):
                t = self.S(1 + (k % 2))
                r = "s%d" % (1 + (k % 2))
                self.stt(t, self.hT[:, k, tok], self.coefP[:, a0 + k:a0 + k + 1], rstd, ALU.mult, ALU.mult,
                         [hs, "s0", "coefP"], [r])
                self.act(self.xn[:, k, tok], t, AF.Identity, [r, "coefP"], [("xn", sb)],
                         bias=self.coefP[:, b0 + k:b0 + k + 1], scale=1.0)

    def proj_residual(self, slabs, src, src_res, gcol, nk):
        dout = 0
        for view, wres in self.stream(slabs):
            for dd in range(4):
                for sb in range(self.NB):
                    tok = slice(sb * 512, (sb + 1) * 512)
                    pb = 4 + ((dout * self.NB + sb) % 4)
                    p = self.ps[pb]
                    self.mm(p[:, :], [(view[:, k, dd * 128:(dd + 1) * 128], src[:, k, tok]) for k in range(nk)],
                            [wres] + src_res(sb), ["ps%d" % pb])
                    self.stt(self.hT[:, dout, tok], p[:, :], self.coefP[:, gcol + dout:gcol + dout + 1],
                             self.hT[:, dout, tok], ALU.mult, ALU.add, ["ps%d" % pb, ("h", sb), "coefP"], [("h", sb)])
                dout += 1

    def ffn(self, l):
        self.modulate(l, 1)
        _, _, g2 = self.coef(l, 1)
        items = []
        for (c0, n) in FFG:
            items.append((self.w_ffn_in[l, :, c0 * 128:(c0 + n) * 128], "K8"))
            items.append((self.w_ffn_in[l, :, DFF + c0 * 128:DFF + (c0 + n) * 128], "K8"))
            items.append((self.w_ffn_out[l, c0 * 128:(c0 + n) * 128, :], "R4"))
        st = self.stream(items)
        a = self.og
        for (c0, n) in FFG:
            wg, rg = next(st)
            wu, ru = next(st)
            wo, ro = next(st)
            for sb in range(self.NB):
                tok = slice(sb * 512, (sb + 1) * 512)
                for c in range(n):
                    pg, pu = (0, 1) if (c % 2 == 0) else (2, 3)
                    self.mm(self.ps[pg][:, :], [(wg[:, k, c * 128:(c + 1) * 128], self.xn[:, k, tok]) for k in range(8)],
                            [rg, ("xn", sb)], ["ps%d" % pg])
                    self.mm(self.ps[pu][:, :], [(wu[:, k, c * 128:(c + 1) * 128], self.xn[:, k, tok]) for k in range(8)],
                            [ru, ("xn", sb)], ["ps%d" % pu])
                    sg = self.S(c % 2)
                    self.act(sg, self.ps[pg][:, :], AF.Silu, ["ps%d" % pg], ["s%d" % (c % 2)])
                    self.tt("dve", a[:, c, tok], sg, self.ps[pu][:, :], ALU.mult, ["s%d" % (c % 2), "ps%d" % pu],
                            [("og", c, sb)])
            for sb in range(self.NB):
                tok = slice(sb * 512, (sb + 1) * 512)
                for dout in range(8):
                    pb = 4 + (dout % 4)
                    p = self.ps[pb]
                    self.mm(p[:, :], [(wo[:, c, dout * 128:(dout + 1) * 128], a[:, c, tok]) for c in range(n)],
                            [ro] + [("og", c, sb) for c in range(n)], ["ps%d" % pb])
                    self.stt(self.hT[:, dout, tok], p[:, :], self.coefP[:, g2 + dout:g2 + dout + 1],
                             self.hT[:, dout, tok], ALU.mult, ALU.add, ["ps%d" % pb, ("h", sb), "coefP"], [("h", sb)])

    def headA_sub(self, l, j, sb, w, wres, blk=0, state_only=False):
        kb, ps = self.kb, self.ps
        tok = slice(sb * 512, (sb + 1) * 512)
        xr = ("xn", sb)
        S, Bb = self.S, self.B
        xk = lambda k: self.xn[:, k, tok]
        so = state_only
        if not so:
            self.mm(ps[0][:, :], [(w[:, k, 0:128], xk(k)) for k in range(8)], [wres, xr], ["ps0"])
        self.mm(ps[1][:, :], [(w[:, k, 128:256], xk(k)) for k in range(8)], [wres, xr], ["ps1"])
        if not so:
            self.mm(ps[2][:, :], [(w[:, k, 384:512], xk(k)) for k in range(8)], [wres, xr], ["ps2"])
            self.act(S(0), ps[0][:, :], AF.Silu, ["ps0"], ["s0"])
            self.act(Bb(0), ps[2][:, :], AF.Silu, ["ps2"], ["b0"])
        lbc = self.lbs[:, l, j:j + 1]
        self.act(S(1), ps[1][:, :], AF.Exp, ["ps1"], ["s1"], scale=-1.0)
        self.act(S(2), S(1), AF.Ln, ["s1", "lbs"], ["s2"], scale=lbc, bias=self.epsc[:, 1:2])
        self.act(S(3), S(1), AF.Ln, ["s1"], ["s3"], scale=1.0, bias=self.epsc[:, 1:2])
        self.tt("dve", S(2), S(2), S(3), ALU.subtract, ["s2", "s3"], ["s2"])
        Gc = self.G[:, 1:513]
        kb.emit("dve", lambda e: e.tensor_tensor_scan(out=Gc, data0=self.ones64[:, :], data1=S(2), initial=0.0,
                                                      op0=ALU.mult, op1=ALU.add), ["s2", "ones64"], ["G"])
        self.act(S(4), S(2), AF.Exp, ["s2"], ["s4"])
        self.ts("dve", S(3), S(4), -1.0, 1.0, ALU.mult, ALU.add, ["s4"], ["s3"])
        G3 = Gc.rearrange("p (c t) -> p c t", c=8)
        bc = lambda a: a.unsqueeze(2).broadcast_to([128, 8, 64])
        v3 = lambda i: self.scr[:, i, :].rearrange("p (c t) -> p c t", c=8)
        if not so:
            self.tt("dve", v3(5), G3, bc(self.G[:, 32:513:64]), ALU.subtract, ["G", "G0"], ["s5"])
        self.tt("dve", v3(8), G3, bc(self.G[:, 0:512:64]), ALU.subtract, ["G", "G0"], ["s8"])
        self.tt("dve", v3(9), G3, bc(self.G[:, 64:513:64]), ALU.subtract, ["G", "G0"], ["s9"])
        if not so:
            self.act(S(6), S(5), AF.Exp, ["s5"], ["s6"])
            self.act(S(7), S(5), AF.Exp, ["s5"], ["s7"], scale=-1.0)
        self.act(S(8), S(8), AF.Exp, ["s8"], ["s8"])
        self.act(S(9), S(9), AF.Exp, ["s9"], ["s9"], scale=-1.0)
        if not so:
            self.tt("dve", Bb(1), S(0), S(6), ALU.mult, ["s0", "s6"], ["b1"])
            self.tt("dve", S(10), S(0), S(8), ALU.mult, ["s0", "s8"], ["s10"])
            self.tt("dve", Bb(2), S(3), S(7), ALU.mult, ["s3", "s7"], ["b2"])
        self.tt("dve", Bb(3), S(3), S(9), ALU.mult, ["s3", "s9"], ["b3"])
        vtok = self.bscr[0:64, 6:8, :].rearrange("p a (c d) -> p (a c) d", d=128)
        for half in range(2):
            for cc in range(4):
                c = half * 4 + cc
                t0 = sb * 512 + c * 64
                self.mm(ps[3][0:64, cc * 128:(cc + 1) * 128],
                        [(self.xn[:, k, t0:t0 + 64], w[:, k, 256:384]) for k in range(8)], [wres, xr], ["ps3"])
            if self.rmode:
                self.act(self.bscr[0:64, 6 + half, :], ps[3][0:64, :], AF.Identity, ["ps3", "bmask"], ["b%d" % (6 + half)],
                         scale=self.bmask[0:64, blk:blk + 1], bias=0.0)
            else:
                self.cp("act", self.bscr[0:64, 6 + half, :], ps[3][0:64, :], ["ps3"], ["b%d" % (6 + half)])
        pk = ps[5][:, :].bitcast(BF16)
        for c in range(8):
            kb.emit("pe", lambda e, c=c: e.transpose(out=pk[0:64, c * 128:(c + 1) * 128],
                                                     in_=Bb(3)[:, c * 64:(c + 1) * 64], identity=self.identb[:, :]),
                    ["b3", "identb"], ["ps5"], inc=(c == 7))
        kh = self.bscr[0:64, 8:10, :].rearrange("p a (c d) -> p (a c) d", d=128)
        self.cp("act", self.bscr[0:64, 8:10, :].rearrange("p a n -> p (a n)"), pk[0:64, :], ["ps5"], ["b8", "b9"])
        if not so:
            for c in range(8):
                c0 = c * 64
                kb.emit("pe", lambda e, c0=c0: e.matmul(ps[4][0:64, c0 + 32:c0 + 64], lhsT=Bb(2)[:, c0:c0 + 64],
                                                        rhs=Bb(1)[:, c0 + 32:c0 + 64], start=True, stop=True),
                        ["b1", "b2"], ["ps4"], inc=False)
                kb.emit("pe", lambda e, c0=c0: e.matmul(ps[4][0:32, c0:c0 + 32], lhsT=Bb(2)[:, c0:c0 + 32],
                                                        rhs=Bb(1)[:, c0:c0 + 32], start=True, stop=True),
                        ["b1", "b2"], ["ps4"], inc=(c == 7))
            attm = self.attm[:, :, :]
            p43 = ps[4][0:64, :].rearrange("p (c t) -> p c t", c=8)
            self.tt("dve", self.attm[:, :, 32:64], p43[:, :, 32:64],
                    self.trimask[:, 32:64].unsqueeze(1).broadcast_to([64, 8, 32]), ALU.mult, ["ps4", "trimask"], ["attm"])
            self.tt("dve", self.attm[0:32, :, 0:32], p43[0:32, :, 0:32],
                    self.trimask[0:32, 0:32].unsqueeze(1).broadcast_to([32, 8, 32]), ALU.mult, ["ps4", "trimask"], ["attm"])
        Sj = self.Sst[:, l * 8 + j, :]
        sres = ("S", l, j)
        for c in range(8):
            cs = slice(c * 64, (c + 1) * 64)
            if not so:
                kb.emit("pe", lambda e, c=c, cs=cs: e.matmul(ps[7][:, cs], lhsT=vtok[:, c, :], rhs=self.attm[:, c, :], start=True, stop=False),
                        ["b6", "b7", "attm"], ["ps7"], inc=False)
                kb.emit("pe", lambda e, cs=cs: e.matmul(ps[7][:, cs], lhsT=Sj, rhs=S(10)[:, cs], start=False, stop=True),
                        [sres, "s10"], ["ps7"], inc=True)
            sl = c % 4
            pn = ps[6][:, sl * 128:(sl + 1) * 128]
            kb.emit("pe", lambda e, c=c, pn=pn: e.matmul(pn, lhsT=kh[:, c, :], rhs=vtok[:, c, :], start=True, stop=True),
                    ["b8", "b9", "b6", "b7"], ["ps6"], inc=True)
            dcol = self.scr[:, 8, c * 64 + 63:c * 64 + 64]
            self.stt(Sj, Sj, dcol, pn, ALU.mult, ALU.add, [sres, "s8", "ps6"], [sres])
        if so:
            return
        self.act(Bb(4), ps[7][:, :], AF.Square, ["ps7"], ["b4"])
        self.mm(ps[0][:, :], [(self.onesb[:, :], Bb(4))], ["onesb", "b4"], ["ps0"])
        self.act(S(11), ps[0][:, :], AF.Ln, ["ps0"], ["s11"], scale=1.0 / 128, bias=self.epsc[:, 0:1])
        self.act(S(11), S(11), AF.Exp, ["s11"], ["s11"], scale=-0.5)
        self.stt(S(12), ps[7][:, :], self.vecs[:, V_GN + l:V_GN + l + 1], S(11), ALU.mult, ALU.mult,
                 ["ps7", "s11", "vecs"], ["s12"])
        self.tt("dve", self.og[:, j, tok], S(12), Bb(0), ALU.mult, ["s12", "b0"], [("og", j, sb)])

    def layerA(self, blk, l, state_only=False):
        self.modulate(l, 0)
        items = [(self.w_in_h[l, j, :, :], "K8") for j in range(8)]
        for j, (w, wres) in enumerate(self.stream(items)):
            for sb in range(self.NB):
                self.headA_sub(l, j, sb, w, wres, blk=blk, state_only=state_only)
        if state_only:
            return
        _, _, g1 = self.coef(l, 0)
        self.proj_residual([(self.w_o_a[l, :, 0:512], "K8"), (self.w_o_a[l, :, 512:1024], "K8")], self.og,
                           lambda sb: [("og", j, sb) for j in range(8)], g1, 8)

    def kvproj(self, blk):
        T = self.T
        self.modulate(4, 0)
        (w, wres), = list(self.stream([(self.w_kv[:, :], "K8")]))
        last = (blk == self.NBLK - 1)
        for c in range(2):
            for sb in range(self.NB):
                tok = slice(sb * 512, (sb + 1) * 512)
                pb = c * 2 + (sb % 2)
                self.mm(self.ps[pb][:, :], [(w[:, k, c * 128:(c + 1) * 128], self.xn[:, k, tok]) for k in range(8)],
                        [wres, ("xn", sb)], ["ps%d" % pb])
                self.cp("act", self.KT[:, c, 128 + sb * 512:128 + (sb + 1) * 512], self.ps[pb][:, :], ["ps%d" % pb], ["KT"])
                if last and sb == self.NB - 1:
                    self.cp("dve", self.scr[:, 0, c * 128:(c + 1) * 128], self.ps[pb][:, 384:512], ["ps%d" % pb], ["s0"])
        if last:
            self.ld(self.kT_p[:, :, :], self.scr[:, 0, 0:256].rearrange("p (c t) -> p c t", c=2), ["kT_p"], "ok", reads=["s0"])
            self.outs.append("kT_p")
        for tt_ in range(T // 128):
            pb = 4 + (tt_ % 4)
            self.mm(self.ps[pb][:, 0:256], [(self.xn[:, k, tt_ * 128:(tt_ + 1) * 128], w[:, k, 256:512]) for k in range(8)],
                    [wres, ("xn", tt_ // 4)], ["ps%d" % pb])
            self.cp("act", self.V[:, 1 + tt_, :], self.ps[pb][:, 0:256], ["ps%d" % pb], ["V"])
            if last and tt_ == T // 128 - 1:
                self.cp("dve", self.scr[:, 1, 0:256], self.ps[pb][:, 0:256], ["ps%d" % pb], ["s1"])
                self.ld(self.v_p[:, :], self.scr[:, 1, 0:256], ["v_p"], "ov", reads=["s1"])
                self.outs.append("v_p")

    def layerB(self, blk, jb):
        l = 2 + jb
        kb, ps, T = self.kb, self.ps, self.T
        self.modulate(l, 0)
        items = [(self.w_q_p[jb, :, 0:512], "K8"), (self.w_q_p[jb, :, 512:1024], "K8")]
        i = 0
        for w, wres in self.stream(items):
            for dd in range(4):
                for sb in range(self.NB):
                    tok = slice(sb * 512, (sb + 1) * 512)
                    pb = 6 + ((i * self.NB + sb) % 2)
                    self.mm(ps[pb][:, :], [(w[:, k, dd * 128:(dd + 1) * 128], self.xn[:, k, tok]) for k in range(8)],
                            [wres, ("xn", sb)], ["ps%d" % pb])
                    self.act(self.qT[:, i, tok], ps[pb][:, :], AF.Copy, ["ps%d" % pb], [("qT", i)], scale=0.125)
                i += 1
        it = 0
        for i in range(8):
            c = i // 4
            for qt in range(T // 128):
                qtok = slice(qt * 128, (qt + 1) * 128)
                first = (qt == 0 and not any(t in ("prekv", "own") for t in self.types[:blk]))
                halo_m = (self.rmode and qt == 0 and blk == self.own_blks[0])
                nk = 128 if first else 256
                koff = qt * 128 + (128 if first else 0)
                for half in range(2):
                    hp = slice(half * 64, (half + 1) * 64)
                    slot = i * 2 + half
                    par = it % 2
                    it += 1
                    pl = ps[par]
                    plr = "ps%d" % par
                    kb.emit("pe", lambda e, pl=pl, hp=hp, i=i, qtok=qtok, c=c, koff=koff, nk=nk: e.matmul(
                        pl[:, 0:nk], lhsT=self.qT[hp, i, qtok], rhs=self.KT[hp, c, koff:koff + nk], start=True, stop=True),
                        [("qT", i), "KT"], [plr])
                    s = self.scr[:, par, 0:nk]
                    sr = "s%d" % par
                    self.tt("dve", s, pl[:, 0:nk], self.bias[:, slot, 256 - nk:256], ALU.add, [plr, "bias"], [sr])
                    if halo_m:
                        self.ts("dve", s[:, 0:128], s[:, 0:128], self.hmask[:, 0:1], None, ALU.add, ALU.bypass, [sr, "hmask"], [sr])
                    sm = self.small[:, par * 8:par * 8 + 8]
                    smr = "sm%d" % par
                    kb.emit("dve", lambda e, sm=sm, s=s: e.tensor_reduce(out=sm[:, 0:1], in_=s, axis=AX.X, op=ALU.max), [sr], [smr])
                    sk = self.sinks_bc[:, jb * 16 + slot:jb * 16 + slot + 1]
                    self.tt("dve", sm[:, 1:2], sm[:, 0:1], sk, ALU.max, [smr, "sinks"], [smr])
                    self.ts("dve", sm[:, 2:3], sm[:, 1:2], -1.0, None, ALU.mult, ALU.bypass, [smr], [smr])
                    self.act(s, s, AF.Exp, [sr, smr], [sr, smr], bias=sm[:, 2:3], scale=1.0, accum_out=sm[:, 3:4])
                    self.act(sm[:, 4:5], sk, AF.Exp, [smr, "sinks"], [smr], bias=sm[:, 2:3], scale=1.0)
                    self.tt("dve", sm[:, 5:6], sm[:, 3:4], sm[:, 4:5], ALU.add, [smr], [smr])
                    kb.emit("dve", lambda e, sm=sm: e.reciprocal(out=sm[:, 6:7], in_=sm[:, 5:6]), [smr], [smr])
                    pn = self.bscr[:, par, 0:nk]
                    pnr = "b%d" % par
                    self.ts("dve", pn, s, sm[:, 6:7], None, ALU.mult, ALU.bypass, [sr, smr], [pnr])
                    ptb = ps[2 + par][:, :].bitcast(BF16)
                    ptr = "ps%d" % (2 + par)
                    nkb = nk // 128
                    for kb2 in range(nkb):
                        kb.emit("pe", lambda e, ptb=ptb, pn=pn, kb2=kb2: e.transpose(
                            out=ptb[:, kb2 * 128:(kb2 + 1) * 128], in_=pn[:, kb2 * 128:(kb2 + 1) * 128], identity=self.identb[:, :]),
                            [pnr, "identb"], [ptr], inc=(kb2 == nkb - 1))
                    pT = self.bscr[:, 2 + par, 0:nk]
                    pTr = "b%d" % (2 + par)
                    self.cp("act", pT, ptb[:, 0:nk], [ptr], [pTr])
                    po = ps[4 + par][hp, 0:128]
                    por = "ps%d" % (4 + par)
                    vcol = (2 * c + half) * 64
                    for kb2 in range(nkb):
                        vt = (qt + kb2) if not first else 1
                        kb.emit("pe", lambda e, po=po, vt=vt, vcol=vcol, pT=pT, kb2=kb2, nkb=nkb: e.matmul(
                            po, lhsT=self.V[:, vt, vcol:vcol + 64], rhs=pT[:, kb2 * 128:(kb2 + 1) * 128],
                            start=(kb2 == 0), stop=(kb2 == nkb - 1)), ["V", pTr], [por], inc=(kb2 == nkb - 1))
                    self.cp("dve", self.og[hp, i, qtok], po, [por], [("og", i, qt // 4)])
        _, _, g1 = self.coef(l, 0)
        self.proj_residual([(self.w_o_p[jb, :, 0:512], "K8"), (self.w_o_p[jb, :, 512:1024], "K8")], self.og,
                           lambda sb: [("og", j, sb) for j in range(8)], g1, 8)

    def final(self, blk):
        T = self.T
        for sb in range(self.NB):
            tok = slice(sb * 512, (sb + 1) * 512)
            hs = ("h", sb)
            sqr = ["b%d" % k for k in range(8)]
            self.act(self.bscr[:, 0:8, :], self.hT[:, :, tok], AF.Square, [hs], sqr)
            self.mm(self.ps[0][:, :], [(self.onesb[:, :], self.bscr[:, k, :]) for k in range(8)], ["onesb"] + sqr, ["ps0"])
            rstd = self.S(0)
            self.act(rstd, self.ps[0][:, :], AF.Ln, ["ps0"], ["s0"], scale=1.0 / D, bias=self.epsc[:, 0:1])
            self.act(rstd, rstd, AF.Exp, ["s0"], ["s0"], scale=-0.5)
            for k in range(8):
                self.stt(self.scr[:, 4 + k, :], self.hT[:, k, tok], self.vecs[:, V_FIN + k:V_FIN + k + 1], rstd,
                         ALU.mult, ALU.mult, [hs, "s0", "vecs"], ["s%d" % (4 + k)])
            ob = self.own_blks.index(blk)
            self.ld(self.yT[:, :, ob * T + sb * 512:ob * T + (sb + 1) * 512], self.scr[:, 4:12, :], ["yT"], "y%d" % sb,
                    reads=["s%d" % (4 + k) for k in range(8)])
        if "yT" not in self.outs:
            self.outs.append("yT")

    def block(self, blk):
        T = self.T
        for sb in range(self.NB):
            self.ld(self.hT[:, :, sb * 512:(sb + 1) * 512], self.xT[:, :, blk * T + sb * 512:blk * T + (sb + 1) * 512],
                    [("h", sb)], "x%d" % sb)
        typ = self.types[blk]
        for l in range(self.nlayers):
            if typ == "pre" and l >= 1:
                if l == 1:
                    self.layerA(blk, 1, state_only=True)
                continue
            if typ == "prekv" and l >= 2:
                continue
            if l < 2:
                self.layerA(blk, l)
            else:
                self.layerB(blk, l - 2)
            if self.dbg == l + 10 and blk == 0:
                self.ld(self.dbg_d[:, :, :], self.hT[:, :, :], ["dbg"], "o2", reads=[("h", sb) for sb in range(self.NB)])
                self.outs.append("dbg")
            self.ffn(l)
            if l == 1:
                self.kvproj(blk)
            if self.dbg == l and blk == 0:
                self.ld(self.dbg_d[:, :, :], self.hT[:, :, :], ["dbg"], "o2", reads=[("h", sb) for sb in range(self.NB)])
                self.outs.append("dbg")
        if self.nlayers == 4 and typ == "own":
            self.final(blk)
        if blk < self.NBLK - 1 and self.nlayers > 1 and typ != "pre":
            self.cp("dve", self.KT[:, :, 0:128], self.KT[:, :, T:T + 128], ["KT"], ["KT"])
            self.cp("dve", self.V[:, 0, :], self.V[:, T // 128, :], ["V"], ["V"])
        if blk == self.NBLK - 1:
            self.ld(self.st_p.ap().rearrange("l j k v -> k (l j) v"), self.Sst[:, :, :], ["st_p"], "ost", reads=[("S", l, j) for l in range(2) for j in range(8)])
            self.outs.append("st_p")

    def ada_phase(self):
        kb, ps = self.kb, self.ps
        cs = self.scr[:, 0, 0:136].rearrange("p (k n) -> p k n", k=8)
        self.ld(cs, self.cT[:, :, :], ["s0"], "c17")
        csb = self.csb
        self.act(csb[:, :, :], cs, AF.Silu, ["s0"], ["csb"])
        items = []
        for l in range(4):
            for s in range(12):
                items.append((self.w_ada[l, :, s * 512:(s + 1) * 512], "K8"))
        for s in range(4):
            items.append((self.w_ada_kv[:, s * 512:(s + 1) * 512], "K8"))
        n = 0
        for w, wres in self.stream(items):
            l, s = (n // 12, n % 12) if n < 48 else (4, n - 48)
            n += 1
            for dd in range(4):
                ch = s * 4 + dd
                gidx = l * 48 + ch
                bcol = (V_BADA + l * 48 + ch) if l < 4 else (V_BADAKV + ch)
                pb = gidx % 4
                self.mm(ps[pb][:, 0:17], [(w[:, k, dd * 128:(dd + 1) * 128], csb[:, k, :]) for k in range(8)],
                        [wres, "csb"], ["ps%d" % pb])
                self.ts("dve", self.adaT[:, gidx, :], ps[pb][:, 0:17], self.vecs[:, bcol:bcol + 1], None, ALU.add, ALU.bypass,
                        ["ps%d" % pb, "vecs"], ["adaT"])
        aT = self.adaT
        for l in range(5):
            for which in range(2 if l < 4 else 1):
                base = l * 48 + which * 24
                nw = self.vecs[:, V_NORM + (l * 2 + which) * 8:V_NORM + (l * 2 + which) * 8 + 8] if l < 4 else self.vecs[:, V_KVN:V_KVN + 8]
                self.stt(self.coefP[:, base:base + 8], aT[:, base + 8:base + 16, 0], 1.0, nw, ALU.add, ALU.mult,
                         ["adaT", "vecs"], ["coefP"])
                self.cp("dve", self.coefP[:, base + 8:base + 16], aT[:, base:base + 8, 0], ["adaT"], ["coefP"])
                if l < 4:
                    self.cp("dve", self.coefP[:, base + 16:base + 24], aT[:, base + 16:base + 24, 0], ["adaT"], ["coefP"])
                if self.sample:
                    self.stt(self.coefS[:, base:base + 8, :], aT[:, base + 8:base + 16, 1:17], 1.0,
                             nw.unsqueeze(2).broadcast_to([128, 8, 16]), ALU.add, ALU.mult, ["adaT", "vecs"], ["coefS"])
                    self.cp("dve", self.coefS[:, base + 8:base + 16, :], aT[:, base:base + 8, 1:17], ["adaT"], ["coefS"])
                    if l < 4:
                        self.cp("dve", self.coefS[:, base + 16:base + 24, :], aT[:, base + 16:base + 24, 1:17], ["adaT"], ["coefS"])

    def build(self):
        self.setup()
        self.ada_phase()
        if self.sample:
            self.sample_phase()
        self.kb.barrier()
        for blk in range(self.NBLK):
            self.block(blk)
        self.kb.finish(self.outs)
        return self.nc


def _fm(v):
    v = np.asarray(v, np.float32)
    return np.ascontiguousarray(v.reshape(-1, 128).T)


def _consts():
    eline = np.zeros((33, 384), np.float32)
    eline[32, :] = MASKV
    for ip in range(128, 256):
        eline[int(t5_bucket_np(255 - ip)), ip] = 1.0
        eline[32, ip] = 0.0
    eline_s = np.zeros((33, 128), np.float32)
    for r in range(128):
        eline_s[int(t5_bucket_np(127 - r)), r] = 1.0
    ident = np.eye(128, dtype=np.float32)
    tri = (np.arange(64)[:, None] <= np.arange(64)[None, :]).astype(np.float32)
    hm = np.zeros((128, 2), np.float32)
    hm[:64, 0] = 1.0
    hm[64:, 1] = 1.0
    return dict(eline=eline, eline_s=eline_s, ident=ident, trimask=tri, halfmask=hm)


def prep_shared(inp):
    f32 = lambda a: np.ascontiguousarray(np.asarray(a, np.float32))
    sh = slot_heads()
    w_in_a = f32(inp["w_in_a"])
    d = {}
    d["w_in_h"] = np.ascontiguousarray(w_in_a.reshape(2, 1024, 4, 8, 128).transpose(0, 3, 1, 2, 4).reshape(2, 8, 1024, 512))
    d["w_o_a"] = f32(inp["w_o_a"])
    d["w_kv"] = f32(inp["w_kv"])
    d["w_ada"] = f32(inp["w_ada"])
    d["w_ada_kv"] = f32(inp["w_ada_kv"])
    wq = f32(inp["w_q_b"]).reshape(2, 1024, 16, 64)
    d["w_q_p"] = np.ascontiguousarray(wq[:, :, sh, :].reshape(2, 1024, 1024))
    wo = f32(inp["w_o_b"]).reshape(2, 16, 64, 1024)
    d["w_o_p"] = np.ascontiguousarray(wo[:, sh, :, :].reshape(2, 1024, 1024))
    d["w_ffn_in"] = f32(inp["w_ffn_in"])
    d["w_ffn_out"] = f32(inp["w_ffn_out"])
    sk = f32(inp["sinks_b"])[:, sh]
    d["sinks_bc"] = np.ascontiguousarray(np.broadcast_to(sk.reshape(1, 32), (128, 32)))
    d["sink_s"] = np.ascontiguousarray(sk.T)
    rb = f32(inp["rel_bias"])[:, sh]
    d["rb_ext"] = np.ascontiguousarray(np.concatenate([rb, np.ones((1, 16), np.float32)], 0))
    vecs = np.zeros((128, NV), np.float32)
    nw = f32(inp["norm_w"])
    for l in range(4):
        for wh in range(2):
            vecs[:, V_NORM + (l * 2 + wh) * 8:V_NORM + (l * 2 + wh) * 8 + 8] = _fm(nw[l, wh])
    vecs[:, V_KVN:V_KVN + 8] = _fm(inp["kv_norm_w"])
    vecs[:, V_FIN:V_FIN + 8] = _fm(inp["final_norm_w"])
    lb = f32(inp["lb_a"])
    vecs[:, V_LB:V_LB + 8] = _fm(lb[0])
    vecs[:, V_LB + 8:V_LB + 16] = _fm(lb[1])
    ba = f32(inp["b_ada"])
    for l in range(4):
        vecs[:, V_BADA + l * 48:V_BADA + (l + 1) * 48] = _fm(ba[l])
    vecs[:, V_BADAKV:V_BADAKV + 16] = _fm(inp["b_ada_kv"])
    gn = f32(inp["gnorm_a"])
    vecs[:, V_GN] = gn[0]
    vecs[:, V_GN + 1] = gn[1]
    d["vecs"] = vecs
    d.update(_consts())
    return d


def prep_core(inp, shared, core, T, NBLK, seq=None, win=None):
    f32 = lambda a: np.ascontiguousarray(np.asarray(a, np.float32))
    NTOK = T * NBLK
    d = dict(shared)
    if win is None:
        seq = core % 2 if seq is None else seq
        x = f32(inp["x_prompt"])[seq, :NTOK]
        bm = np.ones((128, NBLK), np.float32)
    else:
        seq, start, end = win
        assert end - start == NTOK
        x = np.zeros((NTOK, 1024), np.float32)
        v0 = max(start, 0)
        x[v0 - start:] = f32(inp["x_prompt"])[seq, v0:end]
        bm = np.zeros((128, NBLK), np.float32)
        for b in range(NBLK):
            bm[:, b] = 1.0 if start + b * T >= 0 else 0.0
    d["bmask"] = bm
    d["xT"] = np.ascontiguousarray(x.T.reshape(8, 128, NTOK).transpose(1, 0, 2))
    bs = slice(core * 16, (core + 1) * 16)
    c17 = np.concatenate([f32(inp["c_prompt"])[seq][None], f32(inp["c_sample"])[bs]], 0)
    d["cT"] = np.ascontiguousarray(c17.T.reshape(8, 128, 17).transpose(1, 0, 2))
    xs = f32(inp["x_sample"])[bs, 0]
    d["xsT"] = np.ascontiguousarray(xs.T.reshape(8, 128, 16).transpose(1, 0, 2))
    d["state"] = f32(inp["state_hgrn"])[bs]
    ck = f32(inp["cache_swa_k"])[bs].reshape(16, 128, 256)
    d["ck"] = ck
    d["cv"] = f32(inp["cache_swa_v"])[bs].reshape(16, 128, 256)
    d["ckT"] = np.ascontiguousarray(ck[:, 1:128, :].transpose(0, 2, 1))
    return d


def _sample_methods():
    def bc3(a, n):
        return a.unsqueeze(2).broadcast_to([a.shape[0], a.shape[1], n])

    def sample_modulate(self, l, which):
        a0, b0, _ = self.coef(l, which)
        S, ps = self.S, self.ps
        sq = self.B(0)[:, 0:128].rearrange("p (k n) -> p k n", k=8)
        self.act(sq, self.hs[:, :, :], AF.Square, ["hs"], ["b0"])
        self.mm(ps[0][:, 0:16], [(self.onesb[:, :], sq[:, k, :]) for k in range(8)], ["onesb", "b0"], ["ps0"])
        rstd = S(0)[:, 0:16]
        self.act(rstd, ps[0][:, 0:16], AF.Ln, ["ps0"], ["s0"], scale=1.0 / D, bias=self.epsc[:, 0:1])
        self.act(rstd, rstd, AF.Exp, ["s0"], ["s0"], scale=-0.5)
        t = S(1)[:, 0:128].rearrange("p (k n) -> p k n", k=8)
        self.tt("dve", t, self.hs[:, :, :], rstd.unsqueeze(1).broadcast_to([128, 8, 16]), ALU.mult, ["hs", "s0"], ["s1"])
        self.tt("dve", t, t, self.coefS[:, a0:a0 + 8, :], ALU.mult, ["s1", "coefS"], ["s1"])
        self.tt("dve", self.xns[:, :, :], t, self.coefS[:, b0:b0 + 8, :], ALU.add, ["s1", "coefS"], ["xns"])

    def sample_proj(self, slabs, src, src_res, gcol, nk):
        dout = 0
        for view, wres in self.stream(slabs):
            for dd in range(4):
                pb = 4 + (dout % 4)
                p = self.ps[pb]
                self.mm(p[:, 0:16], [(view[:, k, dd * 128:(dd + 1) * 128], src[:, k, :]) for k in range(nk)],
                        [wres, src_res], ["ps%d" % pb])
                tmp = self.S(3)[:, 0:16]
                self.tt("dve", tmp, p[:, 0:16], self.coefS[:, gcol + dout, :], ALU.mult, ["ps%d" % pb, "coefS"], ["s3"])
                self.tt("dve", self.hs[:, dout, :], self.hs[:, dout, :], tmp, ALU.add, ["hs", "s3"], ["hs"])
                dout += 1

    def sample_ffn(self, l):
        self.sample_modulate(l, 1)
        _, _, g2 = self.coef(l, 1)
        items = []
        for (c0, n) in FFG:
            items.append((self.w_ffn_in[l, :, c0 * 128:(c0 + n) * 128], "K8"))
            items.append((self.w_ffn_in[l, :, DFF + c0 * 128:DFF + (c0 + n) * 128], "K8"))
            items.append((self.w_ffn_out[l, c0 * 128:(c0 + n) * 128, :], "R4"))
        st = self.stream(items)
        ps = self.ps
        for (c0, n) in FFG:
            wg, rg = next(st)
            wu, ru = next(st)
            wo, ro = next(st)
            for c in range(n):
                self.mm(ps[0][:, 0:16], [(wg[:, k, c * 128:(c + 1) * 128], self.xns[:, k, :]) for k in range(8)], [rg, "xns"], ["ps0"])
                self.mm(ps[1][:, 0:16], [(wu[:, k, c * 128:(c + 1) * 128], self.xns[:, k, :]) for k in range(8)], [ru, "xns"], ["ps1"])
                sg = self.S(0)[:, 0:16]
                self.act(sg, ps[0][:, 0:16], AF.Silu, ["ps0"], ["s0"])
                self.tt("dve", self.as_[:, c, :], sg, ps[1][:, 0:16], ALU.mult, ["s0", "ps1"], ["as"])
            for dout in range(8):
                pb = 4 + (dout % 4)
                self.mm(ps[pb][:, 0:16], [(wo[:, c, dout * 128:(dout + 1) * 128], self.as_[:, c, :]) for c in range(n)],
                        [ro, "as"], ["ps%d" % pb])
                tmp = self.S(3)[:, 0:16]
                self.tt("dve", tmp, ps[pb][:, 0:16], self.coefS[:, g2 + dout, :], ALU.mult, ["ps%d" % pb, "coefS"], ["s3"])
                self.tt("dve", self.hs[:, dout, :], self.hs[:, dout, :], tmp, ALU.add, ["hs", "s3"], ["hs"])

    def sampleA(self, l):
        kb, ps, S = self.kb, self.ps, self.S
        self.sample_modulate(l, 0)
        items = [(self.w_in_h[l, j, :, :], "K8") for j in range(8)]
        for j, (w, wres) in enumerate(self.stream(items)):
            self.ld(self.Sin[:, :, :], self.state_d[:, l, j].rearrange("b k v -> k b v"), ["Sin"], "si")
            pp = ps[0]
            for part in range(4):
                self.mm(pp[:, part * 16:(part + 1) * 16],
                        [(w[:, k, part * 128:(part + 1) * 128], self.xns[:, k, :]) for k in range(8)], [wres, "xns"], ["ps0"])
            q, gate = S(0)[:, 0:16], S(0)[:, 16:32]
            self.act(q, pp[:, 0:16], AF.Silu, ["ps0"], ["s0"])
            self.act(gate, pp[:, 48:64], AF.Silu, ["ps0"], ["s0"])
            e, L1, L2, lg, f, kk, v = [S(1)[:, i * 16:(i + 1) * 16] for i in range(7)]
            self.act(e, pp[:, 16:32], AF.Exp, ["ps0"], ["s1"], scale=-1.0)
            self.act(L1, e, AF.Ln, ["s1", "lbs"], ["s1"], scale=self.lbs[:, l, j:j + 1], bias=self.epsc[:, 1:2])
            self.act(L2, e, AF.Ln, ["s1"], ["s1"], scale=1.0, bias=self.epsc[:, 1:2])
            self.tt("dve", lg, L1, L2, ALU.subtract, ["s1"], ["s1"])
            self.act(f, lg, AF.Exp, ["s1"], ["s1"])
            self.ts("dve", kk, f, -1.0, 1.0, ALU.mult, ALU.add, ["s1"], ["s1"])
            self.cp("dve", v, pp[:, 32:48], ["ps0"], ["s1"])
            rd = self.scr[:, 4:8, :].rearrange("p a (b v) -> p (a b) v", v=128)
            self.tt("dve", rd, self.ident[:, :].unsqueeze(1).broadcast_to([128, 16, 128]), bc3(v, 128), ALU.mult,
                    ["ident", "s1"], ["s4", "s5", "s6", "s7"])
            for qd in range(4):
                self.mm(ps[4 + qd][:, :], [(self.ones64[:, 0:128], self.scr[:, 4 + qd, :])], ["ones64", "s%d" % (4 + qd)],
                        ["ps%d" % (4 + qd)])
                self.tt("dve", self.scr[:, 8 + qd, :].rearrange("p (b v) -> p b v", v=128),
                        ps[4 + qd][:, :].rearrange("p (b v) -> p b v", v=128), bc3(kk[:, 4 * qd:4 * qd + 4], 128), ALU.mult,
                        ["ps%d" % (4 + qd), "s1"], ["s%d" % (8 + qd)])
            self.tt("dve", self.Sout[:, :, :], self.Sin[:, :, :], bc3(f, 128), ALU.mult, ["Sin", "s1"], ["Sout"])
            self.tt("dve", self.Sout[:, :, :], self.Sout[:, :, :], self.scr[:, 8:12, :].rearrange("p a (b v) -> p (a b) v", v=128),
                    ALU.add, ["Sout", "s8", "s9", "s10", "s11"], ["Sout"])
            self.ld(self.st_s[:, l, j].rearrange("b k v -> k b v"), self.Sout[:, :, :], ["st_s"], "so", reads=["Sout"])
            po2 = ps[1]
            q2 = S(0)[:, 32:64].rearrange("p (b t) -> p b t", t=2)
            self.cp("dve", q2, bc3(q, 2), ["s0"], ["s0"])
            for b in range(16):
                kb.emit("pe", lambda e, b=b: e.matmul(po2[:, 2 * b:2 * b + 2], lhsT=self.Sout[:, b, :], rhs=q2[:, b, :], start=True, stop=True),
                        ["Sout", "s0"], ["ps1"], inc=(b == 15))
            po = po2[:, 0:32:2]
            osq = self.B(1)[:, 0:16]
            self.act(osq, po, AF.Square, ["ps1"], ["b1"])
            self.mm(ps[2][:, 0:16], [(self.onesb[:, :], osq)], ["onesb", "b1"], ["ps2"])
            rstd = S(2)[:, 0:16]
            self.act(rstd, ps[2][:, 0:16], AF.Ln, ["ps2"], ["s2"], scale=1.0 / 128, bias=self.epsc[:, 0:1])
            self.act(rstd, rstd, AF.Exp, ["s2"], ["s2"], scale=-0.5)
            t = S(2)[:, 16:32]
            self.stt(t, po, self.vecs[:, V_GN + l:V_GN + l + 1], rstd, ALU.mult, ALU.mult, ["ps1", "s2", "vecs"], ["s2"])
            self.tt("dve", self.ogs[:, j, :], t, gate, ALU.mult, ["s2", "s0"], ["ogs"])
        if "st_s" not in self.outs:
            self.outs.append("st_s")
        _, _, g1 = self.coef(l, 0)
        self.sample_proj([(self.w_o_a[l, :, 0:512], "K8"), (self.w_o_a[l, :, 512:1024], "K8")], self.ogs, "ogs", g1, 8)

    def sample_kv(self):
        ps, S = self.ps, self.S
        self.sample_modulate(4, 0)
        self.kb.dma("pool", "cv", self.Vs[0:112, :, :], self.cv_d[:, 1:113, :].rearrange("b s n -> s b n"), [], ["Vs"])
        self.kb.dma("pool", "cv2", self.Vs[112:127, :, :], self.cv_d[:, 113:128, :].rearrange("b s n -> s b n"), [], ["Vs"])
        ckv = self.ckT_d.ap().rearrange("b (c p) t -> p c b t", p=128)
        for c in range(2):
            self.kb.dma("pool", "ck%d" % c, self.KTs[:, c, :, 0:127], ckv[:, c, :, :], [], ["KTs"])
        self.ld(self.ck_s[:, 0:127, :], self.ck_d[:, 1:128, :], ["ck_s"], "ock")
        self.ld(self.cv_s[:, 0:127, :], self.cv_d[:, 1:128, :], ["cv_s"], "ocv")
        (w, wres), = list(self.stream([(self.w_kv[:, :], "K8")]))
        for c in range(2):
            self.mm(ps[0][:, c * 16:(c + 1) * 16], [(w[:, k, c * 128:(c + 1) * 128], self.xns[:, k, :]) for k in range(8)],
                    [wres, "xns"], ["ps0"])
        self.cp("dve", self.KTs[:, :, :, 127], ps[0][:, 0:32].rearrange("p (c b) -> p c b", c=2), ["ps0"], ["KTs"])
        self.mm(ps[1][0:16, :], [(self.xns[:, k, :], w[:, k, :]) for k in range(8)], [wres, "xns"], ["ps1"])
        rowf = S(0)[0:16, :]
        self.cp("dve", rowf, ps[1][0:16, :], ["ps1"], ["s0"])
        self.ld(self.ck_s[:, 127, :], rowf[:, 0:256], ["ck_s"], "ock2", reads=["s0"])
        self.ld(self.cv_s[:, 127, :], rowf[:, 256:512], ["cv_s"], "ocv2", reads=["s0"])
        rowb = self.B(0)[0:16, 0:256]
        self.cp("dve", rowb, ps[1][0:16, 256:512], ["ps1"], ["b0"])
        self.ld(self.rowk_d[:, :], rowb, ["rowk"], "rk", reads=["b0"])
        self.ld(self.Vs[127:128, :, :], self.rowk_d.ap().rearrange("(o b) n -> o b n", o=1), ["Vs"], "rk2", reads=["rowk"])
        self.outs += ["ck_s", "cv_s"]

    def sampleB(self, jb):
        kb, ps, S = self.kb, self.ps, self.S
        l = 2 + jb
        self.sample_modulate(l, 0)
        i = 0
        for w, wres in self.stream([(self.w_q_p[jb, :, 0:512], "K8"), (self.w_q_p[jb, :, 512:1024], "K8")]):
            for dd in range(4):
                pb = i % 4
                self.mm(ps[pb][:, 0:16], [(w[:, k, dd * 128:(dd + 1) * 128], self.xns[:, k, :]) for k in range(8)],
                        [wres, "xns"], ["ps%d" % pb])
                self.act(self.qTs[:, i, :], ps[pb][:, 0:16], AF.Copy, ["ps%d" % pb], ["qTs"], scale=0.125)
                i += 1
        kb.emit("dve", lambda e: e.memset(self.Qb2[:, :, :, :], 0.0), [], ["Qb2"])
        for c in range(2):
            for half in range(2):
                hp = slice(half * 64, (half + 1) * 64)
                self.cp("dve", self.Qb2[hp, :, c, c * 8 + half:c * 8 + 8:2],
                        self.qTs[hp, 4 * c:4 * c + 4, :].rearrange("p i b -> p b i"), ["qTs"], ["Qb2"])
        for b in range(16):
            pb = 4 + b // 4
            self.mm(ps[pb][0:16, (b % 4) * 128:(b % 4 + 1) * 128],
                    [(self.Qb2[:, b, c, :], self.KTs[:, c, b, :]) for c in range(2)], ["Qb2", "KTs"], ["ps%d" % pb])
        sv = lambda qd: self.scr[0:16, 4 + qd, :].rearrange("p (b t) -> p b t", t=128)
        for qd in range(4):
            self.tt("dve", sv(qd), ps[4 + qd][0:16, :].rearrange("p (b t) -> p b t", t=128),
                    self.bias_s[0:16, :].unsqueeze(1).broadcast_to([16, 4, 128]), ALU.add, ["ps%d" % (4 + qd), "bias_s"], ["s%d" % (4 + qd)])
        s_all = self.scr[0:16, 4:8, :].rearrange("p a (b t) -> p (a b) t", t=128)
        sr = ["s4", "s5", "s6", "s7"]
        sm = self.small
        mx, rs, es, dn = sm[0:16, 0:16], sm[0:16, 16:32], sm[0:16, 32:48], sm[0:16, 48:64]
        kb.emit("dve", lambda e: e.tensor_reduce(out=mx, in_=s_all, axis=AX.X, op=ALU.max), sr, ["sm0"])
        skc = self.sink_s[0:16, jb:jb + 1]
        self.ts("dve", mx, mx, skc, None, ALU.max, ALU.bypass, ["sm0", "sink_s"], ["sm0"])
        self.tt("dve", s_all, s_all, bc3(mx, 128), ALU.subtract, sr + ["sm0"], sr)
        self.act(self.scr[0:16, 4:8, :], self.scr[0:16, 4:8, :], AF.Exp, sr, sr)
        kb.emit("dve", lambda e: e.tensor_reduce(out=rs, in_=s_all, axis=AX.X, op=ALU.add), sr, ["sm0"])
        self.act(es, mx, AF.Exp, ["sm0", "sink_s"], ["sm0"], scale=-1.0, bias=skc)
        self.tt("dve", dn, rs, es, ALU.add, ["sm0"], ["sm0"])
        kb.emit("dve", lambda e: e.reciprocal(out=dn, in_=dn), ["sm0"], ["sm0"])
        pn = self.bscr[0:16, 4:8, :].rearrange("p a (b t) -> p (a b) t", t=128)
        pnr = ["b4", "b5", "b6", "b7"]
        self.tt("dve", pn, s_all, bc3(dn, 128), ALU.mult, sr + ["sm0"], pnr)
        ptb = ps[0][:, :].bitcast(BF16)
        for b in range(16):
            kb.emit("pe", lambda e, b=b: e.transpose(out=ptb[:, b * 16:(b + 1) * 16], in_=pn[:, b, :], identity=self.identb[0:16, 0:16]),
                    pnr + ["identb"], ["ps0"], inc=(b == 15))
        self.cp("act", self.pTs[:, :, :].rearrange("p b s -> p (b s)"), ptb[:, 0:256], ["ps0"], ["pTs"])
        pv = ps[1]
        for b in range(16):
            for c in range(2):
                o0 = (b * 2 + c) * 16
                kb.emit("pe", lambda e, b=b, c=c, o0=o0: e.matmul(pv[:, o0:o0 + 16], lhsT=self.Vs[:, b, c * 128:(c + 1) * 128],
                                                                  rhs=self.pTs[:, b, :], start=True, stop=True),
                        ["Vs", "pTs"], ["ps1"], inc=(b == 15 and c == 1))
        pvv = pv[:, :].rearrange("p (b c s) -> p b c s", b=16, c=2)
        for c in range(2):
            for half in range(2):
                hp = slice(half * 64, (half + 1) * 64)
                self.cp("dve", self.ogs[hp, 4 * c:4 * c + 4, :].rearrange("p i b -> p b i"),
                        pvv[hp, :, c, c * 8 + half:c * 8 + 8:2], ["ps1"], ["ogs"])
        _, _, g1 = self.coef(l, 0)
        self.sample_proj([(self.w_o_p[jb, :, 0:512], "K8"), (self.w_o_p[jb, :, 512:1024], "K8")], self.ogs, "ogs", g1, 8)

    def sample_phase(self):
        S, ps = self.S, self.ps
        self.ld(self.hs[:, :, :], self.xsT[:, :, :], ["hs"], "c20")
        self.ld(self.sink_s[0:16, 0:2], self.sink_s_d[:, :], ["sink_s"], "c21")
        rb = S(12)[0:33, 0:16]
        el = S(13)[0:33, 0:128]
        self.ld(rb, self.rb_ext_d[:, :], ["s12"], "c22")
        self.ld(el, self.eline_s_d[:, :], ["s13"], "c23")
        self.mm(ps[3][0:16, 0:128], [(rb, el)], ["s12", "s13"], ["ps3"])
        self.cp("dve", self.bias_s[0:16, :], ps[3][0:16, 0:128], ["ps3"], ["bias_s"])
        for l in range(4):
            if l < 2:
                self.sampleA(l)
            else:
                self.sampleB(l - 2)
            self.sample_ffn(l)
            if l == 1:
                self.sample_kv()
        sq = self.B(0)[:, 0:128].rearrange("p (k n) -> p k n", k=8)
        self.act(sq, self.hs[:, :, :], AF.Square, ["hs"], ["b0"])
        self.mm(ps[0][:, 0:16], [(self.onesb[:, :], sq[:, k, :]) for k in range(8)], ["onesb", "b0"], ["ps0"])
        rstd = S(0)[:, 0:16]
        self.act(rstd, ps[0][:, 0:16], AF.Ln, ["ps0"], ["s0"], scale=1.0 / D, bias=self.epsc[:, 0:1])
        self.act(rstd, rstd, AF.Exp, ["s0"], ["s0"], scale=-0.5)
        t = S(1)[:, 0:128].rearrange("p (k n) -> p k n", k=8)
        self.tt("dve", t, self.hs[:, :, :], rstd.unsqueeze(1).broadcast_to([128, 8, 16]), ALU.mult, ["hs", "s0"], ["s1"])
        self.tt("dve", t, t, bc3(self.vecs[:, V_FIN:V_FIN + 8], 16), ALU.mult, ["s1", "vecs"], ["s1"])
        self.ld(self.y_s[:, :, :], t, ["y_s"], "oys", reads=["s1"])
        self.outs.append("y_s")

    for f in (sample_modulate, sample_proj, sample_ffn, sampleA, sample_kv, sampleB, sample_phase):
        setattr(Prog, f.__name__, f)


_sample_methods()


T_BLK = 1024
N_PRE = 5
N_OWN = 2
OWN_TOK = T_BLK * N_OWN
_CACHE = {}


def kernel(**inputs):
    if "nc" not in _CACHE:
        prog = Prog(T=T_BLK, sample=True, NPRE=N_PRE, NOWN=N_OWN)
        _CACHE["nc"] = prog.build()
    nc = _CACHE["nc"]
    NBLK = N_PRE + 1 + N_OWN
    NTOK = T_BLK * NBLK
    shared = prep_shared(inputs)
    in_maps = []
    for c in range(8):
        seq, p = c // 4, c % 4
        end = (p + 1) * OWN_TOK
        in_maps.append(prep_core(inputs, shared, c, T_BLK, NBLK, win=(seq, end - NTOK, end)))
    res = run_bass_kernel_spmd(nc, in_maps, core_ids=list(range(8)))
    r = res.results
    g = lambda c, n, shp: np.asarray(r[c][n], np.float32).reshape(shp)
    y_prompt = np.stack([np.concatenate([g(4 * s + p, "yT", (128, 8, OWN_TOK)).transpose(2, 1, 0).reshape(OWN_TOK, 1024)
                                         for p in range(4)], 0) for s in range(2)])
    y_sample = np.concatenate([g(c, "y_s", (128, 8, 16)).transpose(2, 1, 0).reshape(16, 1, 1024) for c in range(8)], 0)
    last = [3, 7]
    st_p = np.stack([g(c, "st_p", (2, 8, 128, 128)) for c in last])
    st_s = np.concatenate([g(c, "st_s", (16, 2, 8, 128, 128)) for c in range(8)], 0)
    k_p = np.stack([g(c, "kT_p", (2, 64, 2, 128)).transpose(3, 2, 0, 1).reshape(128, 4, 64) for c in last])
    v_p = np.stack([g(c, "v_p", (128, 4, 64)) for c in last])
    k_s = np.concatenate([g(c, "ck_s", (16, 128, 4, 64)) for c in range(8)], 0)
    v_s = np.concatenate([g(c, "cv_s", (16, 128, 4, 64)) for c in range(8)], 0)
    f = np.ascontiguousarray
    return (f(y_prompt), f(y_sample), f(st_p), f(st_s), f(k_p), f(v_p), f(k_s), f(v_s))
```

```python
from contextlib import ExitStack
import numpy as np
import concourse.bass as bass
import concourse.mybir as mybir
from concourse.bass_utils import run_bass_kernel_spmd

F32 = mybir.dt.float32
BF16 = mybir.dt.bfloat16
I32 = mybir.dt.int32
AF = mybir.ActivationFunctionType
ALU = mybir.AluOpType
AX = mybir.AxisListType

SAME_ENGINE_SYNC = True


class KB:
    ENG = ("pe", "act", "dve", "pool", "sp")

    def __init__(self, nc):
        self.nc = nc
        self.stack = ExitStack()
        self.h = {"pe": nc.tensor, "act": nc.scalar, "dve": nc.vector, "pool": nc.gpsimd, "sp": nc.sync}
        self.prog = {e: [] for e in self.ENG}
        self.cnt = {e: 0 for e in self.ENG}
        self.seen = {e: {} for e in self.ENG}
        self.res_w = {}
        self.res_r = {}
        self.sems = {}
        self.n_inst = 0

    def sbuf(self, name, shape, dtype):
        return self.stack.enter_context(self.nc.sbuf_tensor("t_" + name, shape, dtype))

    def psum(self, name, shape, dtype):
        return self.stack.enter_context(self.nc.psum_tensor("p_" + name, shape, dtype))

    def _sem(self, key):
        if key not in self.sems:
            self.sems[key] = self.stack.enter_context(self.nc.semaphore("s_" + key.replace(":", "_")))
            self.cnt.setdefault(key, 0)
        return self.sems[key]

    @staticmethod
    def _is_psum(r):
        return (isinstance(r, str) and r.startswith("ps")) or (isinstance(r, tuple) and str(r[0]).startswith("ps"))

    def _waits(self, e, reads, writes):
        w = {}

        def need(key, val):
            if key == e and (not SAME_ENGINE_SYNC or val > self.cnt[e]):
                return
            if val > w.get(key, 0):
                w[key] = val
        for r in reads:
            lw = self.res_w.get(r)
            if lw:
                need(*lw)
            if self._is_psum(r):
                for k, v in self.res_r.get(r, {}).items():
                    if k != e:
                        need(k, v)
        for x in writes:
            lw = self.res_w.get(x)
            if lw:
                need(*lw)
            for k, v in self.res_r.get(x, {}).items():
                need(k, v)
        out = []
        for k, v in w.items():
            if self.seen[e].get(k, 0) < v:
                self.seen[e][k] = v
                out.append((k, v))
        return out

    def _mark(self, key, val, reads, writes):
        for r in reads:
            d = self.res_r.setdefault(r, {})
            if d.get(key, 0) < val:
                d[key] = val
        for x in writes:
            self.res_w[x] = (key, val)
            self.res_r[x] = {}

    def emit(self, e, fn, reads, writes, inc=True):
        waits = self._waits(e, reads, writes)
        self._sem(e)
        if inc:
            self.cnt[e] += 1
            val = self.cnt[e]
        else:
            val = self.cnt[e] + 1
        self._mark(e, val, reads, writes)
        self.prog[e].append((waits, fn, (e, 1) if inc else None))
        self.n_inst += 1

    def dma(self, q, key, out, in_, reads, writes, **kw):
        key = "d:" + key
        self._sem(key)
        waits = self._waits(q, reads, writes)
        self.cnt[key] += 16
        val = self.cnt[key]
        self._mark(key, val, reads, writes)
        self.prog[q].append((waits, lambda e: e.dma_start(out=out, in_=in_, **kw), (key, 16)))
        self.n_inst += 1

    def collective(self, kind, in_ap, out_ap, reads, writes, key, groups=None):
        key = "c:" + key
        self._sem(key)
        waits = self._waits("pool", reads, writes)
        self.cnt[key] += 1
        val = self.cnt[key]
        self._mark(key, val, reads, writes)
        groups = groups or [list(range(8))]
        self.prog["pool"].append((waits, lambda e: e.collective_compute(
            kind, ALU.bypass, replica_groups=groups, ins=[in_ap.opt()], outs=[out_ap.opt()]), (key, 1)))
        self.n_inst += 1

    def barrier(self):
        for e in self.ENG:
            waits = []
            for k, v in self.cnt.items():
                if v > 0 and k != e and self.seen[e].get(k, 0) < v:
                    self.seen[e][k] = v
                    waits.append((k, v))
            self.prog[e].append((waits, None, None))

    def finish(self, final_res):
        waits = self._waits("sp", final_res, [])
        self.prog["sp"].append((waits, None, None))
        nc = self.nc
        with nc.Block() as block:
            def run(e):
                def body(eng):
                    for waits, fn, inc in self.prog[e]:
                        for k, v in waits:
                            eng.wait_ge(self.sems[k], v)
                        if fn is None:
                            continue
                        ins = fn(eng)
                        if inc is not None:
                            ins.then_inc(self.sems[inc[0]], inc[1])
                return body
            block.tensor(run("pe"))
            block.scalar(run("act"))
            block.vector(run("dve"))
            block.gpsimd(run("pool"))
            block.sync(run("sp"))
        self.stack.close()


def kernel(**inputs):
    raise NotImplementedError


D = 1024
KD = 8
DFF = 2816
NFFC = 22
EPS = 1e-6
MASKV = -1e30
FFG = [(0, 4), (4, 4), (8, 4), (12, 4), (16, 4), (20, 2)]

V_NORM = 0
V_KVN = 64
V_FIN = 72
V_LB = 80
V_BADA = 96
V_BADAKV = 288
V_GN = 304
NV = 306
NCOEF = 208


def slot_heads():
    out = []
    for i in range(8):
        for half in range(2):
            out.append(i + 4 * half if i < 4 else 8 + (i - 4) + 4 * half)
    return out


def t5_bucket_np(d):
    d = np.asarray(d)
    dd = np.clip(d, 0, 127)
    large = 16 + (np.log(np.maximum(dd, 1).astype(np.float32) / np.float32(16)) / np.float32(np.log(8.0))
                  * np.float32(16)).astype(np.int32)
    large = np.clip(large, 0, 31)
    return np.where(dd < 16, dd, large)


class Prog:
    def __init__(self, T=1024, NBLK=8, sample=True, dbg=None, nlayers=4, NPRE=None, NOWN=None):
        if NPRE is None:
            self.types = ["own"] * NBLK
        else:
            self.types = ["pre"] * NPRE + ["prekv"] + ["own"] * NOWN
            NBLK = len(self.types)
        self.rmode = NPRE is not None
        self.own_blks = [i for i, t in enumerate(self.types) if t == "own"]
        self.T, self.NBLK, self.NB = T, NBLK, T // 512
        self.sample = sample
        self.dbg = dbg
        self.nlayers = nlayers
        self.NTOK = T * NBLK
        nc = self.nc = bass.Bass("TRN2", target_bir_lowering=False)
        kb = self.kb = KB(nc)
        di = lambda n, s: nc.dram_tensor(n, s, F32, kind="ExternalInput")
        do = lambda n, s: nc.dram_tensor(n, s, F32, kind="ExternalOutput")
        NTOK = self.NTOK
        self.xT = di("xT", [128, 8, NTOK])
        self.cT = di("cT", [128, 8, 17])
        self.xsT = di("xsT", [128, 8, 16])
        self.vecs_d = di("vecs", [128, NV])
        self.w_in_h = di("w_in_h", [2, 8, 1024, 512])
        self.w_o_a = di("w_o_a", [2, 1024, 1024])
        self.w_kv = di("w_kv", [1024, 512])
        self.w_ada = di("w_ada", [4, 1024, 6144])
        self.w_ada_kv = di("w_ada_kv", [1024, 2048])
        self.w_q_p = di("w_q_p", [2, 1024, 1024])
        self.w_o_p = di("w_o_p", [2, 1024, 1024])
        self.w_ffn_in = di("w_ffn_in", [4, 1024, 5632])
        self.w_ffn_out = di("w_ffn_out", [4, 2816, 1024])
        self.sinks_bc_d = di("sinks_bc", [128, 32])
        self.sink_s_d = di("sink_s", [16, 2])
        self.rb_ext_d = di("rb_ext", [33, 16])
        self.eline_d = di("eline", [33, 384])
        self.eline_s_d = di("eline_s", [33, 128])
        self.ident_d = di("ident", [128, 128])
        self.trimask_d = di("trimask", [64, 64])
        self.halfmask_d = di("halfmask", [128, 2])
        self.state_d = di("state", [16, 2, 8, 128, 128])
        self.ck_d = di("ck", [16, 128, 256])
        self.cv_d = di("cv", [16, 128, 256])
        self.ckT_d = di("ckT", [16, 256, 127])
        self.bmask_d = di("bmask", [128, NBLK])
        self.yT = do("yT", [128, 8, T * len(self.own_blks)])
        self.st_p = do("st_p", [2, 8, 128, 128])
        self.kT_p = do("kT_p", [128, 2, 128])
        self.v_p = do("v_p", [128, 256])
        self.y_s = do("y_s", [128, 8, 16])
        self.st_s = do("st_s", [16, 2, 8, 128, 128])
        self.ck_s = do("ck_s", [16, 128, 256])
        self.cv_s = do("cv_s", [16, 128, 256])
        if dbg is not None:
            self.dbg_d = do("dbg", [128, 8, T])
        self.lines_d = nc.dram_tensor("lines", [16, 128, 384], F32)
        self.rowk_d = nc.dram_tensor("rowk", [16, 256], BF16)
        self.outs = []
        self.alloc()

    def alloc(self):
        kb, T = self.kb, self.T
        sb = kb.sbuf
        W = 8 * T + 3 * 4 * T
        W = max(W, 16384)
        self.arena = sb("arena", [128, W], F32)
        ar = self.arena
        self.hT = ar[:, 0:8 * T].rearrange("p (k t) -> p k t", k=8)
        o = 8 * T
        self.xn = ar[:, o:o + 4 * T].bitcast(BF16).rearrange("p (k t) -> p k t", k=8)
        self.og = ar[:, o + 4 * T:o + 8 * T].bitcast(BF16).rearrange("p (k t) -> p k t", k=8)
        self.qT = ar[:, o + 8 * T:o + 12 * T].bitcast(BF16).rearrange("p (k t) -> p k t", k=8)
        def carve(off, words, dt=F32):
            v = ar[:, off:off + words]
            return v if dt == F32 else v.bitcast(dt)
        o = 0
        self.adaT = carve(o, 208 * 17).rearrange("p (g n) -> p g n", n=17); o += 208 * 17
        self.coefS = carve(o, 208 * 16).rearrange("p (g n) -> p g n", n=16); o += 208 * 16
        self.csb = carve(o, 68, BF16).rearrange("p (k n) -> p k n", k=8); o += 68
        self.hs = carve(o, 128).rearrange("p (k n) -> p k n", k=8); o += 128
        self.xns = carve(o, 64, BF16).rearrange("p (k n) -> p k n", k=8); o += 64
        self.ogs = carve(o, 64, BF16).rearrange("p (k n) -> p k n", k=8); o += 64
        self.as_ = carve(o, 32, BF16).rearrange("p (k n) -> p k n", k=4); o += 32
        self.qTs = carve(o, 64, BF16).rearrange("p (k n) -> p k n", k=8); o += 64
        self.Qb2 = carve(o, 256, BF16).rearrange("p (b c s) -> p b c s", b=16, c=2); o += 256
        self.Sin = carve(o, 2048).rearrange("p (b v) -> p b v", b=16); o += 2048
        self.Sout = carve(o, 2048).rearrange("p (b v) -> p b v", b=16); o += 2048
        self.KTs = carve(o, 2048, BF16).rearrange("p (c b t) -> p c b t", c=2, b=16); o += 2048
        self.Vs = carve(o, 2048, BF16).rearrange("p (b n) -> p b n", b=16); o += 2048
        self.pTs = carve(o, 128, BF16).rearrange("p (b s) -> p b s", b=16); o += 128
        self.bias_s = carve(o, 128); o += 128
        self.sink_s = carve(o, 2); o += 2
        assert o <= W, (o, W)
        self.NSLOT = 5
        self.wsl = [sb("wsl%d" % i, [128, 4096], BF16) for i in range(self.NSLOT)]
        self.NS = 14
        self.scr = sb("scr", [128, self.NS, 512], F32)
        self.NBS = 10
        self.bscr = sb("bscr", [128, self.NBS, 512], BF16)
        self.G = sb("G", [128, 513], F32)
        self.KT = sb("KT", [128, 2, 128 + T], BF16)
        self.V = sb("V", [128, 1 + T // 128, 256], BF16)
        self.bias = sb("bias", [128, 16, 256], F32)
        self.Sst = sb("Sst", [128, 16, 128], F32)
        self.coefP = sb("coefP", [128, NCOEF], F32)
        self.vecs = sb("vecs", [128, NV], F32)
        self.lbs = sb("lbs", [128, 2, 8], F32)
        self.sinks_bc = sb("sinks_bc", [128, 32], F32)
        self.ident = sb("ident", [128, 128], F32)
        self.identb = sb("identb", [128, 128], BF16)
        self.onesb = sb("onesb", [128, 128], BF16)
        self.ones64 = sb("ones64", [128, 512], F32)
        self.trimask = sb("trimask", [64, 64], F32)
        self.small = sb("small", [128, 64], F32)
        self.epsc = sb("epsc", [128, 2], F32)
        self.bmask = sb("bmask", [128, self.NBLK], F32)
        self.hmask = sb("hmask", [128, 1], F32)
        self.attm = sb("attm", [64, 8, 64], BF16)
        self.ps = [kb.psum("ps%d" % i, [128, 512], F32) for i in range(8)]
        self.ws_i = 0

    def S(self, i):
        return self.scr[:, i, :]

    def B(self, i):
        return self.bscr[:, i, :]

    def act(self, out, in_, func, reads, writes, **kw):
        self.kb.emit("act", lambda e: e.activation(out=out, in_=in_, func=func, **kw), reads, writes)

    def tt(self, eng, out, a, b, op, reads, writes):
        self.kb.emit(eng, lambda e: e.tensor_tensor(out=out, in0=a, in1=b, op=op), reads, writes)

    def ts(self, eng, out, a, s1, s2, op0, op1, reads, writes):
        self.kb.emit(eng, lambda e: e.tensor_scalar(out=out, in0=a, scalar1=s1, scalar2=s2, op0=op0, op1=op1),
                     reads, writes)

    def stt(self, out, a, s, b, op0, op1, reads, writes):
        self.kb.emit("dve", lambda e: e.scalar_tensor_tensor(out=out, in0=a, scalar=s, in1=b, op0=op0, op1=op1),
                     reads, writes)

    def cp(self, eng, out, in_, reads, writes):
        if eng == "act":
            self.act(out, in_, AF.Copy, reads, writes)
        else:
            self.kb.emit(eng, lambda e: e.tensor_copy(out=out, in_=in_), reads, writes)

    def mm(self, out, pairs, reads, writes):
        n = len(pairs)
        for i, (l, r) in enumerate(pairs):
            self.kb.emit("pe", lambda e, l=l, r=r, i=i: e.matmul(out, lhsT=l, rhs=r, start=(i == 0), stop=(i == n - 1)),
                         reads, writes, inc=(i == n - 1))

    def ld(self, out, in_, writes, key, q="sp", reads=()):
        self.kb.dma(q, key, out, in_, list(reads), writes)

    def slab(self, dram_ap, kind):
        i = self.ws_i % self.NSLOT
        self.ws_i += 1
        t = self.wsl[i]
        res = ("w", i)
        if kind == "K8":
            ncol = dram_ap.shape[1]
            view = t[:, :].rearrange("p (k n) -> p k n", k=8)
            self.kb.dma("pool", "w%d" % i, view[:, :, 0:ncol], dram_ap.rearrange("(k p) n -> p k n", p=128), [], [res])
        else:
            nr = dram_ap.shape[0] // 128
            view = t[:, :].rearrange("p (c n) -> p c n", c=4)
            self.kb.dma("pool", "w%d" % i, view[:, 0:nr, :], dram_ap.rearrange("(c p) n -> p c n", p=128), [], [res])
        return view, res

    def stream(self, items):
        q = []
        it = iter(items)
        for _ in range(2):
            x = next(it, None)
            if x is not None:
                q.append(self.slab(*x))
        while q:
            cur = q.pop(0)
            x = next(it, None)
            if x is not None:
                q.append(self.slab(*x))
            yield cur

    def setup(self):
        kb = self.kb
        self.ld(self.vecs[:, :], self.vecs_d[:, :], ["vecs"], "c11")
        self.ld(self.sinks_bc[:, :], self.sinks_bc_d[:, :], ["sinks"], "c12")
        self.ld(self.ident[:, :], self.ident_d[:, :], ["ident"], "c13")
        self.ld(self.trimask[:, :], self.trimask_d[:, :], ["trimask"], "c14")
        kb.emit("dve", lambda e: e.tensor_copy(out=self.identb[:, :], in_=self.ident[:, :]), ["ident"], ["identb"])
        self.ld(self.bmask[:, :], self.bmask_d[:, :], ["bmask"], "c30")
        if self.rmode:
            pk = self.types.index("prekv")
            self.ts("dve", self.hmask[:, 0:1], self.bmask[:, pk:pk + 1], 1.0, -MASKV, ALU.subtract, ALU.mult, ["bmask"], ["hmask"])
        kb.emit("dve", lambda e: e.memset(self.onesb[:, :], 1.0), [], ["onesb"])
        kb.emit("dve", lambda e: e.memset(self.ones64[:, :], 1.0), [], ["ones64"])
        kb.emit("dve", lambda e: e.memset(self.G[:, 0:1], 0.0), [], ["G0"])
        kb.emit("dve", lambda e: e.memset(self.attm[:, :, :], 0.0), [], ["attm"])
        kb.emit("dve", lambda e: e.memset(self.epsc[:, 0:1], EPS), [], ["epsc"])
        kb.emit("dve", lambda e: e.memset(self.epsc[:, 1:2], 1.0), [], ["epsc"])
        kb.emit("dve", lambda e: e.memset(self.Sst[:, :, :], 0.0), [], ["Sst"])
        kb.emit("dve", lambda e: e.memset(self.lbs[:, 0, :], 0.0), [], ["lbs"])
        t = self.small[:, 0:8]
        self.tt("dve", t, self.vecs[:, V_LB:V_LB + 8], self.vecs[:, V_LB + 8:V_LB + 16], ALU.subtract, ["vecs"], ["small"])
        self.act(t, t, AF.Exp, ["small"], ["small"])
        self.ts("dve", t, t, 1.0, None, ALU.add, ALU.bypass, ["small"], ["small"])
        kb.emit("dve", lambda e: e.reciprocal(out=self.lbs[:, 1, :], in_=t), ["small"], ["lbs"])
        rb = self.S(0)[0:33, 0:16]
        el = self.S(1)[0:33, 0:384]
        self.ld(rb, self.rb_ext_d[:, :], ["s0"], "c15")
        self.ld(el, self.eline_d[:, :], ["s1"], "c16")
        pl = self.ps[0][0:16, 0:384]
        self.mm(pl, [(rb, el)], ["s0", "s1"], ["ps0"])
        ln = self.S(2)[0:16, 0:384]
        self.cp("dve", ln, pl, ["ps0"], ["s2"])
        src = bass.AP(ln.tensor, ln.offset, [list(ln.ap[0]), [0, 128], [1, 384]])
        self.ld(self.lines_d[:, :, :], src, ["lines"], "c1", reads=["s2"])
        tv = bass.AP(self.lines_d, 127, [[383, 128], [128 * 384, 16], [1, 256]])
        self.ld(self.bias[:, :, :], tv, ["bias"], "c1", reads=["lines"])

    def coef(self, l, which):
        if l == 4:
            return 192, 200, None
        base = l * 48 + which * 24
        return base, base + 8, base + 16

    def modulate(self, l, which):
        a0, b0, _ = self.coef(l, which)
        for sb in range(self.NB):
            tok = slice(sb * 512, (sb + 1) * 512)
            hs = ("h", sb)
            sq = self.bscr[:, 0:8, :]
            sqr = ["b%d" % k for k in range(8)]
            self.act(sq, self.hT[:, :, tok], AF.Square, [hs], sqr)
            pss = self.ps[0]
            self.mm(pss[:, :], [(self.onesb[:, :], self.bscr[:, k, :]) for k in range(8)], ["onesb"] + sqr, ["ps0"])
            rstd = self.S(0)
            self.act(rstd, pss[:, :], AF.Ln, ["ps0"], ["s0"], scale=1.0 / D, bias=self.epsc[:, 0:1])
            self.act(rstd, rstd, AF.Exp, ["s0"], ["s0"], scale=-0.5)
            for k in range(8):
                t = self.S(1 + (k % 2))
                r = "s%d" % (1 + (k % 2))
                self.stt(t, self.hT[:, k, tok], self.coefP[:, a0 + k:a0 + k + 1], rstd, ALU.mult, ALU.mult,
                         [hs, "s0", "coefP"], [r])
                self.act(self.xn[:, k, tok], t, AF.Identity, [r, "coefP"], [("xn", sb)],
                         bias=self.coefP[:, b0 + k:b0 + k + 1], scale=1.0)

    def proj_residual(self, slabs, src, src_res, gcol, nk):
        dout = 0
        for view, wres in self.stream(slabs):
            for dd in range(4):
                for sb in range(self.NB):
                    tok = slice(sb * 512, (sb + 1) * 512)
                    pb = 4 + ((dout * self.NB + sb) % 4)
                    p = self.ps[pb]
                    self.mm(p[:, :], [(view[:, k, dd * 128:(dd + 1) * 128], src[:, k, tok]) for k in range(nk)],
                            [wres] + src_res(sb), ["ps%d" % pb])
                    self.stt(self.hT[:, dout, tok], p[:, :], self.coefP[:, gcol + dout:gcol + dout + 1],
                             self.hT[:, dout, tok], ALU.mult, ALU.add, ["ps%d" % pb, ("h", sb), "coefP"], [("h", sb)])
                dout += 1

    def ffn(self, l):
        self.modulate(l, 1)
        _, _, g2 = self.coef(l, 1)
        items = []
        for (c0, n) in FFG:
            items.append((self.w_ffn_in[l, :, c0 * 128:(c0 + n) * 128], "K8"))
            items.append((self.w_ffn_in[l, :, DFF + c0 * 128:DFF + (c0 + n) * 128], "K8"))
            items.append((self.w_ffn_out[l, c0 * 128:(c0 + n) * 128, :], "R4"))
        st = self.stream(items)
        a = self.og
        for (c0, n) in FFG:
            wg, rg = next(st)
            wu, ru = next(st)
            wo, ro = next(st)
            for sb in range(self.NB):
                tok = slice(sb * 512, (sb + 1) * 512)
                for c in range(n):
                    pg, pu = (0, 1) if (c % 2 == 0) else (2, 3)
                    self.mm(self.ps[pg][:, :], [(wg[:, k, c * 128:(c + 1) * 128], self.xn[:, k, tok]) for k in range(8)],
                            [rg, ("xn", sb)], ["ps%d" % pg])
                    self.mm(self.ps[pu][:, :], [(wu[:, k, c * 128:(c + 1) * 128], self.xn[:, k, tok]) for k in range(8)],
                            [ru, ("xn", sb)], ["ps%d" % pu])
                    sg = self.S(c % 2)
                    self.act(sg, self.ps[pg][:, :], AF.Silu, ["ps%d" % pg], ["s%d" % (c % 2)])
                    self.tt("dve", a[:, c, tok], sg, self.ps[pu][:, :], ALU.mult, ["s%d" % (c % 2), "ps%d" % pu],
                            [("og", c, sb)])
            for sb in range(self.NB):
                tok = slice(sb * 512, (sb + 1) * 512)
                for dout in range(8):
                    pb = 4 + (dout % 4)
                    p = self.ps[pb]
                    self.mm(p[:, :], [(wo[:, c, dout * 128:(dout + 1) * 128], a[:, c, tok]) for c in range(n)],
                            [ro] + [("og", c, sb) for c in range(n)], ["ps%d" % pb])
                    self.stt(self.hT[:, dout, tok], p[:, :], self.coefP[:, g2 + dout:g2 + dout + 1],
                             self.hT[:, dout, tok], ALU.mult, ALU.add, ["ps%d" % pb, ("h", sb), "coefP"], [("h", sb)])

    def headA_proj(self, l, j, sb, w, wres, state_only=False):
        ps = self.ps
        tok = slice(sb * 512, (sb + 1) * 512)
        xr = ("xn", sb)
        xk = lambda k: self.xn[:, k, tok]
        if not state_only:
            self.mm(ps[0][:, :], [(w[:, k, 0:128], xk(k)) for k in range(8)], [wres, xr], ["ps0"])
        self.mm(ps[1][:, :], [(w[:, k, 128:256], xk(k)) for k in range(8)], [wres, xr], ["ps1"])
        if not state_only:
            self.mm(ps[2][:, :], [(w[:, k, 384:512], xk(k)) for k in range(8)], [wres, xr], ["ps2"])

    def headA_evac(self, state_only=False):
        ps, S, Bb = self.ps, self.S, self.B
        if not state_only:
            self.act(S(0), ps[0][:, :], AF.Silu, ["ps0"], ["s0"])
            self.act(Bb(0), ps[2][:, :], AF.Silu, ["ps2"], ["b0"])
        self.act(S(1), ps[1][:, :], AF.Exp, ["ps1"], ["s1"], scale=-1.0)

    def headA_sub(self, l, j, sb, w, wres, blk=0, state_only=False):
        kb, ps = self.kb, self.ps
        tok = slice(sb * 512, (sb + 1) * 512)
        xr = ("xn", sb)
        S, Bb = self.S, self.B
        xk = lambda k: self.xn[:, k, tok]
        so = state_only
        lbc = self.lbs[:, l, j:j + 1]
        self.act(S(2), S(1), AF.Ln, ["s1", "lbs"], ["s2"], scale=lbc, bias=self.epsc[:, 1:2])
        self.act(S(3), S(1), AF.Ln, ["s1"], ["s3"], scale=1.0, bias=self.epsc[:, 1:2])
        self.tt("dve", S(2), S(2), S(3), ALU.subtract, ["s2", "s3"], ["s2"])
        Gc = self.G[:, 1:513]
        kb.emit("dve", lambda e: e.tensor_tensor_scan(out=Gc, data0=self.ones64[:, :], data1=S(2), initial=0.0,
                                                      op0=ALU.mult, op1=ALU.add), ["s2", "ones64"], ["G"])
        self.act(S(4), S(2), AF.Exp, ["s2"], ["s4"])
        self.ts("dve", S(3), S(4), -1.0, 1.0, ALU.mult, ALU.add, ["s4"], ["s3"])
        G3 = Gc.rearrange("p (c t) -> p c t", c=8)
        bc = lambda a: a.unsqueeze(2).broadcast_to([128, 8, 64])
        v3 = lambda i: self.scr[:, i, :].rearrange("p (c t) -> p c t", c=8)
        if not so:
            self.tt("dve", v3(5), G3, bc(self.G[:, 32:513:64]), ALU.subtract, ["G", "G0"], ["s5"])
        self.tt("dve", v3(8), G3, bc(self.G[:, 0:512:64]), ALU.subtract, ["G", "G0"], ["s8"])
        self.tt("dve", v3(9), G3, bc(self.G[:, 64:513:64]), ALU.subtract, ["G", "G0"], ["s9"])
        if not so:
            self.act(S(6), S(5), AF.Exp, ["s5"], ["s6"])
            self.act(S(7), S(5), AF.Exp, ["s5"], ["s7"], scale=-1.0)
        self.act(S(8), S(8), AF.Exp, ["s8"], ["s8"])
        self.act(S(9), S(9), AF.Exp, ["s9"], ["s9"], scale=-1.0)
        if not so:
            self.tt("dve", Bb(1), S(0), S(6), ALU.mult, ["s0", "s6"], ["b1"])
            self.tt("dve", S(10), S(0), S(8), ALU.mult, ["s0", "s8"], ["s10"])
            self.tt("dve", Bb(2), S(3), S(7), ALU.mult, ["s3", "s7"], ["b2"])
        self.tt("dve", Bb(3), S(3), S(9), ALU.mult, ["s3", "s9"], ["b3"])
        vtok = self.bscr[0:64, 6:8, :].rearrange("p a (c d) -> p (a c) d", d=128)
        for half in range(2):
            for cc in range(4):
                c = half * 4 + cc
                t0 = sb * 512 + c * 64
                self.mm(ps[3][0:64, cc * 128:(cc + 1) * 128],
                        [(self.xn[:, k, t0:t0 + 64], w[:, k, 256:384]) for k in range(8)], [wres, xr], ["ps3"])
            if self.rmode:
                self.act(self.bscr[0:64, 6 + half, :], ps[3][0:64, :], AF.Identity, ["ps3", "bmask"], ["b%d" % (6 + half)],
                         scale=self.bmask[0:64, blk:blk + 1], bias=0.0)
            else:
                self.cp("act", self.bscr[0:64, 6 + half, :], ps[3][0:64, :], ["ps3"], ["b%d" % (6 + half)])
        pk = ps[5][:, :].bitcast(BF16)
        for c in range(8):
            kb.emit("pe", lambda e, c=c: e.transpose(out=pk[0:64, c * 128:(c + 1) * 128],
                                                     in_=Bb(3)[:, c * 64:(c + 1) * 64], identity=self.identb[:, :]),
                    ["b3", "identb"], ["ps5"], inc=(c == 7))
        kh = self.bscr[0:64, 8:10, :].rearrange("p a (c d) -> p (a c) d", d=128)
        self.cp("act", self.bscr[0:64, 8:10, :].rearrange("p a n -> p (a n)"), pk[0:64, :], ["ps5"], ["b8", "b9"])
        if not so:
            for c in range(8):
                c0 = c * 64
                kb.emit("pe", lambda e, c0=c0: e.matmul(ps[4][0:64, c0 + 32:c0 + 64], lhsT=Bb(2)[:, c0:c0 + 64],
                                                        rhs=Bb(1)[:, c0 + 32:c0 + 64], start=True, stop=True),
                        ["b1", "b2"], ["ps4"], inc=False)
                kb.emit("pe", lambda e, c0=c0: e.matmul(ps[4][0:32, c0:c0 + 32], lhsT=Bb(2)[:, c0:c0 + 32],
                                                        rhs=Bb(1)[:, c0:c0 + 32], start=True, stop=True),
                        ["b1", "b2"], ["ps4"], inc=(c == 7))
            attm = self.attm[:, :, :]
            p43 = ps[4][0:64, :].rearrange("p (c t) -> p c t", c=8)
            self.tt("dve", self.attm[:, :, 32:64], p43[:, :, 32:64],
                    self.trimask[:, 32:64].unsqueeze(1).broadcast_to([64, 8, 32]), ALU.mult, ["ps4", "trimask"], ["attm"])
            self.tt("dve", self.attm[0:32, :, 0:32], p43[0:32, :, 0:32],
                    self.trimask[0:32, 0:32].unsqueeze(1).broadcast_to([32, 8, 32]), ALU.mult, ["ps4", "trimask"], ["attm"])
        Sj = self.Sst[:, l * 8 + j, :]
        sres = ("S", l, j)
        for c in range(8):
            cs = slice(c * 64, (c + 1) * 64)
            if not so:
                kb.emit("pe", lambda e, c=c, cs=cs: e.matmul(ps[7][:, cs], lhsT=vtok[:, c, :], rhs=self.attm[:, c, :], start=True, stop=False),
                        ["b6", "b7", "attm"], ["ps7"], inc=False)
                kb.emit("pe", lambda e, cs=cs: e.matmul(ps[7][:, cs], lhsT=Sj, rhs=S(10)[:, cs], start=False, stop=True),
                        [sres, "s10"], ["ps7"], inc=True)
            sl = c % 4
            pn = ps[6][:, sl * 128:(sl + 1) * 128]
            kb.emit("pe", lambda e, c=c, pn=pn: e.matmul(pn, lhsT=kh[:, c, :], rhs=vtok[:, c, :], start=True, stop=True),
                    ["b8", "b9", "b6", "b7"], ["ps6"], inc=True)
            dcol = self.scr[:, 8, c * 64 + 63:c * 64 + 64]
            self.stt(Sj, Sj, dcol, pn, ALU.mult, ALU.add, [sres, "s8", "ps6"], [sres])
        if so:
            return
        self.act(Bb(4), ps[7][:, :], AF.Square, ["ps7"], ["b4"])
        self.mm(ps[5][:, :], [(self.onesb[:, :], Bb(4))], ["onesb", "b4"], ["ps5"])
        self.act(S(11), ps[5][:, :], AF.Ln, ["ps5"], ["s11"], scale=1.0 / 128, bias=self.epsc[:, 0:1])
        self.act(S(11), S(11), AF.Exp, ["s11"], ["s11"], scale=-0.5)
        self.stt(S(12), ps[7][:, :], self.vecs[:, V_GN + l:V_GN + l + 1], S(11), ALU.mult, ALU.mult,
                 ["ps7", "s11", "vecs"], ["s12"])
        self.tt("dve", self.og[:, j, tok], S(12), Bb(0), ALU.mult, ["s12", "b0"], [("og", j, sb)])

    def layerA(self, blk, l, state_only=False):
        self.modulate(l, 0)
        items = [(self.w_in_h[l, j, :, :], "K8") for j in range(8)]
        st = self.stream(items)
        slabs = {}

        def slab_of(j):
            if j not in slabs:
                slabs[j] = next(st)
                slabs.pop(j - 2, None)
            return slabs[j]
        seq = [(j, sb) for j in range(8) for sb in range(self.NB)]
        w0, r0 = slab_of(0)
        self.headA_proj(l, 0, 0, w0, r0, state_only)
        for n, (j, sb) in enumerate(seq):
            w, wres = slab_of(j)
            self.headA_evac(state_only)
            if n + 1 < len(seq):
                j2, sb2 = seq[n + 1]
                w2, r2 = slab_of(j2)
                self.headA_proj(l, j2, sb2, w2, r2, state_only)
            self.headA_sub(l, j, sb, w, wres, blk=blk, state_only=state_only)
        if state_only:
            return
        _, _, g1 = self.coef(l, 0)
        self.proj_residual([(self.w_o_a[l, :, 0:512], "K8"), (self.w_o_a[l, :, 512:1024], "K8")], self.og,
                           lambda sb: [("og", j, sb) for j in range(8)], g1, 8)

    def kvproj(self, blk):
        T = self.T
        self.modulate(4, 0)
        (w, wres), = list(self.stream([(self.w_kv[:, :], "K8")]))
        last = (blk == self.NBLK - 1)
        for c in range(2):
            for sb in range(self.NB):
                tok = slice(sb * 512, (sb + 1) * 512)
                pb = c * 2 + (sb % 2)
                self.mm(self.ps[pb][:, :], [(w[:, k, c * 128:(c + 1) * 128], self.xn[:, k, tok]) for k in range(8)],
                        [wres, ("xn", sb)], ["ps%d" % pb])
                self.cp("act", self.KT[:, c, 128 + sb * 512:128 + (sb + 1) * 512], self.ps[pb][:, :], ["ps%d" % pb], ["KT"])
                if last and sb == self.NB - 1:
                    self.cp("dve", self.scr[:, 0, c * 128:(c + 1) * 128], self.ps[pb][:, 384:512], ["ps%d" % pb], ["s0"])
        if last:
            self.ld(self.kT_p[:, :, :], self.scr[:, 0, 0:256].rearrange("p (c t) -> p c t", c=2), ["kT_p"], "ok", reads=["s0"])
            self.outs.append("kT_p")
        for tt_ in range(T // 128):
            pb = 4 + (tt_ % 4)
            self.mm(self.ps[pb][:, 0:256], [(self.xn[:, k, tt_ * 128:(tt_ + 1) * 128], w[:, k, 256:512]) for k in range(8)],
                    [wres, ("xn", tt_ // 4)], ["ps%d" % pb])
            self.cp("act", self.V[:, 1 + tt_, :], self.ps[pb][:, 0:256], ["ps%d" % pb], ["V"])
            if last and tt_ == T // 128 - 1:
                self.cp("dve", self.scr[:, 1, 0:256], self.ps[pb][:, 0:256], ["ps%d" % pb], ["s1"])
                self.ld(self.v_p[:, :], self.scr[:, 1, 0:256], ["v_p"], "ov", reads=["s1"])
                self.outs.append("v_p")

    def layerB(self, blk, jb):
        l = 2 + jb
        kb, ps, T = self.kb, self.ps, self.T
        self.modulate(l, 0)
        items = [(self.w_q_p[jb, :, 0:512], "K8"), (self.w_q_p[jb, :, 512:1024], "K8")]
        i = 0
        for w, wres in self.stream(items):
            for dd in range(4):
                for sb in range(self.NB):
                    tok = slice(sb * 512, (sb + 1) * 512)
                    pb = 6 + ((i * self.NB + sb) % 2)
                    self.mm(ps[pb][:, :], [(w[:, k, dd * 128:(dd + 1) * 128], self.xn[:, k, tok]) for k in range(8)],
                            [wres, ("xn", sb)], ["ps%d" % pb])
                    self.act(self.qT[:, i, tok], ps[pb][:, :], AF.Copy, ["ps%d" % pb], [("qT", i)], scale=0.125)
                i += 1
        it = 0
        for i in range(8):
            c = i // 4
            for qt in range(T // 128):
                qtok = slice(qt * 128, (qt + 1) * 128)
                first = (qt == 0 and not any(t in ("prekv", "own") for t in self.types[:blk]))
                halo_m = (self.rmode and qt == 0 and blk == self.own_blks[0])
                nk = 128 if first else 256
                koff = qt * 128 + (128 if first else 0)
                for half in range(2):
                    hp = slice(half * 64, (half + 1) * 64)
                    slot = i * 2 + half
                    par = it % 2
                    it += 1
                    pl = ps[par]
                    plr = "ps%d" % par
                    kb.emit("pe", lambda e, pl=pl, hp=hp, i=i, qtok=qtok, c=c, koff=koff, nk=nk: e.matmul(
                        pl[:, 0:nk], lhsT=self.qT[hp, i, qtok], rhs=self.KT[hp, c, koff:koff + nk], start=True, stop=True),
                        [("qT", i), "KT"], [plr])
                    s = self.scr[:, par, 0:nk]
                    sr = "s%d" % par
                    self.tt("dve", s, pl[:, 0:nk], self.bias[:, slot, 256 - nk:256], ALU.add, [plr, "bias"], [sr])
                    if halo_m:
                        self.ts("dve", s[:, 0:128], s[:, 0:128], self.hmask[:, 0:1], None, ALU.add, ALU.bypass, [sr, "hmask"], [sr])
                    sm = self.small[:, par * 8:par * 8 + 8]
                    smr = "sm%d" % par
                    kb.emit("dve", lambda e, sm=sm, s=s: e.tensor_reduce(out=sm[:, 0:1], in_=s, axis=AX.X, op=ALU.max), [sr], [smr])
                    sk = self.sinks_bc[:, jb * 16 + slot:jb * 16 + slot + 1]
                    self.tt("dve", sm[:, 1:2], sm[:, 0:1], sk, ALU.max, [smr, "sinks"], [smr])
                    self.ts("dve", sm[:, 2:3], sm[:, 1:2], -1.0, None, ALU.mult, ALU.bypass, [smr], [smr])
                    self.act(s, s, AF.Exp, [sr, smr], [sr, smr], bias=sm[:, 2:3], scale=1.0, accum_out=sm[:, 3:4])
                    self.act(sm[:, 4:5], sk, AF.Exp, [smr, "sinks"], [smr], bias=sm[:, 2:3], scale=1.0)
                    self.tt("dve", sm[:, 5:6], sm[:, 3:4], sm[:, 4:5], ALU.add, [smr], [smr])
                    kb.emit("dve", lambda e, sm=sm: e.reciprocal(out=sm[:, 6:7], in_=sm[:, 5:6]), [smr], [smr])
                    pn = self.bscr[:, par, 0:nk]
                    pnr = "b%d" % par
                    self.ts("dve", pn, s, sm[:, 6:7], None, ALU.mult, ALU.bypass, [sr, smr], [pnr])
                    ptb = ps[2 + par][:, :].bitcast(BF16)
                    ptr = "ps%d" % (2 + par)
                    nkb = nk // 128
                    for kb2 in range(nkb):
                        kb.emit("pe", lambda e, ptb=ptb, pn=pn, kb2=kb2: e.transpose(
                            out=ptb[:, kb2 * 128:(kb2 + 1) * 128], in_=pn[:, kb2 * 128:(kb2 + 1) * 128], identity=self.identb[:, :]),
                            [pnr, "identb"], [ptr], inc=(kb2 == nkb - 1))
                    pT = self.bscr[:, 2 + par, 0:nk]
                    pTr = "b%d" % (2 + par)
                    self.cp("act", pT, ptb[:, 0:nk], [ptr], [pTr])
                    po = ps[4 + par][hp, 0:128]
                    por = "ps%d" % (4 + par)
                    vcol = (2 * c + half) * 64
                    for kb2 in range(nkb):
                        vt = (qt + kb2) if not first else 1
                        kb.emit("pe", lambda e, po=po, vt=vt, vcol=vcol, pT=pT, kb2=kb2, nkb=nkb: e.matmul(
                            po, lhsT=self.V[:, vt, vcol:vcol + 64], rhs=pT[:, kb2 * 128:(kb2 + 1) * 128],
                            start=(kb2 == 0), stop=(kb2 == nkb - 1)), ["V", pTr], [por], inc=(kb2 == nkb - 1))
                    self.cp("dve", self.og[hp, i, qtok], po, [por], [("og", i, qt // 4)])
        _, _, g1 = self.coef(l, 0)
        self.proj_residual([(self.w_o_p[jb, :, 0:512], "K8"), (self.w_o_p[jb, :, 512:1024], "K8")], self.og,
                           lambda sb: [("og", j, sb) for j in range(8)], g1, 8)

    def final(self, blk):
        T = self.T
        for sb in range(self.NB):
            tok = slice(sb * 512, (sb + 1) * 512)
            hs = ("h", sb)
            sqr = ["b%d" % k for k in range(8)]
            self.act(self.bscr[:, 0:8, :], self.hT[:, :, tok], AF.Square, [hs], sqr)
            self.mm(self.ps[0][:, :], [(self.onesb[:, :], self.bscr[:, k, :]) for k in range(8)], ["onesb"] + sqr, ["ps0"])
            rstd = self.S(0)
            self.act(rstd, self.ps[0][:, :], AF.Ln, ["ps0"], ["s0"], scale=1.0 / D, bias=self.epsc[:, 0:1])
            self.act(rstd, rstd, AF.Exp, ["s0"], ["s0"], scale=-0.5)
            for k in range(8):
                self.stt(self.scr[:, 4 + k, :], self.hT[:, k, tok], self.vecs[:, V_FIN + k:V_FIN + k + 1], rstd,
                         ALU.mult, ALU.mult, [hs, "s0", "vecs"], ["s%d" % (4 + k)])
            ob = self.own_blks.index(blk)
            self.ld(self.yT[:, :, ob * T + sb * 512:ob * T + (sb + 1) * 512], self.scr[:, 4:12, :], ["yT"], "y%d" % sb,
                    reads=["s%d" % (4 + k) for k in range(8)])
        if "yT" not in self.outs:
            self.outs.append("yT")

    def block(self, blk):
        T = self.T
        for sb in range(self.NB):
            self.ld(self.hT[:, :, sb * 512:(sb + 1) * 512], self.xT[:, :, blk * T + sb * 512:blk * T + (sb + 1) * 512],
                    [("h", sb)], "x%d" % sb)
        typ = self.types[blk]
        for l in range(self.nlayers):
            if typ == "pre" and l >= 1:
                if l == 1:
                    self.layerA(blk, 1, state_only=True)
                continue
            if typ == "prekv" and l >= 2:
                continue
            if l < 2:
                self.layerA(blk, l)
            else:
                self.layerB(blk, l - 2)
            if self.dbg == l + 10 and blk == 0:
                self.ld(self.dbg_d[:, :, :], self.hT[:, :, :], ["dbg"], "o2", reads=[("h", sb) for sb in range(self.NB)])
                self.outs.append("dbg")
            self.ffn(l)
            if l == 1:
                self.kvproj(blk)
            if self.dbg == l and blk == 0:
                self.ld(self.dbg_d[:, :, :], self.hT[:, :, :], ["dbg"], "o2", reads=[("h", sb) for sb in range(self.NB)])
                self.outs.append("dbg")
        if self.nlayers == 4 and typ == "own":
            self.final(blk)
        if blk < self.NBLK - 1 and self.nlayers > 1 and typ != "pre":
            self.cp("dve", self.KT[:, :, 0:128], self.KT[:, :, T:T + 128], ["KT"], ["KT"])
            self.cp("dve", self.V[:, 0, :], self.V[:, T // 128, :], ["V"], ["V"])
        if blk == self.NBLK - 1:
            self.ld(self.st_p.ap().rearrange("l j k v -> k (l j) v"), self.Sst[:, :, :], ["st_p"], "ost", reads=[("S", l, j) for l in range(2) for j in range(8)])
            self.outs.append("st_p")

    def ada_phase(self):
        kb, ps = self.kb, self.ps
        cs = self.scr[:, 0, 0:136].rearrange("p (k n) -> p k n", k=8)
        self.ld(cs, self.cT[:, :, :], ["s0"], "c17")
        csb = self.csb
        self.act(csb[:, :, :], cs, AF.Silu, ["s0"], ["csb"])
        items = []
        for l in range(4):
            for s in range(12):
                items.append((self.w_ada[l, :, s * 512:(s + 1) * 512], "K8"))
        for s in range(4):
            items.append((self.w_ada_kv[:, s * 512:(s + 1) * 512], "K8"))
        n = 0
        for w, wres in self.stream(items):
            l, s = (n // 12, n % 12) if n < 48 else (4, n - 48)
            n += 1
            for dd in range(4):
                ch = s * 4 + dd
                gidx = l * 48 + ch
                bcol = (V_BADA + l * 48 + ch) if l < 4 else (V_BADAKV + ch)
                pb = gidx % 4
                self.mm(ps[pb][:, 0:17], [(w[:, k, dd * 128:(dd + 1) * 128], csb[:, k, :]) for k in range(8)],
                        [wres, "csb"], ["ps%d" % pb])
                self.ts("dve", self.adaT[:, gidx, :], ps[pb][:, 0:17], self.vecs[:, bcol:bcol + 1], None, ALU.add, ALU.bypass,
                        ["ps%d" % pb, "vecs"], ["adaT"])
        aT = self.adaT
        for l in range(5):
            for which in range(2 if l < 4 else 1):
                base = l * 48 + which * 24
                nw = self.vecs[:, V_NORM + (l * 2 + which) * 8:V_NORM + (l * 2 + which) * 8 + 8] if l < 4 else self.vecs[:, V_KVN:V_KVN + 8]
                self.stt(self.coefP[:, base:base + 8], aT[:, base + 8:base + 16, 0], 1.0, nw, ALU.add, ALU.mult,
                         ["adaT", "vecs"], ["coefP"])
                self.cp("dve", self.coefP[:, base + 8:base + 16], aT[:, base:base + 8, 0], ["adaT"], ["coefP"])
                if l < 4:
                    self.cp("dve", self.coefP[:, base + 16:base + 24], aT[:, base + 16:base + 24, 0], ["adaT"], ["coefP"])
                if self.sample:
                    self.stt(self.coefS[:, base:base + 8, :], aT[:, base + 8:base + 16, 1:17], 1.0,
                             nw.unsqueeze(2).broadcast_to([128, 8, 16]), ALU.add, ALU.mult, ["adaT", "vecs"], ["coefS"])
                    self.cp("dve", self.coefS[:, base + 8:base + 16, :], aT[:, base:base + 8, 1:17], ["adaT"], ["coefS"])
                    if l < 4:
                        self.cp("dve", self.coefS[:, base + 16:base + 24, :], aT[:, base + 16:base + 24, 1:17], ["adaT"], ["coefS"])

    def build(self):
        self.setup()
        self.ada_phase()
        if self.sample:
            self.sample_phase()
        self.kb.barrier()
        for blk in range(self.NBLK):
            self.block(blk)
        self.kb.finish(self.outs)
        return self.nc


def _fm(v):
    v = np.asarray(v, np.float32)
    return np.ascontiguousarray(v.reshape(-1, 128).T)


def _consts():
    eline = np.zeros((33, 384), np.float32)
    eline[32, :] = MASKV
    for ip in range(128, 256):
        eline[int(t5_bucket_np(255 - ip)), ip] = 1.0
        eline[32, ip] = 0.0
    eline_s = np.zeros((33, 128), np.float32)
    for r in range(128):
        eline_s[int(t5_bucket_np(127 - r)), r] = 1.0
    ident = np.eye(128, dtype=np.float32)
    tri = (np.arange(64)[:, None] <= np.arange(64)[None, :]).astype(np.float32)
    hm = np.zeros((128, 2), np.float32)
    hm[:64, 0] = 1.0
    hm[64:, 1] = 1.0
    return dict(eline=eline, eline_s=eline_s, ident=ident, trimask=tri, halfmask=hm)


def prep_shared(inp):
    f32 = lambda a: np.ascontiguousarray(np.asarray(a, np.float32))
    sh = slot_heads()
    w_in_a = f32(inp["w_in_a"])
    d = {}
    d["w_in_h"] = np.ascontiguousarray(w_in_a.reshape(2, 1024, 4, 8, 128).transpose(0, 3, 1, 2, 4).reshape(2, 8, 1024, 512))
    d["w_o_a"] = f32(inp["w_o_a"])
    d["w_kv"] = f32(inp["w_kv"])
    d["w_ada"] = f32(inp["w_ada"])
    d["w_ada_kv"] = f32(inp["w_ada_kv"])
    wq = f32(inp["w_q_b"]).reshape(2, 1024, 16, 64)
    d["w_q_p"] = np.ascontiguousarray(wq[:, :, sh, :].reshape(2, 1024, 1024))
    wo = f32(inp["w_o_b"]).reshape(2, 16, 64, 1024)
    d["w_o_p"] = np.ascontiguousarray(wo[:, sh, :, :].reshape(2, 1024, 1024))
    d["w_ffn_in"] = f32(inp["w_ffn_in"])
    d["w_ffn_out"] = f32(inp["w_ffn_out"])
    sk = f32(inp["sinks_b"])[:, sh]
    d["sinks_bc"] = np.ascontiguousarray(np.broadcast_to(sk.reshape(1, 32), (128, 32)))
    d["sink_s"] = np.ascontiguousarray(sk.T)
    rb = f32(inp["rel_bias"])[:, sh]
    d["rb_ext"] = np.ascontiguousarray(np.concatenate([rb, np.ones((1, 16), np.float32)], 0))
    vecs = np.zeros((128, NV), np.float32)
    nw = f32(inp["norm_w"])
    for l in range(4):
        for wh in range(2):
            vecs[:, V_NORM + (l * 2 + wh) * 8:V_NORM + (l * 2 + wh) * 8 + 8] = _fm(nw[l, wh])
    vecs[:, V_KVN:V_KVN + 8] = _fm(inp["kv_norm_w"])
    vecs[:, V_FIN:V_FIN + 8] = _fm(inp["final_norm_w"])
    lb = f32(inp["lb_a"])
    vecs[:, V_LB:V_LB + 8] = _fm(lb[0])
    vecs[:, V_LB + 8:V_LB + 16] = _fm(lb[1])
    ba = f32(inp["b_ada"])
    for l in range(4):
        vecs[:, V_BADA + l * 48:V_BADA + (l + 1) * 48] = _fm(ba[l])
    vecs[:, V_BADAKV:V_BADAKV + 16] = _fm(inp["b_ada_kv"])
    gn = f32(inp["gnorm_a"])
    vecs[:, V_GN] = gn[0]
    vecs[:, V_GN + 1] = gn[1]
    d["vecs"] = vecs
    d.update(_consts())
    return d


def prep_core(inp, shared, core, T, NBLK, seq=None, win=None):
    f32 = lambda a: np.ascontiguousarray(np.asarray(a, np.float32))
    NTOK = T * NBLK
    d = dict(shared)
    if win is None:
        seq = core % 2 if seq is None else seq
        x = f32(inp["x_prompt"])[seq, :NTOK]
        bm = np.ones((128, NBLK), np.float32)
    else:
        seq, start, end = win
        assert end - start == NTOK
        x = np.zeros((NTOK, 1024), np.float32)
        v0 = max(start, 0)
        x[v0 - start:] = f32(inp["x_prompt"])[seq, v0:end]
        bm = np.zeros((128, NBLK), np.float32)
        for b in range(NBLK):
            bm[:, b] = 1.0 if start + b * T >= 0 else 0.0
    d["bmask"] = bm
    d["xT"] = np.ascontiguousarray(x.T.reshape(8, 128, NTOK).transpose(1, 0, 2))
    bs = slice(core * 16, (core + 1) * 16)
    c17 = np.concatenate([f32(inp["c_prompt"])[seq][None], f32(inp["c_sample"])[bs]], 0)
    d["cT"] = np.ascontiguousarray(c17.T.reshape(8, 128, 17).transpose(1, 0, 2))
    xs = f32(inp["x_sample"])[bs, 0]
    d["xsT"] = np.ascontiguousarray(xs.T.reshape(8, 128, 16).transpose(1, 0, 2))
    d["state"] = f32(inp["state_hgrn"])[bs]
    ck = f32(inp["cache_swa_k"])[bs].reshape(16, 128, 256)
    d["ck"] = ck
    d["cv"] = f32(inp["cache_swa_v"])[bs].reshape(16, 128, 256)
    d["ckT"] = np.ascontiguousarray(ck[:, 1:128, :].transpose(0, 2, 1))
    return d


def _sample_methods():
    def bc3(a, n):
        return a.unsqueeze(2).broadcast_to([a.shape[0], a.shape[1], n])

    def sample_modulate(self, l, which):
        a0, b0, _ = self.coef(l, which)
        S, ps = self.S, self.ps
        sq = self.B(0)[:, 0:128].rearrange("p (k n) -> p k n", k=8)
        self.act(sq, self.hs[:, :, :], AF.Square, ["hs"], ["b0"])
        self.mm(ps[0][:, 0:16], [(self.onesb[:, :], sq[:, k, :]) for k in range(8)], ["onesb", "b0"], ["ps0"])
        rstd = S(0)[:, 0:16]
        self.act(rstd, ps[0][:, 0:16], AF.Ln, ["ps0"], ["s0"], scale=1.0 / D, bias=self.epsc[:, 0:1])
        self.act(rstd, rstd, AF.Exp, ["s0"], ["s0"], scale=-0.5)
        t = S(1)[:, 0:128].rearrange("p (k n) -> p k n", k=8)
        self.tt("dve", t, self.hs[:, :, :], rstd.unsqueeze(1).broadcast_to([128, 8, 16]), ALU.mult, ["hs", "s0"], ["s1"])
        self.tt("dve", t, t, self.coefS[:, a0:a0 + 8, :], ALU.mult, ["s1", "coefS"], ["s1"])
        self.tt("dve", self.xns[:, :, :], t, self.coefS[:, b0:b0 + 8, :], ALU.add, ["s1", "coefS"], ["xns"])

    def sample_proj(self, slabs, src, src_res, gcol, nk):
        dout = 0
        for view, wres in self.stream(slabs):
            for dd in range(4):
                pb = 4 + (dout % 4)
                p = self.ps[pb]
                self.mm(p[:, 0:16], [(view[:, k, dd * 128:(dd + 1) * 128], src[:, k, :]) for k in range(nk)],
                        [wres, src_res], ["ps%d" % pb])
                tmp = self.S(3)[:, 0:16]
                self.tt("dve", tmp, p[:, 0:16], self.coefS[:, gcol + dout, :], ALU.mult, ["ps%d" % pb, "coefS"], ["s3"])
                self.tt("dve", self.hs[:, dout, :], self.hs[:, dout, :], tmp, ALU.add, ["hs", "s3"], ["hs"])
                dout += 1

    def sample_ffn(self, l):
        self.sample_modulate(l, 1)
        _, _, g2 = self.coef(l, 1)
        items = []
        for (c0, n) in FFG:
            items.append((self.w_ffn_in[l, :, c0 * 128:(c0 + n) * 128], "K8"))
            items.append((self.w_ffn_in[l, :, DFF + c0 * 128:DFF + (c0 + n) * 128], "K8"))
            items.append((self.w_ffn_out[l, c0 * 128:(c0 + n) * 128, :], "R4"))
        st = self.stream(items)
        ps = self.ps
        for (c0, n) in FFG:
            wg, rg = next(st)
            wu, ru = next(st)
            wo, ro = next(st)
            for c in range(n):
                self.mm(ps[0][:, 0:16], [(wg[:, k, c * 128:(c + 1) * 128], self.xns[:, k, :]) for k in range(8)], [rg, "xns"], ["ps0"])
                self.mm(ps[1][:, 0:16], [(wu[:, k, c * 128:(c + 1) * 128], self.xns[:, k, :]) for k in range(8)], [ru, "xns"], ["ps1"])
                sg = self.S(0)[:, 0:16]
                self.act(sg, ps[0][:, 0:16], AF.Silu, ["ps0"], ["s0"])
                self.tt("dve", self.as_[:, c, :], sg, ps[1][:, 0:16], ALU.mult, ["s0", "ps1"], ["as"])
            for dout in range(8):
                pb = 4 + (dout % 4)
                self.mm(ps[pb][:, 0:16], [(wo[:, c, dout * 128:(dout + 1) * 128], self.as_[:, c, :]) for c in range(n)],
                        [ro, "as"], ["ps%d" % pb])
                tmp = self.S(3)[:, 0:16]
                self.tt("dve", tmp, ps[pb][:, 0:16], self.coefS[:, g2 + dout, :], ALU.mult, ["ps%d" % pb, "coefS"], ["s3"])
                self.tt("dve", self.hs[:, dout, :], self.hs[:, dout, :], tmp, ALU.add, ["hs", "s3"], ["hs"])

    def sampleA(self, l):
        kb, ps, S = self.kb, self.ps, self.S
        self.sample_modulate(l, 0)
        items = [(self.w_in_h[l, j, :, :], "K8") for j in range(8)]
        for j, (w, wres) in enumerate(self.stream(items)):
            self.ld(self.Sin[:, :, :], self.state_d[:, l, j].rearrange("b k v -> k b v"), ["Sin"], "si")
            pp = ps[0]
            for part in range(4):
                self.mm(pp[:, part * 16:(part + 1) * 16],
                        [(w[:, k, part * 128:(part + 1) * 128], self.xns[:, k, :]) for k in range(8)], [wres, "xns"], ["ps0"])
            q, gate = S(0)[:, 0:16], S(0)[:, 16:32]
            self.act(q, pp[:, 0:16], AF.Silu, ["ps0"], ["s0"])
            self.act(gate, pp[:, 48:64], AF.Silu, ["ps0"], ["s0"])
            e, L1, L2, lg, f, kk, v = [S(1)[:, i * 16:(i + 1) * 16] for i in range(7)]
            self.act(e, pp[:, 16:32], AF.Exp, ["ps0"], ["s1"], scale=-1.0)
            self.act(L1, e, AF.Ln, ["s1", "lbs"], ["s1"], scale=self.lbs[:, l, j:j + 1], bias=self.epsc[:, 1:2])
            self.act(L2, e, AF.Ln, ["s1"], ["s1"], scale=1.0, bias=self.epsc[:, 1:2])
            self.tt("dve", lg, L1, L2, ALU.subtract, ["s1"], ["s1"])
            self.act(f, lg, AF.Exp, ["s1"], ["s1"])
            self.ts("dve", kk, f, -1.0, 1.0, ALU.mult, ALU.add, ["s1"], ["s1"])
            self.cp("dve", v, pp[:, 32:48], ["ps0"], ["s1"])
            rd = self.scr[:, 4:8, :].rearrange("p a (b v) -> p (a b) v", v=128)
            self.tt("dve", rd, self.ident[:, :].unsqueeze(1).broadcast_to([128, 16, 128]), bc3(v, 128), ALU.mult,
                    ["ident", "s1"], ["s4", "s5", "s6", "s7"])
            for qd in range(4):
                self.mm(ps[4 + qd][:, :], [(self.ones64[:, 0:128], self.scr[:, 4 + qd, :])], ["ones64", "s%d" % (4 + qd)],
                        ["ps%d" % (4 + qd)])
                self.tt("dve", self.scr[:, 8 + qd, :].rearrange("p (b v) -> p b v", v=128),
                        ps[4 + qd][:, :].rearrange("p (b v) -> p b v", v=128), bc3(kk[:, 4 * qd:4 * qd + 4], 128), ALU.mult,
                        ["ps%d" % (4 + qd), "s1"], ["s%d" % (8 + qd)])
            self.tt("dve", self.Sout[:, :, :], self.Sin[:, :, :], bc3(f, 128), ALU.mult, ["Sin", "s1"], ["Sout"])
            self.tt("dve", self.Sout[:, :, :], self.Sout[:, :, :], self.scr[:, 8:12, :].rearrange("p a (b v) -> p (a b) v", v=128),
                    ALU.add, ["Sout", "s8", "s9", "s10", "s11"], ["Sout"])
            self.ld(self.st_s[:, l, j].rearrange("b k v -> k b v"), self.Sout[:, :, :], ["st_s"], "so", reads=["Sout"])
            po2 = ps[1]
            q2 = S(0)[:, 32:64].rearrange("p (b t) -> p b t", t=2)
            self.cp("dve", q2, bc3(q, 2), ["s0"], ["s0"])
            for b in range(16):
                kb.emit("pe", lambda e, b=b: e.matmul(po2[:, 2 * b:2 * b + 2], lhsT=self.Sout[:, b, :], rhs=q2[:, b, :], start=True, stop=True),
                        ["Sout", "s0"], ["ps1"], inc=(b == 15))
            po = po2[:, 0:32:2]
            osq = self.B(1)[:, 0:16]
            self.act(osq, po, AF.Square, ["ps1"], ["b1"])
            self.mm(ps[2][:, 0:16], [(self.onesb[:, :], osq)], ["onesb", "b1"], ["ps2"])
            rstd = S(2)[:, 0:16]
            self.act(rstd, ps[2][:, 0:16], AF.Ln, ["ps2"], ["s2"], scale=1.0 / 128, bias=self.epsc[:, 0:1])
            self.act(rstd, rstd, AF.Exp, ["s2"], ["s2"], scale=-0.5)
            t = S(2)[:, 16:32]
            self.stt(t, po, self.vecs[:, V_GN + l:V_GN + l + 1], rstd, ALU.mult, ALU.mult, ["ps1", "s2", "vecs"], ["s2"])
            self.tt("dve", self.ogs[:, j, :], t, gate, ALU.mult, ["s2", "s0"], ["ogs"])
        if "st_s" not in self.outs:
            self.outs.append("st_s")
        _, _, g1 = self.coef(l, 0)
        self.sample_proj([(self.w_o_a[l, :, 0:512], "K8"), (self.w_o_a[l, :, 512:1024], "K8")], self.ogs, "ogs", g1, 8)

    def sample_kv(self):
        ps, S = self.ps, self.S
        self.sample_modulate(4, 0)
        self.kb.dma("pool", "cv", self.Vs[0:112, :, :], self.cv_d[:, 1:113, :].rearrange("b s n -> s b n"), [], ["Vs"])
        self.kb.dma("pool", "cv2", self.Vs[112:127, :, :], self.cv_d[:, 113:128, :].rearrange("b s n -> s b n"), [], ["Vs"])
        ckv = self.ckT_d.ap().rearrange("b (c p) t -> p c b t", p=128)
        for c in range(2):
            self.kb.dma("pool", "ck%d" % c, self.KTs[:, c, :, 0:127], ckv[:, c, :, :], [], ["KTs"])
        self.ld(self.ck_s[:, 0:127, :], self.ck_d[:, 1:128, :], ["ck_s"], "ock")
        self.ld(self.cv_s[:, 0:127, :], self.cv_d[:, 1:128, :], ["cv_s"], "ocv")
        (w, wres), = list(self.stream([(self.w_kv[:, :], "K8")]))
        for c in range(2):
            self.mm(ps[0][:, c * 16:(c + 1) * 16], [(w[:, k, c * 128:(c + 1) * 128], self.xns[:, k, :]) for k in range(8)],
                    [wres, "xns"], ["ps0"])
        self.cp("dve", self.KTs[:, :, :, 127], ps[0][:, 0:32].rearrange("p (c b) -> p c b", c=2), ["ps0"], ["KTs"])
        self.mm(ps[1][0:16, :], [(self.xns[:, k, :], w[:, k, :]) for k in range(8)], [wres, "xns"], ["ps1"])
        rowf = S(0)[0:16, :]
        self.cp("dve", rowf, ps[1][0:16, :], ["ps1"], ["s0"])
        self.ld(self.ck_s[:, 127, :], rowf[:, 0:256], ["ck_s"], "ock2", reads=["s0"])
        self.ld(self.cv_s[:, 127, :], rowf[:, 256:512], ["cv_s"], "ocv2", reads=["s0"])
        rowb = self.B(0)[0:16, 0:256]
        self.cp("dve", rowb, ps[1][0:16, 256:512], ["ps1"], ["b0"])
        self.ld(self.rowk_d[:, :], rowb, ["rowk"], "rk", reads=["b0"])
        self.ld(self.Vs[127:128, :, :], self.rowk_d.ap().rearrange("(o b) n -> o b n", o=1), ["Vs"], "rk2", reads=["rowk"])
        self.outs += ["ck_s", "cv_s"]

    def sampleB(self, jb):
        kb, ps, S = self.kb, self.ps, self.S
        l = 2 + jb
        self.sample_modulate(l, 0)
        i = 0
        for w, wres in self.stream([(self.w_q_p[jb, :, 0:512], "K8"), (self.w_q_p[jb, :, 512:1024], "K8")]):
            for dd in range(4):
                pb = i % 4
                self.mm(ps[pb][:, 0:16], [(w[:, k, dd * 128:(dd + 1) * 128], self.xns[:, k, :]) for k in range(8)],
                        [wres, "xns"], ["ps%d" % pb])
                self.act(self.qTs[:, i, :], ps[pb][:, 0:16], AF.Copy, ["ps%d" % pb], ["qTs"], scale=0.125)
                i += 1
        kb.emit("dve", lambda e: e.memset(self.Qb2[:, :, :, :], 0.0), [], ["Qb2"])
        for c in range(2):
            for half in range(2):
                hp = slice(half * 64, (half + 1) * 64)
                self.cp("dve", self.Qb2[hp, :, c, c * 8 + half:c * 8 + 8:2],
                        self.qTs[hp, 4 * c:4 * c + 4, :].rearrange("p i b -> p b i"), ["qTs"], ["Qb2"])
        for b in range(16):
            pb = 4 + b // 4
            self.mm(ps[pb][0:16, (b % 4) * 128:(b % 4 + 1) * 128],
                    [(self.Qb2[:, b, c, :], self.KTs[:, c, b, :]) for c in range(2)], ["Qb2", "KTs"], ["ps%d" % pb])
        sv = lambda qd: self.scr[0:16, 4 + qd, :].rearrange("p (b t) -> p b t", t=128)
        for qd in range(4):
            self.tt("dve", sv(qd), ps[4 + qd][0:16, :].rearrange("p (b t) -> p b t", t=128),
                    self.bias_s[0:16, :].unsqueeze(1).broadcast_to([16, 4, 128]), ALU.add, ["ps%d" % (4 + qd), "bias_s"], ["s%d" % (4 + qd)])
        s_all = self.scr[0:16, 4:8, :].rearrange("p a (b t) -> p (a b) t", t=128)
        sr = ["s4", "s5", "s6", "s7"]
        sm = self.small
        mx, rs, es, dn = sm[0:16, 0:16], sm[0:16, 16:32], sm[0:16, 32:48], sm[0:16, 48:64]
        kb.emit("dve", lambda e: e.tensor_reduce(out=mx, in_=s_all, axis=AX.X, op=ALU.max), sr, ["sm0"])
        skc = self.sink_s[0:16, jb:jb + 1]
        self.ts("dve", mx, mx, skc, None, ALU.max, ALU.bypass, ["sm0", "sink_s"], ["sm0"])
        self.tt("dve", s_all, s_all, bc3(mx, 128), ALU.subtract, sr + ["sm0"], sr)
        self.act(self.scr[0:16, 4:8, :], self.scr[0:16, 4:8, :], AF.Exp, sr, sr)
        kb.emit("dve", lambda e: e.tensor_reduce(out=rs, in_=s_all, axis=AX.X, op=ALU.add), sr, ["sm0"])
        self.act(es, mx, AF.Exp, ["sm0", "sink_s"], ["sm0"], scale=-1.0, bias=skc)
        self.tt("dve", dn, rs, es, ALU.add, ["sm0"], ["sm0"])
        kb.emit("dve", lambda e: e.reciprocal(out=dn, in_=dn), ["sm0"], ["sm0"])
        pn = self.bscr[0:16, 4:8, :].rearrange("p a (b t) -> p (a b) t", t=128)
        pnr = ["b4", "b5", "b6", "b7"]
        self.tt("dve", pn, s_all, bc3(dn, 128), ALU.mult, sr + ["sm0"], pnr)
        ptb = ps[0][:, :].bitcast(BF16)
        for b in range(16):
            kb.emit("pe", lambda e, b=b: e.transpose(out=ptb[:, b * 16:(b + 1) * 16], in_=pn[:, b, :], identity=self.identb[0:16, 0:16]),
                    pnr + ["identb"], ["ps0"], inc=(b == 15))
        self.cp("act", self.pTs[:, :, :].rearrange("p b s -> p (b s)"), ptb[:, 0:256], ["ps0"], ["pTs"])
        pv = ps[1]
        for b in range(16):
            for c in range(2):
                o0 = (b * 2 + c) * 16
                kb.emit("pe", lambda e, b=b, c=c, o0=o0: e.matmul(pv[:, o0:o0 + 16], lhsT=self.Vs[:, b, c * 128:(c + 1) * 128],
                                                                  rhs=self.pTs[:, b, :], start=True, stop=True),
                        ["Vs", "pTs"], ["ps1"], inc=(b == 15 and c == 1))
        pvv = pv[:, :].rearrange("p (b c s) -> p b c s", b=16, c=2)
        for c in range(2):
            for half in range(2):
                hp = slice(half * 64, (half + 1) * 64)
                self.cp("dve", self.ogs[hp, 4 * c:4 * c + 4, :].rearrange("p i b -> p b i"),
                        pvv[hp, :, c, c * 8 + half:c * 8 + 8:2], ["ps1"], ["ogs"])
        _, _, g1 = self.coef(l, 0)
        self.sample_proj([(self.w_o_p[jb, :, 0:512], "K8"), (self.w_o_p[jb, :, 512:1024], "K8")], self.ogs, "ogs", g1, 8)

    def sample_phase(self):
        S, ps = self.S, self.ps
        self.ld(self.hs[:, :, :], self.xsT[:, :, :], ["hs"], "c20")
        self.ld(self.sink_s[0:16, 0:2], self.sink_s_d[:, :], ["sink_s"], "c21")
        rb = S(12)[0:33, 0:16]
        el = S(13)[0:33, 0:128]
        self.ld(rb, self.rb_ext_d[:, :], ["s12"], "c22")
        self.ld(el, self.eline_s_d[:, :], ["s13"], "c23")
        self.mm(ps[3][0:16, 0:128], [(rb, el)], ["s12", "s13"], ["ps3"])
        self.cp("dve", self.bias_s[0:16, :], ps[3][0:16, 0:128], ["ps3"], ["bias_s"])
        for l in range(4):
            if l < 2:
                self.sampleA(l)
            else:
                self.sampleB(l - 2)
            self.sample_ffn(l)
            if l == 1:
                self.sample_kv()
        sq = self.B(0)[:, 0:128].rearrange("p (k n) -> p k n", k=8)
        self.act(sq, self.hs[:, :, :], AF.Square, ["hs"], ["b0"])
        self.mm(ps[0][:, 0:16], [(self.onesb[:, :], sq[:, k, :]) for k in range(8)], ["onesb", "b0"], ["ps0"])
        rstd = S(0)[:, 0:16]
        self.act(rstd, ps[0][:, 0:16], AF.Ln, ["ps0"], ["s0"], scale=1.0 / D, bias=self.epsc[:, 0:1])
        self.act(rstd, rstd, AF.Exp, ["s0"], ["s0"], scale=-0.5)
        t = S(1)[:, 0:128].rearrange("p (k n) -> p k n", k=8)
        self.tt("dve", t, self.hs[:, :, :], rstd.unsqueeze(1).broadcast_to([128, 8, 16]), ALU.mult, ["hs", "s0"], ["s1"])
        self.tt("dve", t, t, bc3(self.vecs[:, V_FIN:V_FIN + 8], 16), ALU.mult, ["s1", "vecs"], ["s1"])
        self.ld(self.y_s[:, :, :], t, ["y_s"], "oys", reads=["s1"])
        self.outs.append("y_s")

    for f in (sample_modulate, sample_proj, sample_ffn, sampleA, sample_kv, sampleB, sample_phase):
        setattr(Prog, f.__name__, f)


_sample_methods()


T_BLK = 1024
N_PRE = 5
N_OWN = 2
OWN_TOK = T_BLK * N_OWN
_CACHE = {}


def kernel(**inputs):
    if "nc" not in _CACHE:
        prog = Prog(T=T_BLK, sample=True, NPRE=N_PRE, NOWN=N_OWN)
        _CACHE["nc"] = prog.build()
    nc = _CACHE["nc"]
    NBLK = N_PRE + 1 + N_OWN
    NTOK = T_BLK * NBLK
    shared = prep_shared(inputs)
    in_maps = []
    for c in range(8):
        seq, p = c // 4, c % 4
        end = (p + 1) * OWN_TOK
        in_maps.append(prep_core(inputs, shared, c, T_BLK, NBLK, win=(seq, end - NTOK, end)))
    res = run_bass_kernel_spmd(nc, in_maps, core_ids=list(range(8)))
    r = res.results
    g = lambda c, n, shp: np.asarray(r[c][n], np.float32).reshape(shp)
    y_prompt = np.stack([np.concatenate([g(4 * s + p, "yT", (128, 8, OWN_TOK)).transpose(2, 1, 0).reshape(OWN_TOK, 1024)
                                         for p in range(4)], 0) for s in range(2)])
    y_sample = np.concatenate([g(c, "y_s", (128, 8, 16)).transpose(2, 1, 0).reshape(16, 1, 1024) for c in range(8)], 0)
    last = [3, 7]
    st_p = np.stack([g(c, "st_p", (2, 8, 128, 128)) for c in last])
    st_s = np.concatenate([g(c, "st_s", (16, 2, 8, 128, 128)) for c in range(8)], 0)
    k_p = np.stack([g(c, "kT_p", (2, 64, 2, 128)).transpose(3, 2, 0, 1).reshape(128, 4, 64) for c in last])
    v_p = np.stack([g(c, "v_p", (128, 4, 64)) for c in last])
    k_s = np.concatenate([g(c, "ck_s", (16, 128, 4, 64)) for c in range(8)], 0)
    v_s = np.concatenate([g(c, "cv_s", (16, 128, 4, 64)) for c in range(8)], 0)
    f = np.ascontiguousarray
    return (f(y_prompt), f(y_sample), f(st_p), f(st_s), f(k_p), f(v_p), f(k_s), f(v_s))
```

```python
from contextlib import ExitStack
import numpy as np
import concourse.bass as bass
import concourse.mybir as mybir
from concourse.bass_utils import run_bass_kernel_spmd

F32 = mybir.dt.float32
BF16 = mybir.dt.bfloat16
I32 = mybir.dt.int32
AF = mybir.ActivationFunctionType
ALU = mybir.AluOpType
AX = mybir.AxisListType

SAME_ENGINE_SYNC = True


class KB:
    ENG = ("pe", "act", "dve", "pool", "sp")

    def __init__(self, nc):
        self.nc = nc
        self.stack = ExitStack()
        self.h = {"pe": nc.tensor, "act": nc.scalar, "dve": nc.vector, "pool": nc.gpsimd, "sp": nc.sync}
        self.prog = {e: [] for e in self.ENG}
        self.cnt = {e: 0 for e in self.ENG}
        self.seen = {e: {} for e in self.ENG}
        self.res_w = {}
        self.res_r = {}
        self.sems = {}
        self.n_inst = 0

    def sbuf(self, name, shape, dtype):
        return self.stack.enter_context(self.nc.sbuf_tensor("t_" + name, shape, dtype))

    def psum(self, name, shape, dtype):
        return self.stack.enter_context(self.nc.psum_tensor("p_" + name, shape, dtype))

    def _sem(self, key):
        if key not in self.sems:
            self.sems[key] = self.stack.enter_context(self.nc.semaphore("s_" + key.replace(":", "_")))
            self.cnt.setdefault(key, 0)
        return self.sems[key]

    @staticmethod
    def _is_psum(r):
        return (isinstance(r, str) and r.startswith("ps")) or (isinstance(r, tuple) and str(r[0]).startswith("ps"))

    def _waits(self, e, reads, writes):
        w = {}

        def need(key, val):
            if key == e and (not SAME_ENGINE_SYNC or val > self.cnt[e]):
                return
            if val > w.get(key, 0):
                w[key] = val
        for r in reads:
            lw = self.res_w.get(r)
            if lw:
                need(*lw)
            if self._is_psum(r):
                for k, v in self.res_r.get(r, {}).items():
                    if k != e:
                        need(k, v)
        for x in writes:
            lw = self.res_w.get(x)
            if lw:
                need(*lw)
            for k, v in self.res_r.get(x, {}).items():
                need(k, v)
        out = []
        for k, v in w.items():
            if self.seen[e].get(k, 0) < v:
                self.seen[e][k] = v
                out.append((k, v))
        return out

    def _mark(self, key, val, reads, writes):
        for r in reads:
            d = self.res_r.setdefault(r, {})
            if d.get(key, 0) < val:
                d[key] = val
        for x in writes:
            self.res_w[x] = (key, val)
            self.res_r[x] = {}

    def emit(self, e, fn, reads, writes, inc=True):
        waits = self._waits(e, reads, writes)
        self._sem(e)
        if inc:
            self.cnt[e] += 1
            val = self.cnt[e]
        else:
            val = self.cnt[e] + 1
        self._mark(e, val, reads, writes)
        self.prog[e].append((waits, fn, (e, 1) if inc else None))
        self.n_inst += 1

    def dma(self, q, key, out, in_, reads, writes, **kw):
        key = "d:" + key
        self._sem(key)
        waits = self._waits(q, reads, writes)
        self.cnt[key] += 16
        val = self.cnt[key]
        self._mark(key, val, reads, writes)
        self.prog[q].append((waits, lambda e: e.dma_start(out=out, in_=in_, **kw), (key, 16)))
        self.n_inst += 1

    def collective(self, kind, in_ap, out_ap, reads, writes, key, groups=None):
        key = "c:" + key
        self._sem(key)
        waits = self._waits("pool", reads, writes)
        self.cnt[key] += 1
        val = self.cnt[key]
        self._mark(key, val, reads, writes)
        groups = groups or [list(range(8))]
        self.prog["pool"].append((waits, lambda e: e.collective_compute(
            kind, ALU.bypass, replica_groups=groups, ins=[in_ap.opt()], outs=[out_ap.opt()]), (key, 1)))
        self.n_inst += 1

    def barrier(self):
        for e in self.ENG:
            waits = []
            for k, v in self.cnt.items():
                if v > 0 and k != e and self.seen[e].get(k, 0) < v:
                    self.seen[e][k] = v
                    waits.append((k, v))
            self.prog[e].append((waits, None, None))

    def finish(self, final_res):
        waits = self._waits("sp", final_res, [])
        self.prog["sp"].append((waits, None, None))
        nc = self.nc
        with nc.Block() as block:
            def run(e):
                def body(eng):
                    for waits, fn, inc in self.prog[e]:
                        for k, v in waits:
                            eng.wait_ge(self.sems[k], v)
                        if fn is None:
                            continue
                        ins = fn(eng)
                        if inc is not None:
                            ins.then_inc(self.sems[inc[0]], inc[1])
                return body
            block.tensor(run("pe"))
            block.scalar(run("act"))
            block.vector(run("dve"))
            block.gpsimd(run("pool"))
            block.sync(run("sp"))
        self.stack.close()


def kernel(**inputs):
    raise NotImplementedError


D = 1024
KD = 8
DFF = 2816
NFFC = 22
EPS = 1e-6
MASKV = -1e30
FFG = [(0, 4), (4, 4), (8, 4), (12, 4), (16, 4), (20, 2)]

V_NORM = 0
V_KVN = 64
V_FIN = 72
V_LB = 80
V_BADA = 96
V_BADAKV = 288
V_GN = 304
NV = 306
NCOEF = 208


def slot_heads():
    out = []
    for i in range(8):
        for half in range(2):
            out.append(i + 4 * half if i < 4 else 8 + (i - 4) + 4 * half)
    return out


def t5_bucket_np(d):
    d = np.asarray(d)
    dd = np.clip(d, 0, 127)
    large = 16 + (np.log(np.maximum(dd, 1).astype(np.float32) / np.float32(16)) / np.float32(np.log(8.0))
                  * np.float32(16)).astype(np.int32)
    large = np.clip(large, 0, 31)
    return np.where(dd < 16, dd, large)


class Prog:
    def __init__(self, T=1024, NBLK=8, sample=True, dbg=None, nlayers=4, NPRE=None, NOWN=None):
        if NPRE is None:
            self.types = ["own"] * NBLK
        else:
            self.types = ["pre"] * NPRE + ["prekv"] + ["own"] * NOWN
            NBLK = len(self.types)
        self.rmode = NPRE is not None
        self.own_blks = [i for i, t in enumerate(self.types) if t == "own"]
        self.T, self.NBLK, self.NB = T, NBLK, T // 512
        self.sample = sample
        self.dbg = dbg
        self.nlayers = nlayers
        self.NTOK = T * NBLK
        nc = self.nc = bass.Bass("TRN2", target_bir_lowering=False)
        kb = self.kb = KB(nc)
        di = lambda n, s: nc.dram_tensor(n, s, F32, kind="ExternalInput")
        do = lambda n, s: nc.dram_tensor(n, s, F32, kind="ExternalOutput")
        NTOK = self.NTOK
        self.xT = di("xT", [128, 8, NTOK])
        self.cT = di("cT", [128, 8, 17])
        self.xsT = di("xsT", [128, 8, 16])
        self.vecs_d = di("vecs", [128, NV])
        self.w_in_h = di("w_in_h", [2, 8, 1024, 512])
        self.w_o_a = di("w_o_a", [2, 1024, 1024])
        self.w_kv = di("w_kv", [1024, 512])
        self.w_ada = di("w_ada", [4, 1024, 6144])
        self.w_ada_kv = di("w_ada_kv", [1024, 2048])
        self.w_q_p = di("w_q_p", [2, 1024, 1024])
        self.w_o_p = di("w_o_p", [2, 1024, 1024])
        self.w_ffn_in = di("w_ffn_in", [4, 1024, 5632])
        self.w_ffn_out = di("w_ffn_out", [4, 2816, 1024])
        self.sinks_bc_d = di("sinks_bc", [128, 32])
        self.sink_s_d = di("sink_s", [16, 2])
        self.rb_ext_d = di("rb_ext", [33, 16])
        self.eline_d = di("eline", [33, 384])
        self.eline_s_d = di("eline_s", [33, 128])
        self.ident_d = di("ident", [128, 128])
        self.trimask_d = di("trimask", [64, 64])
        self.halfmask_d = di("halfmask", [128, 2])
        self.state_d = di("state", [16, 2, 8, 128, 128])
        self.ck_d = di("ck", [16, 128, 256])
        self.cv_d = di("cv", [16, 128, 256])
        self.ckT_d = di("ckT", [16, 256, 127])
        self.bmask_d = di("bmask", [128, NBLK])
        self.yT = do("yT", [128, 8, T * len(self.own_blks)])
        self.st_p = do("st_p", [2, 8, 128, 128])
        self.kT_p = do("kT_p", [128, 2, 128])
        self.v_p = do("v_p", [128, 256])
        self.y_s = do("y_s", [128, 8, 16])
        self.st_s = do("st_s", [16, 2, 8, 128, 128])
        self.ck_s = do("ck_s", [16, 128, 256])
        self.cv_s = do("cv_s", [16, 128, 256])
        if dbg is not None:
            self.dbg_d = do("dbg", [128, 8, T])
        self.lines_d = nc.dram_tensor("lines", [16, 128, 384], F32)
        self.rowk_d = nc.dram_tensor("rowk", [16, 256], BF16)
        self.outs = []
        self.alloc()

    def alloc(self):
        kb, T = self.kb, self.T
        sb = kb.sbuf
        W = 8 * T + 3 * 4 * T
        W = max(W, 16384)
        self.arena = sb("arena", [128, W], F32)
        ar = self.arena
        self.hT = ar[:, 0:8 * T].rearrange("p (k t) -> p k t", k=8)
        o = 8 * T
        self.xn = ar[:, o:o + 4 * T].bitcast(BF16).rearrange("p (k t) -> p k t", k=8)
        self.og = ar[:, o + 4 * T:o + 8 * T].bitcast(BF16).rearrange("p (k t) -> p k t", k=8)
        self.qT = ar[:, o + 8 * T:o + 12 * T].bitcast(BF16).rearrange("p (k t) -> p k t", k=8)
        def carve(off, words, dt=F32):
            v = ar[:, off:off + words]
            return v if dt == F32 else v.bitcast(dt)
        o = 0
        self.adaT = carve(o, 208 * 17).rearrange("p (g n) -> p g n", n=17); o += 208 * 17
        self.coefS = carve(o, 208 * 16).rearrange("p (g n) -> p g n", n=16); o += 208 * 16
        self.csb = carve(o, 68, BF16).rearrange("p (k n) -> p k n", k=8); o += 68
        self.hs = carve(o, 128).rearrange("p (k n) -> p k n", k=8); o += 128
        self.xns = carve(o, 64, BF16).rearrange("p (k n) -> p k n", k=8); o += 64
        self.ogs = carve(o, 64, BF16).rearrange("p (k n) -> p k n", k=8); o += 64
        self.as_ = carve(o, 32, BF16).rearrange("p (k n) -> p k n", k=4); o += 32
        self.qTs = carve(o, 64, BF16).rearrange("p (k n) -> p k n", k=8); o += 64
        self.Qb2 = carve(o, 256, BF16).rearrange("p (b c s) -> p b c s", b=16, c=2); o += 256
        self.Sin = carve(o, 2048).rearrange("p (b v) -> p b v", b=16); o += 2048
        self.Sout = carve(o, 2048).rearrange("p (b v) -> p b v", b=16); o += 2048
        self.KTs = carve(o, 2048, BF16).rearrange("p (c b t) -> p c b t", c=2, b=16); o += 2048
        self.Vs = carve(o, 2048, BF16).rearrange("p (b n) -> p b n", b=16); o += 2048
        self.pTs = carve(o, 128, BF16).rearrange("p (b s) -> p b s", b=16); o += 128
        self.bias_s = carve(o, 128); o += 128
        self.sink_s = carve(o, 2); o += 2
        assert o <= W, (o, W)
        self.NSLOT = 5
        self.wsl = [sb("wsl%d" % i, [128, 4096], BF16) for i in range(self.NSLOT)]
        self.NS = 14
        self.scr = sb("scr", [128, self.NS, 512], F32)
        self.NBS = 10
        self.bscr = sb("bscr", [128, self.NBS, 512], BF16)
        self.G = sb("G", [128, 513], F32)
        self.KT = sb("KT", [128, 2, 128 + T], BF16)
        self.V = sb("V", [128, 1 + T // 128, 256], BF16)
        self.bias = sb("bias", [128, 16, 256], F32)
        self.Sst = sb("Sst", [128, 16, 128], F32)
        self.coefP = sb("coefP", [128, NCOEF], F32)
        self.vecs = sb("vecs", [128, NV], F32)
        self.lbs = sb("lbs", [128, 2, 8], F32)
        self.sinks_bc = sb("sinks_bc", [128, 32], F32)
        self.ident = sb("ident", [128, 128], F32)
        self.identb = sb("identb", [128, 128], BF16)
        self.onesb = sb("onesb", [128, 128], BF16)
        self.ones64 = sb("ones64", [128, 512], F32)
        self.trimask = sb("trimask", [64, 64], F32)
        self.small = sb("small", [128, 64], F32)
        self.epsc = sb("epsc", [128, 2], F32)
        self.bmask = sb("bmask", [128, self.NBLK], F32)
        self.hmask = sb("hmask", [128, 1], F32)
        self.attm = sb("attm", [64, 8, 64], BF16)
        self.ps = [kb.psum("ps%d" % i, [128, 512], F32) for i in range(8)]
        self.ws_i = 0

    def S(self, i):
        return self.scr[:, i, :]

    def B(self, i):
        return self.bscr[:, i, :]

    def act(self, out, in_, func, reads, writes, **kw):
        self.kb.emit("act", lambda e: e.activation(out=out, in_=in_, func=func, **kw), reads, writes)

    def tt(self, eng, out, a, b, op, reads, writes):
        self.kb.emit(eng, lambda e: e.tensor_tensor(out=out, in0=a, in1=b, op=op), reads, writes)

    def ts(self, eng, out, a, s1, s2, op0, op1, reads, writes):
        self.kb.emit(eng, lambda e: e.tensor_scalar(out=out, in0=a, scalar1=s1, scalar2=s2, op0=op0, op1=op1),
                     reads, writes)

    def stt(self, out, a, s, b, op0, op1, reads, writes):
        self.kb.emit("dve", lambda e: e.scalar_tensor_tensor(out=out, in0=a, scalar=s, in1=b, op0=op0, op1=op1),
                     reads, writes)

    def cp(self, eng, out, in_, reads, writes):
        if eng == "act":
            self.act(out, in_, AF.Copy, reads, writes)
        else:
            self.kb.emit(eng, lambda e: e.tensor_copy(out=out, in_=in_), reads, writes)

    def mm(self, out, pairs, reads, writes):
        n = len(pairs)
        for i, (l, r) in enumerate(pairs):
            self.kb.emit("pe", lambda e, l=l, r=r, i=i: e.matmul(out, lhsT=l, rhs=r, start=(i == 0), stop=(i == n - 1)),
                         reads, writes, inc=(i == n - 1))

    def ld(self, out, in_, writes, key, q="sp", reads=()):
        self.kb.dma(q, key, out, in_, list(reads), writes)

    def slab(self, dram_ap, kind):
        i = self.ws_i % self.NSLOT
        self.ws_i += 1
        t = self.wsl[i]
        res = ("w", i)
        if kind == "K8":
            ncol = dram_ap.shape[1]
            view = t[:, :].rearrange("p (k n) -> p k n", k=8)
            self.kb.dma("pool", "w%d" % i, view[:, :, 0:ncol], dram_ap.rearrange("(k p) n -> p k n", p=128), [], [res])
        else:
            nr = dram_ap.shape[0] // 128
            view = t[:, :].rearrange("p (c n) -> p c n", c=4)
            self.kb.dma("pool", "w%d" % i, view[:, 0:nr, :], dram_ap.rearrange("(c p) n -> p c n", p=128), [], [res])
        return view, res

    def stream(self, items):
        q = []
        it = iter(items)
        for _ in range(2):
            x = next(it, None)
            if x is not None:
                q.append(self.slab(*x))
        while q:
            cur = q.pop(0)
            x = next(it, None)
            if x is not None:
                q.append(self.slab(*x))
            yield cur

    def setup(self):
        kb = self.kb
        self.ld(self.vecs[:, :], self.vecs_d[:, :], ["vecs"], "c11")
        self.ld(self.sinks_bc[:, :], self.sinks_bc_d[:, :], ["sinks"], "c12")
        self.ld(self.ident[:, :], self.ident_d[:, :], ["ident"], "c13")
        self.ld(self.trimask[:, :], self.trimask_d[:, :], ["trimask"], "c14")
        kb.emit("dve", lambda e: e.tensor_copy(out=self.identb[:, :], in_=self.ident[:, :]), ["ident"], ["identb"])
        self.ld(self.bmask[:, :], self.bmask_d[:, :], ["bmask"], "c30")
        if self.rmode:
            pk = self.types.index("prekv")
            self.ts("dve", self.hmask[:, 0:1], self.bmask[:, pk:pk + 1], 1.0, -MASKV, ALU.subtract, ALU.mult, ["bmask"], ["hmask"])
        kb.emit("dve", lambda e: e.memset(self.onesb[:, :], 1.0), [], ["onesb"])
        kb.emit("dve", lambda e: e.memset(self.ones64[:, :], 1.0), [], ["ones64"])
        kb.emit("dve", lambda e: e.memset(self.G[:, 0:1], 0.0), [], ["G0"])
        kb.emit("dve", lambda e: e.memset(self.attm[:, :, :], 0.0), [], ["attm"])
        kb.emit("dve", lambda e: e.memset(self.epsc[:, 0:1], EPS), [], ["epsc"])
        kb.emit("dve", lambda e: e.memset(self.epsc[:, 1:2], 1.0), [], ["epsc"])
        kb.emit("dve", lambda e: e.memset(self.Sst[:, :, :], 0.0), [], ["Sst"])
        kb.emit("dve", lambda e: e.memset(self.lbs[:, 0, :], 0.0), [], ["lbs"])
        t = self.small[:, 0:8]
        self.tt("dve", t, self.vecs[:, V_LB:V_LB + 8], self.vecs[:, V_LB + 8:V_LB + 16], ALU.subtract, ["vecs"], ["small"])
        self.act(t, t, AF.Exp, ["small"], ["small"])
        self.ts("dve", t, t, 1.0, None, ALU.add, ALU.bypass, ["small"], ["small"])
        kb.emit("dve", lambda e: e.reciprocal(out=self.lbs[:, 1, :], in_=t), ["small"], ["lbs"])
        rb = self.S(0)[0:33, 0:16]
        el = self.S(1)[0:33, 0:384]
        self.ld(rb, self.rb_ext_d[:, :], ["s0"], "c15")
        self.ld(el, self.eline_d[:, :], ["s1"], "c16")
        pl = self.ps[0][0:16, 0:384]
        self.mm(pl, [(rb, el)], ["s0", "s1"], ["ps0"])
        ln = self.S(2)[0:16, 0:384]
        self.cp("dve", ln, pl, ["ps0"], ["s2"])
        src = bass.AP(ln.tensor, ln.offset, [list(ln.ap[0]), [0, 128], [1, 384]])
        self.ld(self.lines_d[:, :, :], src, ["lines"], "c1", reads=["s2"])
        tv = bass.AP(self.lines_d, 127, [[383, 128], [128 * 384, 16], [1, 256]])
        self.ld(self.bias[:, :, :], tv, ["bias"], "c1", reads=["lines"])

    def coef(self, l, which):
        if l == 4:
            return 192, 200, None
        base = l * 48 + which * 24
        return base, base + 8, base + 16

    def modulate(self, l, which):
        a0, b0, _ = self.coef(l, which)
        for sb in range(self.NB):
            tok = slice(sb * 512, (sb + 1) * 512)
            hs = ("h", sb)
            sq = self.bscr[:, 0:8, :]
            sqr = ["b%d" % k for k in range(8)]
            self.act(sq, self.hT[:, :, tok], AF.Square, [hs], sqr)
            pss = self.ps[0]
            self.mm(pss[:, :], [(self.onesb[:, :], self.bscr[:, k, :]) for k in range(8)], ["onesb"] + sqr, ["ps0"])
            rstd = self.S(0)
            self.act(rstd, pss[:, :], AF.Ln, ["ps0"], ["s0"], scale=1.0 / D, bias=self.epsc[:, 0:1])
            self.act(rstd, rstd, AF.Exp, ["s0"], ["s0"], scale=-0.5)
            for k in range(8):
                t = self.S(1 + (k % 2))
                r = "s%d" % (1 + (k % 2))
                self.stt(t, self.hT[:, k, tok], self.coefP[:, a0 + k:a0 + k + 1], rstd, ALU.mult, ALU.mult,
                         [hs, "s0", "coefP"], [r])
                self.act(self.xn[:, k, tok], t, AF.Identity, [r, "coefP"], [("xn", sb)],
                         bias=self.coefP[:, b0 + k:b0 + k + 1], scale=1.0)

    def proj_residual(self, slabs, src, src_res, gcol, nk):
        dout = 0
        for view, wres in self.stream(slabs):
            for dd in range(4):
                for sb in range(self.NB):
                    tok = slice(sb * 512, (sb + 1) * 512)
                    pb = 4 + ((dout * self.NB + sb) % 4)
                    p = self.ps[pb]
                    self.mm(p[:, :], [(view[:, k, dd * 128:(dd + 1) * 128], src[:, k, tok]) for k in range(nk)],
                            [wres] + src_res(sb), ["ps%d" % pb])
                    self.stt(self.hT[:, dout, tok], p[:, :], self.coefP[:, gcol + dout:gcol + dout + 1],
                             self.hT[:, dout, tok], ALU.mult, ALU.add, ["ps%d" % pb, ("h", sb), "coefP"], [("h", sb)])
                dout += 1

    def ffn(self, l):
        self.modulate(l, 1)
        _, _, g2 = self.coef(l, 1)
        items = []
        for (c0, n) in FFG:
            items.append((self.w_ffn_in[l, :, c0 * 128:(c0 + n) * 128], "K8"))
            items.append((self.w_ffn_in[l, :, DFF + c0 * 128:DFF + (c0 + n) * 128], "K8"))
            items.append((self.w_ffn_out[l, c0 * 128:(c0 + n) * 128, :], "R4"))
        st = self.stream(items)
        a = self.og
        for (c0, n) in FFG:
            wg, rg = next(st)
            wu, ru = next(st)
            wo, ro = next(st)
            for sb in range(self.NB):
                tok = slice(sb * 512, (sb + 1) * 512)
                for c in range(n):
                    pg, pu = (0, 1) if (c % 2 == 0) else (2, 3)
                    self.mm(self.ps[pg][:, :], [(wg[:, k, c * 128:(c + 1) * 128], self.xn[:, k, tok]) for k in range(8)],
                            [rg, ("xn", sb)], ["ps%d" % pg])
                    self.mm(self.ps[pu][:, :], [(wu[:, k, c * 128:(c + 1) * 128], self.xn[:, k, tok]) for k in range(8)],
                            [ru, ("xn", sb)], ["ps%d" % pu])
                    sg = self.S(c % 2)
                    self.act(sg, self.ps[pg][:, :], AF.Silu, ["ps%d" % pg], ["s%d" % (c % 2)])
                    self.tt("dve", a[:, c, tok], sg, self.ps[pu][:, :], ALU.mult, ["s%d" % (c % 2), "ps%d" % pu],
                            [("og", c, sb)])
            for sb in range(self.NB):
                tok = slice(sb * 512, (sb + 1) * 512)
                for dout in range(8):
                    pb = 4 + (dout % 4)
                    p = self.ps[pb]
                    self.mm(p[:, :], [(wo[:, c, dout * 128:(dout + 1) * 128], a[:, c, tok]) for c in range(n)],
                            [ro] + [("og", c, sb) for c in range(n)], ["ps%d" % pb])
                    self.stt(self.hT[:, dout, tok], p[:, :], self.coefP[:, g2 + dout:g2 + dout + 1],
                             self.hT[:, dout, tok], ALU.mult, ALU.add, ["ps%d" % pb, ("h", sb), "coefP"], [("h", sb)])

    def headA_proj(self, l, j, sb, w, wres, state_only=False):
        ps = self.ps
        tok = slice(sb * 512, (sb + 1) * 512)
        xr = ("xn", sb)
        xk = lambda k: self.xn[:, k, tok]
        if not state_only:
            self.mm(ps[0][:, :], [(w[:, k, 0:128], xk(k)) for k in range(8)], [wres, xr], ["ps0"])
        self.mm(ps[1][:, :], [(w[:, k, 128:256], xk(k)) for k in range(8)], [wres, xr], ["ps1"])
        if not state_only:
            self.mm(ps[2][:, :], [(w[:, k, 384:512], xk(k)) for k in range(8)], [wres, xr], ["ps2"])

    def headA_evac(self, state_only=False):
        ps, S, Bb = self.ps, self.S, self.B
        if not state_only:
            self.act(S(0), ps[0][:, :], AF.Silu, ["ps0"], ["s0"])
            self.act(Bb(0), ps[2][:, :], AF.Silu, ["ps2"], ["b0"])
        self.act(S(1), ps[1][:, :], AF.Exp, ["ps1"], ["s1"], scale=-1.0)

    def headA_sub(self, l, j, sb, w, wres, blk=0, state_only=False):
        kb, ps = self.kb, self.ps
        tok = slice(sb * 512, (sb + 1) * 512)
        xr = ("xn", sb)
        S, Bb = self.S, self.B
        xk = lambda k: self.xn[:, k, tok]
        so = state_only
        lbc = self.lbs[:, l, j:j + 1]
        self.act(S(2), S(1), AF.Ln, ["s1", "lbs"], ["s2"], scale=lbc, bias=self.epsc[:, 1:2])
        self.act(S(3), S(1), AF.Ln, ["s1"], ["s3"], scale=1.0, bias=self.epsc[:, 1:2])
        self.tt("dve", S(2), S(2), S(3), ALU.subtract, ["s2", "s3"], ["s2"])
        Gc = self.G[:, 1:513]
        kb.emit("dve", lambda e: e.tensor_tensor_scan(out=Gc, data0=self.ones64[:, :], data1=S(2), initial=0.0,
                                                      op0=ALU.mult, op1=ALU.add), ["s2", "ones64"], ["G"])
        self.act(S(4), S(2), AF.Exp, ["s2"], ["s4"])
        self.ts("dve", S(3), S(4), -1.0, 1.0, ALU.mult, ALU.add, ["s4"], ["s3"])
        G3 = Gc.rearrange("p (c t) -> p c t", c=8)
        bc = lambda a: a.unsqueeze(2).broadcast_to([128, 8, 64])
        v3 = lambda i: self.scr[:, i, :].rearrange("p (c t) -> p c t", c=8)
        if not so:
            self.tt("dve", v3(5), G3, bc(self.G[:, 32:513:64]), ALU.subtract, ["G", "G0"], ["s5"])
        self.tt("dve", v3(8), G3, bc(self.G[:, 0:512:64]), ALU.subtract, ["G", "G0"], ["s8"])
        self.tt("dve", v3(9), G3, bc(self.G[:, 64:513:64]), ALU.subtract, ["G", "G0"], ["s9"])
        if not so:
            self.act(S(6), S(5), AF.Exp, ["s5"], ["s6"])
            self.act(S(7), S(5), AF.Exp, ["s5"], ["s7"], scale=-1.0)
        self.act(S(8), S(8), AF.Exp, ["s8"], ["s8"])
        self.act(S(9), S(9), AF.Exp, ["s9"], ["s9"], scale=-1.0)
        if not so:
            self.tt("dve", Bb(1), S(0), S(6), ALU.mult, ["s0", "s6"], ["b1"])
            self.tt("dve", S(10), S(0), S(8), ALU.mult, ["s0", "s8"], ["s10"])
            self.tt("dve", Bb(2), S(3), S(7), ALU.mult, ["s3", "s7"], ["b2"])
        self.tt("dve", Bb(3), S(3), S(9), ALU.mult, ["s3", "s9"], ["b3"])
        vtok = self.bscr[0:64, 6:8, :].rearrange("p a (c d) -> p (a c) d", d=128)
        for half in range(2):
            for cc in range(4):
                c = half * 4 + cc
                t0 = sb * 512 + c * 64
                self.mm(ps[3][0:64, cc * 128:(cc + 1) * 128],
                        [(self.xn[:, k, t0:t0 + 64], w[:, k, 256:384]) for k in range(8)], [wres, xr], ["ps3"])
            if self.rmode:
                self.act(self.bscr[0:64, 6 + half, :], ps[3][0:64, :], AF.Identity, ["ps3", "bmask"], ["b%d" % (6 + half)],
                         scale=self.bmask[0:64, blk:blk + 1], bias=0.0)
            else:
                self.cp("act", self.bscr[0:64, 6 + half, :], ps[3][0:64, :], ["ps3"], ["b%d" % (6 + half)])
        pk = ps[5][:, :].bitcast(BF16)
        for c in range(8):
            kb.emit("pe", lambda e, c=c: e.transpose(out=pk[0:64, c * 128:(c + 1) * 128],
                                                     in_=Bb(3)[:, c * 64:(c + 1) * 64], identity=self.identb[:, :]),
                    ["b3", "identb"], ["ps5"], inc=(c == 7))
        kh = self.bscr[0:64, 8:10, :].rearrange("p a (c d) -> p (a c) d", d=128)
        self.cp("act", self.bscr[0:64, 8:10, :].rearrange("p a n -> p (a n)"), pk[0:64, :], ["ps5"], ["b8", "b9"])
        if not so:
            for c in range(8):
                c0 = c * 64
                kb.emit("pe", lambda e, c0=c0: e.matmul(ps[4][0:64, c0 + 32:c0 + 64], lhsT=Bb(2)[:, c0:c0 + 64],
                                                        rhs=Bb(1)[:, c0 + 32:c0 + 64], start=True, stop=True),
                        ["b1", "b2"], ["ps4"], inc=False)
                kb.emit("pe", lambda e, c0=c0: e.matmul(ps[4][0:32, c0:c0 + 32], lhsT=Bb(2)[:, c0:c0 + 32],
                                                        rhs=Bb(1)[:, c0:c0 + 32], start=True, stop=True),
                        ["b1", "b2"], ["ps4"], inc=(c == 7))
            attm = self.attm[:, :, :]
            p43 = ps[4][0:64, :].rearrange("p (c t) -> p c t", c=8)
            self.tt("dve", self.attm[:, :, 32:64], p43[:, :, 32:64],
                    self.trimask[:, 32:64].unsqueeze(1).broadcast_to([64, 8, 32]), ALU.mult, ["ps4", "trimask"], ["attm"])
            self.tt("dve", self.attm[0:32, :, 0:32], p43[0:32, :, 0:32],
                    self.trimask[0:32, 0:32].unsqueeze(1).broadcast_to([32, 8, 32]), ALU.mult, ["ps4", "trimask"], ["attm"])
        Sj = self.Sst[:, l * 8 + j, :]
        sres = ("S", l, j)
        for c in range(8):
            cs = slice(c * 64, (c + 1) * 64)
            if not so:
                kb.emit("pe", lambda e, c=c, cs=cs: e.matmul(ps[7][:, cs], lhsT=vtok[:, c, :], rhs=self.attm[:, c, :], start=True, stop=False),
                        ["b6", "b7", "attm"], ["ps7"], inc=False)
                kb.emit("pe", lambda e, cs=cs: e.matmul(ps[7][:, cs], lhsT=Sj, rhs=S(10)[:, cs], start=False, stop=True),
                        [sres, "s10"], ["ps7"], inc=True)
            sl = c % 4
            pn = ps[6][:, sl * 128:(sl + 1) * 128]
            kb.emit("pe", lambda e, c=c, pn=pn: e.matmul(pn, lhsT=kh[:, c, :], rhs=vtok[:, c, :], start=True, stop=True),
                    ["b8", "b9", "b6", "b7"], ["ps6"], inc=True)
            dcol = self.scr[:, 8, c * 64 + 63:c * 64 + 64]
            self.stt(Sj, Sj, dcol, pn, ALU.mult, ALU.add, [sres, "s8", "ps6"], [sres])
        if so:
            return
        self.act(Bb(4), ps[7][:, :], AF.Square, ["ps7"], ["b4"])
        self.mm(ps[5][:, :], [(self.onesb[:, :], Bb(4))], ["onesb", "b4"], ["ps5"])
        self.act(S(11), ps[5][:, :], AF.Ln, ["ps5"], ["s11"], scale=1.0 / 128, bias=self.epsc[:, 0:1])
        self.act(S(11), S(11), AF.Exp, ["s11"], ["s11"], scale=-0.5)
        self.stt(S(12), ps[7][:, :], self.vecs[:, V_GN + l:V_GN + l + 1], S(11), ALU.mult, ALU.mult,
                 ["ps7", "s11", "vecs"], ["s12"])
        self.tt("dve", self.og[:, j, tok], S(12), Bb(0), ALU.mult, ["s12", "b0"], [("og", j, sb)])

    def layerA(self, blk, l, state_only=False):
        self.modulate(l, 0)
        items = [(self.w_in_h[l, j, :, :], "K8") for j in range(8)]
        st = self.stream(items)
        slabs = {}

        def slab_of(j):
            if j not in slabs:
                slabs[j] = next(st)
                slabs.pop(j - 2, None)
            return slabs[j]
        seq = [(j, sb) for j in range(8) for sb in range(self.NB)]
        w0, r0 = slab_of(0)
        self.headA_proj(l, 0, 0, w0, r0, state_only)
        for n, (j, sb) in enumerate(seq):
            w, wres = slab_of(j)
            self.headA_evac(state_only)
            if n + 1 < len(seq):
                j2, sb2 = seq[n + 1]
                w2, r2 = slab_of(j2)
                self.headA_proj(l, j2, sb2, w2, r2, state_only)
            self.headA_sub(l, j, sb, w, wres, blk=blk, state_only=state_only)
        if state_only:
            return
        _, _, g1 = self.coef(l, 0)
        self.proj_residual([(self.w_o_a[l, :, 0:512], "K8"), (self.w_o_a[l, :, 512:1024], "K8")], self.og,
                           lambda sb: [("og", j, sb) for j in range(8)], g1, 8)

    def kvproj(self, blk):
        T = self.T
        self.modulate(4, 0)
        (w, wres), = list(self.stream([(self.w_kv[:, :], "K8")]))
        last = (blk == self.NBLK - 1)
        for c in range(2):
            for sb in range(self.NB):
                tok = slice(sb * 512, (sb + 1) * 512)
                pb = c * 2 + (sb % 2)
                self.mm(self.ps[pb][:, :], [(w[:, k, c * 128:(c + 1) * 128], self.xn[:, k, tok]) for k in range(8)],
                        [wres, ("xn", sb)], ["ps%d" % pb])
                self.cp("act", self.KT[:, c, 128 + sb * 512:128 + (sb + 1) * 512], self.ps[pb][:, :], ["ps%d" % pb], ["KT"])
                if last and sb == self.NB - 1:
                    self.cp("dve", self.scr[:, 0, c * 128:(c + 1) * 128], self.ps[pb][:, 384:512], ["ps%d" % pb], ["s0"])
        if last:
            self.ld(self.kT_p[:, :, :], self.scr[:, 0, 0:256].rearrange("p (c t) -> p c t", c=2), ["kT_p"], "ok", reads=["s0"])
            self.outs.append("kT_p")
        for tt_ in range(T // 128):
            pb = 4 + (tt_ % 4)
            self.mm(self.ps[pb][:, 0:256], [(self.xn[:, k, tt_ * 128:(tt_ + 1) * 128], w[:, k, 256:512]) for k in range(8)],
                    [wres, ("xn", tt_ // 4)], ["ps%d" % pb])
            self.cp("act", self.V[:, 1 + tt_, :], self.ps[pb][:, 0:256], ["ps%d" % pb], ["V"])
            if last and tt_ == T // 128 - 1:
                self.cp("dve", self.scr[:, 1, 0:256], self.ps[pb][:, 0:256], ["ps%d" % pb], ["s1"])
                self.ld(self.v_p[:, :], self.scr[:, 1, 0:256], ["v_p"], "ov", reads=["s1"])
                self.outs.append("v_p")

    def layerB(self, blk, jb):
        l = 2 + jb
        kb, ps, T = self.kb, self.ps, self.T
        self.modulate(l, 0)
        items = [(self.w_q_p[jb, :, 0:512], "K8"), (self.w_q_p[jb, :, 512:1024], "K8")]
        i = 0
        for w, wres in self.stream(items):
            for dd in range(4):
                for sb in range(self.NB):
                    tok = slice(sb * 512, (sb + 1) * 512)
                    pb = 6 + ((i * self.NB + sb) % 2)
                    self.mm(ps[pb][:, :], [(w[:, k, dd * 128:(dd + 1) * 128], self.xn[:, k, tok]) for k in range(8)],
                            [wres, ("xn", sb)], ["ps%d" % pb])
                    self.act(self.qT[:, i, tok], ps[pb][:, :], AF.Copy, ["ps%d" % pb], [("qT", i)], scale=0.125)
                i += 1
        it = 0
        for i in range(8):
            c = i // 4
            for qt in range(T // 128):
                qtok = slice(qt * 128, (qt + 1) * 128)
                first = (qt == 0 and not any(t in ("prekv", "own") for t in self.types[:blk]))
                halo_m = (self.rmode and qt == 0 and blk == self.own_blks[0])
                nk = 128 if first else 256
                nkb = nk // 128
                koff = qt * 128 + (128 if first else 0)
                par = it % 2
                it += 1
                s3 = self.scr[:, 2 * par:2 * par + 2, 0:nk]
                sr = ["s%d" % (2 * par), "s%d" % (2 * par + 1)]
                for half in range(2):
                    hp = slice(half * 64, (half + 1) * 64)
                    pl = ps[2 * par + half]
                    plr = "ps%d" % (2 * par + half)
                    kb.emit("pe", lambda e, pl=pl, hp=hp, i=i, qtok=qtok, c=c, koff=koff, nk=nk: e.matmul(
                        pl[:, 0:nk], lhsT=self.qT[hp, i, qtok], rhs=self.KT[hp, c, koff:koff + nk], start=True, stop=True),
                        [("qT", i), "KT"], [plr])
                    self.tt("dve", s3[:, half, :], pl[:, 0:nk], self.bias[:, 2 * i + half, 256 - nk:256], ALU.add,
                            [plr, "bias"], [sr[half]])
                if halo_m:
                    self.ts("dve", s3[:, :, 0:128], s3[:, :, 0:128], self.hmask[:, 0:1], None, ALU.add, ALU.bypass, sr + ["hmask"], sr)
                sm = self.small[:, par * 16:par * 16 + 16]
                smr = "sm%d" % par
                mx, ng, rs, es, dn = sm[:, 0:2], sm[:, 2:4], sm[:, 4:6], sm[:, 6:8], sm[:, 8:10]
                kb.emit("dve", lambda e, mx=mx, s3=s3: e.tensor_reduce(out=mx, in_=s3, axis=AX.X, op=ALU.max), sr, [smr])
                sk = self.sinks_bc[:, jb * 16 + 2 * i:jb * 16 + 2 * i + 2]
                self.tt("dve", mx, mx, sk, ALU.max, [smr, "sinks"], [smr])
                self.tt("dve", s3, s3, mx.unsqueeze(2).broadcast_to([128, 2, nk]), ALU.subtract, sr + [smr], sr)
                self.act(s3, s3, AF.Exp, sr, sr)
                kb.emit("dve", lambda e, rs=rs, s3=s3: e.tensor_reduce(out=rs, in_=s3, axis=AX.X, op=ALU.add), sr, [smr])
                self.tt("dve", ng, sk, mx, ALU.subtract, [smr, "sinks"], [smr])
                self.act(es, ng, AF.Exp, [smr], [smr])
                self.tt("dve", dn, rs, es, ALU.add, [smr], [smr])
                kb.emit("dve", lambda e, dn=dn: e.reciprocal(out=dn, in_=dn), [smr], [smr])
                pn = self.bscr[:, 2 * par:2 * par + 2, 0:nk]
                pnr = ["b%d" % (2 * par), "b%d" % (2 * par + 1)]
                self.tt("dve", pn, s3, dn.unsqueeze(2).broadcast_to([128, 2, nk]), ALU.mult, sr + [smr], pnr)
                ptb = ps[4 + par][:, :].bitcast(BF16)
                ptr = "ps%d" % (4 + par)
                nt = 2 * nkb
                for half in range(2):
                    for kb2 in range(nkb):
                        o0 = (half * nkb + kb2) * 128
                        kb.emit("pe", lambda e, ptb=ptb, pn=pn, half=half, kb2=kb2, o0=o0: e.transpose(
                            out=ptb[:, o0:o0 + 128], in_=pn[:, half, kb2 * 128:(kb2 + 1) * 128], identity=self.identb[:, :]),
                            pnr + ["identb"], [ptr], inc=(half == 1 and kb2 == nkb - 1))
                pT = self.bscr[:, 4 + 2 * par:6 + 2 * par, :].rearrange("p a n -> p (a n)")[:, 0:nt * 128]
                pTr = ["b%d" % (4 + 2 * par), "b%d" % (5 + 2 * par)]
                self.cp("act", pT, ptb[:, 0:nt * 128], [ptr], pTr)
                pob = ps[6 + par]
                por = "ps%d" % (6 + par)
                for half in range(2):
                    hp = slice(half * 64, (half + 1) * 64)
                    vcol = (2 * c + half) * 64
                    for kb2 in range(nkb):
                        vt = (qt + kb2) if not first else 1
                        o0 = (half * nkb + kb2) * 128
                        kb.emit("pe", lambda e, pob=pob, hp=hp, vt=vt, vcol=vcol, pT=pT, kb2=kb2, nkb=nkb, o0=o0: e.matmul(
                            pob[hp, 0:128], lhsT=self.V[:, vt, vcol:vcol + 64], rhs=pT[:, o0:o0 + 128],
                            start=(kb2 == 0), stop=(kb2 == nkb - 1)), ["V"] + pTr, [por], inc=(half == 1 and kb2 == nkb - 1))
                self.cp("dve", self.og[:, i, qtok], pob[:, 0:128], [por], [("og", i, qt // 4)])
        _, _, g1 = self.coef(l, 0)
        self.proj_residual([(self.w_o_p[jb, :, 0:512], "K8"), (self.w_o_p[jb, :, 512:1024], "K8")], self.og,
                           lambda sb: [("og", j, sb) for j in range(8)], g1, 8)

    def final(self, blk):
        T = self.T
        for sb in range(self.NB):
            tok = slice(sb * 512, (sb + 1) * 512)
            hs = ("h", sb)
            sqr = ["b%d" % k for k in range(8)]
            self.act(self.bscr[:, 0:8, :], self.hT[:, :, tok], AF.Square, [hs], sqr)
            self.mm(self.ps[0][:, :], [(self.onesb[:, :], self.bscr[:, k, :]) for k in range(8)], ["onesb"] + sqr, ["ps0"])
            rstd = self.S(0)
            self.act(rstd, self.ps[0][:, :], AF.Ln, ["ps0"], ["s0"], scale=1.0 / D, bias=self.epsc[:, 0:1])
            self.act(rstd, rstd, AF.Exp, ["s0"], ["s0"], scale=-0.5)
            for k in range(8):
                self.stt(self.scr[:, 4 + k, :], self.hT[:, k, tok], self.vecs[:, V_FIN + k:V_FIN + k + 1], rstd,
                         ALU.mult, ALU.mult, [hs, "s0", "vecs"], ["s%d" % (4 + k)])
            ob = self.own_blks.index(blk)
            self.ld(self.yT[:, :, ob * T + sb * 512:ob * T + (sb + 1) * 512], self.scr[:, 4:12, :], ["yT"], "y%d" % sb,
                    reads=["s%d" % (4 + k) for k in range(8)])
        if "yT" not in self.outs:
            self.outs.append("yT")

    def block(self, blk):
        T = self.T
        for sb in range(self.NB):
            self.ld(self.hT[:, :, sb * 512:(sb + 1) * 512], self.xT[:, :, blk * T + sb * 512:blk * T + (sb + 1) * 512],
                    [("h", sb)], "x%d" % sb)
        typ = self.types[blk]
        for l in range(self.nlayers):
            if typ == "pre" and l >= 1:
                if l == 1:
                    self.layerA(blk, 1, state_only=True)
                continue
            if typ == "prekv" and l >= 2:
                continue
            if l < 2:
                self.layerA(blk, l)
            else:
                self.layerB(blk, l - 2)
            if self.dbg == l + 10 and blk == 0:
                self.ld(self.dbg_d[:, :, :], self.hT[:, :, :], ["dbg"], "o2", reads=[("h", sb) for sb in range(self.NB)])
                self.outs.append("dbg")
            self.ffn(l)
            if l == 1:
                self.kvproj(blk)
            if self.dbg == l and blk == 0:
                self.ld(self.dbg_d[:, :, :], self.hT[:, :, :], ["dbg"], "o2", reads=[("h", sb) for sb in range(self.NB)])
                self.outs.append("dbg")
        if self.nlayers == 4 and typ == "own":
            self.final(blk)
        if blk < self.NBLK - 1 and self.nlayers > 1 and typ != "pre":
            self.cp("dve", self.KT[:, :, 0:128], self.KT[:, :, T:T + 128], ["KT"], ["KT"])
            self.cp("dve", self.V[:, 0, :], self.V[:, T // 128, :], ["V"], ["V"])
        if blk == self.NBLK - 1:
            self.ld(self.st_p.ap().rearrange("l j k v -> k (l j) v"), self.Sst[:, :, :], ["st_p"], "ost", reads=[("S", l, j) for l in range(2) for j in range(8)])
            self.outs.append("st_p")

    def ada_phase(self):
        kb, ps = self.kb, self.ps
        cs = self.scr[:, 0, 0:136].rearrange("p (k n) -> p k n", k=8)
        self.ld(cs, self.cT[:, :, :], ["s0"], "c17")
        csb = self.csb
        self.act(csb[:, :, :], cs, AF.Silu, ["s0"], ["csb"])
        items = []
        for l in range(4):
            for s in range(12):
                items.append((self.w_ada[l, :, s * 512:(s + 1) * 512], "K8"))
        for s in range(4):
            items.append((self.w_ada_kv[:, s * 512:(s + 1) * 512], "K8"))
        n = 0
        for w, wres in self.stream(items):
            l, s = (n // 12, n % 12) if n < 48 else (4, n - 48)
            n += 1
            for dd in range(4):
                ch = s * 4 + dd
                gidx = l * 48 + ch
                bcol = (V_BADA + l * 48 + ch) if l < 4 else (V_BADAKV + ch)
                pb = gidx % 4
                self.mm(ps[pb][:, 0:17], [(w[:, k, dd * 128:(dd + 1) * 128], csb[:, k, :]) for k in range(8)],
                        [wres, "csb"], ["ps%d" % pb])
                self.ts("dve", self.adaT[:, gidx, :], ps[pb][:, 0:17], self.vecs[:, bcol:bcol + 1], None, ALU.add, ALU.bypass,
                        ["ps%d" % pb, "vecs"], ["adaT"])
        aT = self.adaT
        for l in range(5):
            for which in range(2 if l < 4 else 1):
                base = l * 48 + which * 24
                nw = self.vecs[:, V_NORM + (l * 2 + which) * 8:V_NORM + (l * 2 + which) * 8 + 8] if l < 4 else self.vecs[:, V_KVN:V_KVN + 8]
                self.stt(self.coefP[:, base:base + 8], aT[:, base + 8:base + 16, 0], 1.0, nw, ALU.add, ALU.mult,
                         ["adaT", "vecs"], ["coefP"])
                self.cp("dve", self.coefP[:, base + 8:base + 16], aT[:, base:base + 8, 0], ["adaT"], ["coefP"])
                if l < 4:
                    self.cp("dve", self.coefP[:, base + 16:base + 24], aT[:, base + 16:base + 24, 0], ["adaT"], ["coefP"])
                if self.sample:
                    self.stt(self.coefS[:, base:base + 8, :], aT[:, base + 8:base + 16, 1:17], 1.0,
                             nw.unsqueeze(2).broadcast_to([128, 8, 16]), ALU.add, ALU.mult, ["adaT", "vecs"], ["coefS"])
                    self.cp("dve", self.coefS[:, base + 8:base + 16, :], aT[:, base:base + 8, 1:17], ["adaT"], ["coefS"])
                    if l < 4:
                        self.cp("dve", self.coefS[:, base + 16:base + 24, :], aT[:, base + 16:base + 24, 1:17], ["adaT"], ["coefS"])

    def build(self):
        self.setup()
        self.ada_phase()
        if self.sample:
            self.sample_phase()
        self.kb.barrier()
        for blk in range(self.NBLK):
            self.block(blk)
        self.kb.finish(self.outs)
        return self.nc


def _fm(v):
    v = np.asarray(v, np.float32)
    return np.ascontiguousarray(v.reshape(-1, 128).T)


def _consts():
    eline = np.zeros((33, 384), np.float32)
    eline[32, :] = MASKV
    for ip in range(128, 256):
        eline[int(t5_bucket_np(255 - ip)), ip] = 1.0
        eline[32, ip] = 0.0
    eline_s = np.zeros((33, 128), np.float32)
    for r in range(128):
        eline_s[int(t5_bucket_np(127 - r)), r] = 1.0
    ident = np.eye(128, dtype=np.float32)
    tri = (np.arange(64)[:, None] <= np.arange(64)[None, :]).astype(np.float32)
    hm = np.zeros((128, 2), np.float32)
    hm[:64, 0] = 1.0
    hm[64:, 1] = 1.0
    return dict(eline=eline, eline_s=eline_s, ident=ident, trimask=tri, halfmask=hm)


def prep_shared(inp):
    f32 = lambda a: np.ascontiguousarray(np.asarray(a, np.float32))
    sh = slot_heads()
    w_in_a = f32(inp["w_in_a"])
    d = {}
    d["w_in_h"] = np.ascontiguousarray(w_in_a.reshape(2, 1024, 4, 8, 128).transpose(0, 3, 1, 2, 4).reshape(2, 8, 1024, 512))
    d["w_o_a"] = f32(inp["w_o_a"])
    d["w_kv"] = f32(inp["w_kv"])
    d["w_ada"] = f32(inp["w_ada"])
    d["w_ada_kv"] = f32(inp["w_ada_kv"])
    wq = f32(inp["w_q_b"]).reshape(2, 1024, 16, 64)
    d["w_q_p"] = np.ascontiguousarray(wq[:, :, sh, :].reshape(2, 1024, 1024))
    wo = f32(inp["w_o_b"]).reshape(2, 16, 64, 1024)
    d["w_o_p"] = np.ascontiguousarray(wo[:, sh, :, :].reshape(2, 1024, 1024))
    d["w_ffn_in"] = f32(inp["w_ffn_in"])
    d["w_ffn_out"] = f32(inp["w_ffn_out"])
    sk = f32(inp["sinks_b"])[:, sh]
    d["sinks_bc"] = np.ascontiguousarray(np.broadcast_to(sk.reshape(1, 32), (128, 32)))
    d["sink_s"] = np.ascontiguousarray(sk.T)
    rb = f32(inp["rel_bias"])[:, sh]
    d["rb_ext"] = np.ascontiguousarray(np.concatenate([rb, np.ones((1, 16), np.float32)], 0))
    vecs = np.zeros((128, NV), np.float32)
    nw = f32(inp["norm_w"])
    for l in range(4):
        for wh in range(2):
            vecs[:, V_NORM + (l * 2 + wh) * 8:V_NORM + (l * 2 + wh) * 8 + 8] = _fm(nw[l, wh])
    vecs[:, V_KVN:V_KVN + 8] = _fm(inp["kv_norm_w"])
    vecs[:, V_FIN:V_FIN + 8] = _fm(inp["final_norm_w"])
    lb = f32(inp["lb_a"])
    vecs[:, V_LB:V_LB + 8] = _fm(lb[0])
    vecs[:, V_LB + 8:V_LB + 16] = _fm(lb[1])
    ba = f32(inp["b_ada"])
    for l in range(4):
        vecs[:, V_BADA + l * 48:V_BADA + (l + 1) * 48] = _fm(ba[l])
    vecs[:, V_BADAKV:V_BADAKV + 16] = _fm(inp["b_ada_kv"])
    gn = f32(inp["gnorm_a"])
    vecs[:, V_GN] = gn[0]
    vecs[:, V_GN + 1] = gn[1]
    d["vecs"] = vecs
    d.update(_consts())
    return d


def prep_core(inp, shared, core, T, NBLK, seq=None, win=None):
    f32 = lambda a: np.ascontiguousarray(np.asarray(a, np.float32))
    NTOK = T * NBLK
    d = dict(shared)
    if win is None:
        seq = core % 2 if seq is None else seq
        x = f32(inp["x_prompt"])[seq, :NTOK]
        bm = np.ones((128, NBLK), np.float32)
    else:
        seq, start, end = win
        assert end - start == NTOK
        x = np.zeros((NTOK, 1024), np.float32)
        v0 = max(start, 0)
        x[v0 - start:] = f32(inp["x_prompt"])[seq, v0:end]
        bm = np.zeros((128, NBLK), np.float32)
        for b in range(NBLK):
            bm[:, b] = 1.0 if start + b * T >= 0 else 0.0
    d["bmask"] = bm
    d["xT"] = np.ascontiguousarray(x.T.reshape(8, 128, NTOK).transpose(1, 0, 2))
    bs = slice(core * 16, (core + 1) * 16)
    c17 = np.concatenate([f32(inp["c_prompt"])[seq][None], f32(inp["c_sample"])[bs]], 0)
    d["cT"] = np.ascontiguousarray(c17.T.reshape(8, 128, 17).transpose(1, 0, 2))
    xs = f32(inp["x_sample"])[bs, 0]
    d["xsT"] = np.ascontiguousarray(xs.T.reshape(8, 128, 16).transpose(1, 0, 2))
    d["state"] = f32(inp["state_hgrn"])[bs]
    ck = f32(inp["cache_swa_k"])[bs].reshape(16, 128, 256)
    d["ck"] = ck
    d["cv"] = f32(inp["cache_swa_v"])[bs].reshape(16, 128, 256)
    d["ckT"] = np.ascontiguousarray(ck[:, 1:128, :].transpose(0, 2, 1))
    return d


def _sample_methods():
    def bc3(a, n):
        return a.unsqueeze(2).broadcast_to([a.shape[0], a.shape[1], n])

    def sample_modulate(self, l, which):
        a0, b0, _ = self.coef(l, which)
        S, ps = self.S, self.ps
        sq = self.B(0)[:, 0:128].rearrange("p (k n) -> p k n", k=8)
        self.act(sq, self.hs[:, :, :], AF.Square, ["hs"], ["b0"])
        self.mm(ps[0][:, 0:16], [(self.onesb[:, :], sq[:, k, :]) for k in range(8)], ["onesb", "b0"], ["ps0"])
        rstd = S(0)[:, 0:16]
        self.act(rstd, ps[0][:, 0:16], AF.Ln, ["ps0"], ["s0"], scale=1.0 / D, bias=self.epsc[:, 0:1])
        self.act(rstd, rstd, AF.Exp, ["s0"], ["s0"], scale=-0.5)
        t = S(1)[:, 0:128].rearrange("p (k n) -> p k n", k=8)
        self.tt("dve", t, self.hs[:, :, :], rstd.unsqueeze(1).broadcast_to([128, 8, 16]), ALU.mult, ["hs", "s0"], ["s1"])
        self.tt("dve", t, t, self.coefS[:, a0:a0 + 8, :], ALU.mult, ["s1", "coefS"], ["s1"])
        self.tt("dve", self.xns[:, :, :], t, self.coefS[:, b0:b0 + 8, :], ALU.add, ["s1", "coefS"], ["xns"])

    def sample_proj(self, slabs, src, src_res, gcol, nk):
        dout = 0
        for view, wres in self.stream(slabs):
            for dd in range(4):
                pb = 4 + (dout % 4)
                p = self.ps[pb]
                self.mm(p[:, 0:16], [(view[:, k, dd * 128:(dd + 1) * 128], src[:, k, :]) for k in range(nk)],
                        [wres, src_res], ["ps%d" % pb])
                tmp = self.S(3)[:, 0:16]
                self.tt("dve", tmp, p[:, 0:16], self.coefS[:, gcol + dout, :], ALU.mult, ["ps%d" % pb, "coefS"], ["s3"])
                self.tt("dve", self.hs[:, dout, :], self.hs[:, dout, :], tmp, ALU.add, ["hs", "s3"], ["hs"])
                dout += 1

    def sample_ffn(self, l):
        self.sample_modulate(l, 1)
        _, _, g2 = self.coef(l, 1)
        items = []
        for (c0, n) in FFG:
            items.append((self.w_ffn_in[l, :, c0 * 128:(c0 + n) * 128], "K8"))
            items.append((self.w_ffn_in[l, :, DFF + c0 * 128:DFF + (c0 + n) * 128], "K8"))
            items.append((self.w_ffn_out[l, c0 * 128:(c0 + n) * 128, :], "R4"))
        st = self.stream(items)
        ps = self.ps
        for (c0, n) in FFG:
            wg, rg = next(st)
            wu, ru = next(st)
            wo, ro = next(st)
            for c in range(n):
                self.mm(ps[0][:, 0:16], [(wg[:, k, c * 128:(c + 1) * 128], self.xns[:, k, :]) for k in range(8)], [rg, "xns"], ["ps0"])
                self.mm(ps[1][:, 0:16], [(wu[:, k, c * 128:(c + 1) * 128], self.xns[:, k, :]) for k in range(8)], [ru, "xns"], ["ps1"])
                sg = self.S(0)[:, 0:16]
                self.act(sg, ps[0][:, 0:16], AF.Silu, ["ps0"], ["s0"])
                self.tt("dve", self.as_[:, c, :], sg, ps[1][:, 0:16], ALU.mult, ["s0", "ps1"], ["as"])
            for dout in range(8):
                pb = 4 + (dout % 4)
                self.mm(ps[pb][:, 0:16], [(wo[:, c, dout * 128:(dout + 1) * 128], self.as_[:, c, :]) for c in range(n)],
                        [ro, "as"], ["ps%d" % pb])
                tmp = self.S(3)[:, 0:16]
                self.tt("dve", tmp, ps[pb][:, 0:16], self.coefS[:, g2 + dout, :], ALU.mult, ["ps%d" % pb, "coefS"], ["s3"])
                self.tt("dve", self.hs[:, dout, :], self.hs[:, dout, :], tmp, ALU.add, ["hs", "s3"], ["hs"])

    def sampleA(self, l):
        kb, ps, S = self.kb, self.ps, self.S
        self.sample_modulate(l, 0)
        items = [(self.w_in_h[l, j, :, :], "K8") for j in range(8)]
        for j, (w, wres) in enumerate(self.stream(items)):
            self.ld(self.Sin[:, :, :], self.state_d[:, l, j].rearrange("b k v -> k b v"), ["Sin"], "si")
            pp = ps[0]
            for part in range(4):
                self.mm(pp[:, part * 16:(part + 1) * 16],
                        [(w[:, k, part * 128:(part + 1) * 128], self.xns[:, k, :]) for k in range(8)], [wres, "xns"], ["ps0"])
            q, gate = S(0)[:, 0:16], S(0)[:, 16:32]
            self.act(q, pp[:, 0:16], AF.Silu, ["ps0"], ["s0"])
            self.act(gate, pp[:, 48:64], AF.Silu, ["ps0"], ["s0"])
            e, L1, L2, lg, f, kk, v = [S(1)[:, i * 16:(i + 1) * 16] for i in range(7)]
            self.act(e, pp[:, 16:32], AF.Exp, ["ps0"], ["s1"], scale=-1.0)
            self.act(L1, e, AF.Ln, ["s1", "lbs"], ["s1"], scale=self.lbs[:, l, j:j + 1], bias=self.epsc[:, 1:2])
            self.act(L2, e, AF.Ln, ["s1"], ["s1"], scale=1.0, bias=self.epsc[:, 1:2])
            self.tt("dve", lg, L1, L2, ALU.subtract, ["s1"], ["s1"])
            self.act(f, lg, AF.Exp, ["s1"], ["s1"])
            self.ts("dve", kk, f, -1.0, 1.0, ALU.mult, ALU.add, ["s1"], ["s1"])
            self.cp("dve", v, pp[:, 32:48], ["ps0"], ["s1"])
            rd = self.scr[:, 4:8, :].rearrange("p a (b v) -> p (a b) v", v=128)
            self.tt("dve", rd, self.ident[:, :].unsqueeze(1).broadcast_to([128, 16, 128]), bc3(v, 128), ALU.mult,
                    ["ident", "s1"], ["s4", "s5", "s6", "s7"])
            for qd in range(4):
                self.mm(ps[4 + qd][:, :], [(self.ones64[:, 0:128], self.scr[:, 4 + qd, :])], ["ones64", "s%d" % (4 + qd)],
                        ["ps%d" % (4 + qd)])
                self.tt("dve", self.scr[:, 8 + qd, :].rearrange("p (b v) -> p b v", v=128),
                        ps[4 + qd][:, :].rearrange("p (b v) -> p b v", v=128), bc3(kk[:, 4 * qd:4 * qd + 4], 128), ALU.mult,
                        ["ps%d" % (4 + qd), "s1"], ["s%d" % (8 + qd)])
            self.tt("dve", self.Sout[:, :, :], self.Sin[:, :, :], bc3(f, 128), ALU.mult, ["Sin", "s1"], ["Sout"])
            self.tt("dve", self.Sout[:, :, :], self.Sout[:, :, :], self.scr[:, 8:12, :].rearrange("p a (b v) -> p (a b) v", v=128),
                    ALU.add, ["Sout", "s8", "s9", "s10", "s11"], ["Sout"])
            self.ld(self.st_s[:, l, j].rearrange("b k v -> k b v"), self.Sout[:, :, :], ["st_s"], "so", reads=["Sout"])
            po2 = ps[1]
            q2 = S(0)[:, 32:64].rearrange("p (b t) -> p b t", t=2)
            self.cp("dve", q2, bc3(q, 2), ["s0"], ["s0"])
            for b in range(16):
                kb.emit("pe", lambda e, b=b: e.matmul(po2[:, 2 * b:2 * b + 2], lhsT=self.Sout[:, b, :], rhs=q2[:, b, :], start=True, stop=True),
                        ["Sout", "s0"], ["ps1"], inc=(b == 15))
            po = po2[:, 0:32:2]
            osq = self.B(1)[:, 0:16]
            self.act(osq, po, AF.Square, ["ps1"], ["b1"])
            self.mm(ps[2][:, 0:16], [(self.onesb[:, :], osq)], ["onesb", "b1"], ["ps2"])
            rstd = S(2)[:, 0:16]
            self.act(rstd, ps[2][:, 0:16], AF.Ln, ["ps2"], ["s2"], scale=1.0 / 128, bias=self.epsc[:, 0:1])
            self.act(rstd, rstd, AF.Exp, ["s2"], ["s2"], scale=-0.5)
            t = S(2)[:, 16:32]
            self.stt(t, po, self.vecs[:, V_GN + l:V_GN + l + 1], rstd, ALU.mult, ALU.mult, ["ps1", "s2", "vecs"], ["s2"])
            self.tt("dve", self.ogs[:, j, :], t, gate, ALU.mult, ["s2", "s0"], ["ogs"])
        if "st_s" not in self.outs:
            self.outs.append("st_s")
        _, _, g1 = self.coef(l, 0)
        self.sample_proj([(self.w_o_a[l, :, 0:512], "K8"), (self.w_o_a[l, :, 512:1024], "K8")], self.ogs, "ogs", g1, 8)

    def sample_kv(self):
        ps, S = self.ps, self.S
        self.sample_modulate(4, 0)
        self.kb.dma("pool", "cv", self.Vs[0:112, :, :], self.cv_d[:, 1:113, :].rearrange("b s n -> s b n"), [], ["Vs"])
        self.kb.dma("pool", "cv2", self.Vs[112:127, :, :], self.cv_d[:, 113:128, :].rearrange("b s n -> s b n"), [], ["Vs"])
        ckv = self.ckT_d.ap().rearrange("b (c p) t -> p c b t", p=128)
        for c in range(2):
            self.kb.dma("pool", "ck%d" % c, self.KTs[:, c, :, 0:127], ckv[:, c, :, :], [], ["KTs"])
        self.ld(self.ck_s[:, 0:127, :], self.ck_d[:, 1:128, :], ["ck_s"], "ock")
        self.ld(self.cv_s[:, 0:127, :], self.cv_d[:, 1:128, :], ["cv_s"], "ocv")
        (w, wres), = list(self.stream([(self.w_kv[:, :], "K8")]))
        for c in range(2):
            self.mm(ps[0][:, c * 16:(c + 1) * 16], [(w[:, k, c * 128:(c + 1) * 128], self.xns[:, k, :]) for k in range(8)],
                    [wres, "xns"], ["ps0"])
        self.cp("dve", self.KTs[:, :, :, 127], ps[0][:, 0:32].rearrange("p (c b) -> p c b", c=2), ["ps0"], ["KTs"])
        self.mm(ps[1][0:16, :], [(self.xns[:, k, :], w[:, k, :]) for k in range(8)], [wres, "xns"], ["ps1"])
        rowf = S(0)[0:16, :]
        self.cp("dve", rowf, ps[1][0:16, :], ["ps1"], ["s0"])
        self.ld(self.ck_s[:, 127, :], rowf[:, 0:256], ["ck_s"], "ock2", reads=["s0"])
        self.ld(self.cv_s[:, 127, :], rowf[:, 256:512], ["cv_s"], "ocv2", reads=["s0"])
        rowb = self.B(0)[0:16, 0:256]
        self.cp("dve", rowb, ps[1][0:16, 256:512], ["ps1"], ["b0"])
        self.ld(self.rowk_d[:, :], rowb, ["rowk"], "rk", reads=["b0"])
        self.ld(self.Vs[127:128, :, :], self.rowk_d.ap().rearrange("(o b) n -> o b n", o=1), ["Vs"], "rk2", reads=["rowk"])
        self.outs += ["ck_s", "cv_s"]

    def sampleB(self, jb):
        kb, ps, S = self.kb, self.ps, self.S
        l = 2 + jb
        self.sample_modulate(l, 0)
        i = 0
        for w, wres in self.stream([(self.w_q_p[jb, :, 0:512], "K8"), (self.w_q_p[jb, :, 512:1024], "K8")]):
            for dd in range(4):
                pb = i % 4
                self.mm(ps[pb][:, 0:16], [(w[:, k, dd * 128:(dd + 1) * 128], self.xns[:, k, :]) for k in range(8)],
                        [wres, "xns"], ["ps%d" % pb])
                self.act(self.qTs[:, i, :], ps[pb][:, 0:16], AF.Copy, ["ps%d" % pb], ["qTs"], scale=0.125)
                i += 1
        kb.emit("dve", lambda e: e.memset(self.Qb2[:, :, :, :], 0.0), [], ["Qb2"])
        for c in range(2):
            for half in range(2):
                hp = slice(half * 64, (half + 1) * 64)
                self.cp("dve", self.Qb2[hp, :, c, c * 8 + half:c * 8 + 8:2],
                        self.qTs[hp, 4 * c:4 * c + 4, :].rearrange("p i b -> p b i"), ["qTs"], ["Qb2"])
        for b in range(16):
            pb = 4 + b // 4
            self.mm(ps[pb][0:16, (b % 4) * 128:(b % 4 + 1) * 128],
                    [(self.Qb2[:, b, c, :], self.KTs[:, c, b, :]) for c in range(2)], ["Qb2", "KTs"], ["ps%d" % pb])
        sv = lambda qd: self.scr[0:16, 4 + qd, :].rearrange("p (b t) -> p b t", t=128)
        for qd in range(4):
            self.tt("dve", sv(qd), ps[4 + qd][0:16, :].rearrange("p (b t) -> p b t", t=128),
                    self.bias_s[0:16, :].unsqueeze(1).broadcast_to([16, 4, 128]), ALU.add, ["ps%d" % (4 + qd), "bias_s"], ["s%d" % (4 + qd)])
        s_all = self.scr[0:16, 4:8, :].rearrange("p a (b t) -> p (a b) t", t=128)
        sr = ["s4", "s5", "s6", "s7"]
        sm = self.small
        mx, rs, es, dn = sm[0:16, 0:16], sm[0:16, 16:32], sm[0:16, 32:48], sm[0:16, 48:64]
        kb.emit("dve", lambda e: e.tensor_reduce(out=mx, in_=s_all, axis=AX.X, op=ALU.max), sr, ["sm0"])
        skc = self.sink_s[0:16, jb:jb + 1]
        self.ts("dve", mx, mx, skc, None, ALU.max, ALU.bypass, ["sm0", "sink_s"], ["sm0"])
        self.tt("dve", s_all, s_all, bc3(mx, 128), ALU.subtract, sr + ["sm0"], sr)
        self.act(self.scr[0:16, 4:8, :], self.scr[0:16, 4:8, :], AF.Exp, sr, sr)
        kb.emit("dve", lambda e: e.tensor_reduce(out=rs, in_=s_all, axis=AX.X, op=ALU.add), sr, ["sm0"])
        self.act(es, mx, AF.Exp, ["sm0", "sink_s"], ["sm0"], scale=-1.0, bias=skc)
        self.tt("dve", dn, rs, es, ALU.add, ["sm0"], ["sm0"])
        kb.emit("dve", lambda e: e.reciprocal(out=dn, in_=dn), ["sm0"], ["sm0"])
        pn = self.bscr[0:16, 4:8, :].rearrange("p a (b t) -> p (a b) t", t=128)
        pnr = ["b4", "b5", "b6", "b7"]
        self.tt("dve", pn, s_all, bc3(dn, 128), ALU.mult, sr + ["sm0"], pnr)
        ptb = ps[0][:, :].bitcast(BF16)
        for b in range(16):
            kb.emit("pe", lambda e, b=b: e.transpose(out=ptb[:, b * 16:(b + 1) * 16], in_=pn[:, b, :], identity=self.identb[0:16, 0:16]),
                    pnr + ["identb"], ["ps0"], inc=(b == 15))
        self.cp("act", self.pTs[:, :, :].rearrange("p b s -> p (b s)"), ptb[:, 0:256], ["ps0"], ["pTs"])
        pv = ps[1]
        for b in range(16):
            for c in range(2):
                o0 = (b * 2 + c) * 16
                kb.emit("pe", lambda e, b=b, c=c, o0=o0: e.matmul(pv[:, o0:o0 + 16], lhsT=self.Vs[:, b, c * 128:(c + 1) * 128],
                                                                  rhs=self.pTs[:, b, :], start=True, stop=True),
                        ["Vs", "pTs"], ["ps1"], inc=(b == 15 and c == 1))
        pvv = pv[:, :].rearrange("p (b c s) -> p b c s", b=16, c=2)
        for c in range(2):
            for half in range(2):
                hp = slice(half * 64, (half + 1) * 64)
                self.cp("dve", self.ogs[hp, 4 * c:4 * c + 4, :].rearrange("p i b -> p b i"),
                        pvv[hp, :, c, c * 8 + half:c * 8 + 8:2], ["ps1"], ["ogs"])
        _, _, g1 = self.coef(l, 0)
        self.sample_proj([(self.w_o_p[jb, :, 0:512], "K8"), (self.w_o_p[jb, :, 512:1024], "K8")], self.ogs, "ogs", g1, 8)

    def sample_phase(self):
        S, ps = self.S, self.ps
        self.ld(self.hs[:, :, :], self.xsT[:, :, :], ["hs"], "c20")
        self.ld(self.sink_s[0:16, 0:2], self.sink_s_d[:, :], ["sink_s"], "c21")
        rb = S(12)[0:33, 0:16]
        el = S(13)[0:33, 0:128]
        self.ld(rb, self.rb_ext_d[:, :], ["s12"], "c22")
        self.ld(el, self.eline_s_d[:, :], ["s13"], "c23")
        self.mm(ps[3][0:16, 0:128], [(rb, el)], ["s12", "s13"], ["ps3"])
        self.cp("dve", self.bias_s[0:16, :], ps[3][0:16, 0:128], ["ps3"], ["bias_s"])
        for l in range(4):
            if l < 2:
                self.sampleA(l)
            else:
                self.sampleB(l - 2)
            self.sample_ffn(l)
            if l == 1:
                self.sample_kv()
        sq = self.B(0)[:, 0:128].rearrange("p (k n) -> p k n", k=8)
        self.act(sq, self.hs[:, :, :], AF.Square, ["hs"], ["b0"])
        self.mm(ps[0][:, 0:16], [(self.onesb[:, :], sq[:, k, :]) for k in range(8)], ["onesb", "b0"], ["ps0"])
        rstd = S(0)[:, 0:16]
        self.act(rstd, ps[0][:, 0:16], AF.Ln, ["ps0"], ["s0"], scale=1.0 / D, bias=self.epsc[:, 0:1])
        self.act(rstd, rstd, AF.Exp, ["s0"], ["s0"], scale=-0.5)
        t = S(1)[:, 0:128].rearrange("p (k n) -> p k n", k=8)
        self.tt("dve", t, self.hs[:, :, :], rstd.unsqueeze(1).broadcast_to([128, 8, 16]), ALU.mult, ["hs", "s0"], ["s1"])
        self.tt("dve", t, t, bc3(self.vecs[:, V_FIN:V_FIN + 8], 16), ALU.mult, ["s1", "vecs"], ["s1"])
        self.ld(self.y_s[:, :, :], t, ["y_s"], "oys", reads=["s1"])
        self.outs.append("y_s")

    for f in (sample_modulate, sample_proj, sample_ffn, sampleA, sample_kv, sampleB, sample_phase):
        setattr(Prog, f.__name__, f)


_sample_methods()


T_BLK = 1024
N_PRE = 5
N_OWN = 2
OWN_TOK = T_BLK * N_OWN
_CACHE = {}


def kernel(**inputs):
    if "nc" not in _CACHE:
        prog = Prog(T=T_BLK, sample=True, NPRE=N_PRE, NOWN=N_OWN)
        _CACHE["nc"] = prog.build()
    nc = _CACHE["nc"]
    NBLK = N_PRE + 1 + N_OWN
    NTOK = T_BLK * NBLK
    shared = prep_shared(inputs)
    in_maps = []
    for c in range(8):
        seq, p = c // 4, c % 4
        end = (p + 1) * OWN_TOK
        in_maps.append(prep_core(inputs, shared, c, T_BLK, NBLK, win=(seq, end - NTOK, end)))
    res = run_bass_kernel_spmd(nc, in_maps, core_ids=list(range(8)))
    r = res.results
    g = lambda c, n, shp: np.asarray(r[c][n], np.float32).reshape(shp)
    y_prompt = np.stack([np.concatenate([g(4 * s + p, "yT", (128, 8, OWN_TOK)).transpose(2, 1, 0).reshape(OWN_TOK, 1024)
                                         for p in range(4)], 0) for s in range(2)])
    y_sample = np.concatenate([g(c, "y_s", (128, 8, 16)).transpose(2, 1, 0).reshape(16, 1, 1024) for c in range(8)], 0)
    last = [3, 7]
    st_p = np.stack([g(c, "st_p", (2, 8, 128, 128)) for c in last])
    st_s = np.concatenate([g(c, "st_s", (16, 2, 8, 128, 128)) for c in range(8)], 0)
    k_p = np.stack([g(c, "kT_p", (2, 64, 2, 128)).transpose(3, 2, 0, 1).reshape(128, 4, 64) for c in last])
    v_p = np.stack([g(c, "v_p", (128, 4, 64)) for c in last])
    k_s = np.concatenate([g(c, "ck_s", (16, 128, 4, 64)) for c in range(8)], 0)
    v_s = np.concatenate([g(c, "cv_s", (16, 128, 4, 64)) for c in range(8)], 0)
    f = np.ascontiguousarray
    return (f(y_prompt), f(y_sample), f(st_p), f(st_s), f(k_p), f(v_p), f(k_s), f(v_s))
```

```python
from contextlib import ExitStack
import numpy as np
import concourse.bass as bass
import concourse.mybir as mybir
from concourse.bass_utils import run_bass_kernel_spmd

F32 = mybir.dt.float32
BF16 = mybir.dt.bfloat16
I32 = mybir.dt.int32
AF = mybir.ActivationFunctionType
ALU = mybir.AluOpType
AX = mybir.AxisListType

SAME_ENGINE_SYNC = True


class KB:
    ENG = ("pe", "act", "dve", "pool", "sp")

    def __init__(self, nc):
        self.nc = nc
        self.stack = ExitStack()
        self.h = {"pe": nc.tensor, "act": nc.scalar, "dve": nc.vector, "pool": nc.gpsimd, "sp": nc.sync}
        self.prog = {e: [] for e in self.ENG}
        self.cnt = {e: 0 for e in self.ENG}
        self.seen = {e: {} for e in self.ENG}
        self.res_w = {}
        self.res_r = {}
        self.sems = {}
        self.n_inst = 0

    def sbuf(self, name, shape, dtype):
        return self.stack.enter_context(self.nc.sbuf_tensor("t_" + name, shape, dtype))

    def psum(self, name, shape, dtype):
        return self.stack.enter_context(self.nc.psum_tensor("p_" + name, shape, dtype))

    def _sem(self, key):
        if key not in self.sems:
            self.sems[key] = self.stack.enter_context(self.nc.semaphore("s_" + key.replace(":", "_")))
            self.cnt.setdefault(key, 0)
        return self.sems[key]

    @staticmethod
    def _is_psum(r):
        return (isinstance(r, str) and r.startswith("ps")) or (isinstance(r, tuple) and str(r[0]).startswith("ps"))

    def _waits(self, e, reads, writes):
        w = {}

        def need(key, val):
            if key == e and (e == "pe" or not SAME_ENGINE_SYNC or val > self.cnt[e]):
                return
            if val > w.get(key, 0):
                w[key] = val
        for r in reads:
            lw = self.res_w.get(r)
            if lw:
                need(*lw)
            if self._is_psum(r):
                for k, v in self.res_r.get(r, {}).items():
                    if k != e:
                        need(k, v)
        for x in writes:
            lw = self.res_w.get(x)
            if lw:
                need(*lw)
            for k, v in self.res_r.get(x, {}).items():
                need(k, v)
        out = []
        for k, v in w.items():
            if self.seen[e].get(k, 0) < v:
                self.seen[e][k] = v
                out.append((k, v))
        return out

    def _mark(self, key, val, reads, writes):
        for r in reads:
            d = self.res_r.setdefault(r, {})
            if d.get(key, 0) < val:
                d[key] = val
        for x in writes:
            self.res_w[x] = (key, val)
            self.res_r[x] = {}

    def emit(self, e, fn, reads, writes, inc=True):
        waits = self._waits(e, reads, writes)
        self._sem(e)
        if inc:
            self.cnt[e] += 1
            val = self.cnt[e]
        else:
            val = self.cnt[e] + 1
        self._mark(e, val, reads, writes)
        self.prog[e].append((waits, fn, (e, 1) if inc else None))
        self.n_inst += 1

    def dma(self, q, key, out, in_, reads, writes, **kw):
        key = "d:" + key
        self._sem(key)
        waits = self._waits(q, reads, writes)
        self.cnt[key] += 16
        val = self.cnt[key]
        self._mark(key, val, reads, writes)
        self.prog[q].append((waits, lambda e: e.dma_start(out=out, in_=in_, **kw), (key, 16)))
        self.n_inst += 1

    def collective(self, kind, in_ap, out_ap, reads, writes, key, groups=None):
        key = "c:" + key
        self._sem(key)
        waits = self._waits("pool", reads, writes)
        self.cnt[key] += 1
        val = self.cnt[key]
        self._mark(key, val, reads, writes)
        groups = groups or [list(range(8))]
        self.prog["pool"].append((waits, lambda e: e.collective_compute(
            kind, ALU.bypass, replica_groups=groups, ins=[in_ap.opt()], outs=[out_ap.opt()]), (key, 1)))
        self.n_inst += 1

    def barrier(self):
        for e in self.ENG:
            waits = []
            for k, v in self.cnt.items():
                if v > 0 and k != e and self.seen[e].get(k, 0) < v:
                    self.seen[e][k] = v
                    waits.append((k, v))
            self.prog[e].append((waits, None, None))

    def finish(self, final_res):
        waits = self._waits("sp", final_res, [])
        self.prog["sp"].append((waits, None, None))
        nc = self.nc
        with nc.Block() as block:
            def run(e):
                def body(eng):
                    for waits, fn, inc in self.prog[e]:
                        for k, v in waits:
                            eng.wait_ge(self.sems[k], v)
                        if fn is None:
                            continue
                        ins = fn(eng)
                        if inc is not None:
                            ins.then_inc(self.sems[inc[0]], inc[1])
                return body
            block.tensor(run("pe"))
            block.scalar(run("act"))
            block.vector(run("dve"))
            block.gpsimd(run("pool"))
            block.sync(run("sp"))
        self.stack.close()


def kernel(**inputs):
    raise NotImplementedError


D = 1024
KD = 8
DFF = 2816
NFFC = 22
EPS = 1e-6
MASKV = -1e30
FFG = [(0, 4), (4, 4), (8, 4), (12, 4), (16, 4), (20, 2)]

V_NORM = 0
V_KVN = 64
V_FIN = 72
V_LB = 80
V_BADA = 96
V_BADAKV = 288
V_GN = 304
NV = 306
NCOEF = 208


def slot_heads():
    out = []
    for i in range(8):
        for half in range(2):
            out.append(i + 4 * half if i < 4 else 8 + (i - 4) + 4 * half)
    return out


def t5_bucket_np(d):
    d = np.asarray(d)
    dd = np.clip(d, 0, 127)
    large = 16 + (np.log(np.maximum(dd, 1).astype(np.float32) / np.float32(16)) / np.float32(np.log(8.0))
                  * np.float32(16)).astype(np.int32)
    large = np.clip(large, 0, 31)
    return np.where(dd < 16, dd, large)


class Prog:
    def __init__(self, T=1024, NBLK=8, sample=True, dbg=None, nlayers=4, NPRE=None, NOWN=None):
        if NPRE is None:
            self.types = ["own"] * NBLK
        else:
            self.types = ["pre"] * NPRE + ["prekv"] + ["own"] * NOWN
            NBLK = len(self.types)
        self.rmode = NPRE is not None
        self.own_blks = [i for i, t in enumerate(self.types) if t == "own"]
        self.T, self.NBLK, self.NB = T, NBLK, T // 512
        self.sample = sample
        self.dbg = dbg
        self.nlayers = nlayers
        self.NTOK = T * NBLK
        nc = self.nc = bass.Bass("TRN2", target_bir_lowering=False)
        kb = self.kb = KB(nc)
        di = lambda n, s: nc.dram_tensor(n, s, F32, kind="ExternalInput")
        do = lambda n, s: nc.dram_tensor(n, s, F32, kind="ExternalOutput")
        NTOK = self.NTOK
        self.xT = di("xT", [128, 8, NTOK])
        self.cT = di("cT", [128, 8, 17])
        self.xsT = di("xsT", [128, 8, 16])
        self.vecs_d = di("vecs", [128, NV])
        self.w_in_h = di("w_in_h", [2, 8, 1024, 512])
        self.w_o_a = di("w_o_a", [2, 1024, 1024])
        self.w_kv = di("w_kv", [1024, 512])
        self.w_ada = di("w_ada", [4, 1024, 6144])
        self.w_ada_kv = di("w_ada_kv", [1024, 2048])
        self.w_q_p = di("w_q_p", [2, 1024, 1024])
        self.w_o_p = di("w_o_p", [2, 1024, 1024])
        self.w_ffn_in = di("w_ffn_in", [4, 1024, 5632])
        self.w_ffn_out = di("w_ffn_out", [4, 2816, 1024])
        self.sinks_bc_d = di("sinks_bc", [128, 32])
        self.sink_s_d = di("sink_s", [16, 2])
        self.rb_ext_d = di("rb_ext", [33, 16])
        self.eline_d = di("eline", [33, 384])
        self.eline_s_d = di("eline_s", [33, 128])
        self.ident_d = di("ident", [128, 128])
        self.trimask_d = di("trimask", [128, 64])
        self.halfmask_d = di("halfmask", [128, 2])
        self.state_d = di("state", [16, 2, 8, 128, 128])
        self.ck_d = di("ck", [16, 128, 256])
        self.cv_d = di("cv", [16, 128, 256])
        self.ckT_d = di("ckT", [16, 256, 127])
        self.bmask_d = di("bmask", [128, NBLK])
        self.yT = do("yT", [128, 8, T * len(self.own_blks)])
        self.st_p = do("st_p", [2, 8, 128, 128])
        self.kT_p = do("kT_p", [128, 2, 128])
        self.v_p = do("v_p", [128, 256])
        self.y_s = do("y_s", [128, 8, 16])
        self.st_s = do("st_s", [16, 2, 8, 128, 128])
        self.ck_s = do("ck_s", [16, 128, 256])
        self.cv_s = do("cv_s", [16, 128, 256])
        if dbg is not None:
            self.dbg_d = do("dbg", [128, 8, T])
        self.lines_d = nc.dram_tensor("lines", [16, 128, 384], F32)
        self.rowk_d = nc.dram_tensor("rowk", [16, 256], BF16)
        self.outs = []
        self.alloc()

    def alloc(self):
        kb, T = self.kb, self.T
        sb = kb.sbuf
        W = 8 * T + 3 * 4 * T
        W = max(W, 16384)
        self.arena = sb("arena", [128, W], F32)
        ar = self.arena
        self.hT = ar[:, 0:8 * T].rearrange("p (k t) -> p k t", k=8)
        o = 8 * T
        self.xn = ar[:, o:o + 4 * T].bitcast(BF16).rearrange("p (k t) -> p k t", k=8)
        self.og = ar[:, o + 4 * T:o + 8 * T].bitcast(BF16).rearrange("p (k t) -> p k t", k=8)
        self.qT = ar[:, o + 8 * T:o + 12 * T].bitcast(BF16).rearrange("p (k t) -> p k t", k=8)
        def carve(off, words, dt=F32):
            v = ar[:, off:off + words]
            return v if dt == F32 else v.bitcast(dt)
        o = 0
        self.adaT = carve(o, 208 * 17).rearrange("p (g n) -> p g n", n=17); o += 208 * 17
        self.coefS = carve(o, 208 * 16).rearrange("p (g n) -> p g n", n=16); o += 208 * 16
        self.csb = carve(o, 68, BF16).rearrange("p (k n) -> p k n", k=8); o += 68
        self.hs = carve(o, 128).rearrange("p (k n) -> p k n", k=8); o += 128
        self.xns = carve(o, 64, BF16).rearrange("p (k n) -> p k n", k=8); o += 64
        self.ogs = carve(o, 64, BF16).rearrange("p (k n) -> p k n", k=8); o += 64
        self.as_ = carve(o, 32, BF16).rearrange("p (k n) -> p k n", k=4); o += 32
        self.qTs = carve(o, 64, BF16).rearrange("p (k n) -> p k n", k=8); o += 64
        self.Qb2 = carve(o, 256, BF16).rearrange("p (b c s) -> p b c s", b=16, c=2); o += 256
        self.Sin = carve(o, 2048).rearrange("p (b v) -> p b v", b=16); o += 2048
        self.Sout = carve(o, 2048).rearrange("p (b v) -> p b v", b=16); o += 2048
        self.KTs = carve(o, 2048, BF16).rearrange("p (c b t) -> p c b t", c=2, b=16); o += 2048
        self.Vs = carve(o, 2048, BF16).rearrange("p (b n) -> p b n", b=16); o += 2048
        self.pTs = carve(o, 128, BF16).rearrange("p (b s) -> p b s", b=16); o += 128
        self.bias_s = carve(o, 128); o += 128
        self.sink_s = carve(o, 2); o += 2
        assert o <= W, (o, W)
        self.NSLOT = 5
        self.wsl = [sb("wsl%d" % i, [128, 4096], BF16) for i in range(self.NSLOT)]
        self.NS = 14
        self.scr = sb("scr", [128, self.NS, 512], F32)
        self.NBS = 10
        self.bscr = sb("bscr", [128, self.NBS, 512], BF16)
        self.G = sb("G", [128, 513], F32)
        self.KT = sb("KT", [128, 2, 128 + T], BF16)
        self.V = sb("V", [128, 1 + T // 128, 256], BF16)
        self.bias = sb("bias", [128, 16, 256], F32)
        self.Sst = sb("Sst", [128, 16, 128], F32)
        self.coefP = sb("coefP", [128, NCOEF], F32)
        self.vecs = sb("vecs", [128, NV], F32)
        self.lbs = sb("lbs", [128, 2, 8], F32)
        self.sinks_bc = sb("sinks_bc", [128, 32], F32)
        self.ident = sb("ident", [128, 128], F32)
        self.identb = sb("identb", [128, 128], BF16)
        self.onesb = sb("onesb", [128, 128], BF16)
        self.ones64 = sb("ones64", [128, 512], F32)
        self.trimask = sb("trimask", [128, 64], F32)
        self.small = sb("small", [128, 64], F32)
        self.epsc = sb("epsc", [128, 2], F32)
        self.bmask = sb("bmask", [128, self.NBLK], F32)
        self.hmask = sb("hmask", [128, 1], F32)
        self.attm = sb("attm", [128, 4, 64], BF16)
        self.ps = [kb.psum("ps%d" % i, [128, 512], F32) for i in range(8)]
        self.ws_i = 0

    def S(self, i):
        return self.scr[:, i, :]

    def B(self, i):
        return self.bscr[:, i, :]

    def act(self, out, in_, func, reads, writes, **kw):
        self.kb.emit("act", lambda e: e.activation(out=out, in_=in_, func=func, **kw), reads, writes)

    def tt(self, eng, out, a, b, op, reads, writes):
        self.kb.emit(eng, lambda e: e.tensor_tensor(out=out, in0=a, in1=b, op=op), reads, writes)

    def ts(self, eng, out, a, s1, s2, op0, op1, reads, writes):
        self.kb.emit(eng, lambda e: e.tensor_scalar(out=out, in0=a, scalar1=s1, scalar2=s2, op0=op0, op1=op1),
                     reads, writes)

    def stt(self, out, a, s, b, op0, op1, reads, writes):
        self.kb.emit("dve", lambda e: e.scalar_tensor_tensor(out=out, in0=a, scalar=s, in1=b, op0=op0, op1=op1),
                     reads, writes)

    def cp(self, eng, out, in_, reads, writes):
        if eng == "act":
            self.act(out, in_, AF.Copy, reads, writes)
        else:
            self.kb.emit(eng, lambda e: e.tensor_copy(out=out, in_=in_), reads, writes)

    def mm(self, out, pairs, reads, writes):
        n = len(pairs)
        for i, (l, r) in enumerate(pairs):
            self.kb.emit("pe", lambda e, l=l, r=r, i=i: e.matmul(out, lhsT=l, rhs=r, start=(i == 0), stop=(i == n - 1)),
                         reads, writes, inc=(i == n - 1))

    def ld(self, out, in_, writes, key, q="sp", reads=()):
        self.kb.dma(q, key, out, in_, list(reads), writes)

    def slab(self, dram_ap, kind):
        i = self.ws_i % self.NSLOT
        self.ws_i += 1
        t = self.wsl[i]
        res = ("w", i)
        if kind == "K8":
            ncol = dram_ap.shape[1]
            view = t[:, :].rearrange("p (k n) -> p k n", k=8)
            self.kb.dma("pool", "w%d" % i, view[:, :, 0:ncol], dram_ap.rearrange("(k p) n -> p k n", p=128), [], [res])
        else:
            nr = dram_ap.shape[0] // 128
            view = t[:, :].rearrange("p (c n) -> p c n", c=4)
            self.kb.dma("pool", "w%d" % i, view[:, 0:nr, :], dram_ap.rearrange("(c p) n -> p c n", p=128), [], [res])
        return view, res

    def stream(self, items):
        q = []
        it = iter(items)
        for _ in range(2):
            x = next(it, None)
            if x is not None:
                q.append(self.slab(*x))
        while q:
            cur = q.pop(0)
            x = next(it, None)
            if x is not None:
                q.append(self.slab(*x))
            yield cur

    def setup(self):
        kb = self.kb
        self.ld(self.vecs[:, :], self.vecs_d[:, :], ["vecs"], "c11")
        self.ld(self.sinks_bc[:, :], self.sinks_bc_d[:, :], ["sinks"], "c12")
        self.ld(self.ident[:, :], self.ident_d[:, :], ["ident"], "c13")
        self.ld(self.trimask[:, :], self.trimask_d[:, :], ["trimask"], "c14")
        kb.emit("dve", lambda e: e.tensor_copy(out=self.identb[:, :], in_=self.ident[:, :]), ["ident"], ["identb"])
        self.ld(self.bmask[:, :], self.bmask_d[:, :], ["bmask"], "c30")
        if self.rmode:
            pk = self.types.index("prekv")
            self.ts("dve", self.hmask[:, 0:1], self.bmask[:, pk:pk + 1], 1.0, -MASKV, ALU.subtract, ALU.mult, ["bmask"], ["hmask"])
        kb.emit("dve", lambda e: e.memset(self.onesb[:, :], 1.0), [], ["onesb"])
        kb.emit("dve", lambda e: e.memset(self.ones64[:, :], 1.0), [], ["ones64"])
        kb.emit("dve", lambda e: e.memset(self.G[:, 0:1], 0.0), [], ["G0"])
        kb.emit("dve", lambda e: e.memset(self.attm[:, :, :], 0.0), [], ["attm"])
        kb.emit("dve", lambda e: e.memset(self.epsc[:, 0:1], EPS), [], ["epsc"])
        kb.emit("dve", lambda e: e.memset(self.epsc[:, 1:2], 1.0), [], ["epsc"])
        kb.emit("dve", lambda e: e.memset(self.Sst[:, :, :], 0.0), [], ["Sst"])
        kb.emit("dve", lambda e: e.memset(self.lbs[:, 0, :], 0.0), [], ["lbs"])
        t = self.small[:, 0:8]
        self.tt("dve", t, self.vecs[:, V_LB:V_LB + 8], self.vecs[:, V_LB + 8:V_LB + 16], ALU.subtract, ["vecs"], ["small"])
        self.act(t, t, AF.Exp, ["small"], ["small"])
        self.ts("dve", t, t, 1.0, None, ALU.add, ALU.bypass, ["small"], ["small"])
        kb.emit("dve", lambda e: e.reciprocal(out=self.lbs[:, 1, :], in_=t), ["small"], ["lbs"])
        rb = self.S(0)[0:33, 0:16]
        el = self.S(1)[0:33, 0:384]
        self.ld(rb, self.rb_ext_d[:, :], ["s0"], "c15")
        self.ld(el, self.eline_d[:, :], ["s1"], "c16")
        pl = self.ps[0][0:16, 0:384]
        self.mm(pl, [(rb, el)], ["s0", "s1"], ["ps0"])
        ln = self.S(2)[0:16, 0:384]
        self.cp("dve", ln, pl, ["ps0"], ["s2"])
        src = bass.AP(ln.tensor, ln.offset, [list(ln.ap[0]), [0, 128], [1, 384]])
        self.ld(self.lines_d[:, :, :], src, ["lines"], "c1", reads=["s2"])
        tv = bass.AP(self.lines_d, 127, [[383, 128], [128 * 384, 16], [1, 256]])
        self.ld(self.bias[:, :, :], tv, ["bias"], "c1", reads=["lines"])

    def coef(self, l, which):
        if l == 4:
            return 192, 200, None
        base = l * 48 + which * 24
        return base, base + 8, base + 16

    def modulate(self, l, which):
        a0, b0, _ = self.coef(l, which)
        for sb in range(self.NB):
            tok = slice(sb * 512, (sb + 1) * 512)
            hs = ("h", sb)
            sq = self.bscr[:, 0:8, :]
            sqr = ["b%d" % k for k in range(8)]
            self.act(sq, self.hT[:, :, tok], AF.Square, [hs], sqr)
            pss = self.ps[0]
            self.mm(pss[:, :], [(self.onesb[:, :], self.bscr[:, k, :]) for k in range(8)], ["onesb"] + sqr, ["ps0"])
            rstd = self.S(0)
            self.act(rstd, pss[:, :], AF.Ln, ["ps0"], ["s0"], scale=1.0 / D, bias=self.epsc[:, 0:1])
            self.act(rstd, rstd, AF.Exp, ["s0"], ["s0"], scale=-0.5)
            for k in range(8):
                t = self.S(1 + (k % 2))
                r = "s%d" % (1 + (k % 2))
                self.stt(t, self.hT[:, k, tok], self.coefP[:, a0 + k:a0 + k + 1], rstd, ALU.mult, ALU.mult,
                         [hs, "s0", "coefP"], [r])
                self.act(self.xn[:, k, tok], t, AF.Identity, [r, "coefP"], [("xn", sb)],
                         bias=self.coefP[:, b0 + k:b0 + k + 1], scale=1.0)

    def proj_residual(self, slabs, src, src_res, gcol, nk):
        dout = 0
        for view, wres in self.stream(slabs):
            for dd in range(4):
                for sb in range(self.NB):
                    tok = slice(sb * 512, (sb + 1) * 512)
                    pb = 4 + ((dout * self.NB + sb) % 4)
                    p = self.ps[pb]
                    self.mm(p[:, :], [(view[:, k, dd * 128:(dd + 1) * 128], src[:, k, tok]) for k in range(nk)],
                            [wres] + src_res(sb), ["ps%d" % pb])
                    self.stt(self.hT[:, dout, tok], p[:, :], self.coefP[:, gcol + dout:gcol + dout + 1],
                             self.hT[:, dout, tok], ALU.mult, ALU.add, ["ps%d" % pb, ("h", sb), "coefP"], [("h", sb)])
                dout += 1

    def ffn(self, l):
        self.modulate(l, 1)
        _, _, g2 = self.coef(l, 1)
        items = []
        for (c0, n) in FFG:
            items.append((self.w_ffn_in[l, :, c0 * 128:(c0 + n) * 128], "K8"))
            items.append((self.w_ffn_in[l, :, DFF + c0 * 128:DFF + (c0 + n) * 128], "K8"))
            items.append((self.w_ffn_out[l, c0 * 128:(c0 + n) * 128, :], "R4"))
        st = self.stream(items)
        a = self.og
        for (c0, n) in FFG:
            wg, rg = next(st)
            wu, ru = next(st)
            wo, ro = next(st)
            for sb in range(self.NB):
                tok = slice(sb * 512, (sb + 1) * 512)
                for c in range(n):
                    pg, pu = (0, 1) if (c % 2 == 0) else (2, 3)
                    self.mm(self.ps[pg][:, :], [(wg[:, k, c * 128:(c + 1) * 128], self.xn[:, k, tok]) for k in range(8)],
                            [rg, ("xn", sb)], ["ps%d" % pg])
                    self.mm(self.ps[pu][:, :], [(wu[:, k, c * 128:(c + 1) * 128], self.xn[:, k, tok]) for k in range(8)],
                            [ru, ("xn", sb)], ["ps%d" % pu])
                    sg = self.S(c % 2)
                    self.act(sg, self.ps[pg][:, :], AF.Silu, ["ps%d" % pg], ["s%d" % (c % 2)])
                    self.tt("dve", a[:, c, tok], sg, self.ps[pu][:, :], ALU.mult, ["s%d" % (c % 2), "ps%d" % pu],
                            [("og", c, sb)])
            for sb in range(self.NB):
                tok = slice(sb * 512, (sb + 1) * 512)
                for dout in range(8):
                    pb = 4 + (dout % 4)
                    p = self.ps[pb]
                    self.mm(p[:, :], [(wo[:, c, dout * 128:(dout + 1) * 128], a[:, c, tok]) for c in range(n)],
                            [ro] + [("og", c, sb) for c in range(n)], ["ps%d" % pb])
                    self.stt(self.hT[:, dout, tok], p[:, :], self.coefP[:, g2 + dout:g2 + dout + 1],
                             self.hT[:, dout, tok], ALU.mult, ALU.add, ["ps%d" % pb, ("h", sb), "coefP"], [("h", sb)])

    def headA_proj(self, l, j, sb, w, wres, state_only=False):
        ps = self.ps
        tok = slice(sb * 512, (sb + 1) * 512)
        xr = ("xn", sb)
        xk = lambda k: self.xn[:, k, tok]
        if not state_only:
            self.mm(ps[0][:, :], [(w[:, k, 0:128], xk(k)) for k in range(8)], [wres, xr], ["ps0"])
        self.mm(ps[1][:, :], [(w[:, k, 128:256], xk(k)) for k in range(8)], [wres, xr], ["ps1"])
        if not state_only:
            self.mm(ps[2][:, :], [(w[:, k, 384:512], xk(k)) for k in range(8)], [wres, xr], ["ps2"])

    def headA_evac(self, state_only=False):
        ps, S, Bb = self.ps, self.S, self.B
        if not state_only:
            self.act(S(0), ps[0][:, :], AF.Silu, ["ps0"], ["s0"])
            self.act(Bb(0), ps[2][:, :], AF.Silu, ["ps2"], ["b0"])
        self.act(S(1), ps[1][:, :], AF.Exp, ["ps1"], ["s1"], scale=-1.0)

    def headA_sub(self, l, j, sb, w, wres, blk=0, state_only=False):
        kb, ps = self.kb, self.ps
        tok = slice(sb * 512, (sb + 1) * 512)
        xr = ("xn", sb)
        S, Bb = self.S, self.B
        xk = lambda k: self.xn[:, k, tok]
        so = state_only
        lbc = self.lbs[:, l, j:j + 1]
        self.act(S(2), S(1), AF.Ln, ["s1", "lbs"], ["s2"], scale=lbc, bias=self.epsc[:, 1:2])
        self.act(S(3), S(1), AF.Ln, ["s1"], ["s3"], scale=1.0, bias=self.epsc[:, 1:2])
        self.tt("dve", S(2), S(2), S(3), ALU.subtract, ["s2", "s3"], ["s2"])
        Gc = self.G[:, 1:513]
        kb.emit("dve", lambda e: e.tensor_tensor_scan(out=Gc, data0=self.ones64[:, :], data1=S(2), initial=0.0,
                                                      op0=ALU.mult, op1=ALU.add), ["s2", "ones64"], ["G"])
        self.act(S(4), S(2), AF.Exp, ["s2"], ["s4"])
        self.ts("dve", S(3), S(4), -1.0, 1.0, ALU.mult, ALU.add, ["s4"], ["s3"])
        G3 = Gc.rearrange("p (c t) -> p c t", c=8)
        bc = lambda a: a.unsqueeze(2).broadcast_to([128, 8, 64])
        v3 = lambda i: self.scr[:, i, :].rearrange("p (c t) -> p c t", c=8)
        if not so:
            self.tt("dve", v3(5), G3, bc(self.G[:, 32:513:64]), ALU.subtract, ["G", "G0"], ["s5"])
        self.tt("dve", v3(8), G3, bc(self.G[:, 0:512:64]), ALU.subtract, ["G", "G0"], ["s8"])
        self.tt("dve", v3(9), G3, bc(self.G[:, 64:513:64]), ALU.subtract, ["G", "G0"], ["s9"])
        if not so:
            self.act(S(6), S(5), AF.Exp, ["s5"], ["s6"])
            self.act(S(7), S(5), AF.Exp, ["s5"], ["s7"], scale=-1.0)
        self.act(S(8), S(8), AF.Exp, ["s8"], ["s8"])
        self.act(S(9), S(9), AF.Exp, ["s9"], ["s9"], scale=-1.0)
        if not so:
            self.tt("dve", Bb(1), S(0), S(6), ALU.mult, ["s0", "s6"], ["b1"])
            self.tt("dve", S(10), S(0), S(8), ALU.mult, ["s0", "s8"], ["s10"])
            self.tt("dve", Bb(2), S(3), S(7), ALU.mult, ["s3", "s7"], ["b2"])
        self.tt("dve", Bb(3), S(3), S(9), ALU.mult, ["s3", "s9"], ["b3"])
        vtok = self.bscr[:, 6, :].rearrange("p (a d) -> p a d", d=128)
        for pp in range(4):
            t0 = sb * 512 + pp * 128
            self.mm(ps[3][:, pp * 128:(pp + 1) * 128],
                    [(self.xn[:, k, t0:t0 + 128], w[:, k, 256:384]) for k in range(8)], [wres, xr], ["ps3"])
        if self.rmode:
            self.act(self.bscr[:, 6, :], ps[3][:, :], AF.Identity, ["ps3", "bmask"], ["b6"],
                     scale=self.bmask[:, blk:blk + 1], bias=0.0)
        else:
            self.cp("act", self.bscr[:, 6, :], ps[3][:, :], ["ps3"], ["b6"])
        pk = ps[5][:, :].bitcast(BF16)
        for pp in range(4):
            kb.emit("pe", lambda e, pp=pp: e.transpose(out=pk[:, pp * 128:(pp + 1) * 128],
                                                       in_=Bb(3)[:, pp * 128:(pp + 1) * 128], identity=self.identb[:, :]),
                    ["b3", "identb"], ["ps5"], inc=(pp == 3))
        kh = self.bscr[:, 8, :].rearrange("p (a d) -> p a d", d=128)
        self.cp("act", self.bscr[:, 8, :], pk[:, 0:512], ["ps5"], ["b8"])
        if not so:
            for c in range(8):
                c0 = c * 64
                r0 = (c % 2) * 64
                q0 = (c // 2) * 64
                kb.emit("pe", lambda e, c0=c0, r0=r0, q0=q0: e.matmul(ps[4][r0:r0 + 64, q0 + 32:q0 + 64], lhsT=Bb(2)[:, c0:c0 + 64],
                                                                      rhs=Bb(1)[:, c0 + 32:c0 + 64], start=True, stop=True),
                        ["b1", "b2"], ["ps4"], inc=False)
                kb.emit("pe", lambda e, c0=c0, r0=r0, q0=q0: e.matmul(ps[4][r0:r0 + 32, q0:q0 + 32], lhsT=Bb(2)[:, c0:c0 + 32],
                                                                      rhs=Bb(1)[:, c0:c0 + 32], start=True, stop=True),
                        ["b1", "b2"], ["ps4"], inc=(c == 7))
            p43 = ps[4][:, 0:256].rearrange("p (c t) -> p c t", c=4)
            self.tt("dve", self.attm[:, :, 32:64], p43[:, :, 32:64],
                    self.trimask[:, 32:64].unsqueeze(1).broadcast_to([128, 4, 32]), ALU.mult, ["ps4", "trimask"], ["attm"])
            for r0 in (0, 64):
                self.tt("dve", self.attm[r0:r0 + 32, :, 0:32], p43[r0:r0 + 32, :, 0:32],
                        self.trimask[r0:r0 + 32, 0:32].unsqueeze(1).broadcast_to([32, 4, 32]), ALU.mult, ["ps4", "trimask"], ["attm"])
        Sj = self.Sst[:, l * 8 + j, :]
        sres = ("S", l, j)
        for c in range(8):
            cs = slice(c * 64, (c + 1) * 64)
            hp = slice((c % 2) * 64, (c % 2) * 64 + 64)
            pp = c // 2
            if not so:
                kb.emit("pe", lambda e, pp=pp, hp=hp, cs=cs: e.matmul(ps[7][:, cs], lhsT=vtok[hp, pp, :], rhs=self.attm[hp, pp, :], start=True, stop=False),
                        ["b6", "attm"], ["ps7"], inc=False)
                kb.emit("pe", lambda e, cs=cs: e.matmul(ps[7][:, cs], lhsT=Sj, rhs=S(10)[:, cs], start=False, stop=True),
                        [sres, "s10"], ["ps7"], inc=True)
            sl = c % 4
            pn = ps[6][:, sl * 128:(sl + 1) * 128]
            kb.emit("pe", lambda e, pp=pp, hp=hp, pn=pn: e.matmul(pn, lhsT=kh[hp, pp, :], rhs=vtok[hp, pp, :], start=True, stop=True),
                    ["b8", "b6"], ["ps6"], inc=True)
            dcol = self.scr[:, 8, c * 64 + 63:c * 64 + 64]
            self.stt(Sj, Sj, dcol, pn, ALU.mult, ALU.add, [sres, "s8", "ps6"], [sres])
            for _ in range(2):
                kb.emit("pe", lambda e: e.matmul(ps[4][:, :], lhsT=self.onesb[:, :], rhs=self.xn[:, 0, 0:512], start=True, stop=True),
                        ["onesb", ("xn", 0)], ["ps4"], inc=False)
        if so:
            return
        self.act(Bb(4), ps[7][:, :], AF.Square, ["ps7"], ["b4"])
        self.mm(ps[5][:, :], [(self.onesb[:, :], Bb(4))], ["onesb", "b4"], ["ps5"])
        self.act(S(11), ps[5][:, :], AF.Ln, ["ps5"], ["s11"], scale=1.0 / 128, bias=self.epsc[:, 0:1])
        self.act(S(11), S(11), AF.Exp, ["s11"], ["s11"], scale=-0.5)
        self.stt(S(12), ps[7][:, :], self.vecs[:, V_GN + l:V_GN + l + 1], S(11), ALU.mult, ALU.mult,
                 ["ps7", "s11", "vecs"], ["s12"])
        self.tt("dve", self.og[:, j, tok], S(12), Bb(0), ALU.mult, ["s12", "b0"], [("og", j, sb)])

    def layerA(self, blk, l, state_only=False):
        self.modulate(l, 0)
        items = [(self.w_in_h[l, j, :, :], "K8") for j in range(8)]
        st = self.stream(items)
        slabs = {}

        def slab_of(j):
            if j not in slabs:
                slabs[j] = next(st)
                slabs.pop(j - 2, None)
            return slabs[j]
        seq = [(j, sb) for j in range(8) for sb in range(self.NB)]
        w0, r0 = slab_of(0)
        self.headA_proj(l, 0, 0, w0, r0, state_only)
        for n, (j, sb) in enumerate(seq):
            w, wres = slab_of(j)
            self.headA_evac(state_only)
            if n + 1 < len(seq):
                j2, sb2 = seq[n + 1]
                w2, r2 = slab_of(j2)
                self.headA_proj(l, j2, sb2, w2, r2, state_only)
            self.headA_sub(l, j, sb, w, wres, blk=blk, state_only=state_only)
        if state_only:
            return
        _, _, g1 = self.coef(l, 0)
        self.proj_residual([(self.w_o_a[l, :, 0:512], "K8"), (self.w_o_a[l, :, 512:1024], "K8")], self.og,
                           lambda sb: [("og", j, sb) for j in range(8)], g1, 8)

    def kvproj(self, blk):
        T = self.T
        self.modulate(4, 0)
        (w, wres), = list(self.stream([(self.w_kv[:, :], "K8")]))
        last = (blk == self.NBLK - 1)
        for c in range(2):
            for sb in range(self.NB):
                tok = slice(sb * 512, (sb + 1) * 512)
                pb = c * 2 + (sb % 2)
                self.mm(self.ps[pb][:, :], [(w[:, k, c * 128:(c + 1) * 128], self.xn[:, k, tok]) for k in range(8)],
                        [wres, ("xn", sb)], ["ps%d" % pb])
                self.cp("act", self.KT[:, c, 128 + sb * 512:128 + (sb + 1) * 512], self.ps[pb][:, :], ["ps%d" % pb], ["KT"])
                if last and sb == self.NB - 1:
                    self.cp("dve", self.scr[:, 0, c * 128:(c + 1) * 128], self.ps[pb][:, 384:512], ["ps%d" % pb], ["s0"])
        if last:
            self.ld(self.kT_p[:, :, :], self.scr[:, 0, 0:256].rearrange("p (c t) -> p c t", c=2), ["kT_p"], "ok", reads=["s0"])
            self.outs.append("kT_p")
        for tt_ in range(T // 128):
            pb = 4 + (tt_ % 4)
            self.mm(self.ps[pb][:, 0:256], [(self.xn[:, k, tt_ * 128:(tt_ + 1) * 128], w[:, k, 256:512]) for k in range(8)],
                    [wres, ("xn", tt_ // 4)], ["ps%d" % pb])
            self.cp("act", self.V[:, 1 + tt_, :], self.ps[pb][:, 0:256], ["ps%d" % pb], ["V"])
            if last and tt_ == T // 128 - 1:
                self.cp("dve", self.scr[:, 1, 0:256], self.ps[pb][:, 0:256], ["ps%d" % pb], ["s1"])
                self.ld(self.v_p[:, :], self.scr[:, 1, 0:256], ["v_p"], "ov", reads=["s1"])
                self.outs.append("v_p")

    def layerB(self, blk, jb):
        l = 2 + jb
        kb, ps, T = self.kb, self.ps, self.T
        self.modulate(l, 0)
        items = [(self.w_q_p[jb, :, 0:512], "K8"), (self.w_q_p[jb, :, 512:1024], "K8")]
        i = 0
        for w, wres in self.stream(items):
            for dd in range(4):
                for sb in range(self.NB):
                    tok = slice(sb * 512, (sb + 1) * 512)
                    pb = 6 + ((i * self.NB + sb) % 2)
                    self.mm(ps[pb][:, :], [(w[:, k, dd * 128:(dd + 1) * 128], self.xn[:, k, tok]) for k in range(8)],
                            [wres, ("xn", sb)], ["ps%d" % pb])
                    self.act(self.qT[:, i, tok], ps[pb][:, :], AF.Copy, ["ps%d" % pb], [("qT", i)], scale=0.125)
                i += 1
        it = 0
        for i in range(8):
            c = i // 4
            for qt in range(T // 128):
                qtok = slice(qt * 128, (qt + 1) * 128)
                first = (qt == 0 and not any(t in ("prekv", "own") for t in self.types[:blk]))
                halo_m = (self.rmode and qt == 0 and blk == self.own_blks[0])
                nk = 128 if first else 256
                nkb = nk // 128
                koff = qt * 128 + (128 if first else 0)
                par = it % 2
                it += 1
                s3 = self.scr[:, 2 * par:2 * par + 2, 0:nk]
                sr = ["s%d" % (2 * par), "s%d" % (2 * par + 1)]
                for half in range(2):
                    hp = slice(half * 64, (half + 1) * 64)
                    pl = ps[2 * par + half]
                    plr = "ps%d" % (2 * par + half)
                    kb.emit("pe", lambda e, pl=pl, hp=hp, i=i, qtok=qtok, c=c, koff=koff, nk=nk: e.matmul(
                        pl[:, 0:nk], lhsT=self.qT[hp, i, qtok], rhs=self.KT[hp, c, koff:koff + nk], start=True, stop=True),
                        [("qT", i), "KT"], [plr])
                    self.tt("dve", s3[:, half, :], pl[:, 0:nk], self.bias[:, 2 * i + half, 256 - nk:256], ALU.add,
                            [plr, "bias"], [sr[half]])
                if halo_m:
                    self.ts("dve", s3[:, :, 0:128], s3[:, :, 0:128], self.hmask[:, 0:1], None, ALU.add, ALU.bypass, sr + ["hmask"], sr)
                sm = self.small[:, par * 16:par * 16 + 16]
                smr = "sm%d" % par
                mx, ng, rs, es, dn = sm[:, 0:2], sm[:, 2:4], sm[:, 4:6], sm[:, 6:8], sm[:, 8:10]
                kb.emit("dve", lambda e, mx=mx, s3=s3: e.tensor_reduce(out=mx, in_=s3, axis=AX.X, op=ALU.max), sr, [smr])
                sk = self.sinks_bc[:, jb * 16 + 2 * i:jb * 16 + 2 * i + 2]
                self.tt("dve", mx, mx, sk, ALU.max, [smr, "sinks"], [smr])
                self.tt("dve", s3, s3, mx.unsqueeze(2).broadcast_to([128, 2, nk]), ALU.subtract, sr + [smr], sr)
                self.act(s3, s3, AF.Exp, sr, sr)
                kb.emit("dve", lambda e, rs=rs, s3=s3: e.tensor_reduce(out=rs, in_=s3, axis=AX.X, op=ALU.add), sr, [smr])
                self.tt("dve", ng, sk, mx, ALU.subtract, [smr, "sinks"], [smr])
                self.act(es, ng, AF.Exp, [smr], [smr])
                self.tt("dve", dn, rs, es, ALU.add, [smr], [smr])
                kb.emit("dve", lambda e, dn=dn: e.reciprocal(out=dn, in_=dn), [smr], [smr])
                pn = self.bscr[:, 2 * par:2 * par + 2, 0:nk]
                pnr = ["b%d" % (2 * par), "b%d" % (2 * par + 1)]
                self.tt("dve", pn, s3, dn.unsqueeze(2).broadcast_to([128, 2, nk]), ALU.mult, sr + [smr], pnr)
                ptb = ps[4 + par][:, :].bitcast(BF16)
                ptr = "ps%d" % (4 + par)
                nt = 2 * nkb
                for half in range(2):
                    for kb2 in range(nkb):
                        o0 = (half * nkb + kb2) * 128
                        kb.emit("pe", lambda e, ptb=ptb, pn=pn, half=half, kb2=kb2, o0=o0: e.transpose(
                            out=ptb[:, o0:o0 + 128], in_=pn[:, half, kb2 * 128:(kb2 + 1) * 128], identity=self.identb[:, :]),
                            pnr + ["identb"], [ptr], inc=(half == 1 and kb2 == nkb - 1))
                pT = self.bscr[:, 4 + 2 * par:6 + 2 * par, :].rearrange("p a n -> p (a n)")[:, 0:nt * 128]
                pTr = ["b%d" % (4 + 2 * par), "b%d" % (5 + 2 * par)]
                self.cp("act", pT, ptb[:, 0:nt * 128], [ptr], pTr)
                pob = ps[6 + par]
                por = "ps%d" % (6 + par)
                for half in range(2):
                    hp = slice(half * 64, (half + 1) * 64)
                    vcol = (2 * c + half) * 64
                    for kb2 in range(nkb):
                        vt = (qt + kb2) if not first else 1
                        o0 = (half * nkb + kb2) * 128
                        kb.emit("pe", lambda e, pob=pob, hp=hp, vt=vt, vcol=vcol, pT=pT, kb2=kb2, nkb=nkb, o0=o0: e.matmul(
                            pob[hp, 0:128], lhsT=self.V[:, vt, vcol:vcol + 64], rhs=pT[:, o0:o0 + 128],
                            start=(kb2 == 0), stop=(kb2 == nkb - 1)), ["V"] + pTr, [por], inc=(half == 1 and kb2 == nkb - 1))
                self.cp("dve", self.og[:, i, qtok], pob[:, 0:128], [por], [("og", i, qt // 4)])
        _, _, g1 = self.coef(l, 0)
        self.proj_residual([(self.w_o_p[jb, :, 0:512], "K8"), (self.w_o_p[jb, :, 512:1024], "K8")], self.og,
                           lambda sb: [("og", j, sb) for j in range(8)], g1, 8)

    def final(self, blk):
        T = self.T
        for sb in range(self.NB):
            tok = slice(sb * 512, (sb + 1) * 512)
            hs = ("h", sb)
            sqr = ["b%d" % k for k in range(8)]
            self.act(self.bscr[:, 0:8, :], self.hT[:, :, tok], AF.Square, [hs], sqr)
            self.mm(self.ps[0][:, :], [(self.onesb[:, :], self.bscr[:, k, :]) for k in range(8)], ["onesb"] + sqr, ["ps0"])
            rstd = self.S(0)
            self.act(rstd, self.ps[0][:, :], AF.Ln, ["ps0"], ["s0"], scale=1.0 / D, bias=self.epsc[:, 0:1])
            self.act(rstd, rstd, AF.Exp, ["s0"], ["s0"], scale=-0.5)
            for k in range(8):
                self.stt(self.scr[:, 4 + k, :], self.hT[:, k, tok], self.vecs[:, V_FIN + k:V_FIN + k + 1], rstd,
                         ALU.mult, ALU.mult, [hs, "s0", "vecs"], ["s%d" % (4 + k)])
            ob = self.own_blks.index(blk)
            self.ld(self.yT[:, :, ob * T + sb * 512:ob * T + (sb + 1) * 512], self.scr[:, 4:12, :], ["yT"], "y%d" % sb,
                    reads=["s%d" % (4 + k) for k in range(8)])
        if "yT" not in self.outs:
            self.outs.append("yT")

    def block(self, blk):
        T = self.T
        for sb in range(self.NB):
            self.ld(self.hT[:, :, sb * 512:(sb + 1) * 512], self.xT[:, :, blk * T + sb * 512:blk * T + (sb + 1) * 512],
                    [("h", sb)], "x%d" % sb)
        typ = self.types[blk]
        for l in range(self.nlayers):
            if typ == "pre" and l >= 1:
                if l == 1:
                    self.layerA(blk, 1, state_only=True)
                continue
            if typ == "prekv" and l >= 2:
                continue
            if l < 2:
                self.layerA(blk, l)
            else:
                self.layerB(blk, l - 2)
            if self.dbg == l + 10 and blk == 0:
                self.ld(self.dbg_d[:, :, :], self.hT[:, :, :], ["dbg"], "o2", reads=[("h", sb) for sb in range(self.NB)])
                self.outs.append("dbg")
            self.ffn(l)
            if l == 1:
                self.kvproj(blk)
            if self.dbg == l and blk == 0:
                self.ld(self.dbg_d[:, :, :], self.hT[:, :, :], ["dbg"], "o2", reads=[("h", sb) for sb in range(self.NB)])
                self.outs.append("dbg")
        if self.nlayers == 4 and typ == "own":
            self.final(blk)
        if blk < self.NBLK - 1 and self.nlayers > 1 and typ != "pre":
            self.cp("dve", self.KT[:, :, 0:128], self.KT[:, :, T:T + 128], ["KT"], ["KT"])
            self.cp("dve", self.V[:, 0, :], self.V[:, T // 128, :], ["V"], ["V"])
        if blk == self.NBLK - 1:
            self.ld(self.st_p.ap().rearrange("l j k v -> k (l j) v"), self.Sst[:, :, :], ["st_p"], "ost", reads=[("S", l, j) for l in range(2) for j in range(8)])
            self.outs.append("st_p")

    def ada_phase(self):
        kb, ps = self.kb, self.ps
        cs = self.scr[:, 0, 0:136].rearrange("p (k n) -> p k n", k=8)
        self.ld(cs, self.cT[:, :, :], ["s0"], "c17")
        csb = self.csb
        self.act(csb[:, :, :], cs, AF.Silu, ["s0"], ["csb"])
        items = []
        for l in range(4):
            for s in range(12):
                items.append((self.w_ada[l, :, s * 512:(s + 1) * 512], "K8"))
        for s in range(4):
            items.append((self.w_ada_kv[:, s * 512:(s + 1) * 512], "K8"))
        n = 0
        for w, wres in self.stream(items):
            l, s = (n // 12, n % 12) if n < 48 else (4, n - 48)
            n += 1
            for dd in range(4):
                ch = s * 4 + dd
                gidx = l * 48 + ch
                bcol = (V_BADA + l * 48 + ch) if l < 4 else (V_BADAKV + ch)
                pb = gidx % 4
                self.mm(ps[pb][:, 0:17], [(w[:, k, dd * 128:(dd + 1) * 128], csb[:, k, :]) for k in range(8)],
                        [wres, "csb"], ["ps%d" % pb])
                self.ts("dve", self.adaT[:, gidx, :], ps[pb][:, 0:17], self.vecs[:, bcol:bcol + 1], None, ALU.add, ALU.bypass,
                        ["ps%d" % pb, "vecs"], ["adaT"])
        aT = self.adaT
        for l in range(5):
            for which in range(2 if l < 4 else 1):
                base = l * 48 + which * 24
                nw = self.vecs[:, V_NORM + (l * 2 + which) * 8:V_NORM + (l * 2 + which) * 8 + 8] if l < 4 else self.vecs[:, V_KVN:V_KVN + 8]
                self.stt(self.coefP[:, base:base + 8], aT[:, base + 8:base + 16, 0], 1.0, nw, ALU.add, ALU.mult,
                         ["adaT", "vecs"], ["coefP"])
                self.cp("dve", self.coefP[:, base + 8:base + 16], aT[:, base:base + 8, 0], ["adaT"], ["coefP"])
                if l < 4:
                    self.cp("dve", self.coefP[:, base + 16:base + 24], aT[:, base + 16:base + 24, 0], ["adaT"], ["coefP"])
                if self.sample:
                    self.stt(self.coefS[:, base:base + 8, :], aT[:, base + 8:base + 16, 1:17], 1.0,
                             nw.unsqueeze(2).broadcast_to([128, 8, 16]), ALU.add, ALU.mult, ["adaT", "vecs"], ["coefS"])
                    self.cp("dve", self.coefS[:, base + 8:base + 16, :], aT[:, base:base + 8, 1:17], ["adaT"], ["coefS"])
                    if l < 4:
                        self.cp("dve", self.coefS[:, base + 16:base + 24, :], aT[:, base + 16:base + 24, 1:17], ["adaT"], ["coefS"])

    def build(self):
        self.setup()
        self.ada_phase()
        if self.sample:
            self.sample_phase()
        self.kb.barrier()
        for blk in range(self.NBLK):
            self.block(blk)
        self.kb.finish(self.outs)
        return self.nc


def _fm(v):
    v = np.asarray(v, np.float32)
    return np.ascontiguousarray(v.reshape(-1, 128).T)


def _consts():
    eline = np.zeros((33, 384), np.float32)
    eline[32, :] = MASKV
    for ip in range(128, 256):
        eline[int(t5_bucket_np(255 - ip)), ip] = 1.0
        eline[32, ip] = 0.0
    eline_s = np.zeros((33, 128), np.float32)
    for r in range(128):
        eline_s[int(t5_bucket_np(127 - r)), r] = 1.0
    ident = np.eye(128, dtype=np.float32)
    tri = (np.arange(64)[:, None] <= np.arange(64)[None, :]).astype(np.float32)
    tri = np.concatenate([tri, tri], 0)
    hm = np.zeros((128, 2), np.float32)
    hm[:64, 0] = 1.0
    hm[64:, 1] = 1.0
    return dict(eline=eline, eline_s=eline_s, ident=ident, trimask=tri, halfmask=hm)


def prep_shared(inp):
    f32 = lambda a: np.ascontiguousarray(np.asarray(a, np.float32))
    sh = slot_heads()
    w_in_a = f32(inp["w_in_a"])
    d = {}
    d["w_in_h"] = np.ascontiguousarray(w_in_a.reshape(2, 1024, 4, 8, 128).transpose(0, 3, 1, 2, 4).reshape(2, 8, 1024, 512))
    d["w_o_a"] = f32(inp["w_o_a"])
    d["w_kv"] = f32(inp["w_kv"])
    d["w_ada"] = f32(inp["w_ada"])
    d["w_ada_kv"] = f32(inp["w_ada_kv"])
    wq = f32(inp["w_q_b"]).reshape(2, 1024, 16, 64)
    d["w_q_p"] = np.ascontiguousarray(wq[:, :, sh, :].reshape(2, 1024, 1024))
    wo = f32(inp["w_o_b"]).reshape(2, 16, 64, 1024)
    d["w_o_p"] = np.ascontiguousarray(wo[:, sh, :, :].reshape(2, 1024, 1024))
    d["w_ffn_in"] = f32(inp["w_ffn_in"])
    d["w_ffn_out"] = f32(inp["w_ffn_out"])
    sk = f32(inp["sinks_b"])[:, sh]
    d["sinks_bc"] = np.ascontiguousarray(np.broadcast_to(sk.reshape(1, 32), (128, 32)))
    d["sink_s"] = np.ascontiguousarray(sk.T)
    rb = f32(inp["rel_bias"])[:, sh]
    d["rb_ext"] = np.ascontiguousarray(np.concatenate([rb, np.ones((1, 16), np.float32)], 0))
    vecs = np.zeros((128, NV), np.float32)
    nw = f32(inp["norm_w"])
    for l in range(4):
        for wh in range(2):
            vecs[:, V_NORM + (l * 2 + wh) * 8:V_NORM + (l * 2 + wh) * 8 + 8] = _fm(nw[l, wh])
    vecs[:, V_KVN:V_KVN + 8] = _fm(inp["kv_norm_w"])
    vecs[:, V_FIN:V_FIN + 8] = _fm(inp["final_norm_w"])
    lb = f32(inp["lb_a"])
    vecs[:, V_LB:V_LB + 8] = _fm(lb[0])
    vecs[:, V_LB + 8:V_LB + 16] = _fm(lb[1])
    ba = f32(inp["b_ada"])
    for l in range(4):
        vecs[:, V_BADA + l * 48:V_BADA + (l + 1) * 48] = _fm(ba[l])
    vecs[:, V_BADAKV:V_BADAKV + 16] = _fm(inp["b_ada_kv"])
    gn = f32(inp["gnorm_a"])
    vecs[:, V_GN] = gn[0]
    vecs[:, V_GN + 1] = gn[1]
    d["vecs"] = vecs
    d.update(_consts())
    return d


def prep_core(inp, shared, core, T, NBLK, seq=None, win=None):
    f32 = lambda a: np.ascontiguousarray(np.asarray(a, np.float32))
    NTOK = T * NBLK
    d = dict(shared)
    if win is None:
        seq = core % 2 if seq is None else seq
        x = f32(inp["x_prompt"])[seq, :NTOK]
        bm = np.ones((128, NBLK), np.float32)
    else:
        seq, start, end = win
        assert end - start == NTOK
        x = np.zeros((NTOK, 1024), np.float32)
        v0 = max(start, 0)
        x[v0 - start:] = f32(inp["x_prompt"])[seq, v0:end]
        bm = np.zeros((128, NBLK), np.float32)
        for b in range(NBLK):
            bm[:, b] = 1.0 if start + b * T >= 0 else 0.0
    d["bmask"] = bm
    d["xT"] = np.ascontiguousarray(x.T.reshape(8, 128, NTOK).transpose(1, 0, 2))
    bs = slice(core * 16, (core + 1) * 16)
    c17 = np.concatenate([f32(inp["c_prompt"])[seq][None], f32(inp["c_sample"])[bs]], 0)
    d["cT"] = np.ascontiguousarray(c17.T.reshape(8, 128, 17).transpose(1, 0, 2))
    xs = f32(inp["x_sample"])[bs, 0]
    d["xsT"] = np.ascontiguousarray(xs.T.reshape(8, 128, 16).transpose(1, 0, 2))
    d["state"] = f32(inp["state_hgrn"])[bs]
    ck = f32(inp["cache_swa_k"])[bs].reshape(16, 128, 256)
    d["ck"] = ck
    d["cv"] = f32(inp["cache_swa_v"])[bs].reshape(16, 128, 256)
    d["ckT"] = np.ascontiguousarray(ck[:, 1:128, :].transpose(0, 2, 1))
    return d


def _sample_methods():
    def bc3(a, n):
        return a.unsqueeze(2).broadcast_to([a.shape[0], a.shape[1], n])

    def sample_modulate(self, l, which):
        a0, b0, _ = self.coef(l, which)
        S, ps = self.S, self.ps
        sq = self.B(0)[:, 0:128].rearrange("p (k n) -> p k n", k=8)
        self.act(sq, self.hs[:, :, :], AF.Square, ["hs"], ["b0"])
        self.mm(ps[0][:, 0:16], [(self.onesb[:, :], sq[:, k, :]) for k in range(8)], ["onesb", "b0"], ["ps0"])
        rstd = S(0)[:, 0:16]
        self.act(rstd, ps[0][:, 0:16], AF.Ln, ["ps0"], ["s0"], scale=1.0 / D, bias=self.epsc[:, 0:1])
        self.act(rstd, rstd, AF.Exp, ["s0"], ["s0"], scale=-0.5)
        t = S(1)[:, 0:128].rearrange("p (k n) -> p k n", k=8)
        self.tt("dve", t, self.hs[:, :, :], rstd.unsqueeze(1).broadcast_to([128, 8, 16]), ALU.mult, ["hs", "s0"], ["s1"])
        self.tt("dve", t, t, self.coefS[:, a0:a0 + 8, :], ALU.mult, ["s1", "coefS"], ["s1"])
        self.tt("dve", self.xns[:, :, :], t, self.coefS[:, b0:b0 + 8, :], ALU.add, ["s1", "coefS"], ["xns"])

    def sample_proj(self, slabs, src, src_res, gcol, nk):
        dout = 0
        for view, wres in self.stream(slabs):
            for dd in range(4):
                pb = 4 + (dout % 4)
                p = self.ps[pb]
                self.mm(p[:, 0:16], [(view[:, k, dd * 128:(dd + 1) * 128], src[:, k, :]) for k in range(nk)],
                        [wres, src_res], ["ps%d" % pb])
                tmp = self.S(3)[:, 0:16]
                self.tt("dve", tmp, p[:, 0:16], self.coefS[:, gcol + dout, :], ALU.mult, ["ps%d" % pb, "coefS"], ["s3"])
                self.tt("dve", self.hs[:, dout, :], self.hs[:, dout, :], tmp, ALU.add, ["hs", "s3"], ["hs"])
                dout += 1

    def sample_ffn(self, l):
        self.sample_modulate(l, 1)
        _, _, g2 = self.coef(l, 1)
        items = []
        for (c0, n) in FFG:
            items.append((self.w_ffn_in[l, :, c0 * 128:(c0 + n) * 128], "K8"))
            items.append((self.w_ffn_in[l, :, DFF + c0 * 128:DFF + (c0 + n) * 128], "K8"))
            items.append((self.w_ffn_out[l, c0 * 128:(c0 + n) * 128, :], "R4"))
        st = self.stream(items)
        ps = self.ps
        for (c0, n) in FFG:
            wg, rg = next(st)
            wu, ru = next(st)
            wo, ro = next(st)
            for c in range(n):
                self.mm(ps[0][:, 0:16], [(wg[:, k, c * 128:(c + 1) * 128], self.xns[:, k, :]) for k in range(8)], [rg, "xns"], ["ps0"])
                self.mm(ps[1][:, 0:16], [(wu[:, k, c * 128:(c + 1) * 128], self.xns[:, k, :]) for k in range(8)], [ru, "xns"], ["ps1"])
                sg = self.S(0)[:, 0:16]
                self.act(sg, ps[0][:, 0:16], AF.Silu, ["ps0"], ["s0"])
                self.tt("dve", self.as_[:, c, :], sg, ps[1][:, 0:16], ALU.mult, ["s0", "ps1"], ["as"])
            for dout in range(8):
                pb = 4 + (dout % 4)
                self.mm(ps[pb][:, 0:16], [(wo[:, c, dout * 128:(dout + 1) * 128], self.as_[:, c, :]) for c in range(n)],
                        [ro, "as"], ["ps%d" % pb])
                tmp = self.S(3)[:, 0:16]
                self.tt("dve", tmp, ps[pb][:, 0:16], self.coefS[:, g2 + dout, :], ALU.mult, ["ps%d" % pb, "coefS"], ["s3"])
                self.tt("dve", self.hs[:, dout, :], self.hs[:, dout, :], tmp, ALU.add, ["hs", "s3"], ["hs"])

    def sampleA(self, l):
        kb, ps, S = self.kb, self.ps, self.S
        self.sample_modulate(l, 0)
        items = [(self.w_in_h[l, j, :, :], "K8") for j in range(8)]
        for j, (w, wres) in enumerate(self.stream(items)):
            self.ld(self.Sin[:, :, :], self.state_d[:, l, j].rearrange("b k v -> k b v"), ["Sin"], "si")
            pp = ps[0]
            for part in range(4):
                self.mm(pp[:, part * 16:(part + 1) * 16],
                        [(w[:, k, part * 128:(part + 1) * 128], self.xns[:, k, :]) for k in range(8)], [wres, "xns"], ["ps0"])
            q, gate = S(0)[:, 0:16], S(0)[:, 16:32]
            self.act(q, pp[:, 0:16], AF.Silu, ["ps0"], ["s0"])
            self.act(gate, pp[:, 48:64], AF.Silu, ["ps0"], ["s0"])
            e, L1, L2, lg, f, kk, v = [S(1)[:, i * 16:(i + 1) * 16] for i in range(7)]
            self.act(e, pp[:, 16:32], AF.Exp, ["ps0"], ["s1"], scale=-1.0)
            self.act(L1, e, AF.Ln, ["s1", "lbs"], ["s1"], scale=self.lbs[:, l, j:j + 1], bias=self.epsc[:, 1:2])
            self.act(L2, e, AF.Ln, ["s1"], ["s1"], scale=1.0, bias=self.epsc[:, 1:2])
            self.tt("dve", lg, L1, L2, ALU.subtract, ["s1"], ["s1"])
            self.act(f, lg, AF.Exp, ["s1"], ["s1"])
            self.ts("dve", kk, f, -1.0, 1.0, ALU.mult, ALU.add, ["s1"], ["s1"])
            self.cp("dve", v, pp[:, 32:48], ["ps0"], ["s1"])
            rd = self.scr[:, 4:8, :].rearrange("p a (b v) -> p (a b) v", v=128)
            self.tt("dve", rd, self.ident[:, :].unsqueeze(1).broadcast_to([128, 16, 128]), bc3(v, 128), ALU.mult,
                    ["ident", "s1"], ["s4", "s5", "s6", "s7"])
            for qd in range(4):
                self.mm(ps[4 + qd][:, :], [(self.ones64[:, 0:128], self.scr[:, 4 + qd, :])], ["ones64", "s%d" % (4 + qd)],
                        ["ps%d" % (4 + qd)])
                self.tt("dve", self.scr[:, 8 + qd, :].rearrange("p (b v) -> p b v", v=128),
                        ps[4 + qd][:, :].rearrange("p (b v) -> p b v", v=128), bc3(kk[:, 4 * qd:4 * qd + 4], 128), ALU.mult,
                        ["ps%d" % (4 + qd), "s1"], ["s%d" % (8 + qd)])
            self.tt("dve", self.Sout[:, :, :], self.Sin[:, :, :], bc3(f, 128), ALU.mult, ["Sin", "s1"], ["Sout"])
            self.tt("dve", self.Sout[:, :, :], self.Sout[:, :, :], self.scr[:, 8:12, :].rearrange("p a (b v) -> p (a b) v", v=128),
                    ALU.add, ["Sout", "s8", "s9", "s10", "s11"], ["Sout"])
            self.ld(self.st_s[:, l, j].rearrange("b k v -> k b v"), self.Sout[:, :, :], ["st_s"], "so", reads=["Sout"])
            po2 = ps[1]
            q2 = S(0)[:, 32:64].rearrange("p (b t) -> p b t", t=2)
            self.cp("dve", q2, bc3(q, 2), ["s0"], ["s0"])
            for b in range(16):
                kb.emit("pe", lambda e, b=b: e.matmul(po2[:, 2 * b:2 * b + 2], lhsT=self.Sout[:, b, :], rhs=q2[:, b, :], start=True, stop=True),
                        ["Sout", "s0"], ["ps1"], inc=(b == 15))
            po = po2[:, 0:32:2]
            osq = self.B(1)[:, 0:16]
            self.act(osq, po, AF.Square, ["ps1"], ["b1"])
            self.mm(ps[2][:, 0:16], [(self.onesb[:, :], osq)], ["onesb", "b1"], ["ps2"])
            rstd = S(2)[:, 0:16]
            self.act(rstd, ps[2][:, 0:16], AF.Ln, ["ps2"], ["s2"], scale=1.0 / 128, bias=self.epsc[:, 0:1])
            self.act(rstd, rstd, AF.Exp, ["s2"], ["s2"], scale=-0.5)
            t = S(2)[:, 16:32]
            self.stt(t, po, self.vecs[:, V_GN + l:V_GN + l + 1], rstd, ALU.mult, ALU.mult, ["ps1", "s2", "vecs"], ["s2"])
            self.tt("dve", self.ogs[:, j, :], t, gate, ALU.mult, ["s2", "s0"], ["ogs"])
        if "st_s" not in self.outs:
            self.outs.append("st_s")
        _, _, g1 = self.coef(l, 0)
        self.sample_proj([(self.w_o_a[l, :, 0:512], "K8"), (self.w_o_a[l, :, 512:1024], "K8")], self.ogs, "ogs", g1, 8)

    def sample_kv(self):
        ps, S = self.ps, self.S
        self.sample_modulate(4, 0)
        self.kb.dma("pool", "cv", self.Vs[0:112, :, :], self.cv_d[:, 1:113, :].rearrange("b s n -> s b n"), [], ["Vs"])
        self.kb.dma("pool", "cv2", self.Vs[112:127, :, :], self.cv_d[:, 113:128, :].rearrange("b s n -> s b n"), [], ["Vs"])
        ckv = self.ckT_d.ap().rearrange("b (c p) t -> p c b t", p=128)
        for c in range(2):
            self.kb.dma("pool", "ck%d" % c, self.KTs[:, c, :, 0:127], ckv[:, c, :, :], [], ["KTs"])
        self.ld(self.ck_s[:, 0:127, :], self.ck_d[:, 1:128, :], ["ck_s"], "ock")
        self.ld(self.cv_s[:, 0:127, :], self.cv_d[:, 1:128, :], ["cv_s"], "ocv")
        (w, wres), = list(self.stream([(self.w_kv[:, :], "K8")]))
        for c in range(2):
            self.mm(ps[0][:, c * 16:(c + 1) * 16], [(w[:, k, c * 128:(c + 1) * 128], self.xns[:, k, :]) for k in range(8)],
                    [wres, "xns"], ["ps0"])
        self.cp("dve", self.KTs[:, :, :, 127], ps[0][:, 0:32].rearrange("p (c b) -> p c b", c=2), ["ps0"], ["KTs"])
        self.mm(ps[1][0:16, :], [(self.xns[:, k, :], w[:, k, :]) for k in range(8)], [wres, "xns"], ["ps1"])
        rowf = S(0)[0:16, :]
        self.cp("dve", rowf, ps[1][0:16, :], ["ps1"], ["s0"])
        self.ld(self.ck_s[:, 127, :], rowf[:, 0:256], ["ck_s"], "ock2", reads=["s0"])
        self.ld(self.cv_s[:, 127, :], rowf[:, 256:512], ["cv_s"], "ocv2", reads=["s0"])
        rowb = self.B(0)[0:16, 0:256]
        self.cp("dve", rowb, ps[1][0:16, 256:512], ["ps1"], ["b0"])
        self.ld(self.rowk_d[:, :], rowb, ["rowk"], "rk", reads=["b0"])
        self.ld(self.Vs[127:128, :, :], self.rowk_d.ap().rearrange("(o b) n -> o b n", o=1), ["Vs"], "rk2", reads=["rowk"])
        self.outs += ["ck_s", "cv_s"]

    def sampleB(self, jb):
        kb, ps, S = self.kb, self.ps, self.S
        l = 2 + jb
        self.sample_modulate(l, 0)
        i = 0
        for w, wres in self.stream([(self.w_q_p[jb, :, 0:512], "K8"), (self.w_q_p[jb, :, 512:1024], "K8")]):
            for dd in range(4):
                pb = i % 4
                self.mm(ps[pb][:, 0:16], [(w[:, k, dd * 128:(dd + 1) * 128], self.xns[:, k, :]) for k in range(8)],
                        [wres, "xns"], ["ps%d" % pb])
                self.act(self.qTs[:, i, :], ps[pb][:, 0:16], AF.Copy, ["ps%d" % pb], ["qTs"], scale=0.125)
                i += 1
        kb.emit("dve", lambda e: e.memset(self.Qb2[:, :, :, :], 0.0), [], ["Qb2"])
        for c in range(2):
            for half in range(2):
                hp = slice(half * 64, (half + 1) * 64)
                self.cp("dve", self.Qb2[hp, :, c, c * 8 + half:c * 8 + 8:2],
                        self.qTs[hp, 4 * c:4 * c + 4, :].rearrange("p i b -> p b i"), ["qTs"], ["Qb2"])
        for b in range(16):
            pb = 4 + b // 4
            self.mm(ps[pb][0:16, (b % 4) * 128:(b % 4 + 1) * 128],
                    [(self.Qb2[:, b, c, :], self.KTs[:, c, b, :]) for c in range(2)], ["Qb2", "KTs"], ["ps%d" % pb])
        sv = lambda qd: self.scr[0:16, 4 + qd, :].rearrange("p (b t) -> p b t", t=128)
        for qd in range(4):
            self.tt("dve", sv(qd), ps[4 + qd][0:16, :].rearrange("p (b t) -> p b t", t=128),
                    self.bias_s[0:16, :].unsqueeze(1).broadcast_to([16, 4, 128]), ALU.add, ["ps%d" % (4 + qd), "bias_s"], ["s%d" % (4 + qd)])
        s_all = self.scr[0:16, 4:8, :].rearrange("p a (b t) -> p (a b) t", t=128)
        sr = ["s4", "s5", "s6", "s7"]
        sm = self.small
        mx, rs, es, dn = sm[0:16, 0:16], sm[0:16, 16:32], sm[0:16, 32:48], sm[0:16, 48:64]
        kb.emit("dve", lambda e: e.tensor_reduce(out=mx, in_=s_all, axis=AX.X, op=ALU.max), sr, ["sm0"])
        skc = self.sink_s[0:16, jb:jb + 1]
        self.ts("dve", mx, mx, skc, None, ALU.max, ALU.bypass, ["sm0", "sink_s"], ["sm0"])
        self.tt("dve", s_all, s_all, bc3(mx, 128), ALU.subtract, sr + ["sm0"], sr)
        self.act(self.scr[0:16, 4:8, :], self.scr[0:16, 4:8, :], AF.Exp, sr, sr)
        kb.emit("dve", lambda e: e.tensor_reduce(out=rs, in_=s_all, axis=AX.X, op=ALU.add), sr, ["sm0"])
        self.act(es, mx, AF.Exp, ["sm0", "sink_s"], ["sm0"], scale=-1.0, bias=skc)
        self.tt("dve", dn, rs, es, ALU.add, ["sm0"], ["sm0"])
        kb.emit("dve", lambda e: e.reciprocal(out=dn, in_=dn), ["sm0"], ["sm0"])
        pn = self.bscr[0:16, 4:8, :].rearrange("p a (b t) -> p (a b) t", t=128)
        pnr = ["b4", "b5", "b6", "b7"]
        self.tt("dve", pn, s_all, bc3(dn, 128), ALU.mult, sr + ["sm0"], pnr)
        ptb = ps[0][:, :].bitcast(BF16)
        for b in range(16):
            kb.emit("pe", lambda e, b=b: e.transpose(out=ptb[:, b * 16:(b + 1) * 16], in_=pn[:, b, :], identity=self.identb[0:16, 0:16]),
                    pnr + ["identb"], ["ps0"], inc=(b == 15))
        self.cp("act", self.pTs[:, :, :].rearrange("p b s -> p (b s)"), ptb[:, 0:256], ["ps0"], ["pTs"])
        pv = ps[1]
        for b in range(16):
            for c in range(2):
                o0 = (b * 2 + c) * 16
                kb.emit("pe", lambda e, b=b, c=c, o0=o0: e.matmul(pv[:, o0:o0 + 16], lhsT=self.Vs[:, b, c * 128:(c + 1) * 128],
                                                                  rhs=self.pTs[:, b, :], start=True, stop=True),
                        ["Vs", "pTs"], ["ps1"], inc=(b == 15 and c == 1))
        pvv = pv[:, :].rearrange("p (b c s) -> p b c s", b=16, c=2)
        for c in range(2):
            for half in range(2):
                hp = slice(half * 64, (half + 1) * 64)
                self.cp("dve", self.ogs[hp, 4 * c:4 * c + 4, :].rearrange("p i b -> p b i"),
                        pvv[hp, :, c, c * 8 + half:c * 8 + 8:2], ["ps1"], ["ogs"])
        _, _, g1 = self.coef(l, 0)
        self.sample_proj([(self.w_o_p[jb, :, 0:512], "K8"), (self.w_o_p[jb, :, 512:1024], "K8")], self.ogs, "ogs", g1, 8)

    def sample_phase(self):
        S, ps = self.S, self.ps
        self.ld(self.hs[:, :, :], self.xsT[:, :, :], ["hs"], "c20")
        self.ld(self.sink_s[0:16, 0:2], self.sink_s_d[:, :], ["sink_s"], "c21")
        rb = S(12)[0:33, 0:16]
        el = S(13)[0:33, 0:128]
        self.ld(rb, self.rb_ext_d[:, :], ["s12"], "c22")
        self.ld(el, self.eline_s_d[:, :], ["s13"], "c23")
        self.mm(ps[3][0:16, 0:128], [(rb, el)], ["s12", "s13"], ["ps3"])
        self.cp("dve", self.bias_s[0:16, :], ps[3][0:16, 0:128], ["ps3"], ["bias_s"])
        for l in range(4):
            if l < 2:
                self.sampleA(l)
            else:
                self.sampleB(l - 2)
            self.sample_ffn(l)
            if l == 1:
                self.sample_kv()
        sq = self.B(0)[:, 0:128].rearrange("p (k n) -> p k n", k=8)
        self.act(sq, self.hs[:, :, :], AF.Square, ["hs"], ["b0"])
        self.mm(ps[0][:, 0:16], [(self.onesb[:, :], sq[:, k, :]) for k in range(8)], ["onesb", "b0"], ["ps0"])
        rstd = S(0)[:, 0:16]
        self.act(rstd, ps[0][:, 0:16], AF.Ln, ["ps0"], ["s0"], scale=1.0 / D, bias=self.epsc[:, 0:1])
        self.act(rstd, rstd, AF.Exp, ["s0"], ["s0"], scale=-0.5)
        t = S(1)[:, 0:128].rearrange("p (k n) -> p k n", k=8)
        self.tt("dve", t, self.hs[:, :, :], rstd.unsqueeze(1).broadcast_to([128, 8, 16]), ALU.mult, ["hs", "s0"], ["s1"])
        self.tt("dve", t, t, bc3(self.vecs[:, V_FIN:V_FIN + 8], 16), ALU.mult, ["s1", "vecs"], ["s1"])
        self.ld(self.y_s[:, :, :], t, ["y_s"], "oys", reads=["s1"])
        self.outs.append("y_s")

    for f in (sample_modulate, sample_proj, sample_ffn, sampleA, sample_kv, sampleB, sample_phase):
        setattr(Prog, f.__name__, f)


_sample_methods()


T_BLK = 1024
N_PRE = 5
N_OWN = 2
OWN_TOK = T_BLK * N_OWN
_CACHE = {}


def kernel(**inputs):
    if "nc" not in _CACHE:
        prog = Prog(T=T_BLK, sample=True, NPRE=N_PRE, NOWN=N_OWN)
        _CACHE["nc"] = prog.build()
    nc = _CACHE["nc"]
    NBLK = N_PRE + 1 + N_OWN
    NTOK = T_BLK * NBLK
    shared = prep_shared(inputs)
    in_maps = []
    for c in range(8):
        seq, p = c // 4, c % 4
        end = (p + 1) * OWN_TOK
        in_maps.append(prep_core(inputs, shared, c, T_BLK, NBLK, win=(seq, end - NTOK, end)))
    res = run_bass_kernel_spmd(nc, in_maps, core_ids=list(range(8)))
    r = res.results
    g = lambda c, n, shp: np.asarray(r[c][n], np.float32).reshape(shp)
    y_prompt = np.stack([np.concatenate([g(4 * s + p, "yT", (128, 8, OWN_TOK)).transpose(2, 1, 0).reshape(OWN_TOK, 1024)
                                         for p in range(4)], 0) for s in range(2)])
    y_sample = np.concatenate([g(c, "y_s", (128, 8, 16)).transpose(2, 1, 0).reshape(16, 1, 1024) for c in range(8)], 0)
    last = [3, 7]
    st_p = np.stack([g(c, "st_p", (2, 8, 128, 128)) for c in last])
    st_s = np.concatenate([g(c, "st_s", (16, 2, 8, 128, 128)) for c in range(8)], 0)
    k_p = np.stack([g(c, "kT_p", (2, 64, 2, 128)).transpose(3, 2, 0, 1).reshape(128, 4, 64) for c in last])
    v_p = np.stack([g(c, "v_p", (128, 4, 64)) for c in last])
    k_s = np.concatenate([g(c, "ck_s", (16, 128, 4, 64)) for c in range(8)], 0)
    v_s = np.concatenate([g(c, "cv_s", (16, 128, 4, 64)) for c in range(8)], 0)
    f = np.ascontiguousarray
    return (f(y_prompt), f(y_sample), f(st_p), f(st_s), f(k_p), f(v_p), f(k_s), f(v_s))
```

```python
from contextlib import ExitStack
import numpy as np
import concourse.bass as bass
import concourse.mybir as mybir
from concourse.bass_utils import run_bass_kernel_spmd

F32 = mybir.dt.float32
BF16 = mybir.dt.bfloat16
I32 = mybir.dt.int32
AF = mybir.ActivationFunctionType
ALU = mybir.AluOpType
AX = mybir.AxisListType

SAME_ENGINE_SYNC = True


class KB:
    ENG = ("pe", "act", "dve", "pool", "sp")

    def __init__(self, nc):
        self.nc = nc
        self.stack = ExitStack()
        self.h = {"pe": nc.tensor, "act": nc.scalar, "dve": nc.vector, "pool": nc.gpsimd, "sp": nc.sync}
        self.prog = {e: [] for e in self.ENG}
        self.cnt = {e: 0 for e in self.ENG}
        self.seen = {e: {} for e in self.ENG}
        self.res_w = {}
        self.res_r = {}
        self.sems = {}
        self.n_inst = 0

    def sbuf(self, name, shape, dtype):
        return self.stack.enter_context(self.nc.sbuf_tensor("t_" + name, shape, dtype))

    def psum(self, name, shape, dtype):
        return self.stack.enter_context(self.nc.psum_tensor("p_" + name, shape, dtype))

    def _sem(self, key):
        if key not in self.sems:
            self.sems[key] = self.stack.enter_context(self.nc.semaphore("s_" + key.replace(":", "_")))
            self.cnt.setdefault(key, 0)
        return self.sems[key]

    @staticmethod
    def _is_psum(r):
        return (isinstance(r, str) and r.startswith("ps")) or (isinstance(r, tuple) and str(r[0]).startswith("ps"))

    def _waits(self, e, reads, writes):
        w = {}

        def need(key, val, raw=False):
            if key == e and (e == "pe" or not SAME_ENGINE_SYNC or val > self.cnt[e]):
                return
            if key == e and not raw:
                return
            if val > w.get(key, 0):
                w[key] = val
        for r in reads:
            lw = self.res_w.get(r)
            if lw:
                need(lw[0], lw[1], raw=True)
            if self._is_psum(r):
                for k, v in self.res_r.get(r, {}).items():
                    if k != e:
                        need(k, v)
        for x in writes:
            lw = self.res_w.get(x)
            if lw:
                need(*lw)
            for k, v in self.res_r.get(x, {}).items():
                need(k, v)
        out = []
        for k, v in w.items():
            if self.seen[e].get(k, 0) < v:
                self.seen[e][k] = v
                out.append((k, v))
        return out

    def _mark(self, key, val, reads, writes):
        for r in reads:
            d = self.res_r.setdefault(r, {})
            if d.get(key, 0) < val:
                d[key] = val
        for x in writes:
            self.res_w[x] = (key, val)
            self.res_r[x] = {}

    def emit(self, e, fn, reads, writes, inc=True):
        waits = self._waits(e, reads, writes)
        self._sem(e)
        if inc:
            self.cnt[e] += 1
            val = self.cnt[e]
        else:
            val = self.cnt[e] + 1
        self._mark(e, val, reads, writes)
        self.prog[e].append((waits, fn, (e, 1) if inc else None))
        self.n_inst += 1

    def dma(self, q, key, out, in_, reads, writes, **kw):
        key = "d:" + key
        self._sem(key)
        waits = self._waits(q, reads, writes)
        self.cnt[key] += 16
        val = self.cnt[key]
        self._mark(key, val, reads, writes)
        self.prog[q].append((waits, lambda e: e.dma_start(out=out, in_=in_, **kw), (key, 16)))
        self.n_inst += 1

    def collective(self, kind, in_ap, out_ap, reads, writes, key, groups=None):
        key = "c:" + key
        self._sem(key)
        waits = self._waits("pool", reads, writes)
        self.cnt[key] += 1
        val = self.cnt[key]
        self._mark(key, val, reads, writes)
        groups = groups or [list(range(8))]
        self.prog["pool"].append((waits, lambda e: e.collective_compute(
            kind, ALU.bypass, replica_groups=groups, ins=[in_ap.opt()], outs=[out_ap.opt()]), (key, 1)))
        self.n_inst += 1

    def barrier(self):
        for e in self.ENG:
            waits = []
            for k, v in self.cnt.items():
                if v > 0 and k != e and self.seen[e].get(k, 0) < v:
                    self.seen[e][k] = v
                    waits.append((k, v))
            self.prog[e].append((waits, None, None))

    def finish(self, final_res):
        waits = self._waits("sp", final_res, [])
        self.prog["sp"].append((waits, None, None))
        nc = self.nc
        with nc.Block() as block:
            def run(e):
                def body(eng):
                    for waits, fn, inc in self.prog[e]:
                        for k, v in waits:
                            eng.wait_ge(self.sems[k], v)
                        if fn is None:
                            continue
                        ins = fn(eng)
                        if inc is not None:
                            ins.then_inc(self.sems[inc[0]], inc[1])
                return body
            block.tensor(run("pe"))
            block.scalar(run("act"))
            block.vector(run("dve"))
            block.gpsimd(run("pool"))
            block.sync(run("sp"))
        self.stack.close()


def kernel(**inputs):
    raise NotImplementedError


D = 1024
KD = 8
DFF = 2816
NFFC = 22
EPS = 1e-6
MASKV = -1e30
FFG = [(0, 4), (4, 4), (8, 4), (12, 4), (16, 4), (20, 2)]

V_NORM = 0
V_KVN = 64
V_FIN = 72
V_LB = 80
V_BADA = 96
V_BADAKV = 288
V_GN = 304
NV = 306
NCOEF = 208


def slot_heads():
    out = []
    for i in range(8):
        for half in range(2):
            out.append(i + 4 * half if i < 4 else 8 + (i - 4) + 4 * half)
    return out


def t5_bucket_np(d):
    d = np.asarray(d)
    dd = np.clip(d, 0, 127)
    large = 16 + (np.log(np.maximum(dd, 1).astype(np.float32) / np.float32(16)) / np.float32(np.log(8.0))
                  * np.float32(16)).astype(np.int32)
    large = np.clip(large, 0, 31)
    return np.where(dd < 16, dd, large)


class Prog:
    def __init__(self, T=1024, NBLK=8, sample=True, dbg=None, nlayers=4, NPRE=None, NOWN=None):
        if NPRE is None:
            self.types = ["own"] * NBLK
        else:
            self.types = ["pre"] * NPRE + ["prekv"] + ["own"] * NOWN
            NBLK = len(self.types)
        self.rmode = NPRE is not None
        self.own_blks = [i for i, t in enumerate(self.types) if t == "own"]
        self.T, self.NBLK, self.NB = T, NBLK, T // 512
        self.sample = sample
        self.dbg = dbg
        self.nlayers = nlayers
        self.NTOK = T * NBLK
        nc = self.nc = bass.Bass("TRN2", target_bir_lowering=False)
        kb = self.kb = KB(nc)
        di = lambda n, s: nc.dram_tensor(n, s, F32, kind="ExternalInput")
        do = lambda n, s: nc.dram_tensor(n, s, F32, kind="ExternalOutput")
        NTOK = self.NTOK
        self.xT = di("xT", [128, 8, NTOK])
        self.cT = di("cT", [128, 8, 17])
        self.xsT = di("xsT", [128, 8, 16])
        self.vecs_d = di("vecs", [128, NV])
        self.w_in_h = di("w_in_h", [2, 8, 1024, 512])
        self.w_o_a = di("w_o_a", [2, 1024, 1024])
        self.w_kv = di("w_kv", [1024, 512])
        self.w_ada = di("w_ada", [4, 1024, 6144])
        self.w_ada_kv = di("w_ada_kv", [1024, 2048])
        self.w_q_p = di("w_q_p", [2, 1024, 1024])
        self.w_o_p = di("w_o_p", [2, 1024, 1024])
        self.w_ffn_in = di("w_ffn_in", [4, 1024, 5632])
        self.w_ffn_out = di("w_ffn_out", [4, 2816, 1024])
        self.sinks_bc_d = di("sinks_bc", [128, 32])
        self.sink_s_d = di("sink_s", [16, 2])
        self.rb_ext_d = di("rb_ext", [33, 16])
        self.eline_d = di("eline", [33, 384])
        self.eline_s_d = di("eline_s", [33, 128])
        self.ident_d = di("ident", [128, 128])
        self.trimask_d = di("trimask", [128, 64])
        self.halfmask_d = di("halfmask", [128, 2])
        self.state_d = di("state", [16, 2, 8, 128, 128])
        self.ck_d = di("ck", [16, 128, 256])
        self.cv_d = di("cv", [16, 128, 256])
        self.ckT_d = di("ckT", [16, 256, 127])
        self.bmask_d = di("bmask", [128, NBLK])
        self.yT = do("yT", [128, 8, T * len(self.own_blks)])
        self.st_p = do("st_p", [2, 8, 128, 128])
        self.kT_p = do("kT_p", [128, 2, 128])
        self.v_p = do("v_p", [128, 256])
        self.y_s = do("y_s", [128, 8, 16])
        self.st_s = do("st_s", [16, 2, 8, 128, 128])
        self.ck_s = do("ck_s", [16, 128, 256])
        self.cv_s = do("cv_s", [16, 128, 256])
        if dbg is not None:
            self.dbg_d = do("dbg", [128, 8, T])
        self.lines_d = nc.dram_tensor("lines", [16, 128, 384], F32)
        self.rowk_d = nc.dram_tensor("rowk", [16, 256], BF16)
        self.outs = []
        self.alloc()

    def alloc(self):
        kb, T = self.kb, self.T
        sb = kb.sbuf
        W = 8 * T + 3 * 4 * T
        W = max(W, 16384)
        self.arena = sb("arena", [128, W], F32)
        ar = self.arena
        self.hT = ar[:, 0:8 * T].rearrange("p (k t) -> p k t", k=8)
        o = 8 * T
        self.xn = ar[:, o:o + 4 * T].bitcast(BF16).rearrange("p (k t) -> p k t", k=8)
        self.og = ar[:, o + 4 * T:o + 8 * T].bitcast(BF16).rearrange("p (k t) -> p k t", k=8)
        self.qT = ar[:, o + 8 * T:o + 12 * T].bitcast(BF16).rearrange("p (k t) -> p k t", k=8)
        def carve(off, words, dt=F32):
            v = ar[:, off:off + words]
            return v if dt == F32 else v.bitcast(dt)
        o = 0
        self.adaT = carve(o, 208 * 17).rearrange("p (g n) -> p g n", n=17); o += 208 * 17
        self.coefS = carve(o, 208 * 16).rearrange("p (g n) -> p g n", n=16); o += 208 * 16
        self.csb = carve(o, 68, BF16).rearrange("p (k n) -> p k n", k=8); o += 68
        self.hs = carve(o, 128).rearrange("p (k n) -> p k n", k=8); o += 128
        self.xns = carve(o, 64, BF16).rearrange("p (k n) -> p k n", k=8); o += 64
        self.ogs = carve(o, 64, BF16).rearrange("p (k n) -> p k n", k=8); o += 64
        self.as_ = carve(o, 32, BF16).rearrange("p (k n) -> p k n", k=4); o += 32
        self.qTs = carve(o, 64, BF16).rearrange("p (k n) -> p k n", k=8); o += 64
        self.Qb2 = carve(o, 256, BF16).rearrange("p (b c s) -> p b c s", b=16, c=2); o += 256
        self.Sin = carve(o, 2048).rearrange("p (b v) -> p b v", b=16); o += 2048
        self.Sout = carve(o, 2048).rearrange("p (b v) -> p b v", b=16); o += 2048
        self.KTs = carve(o, 2048, BF16).rearrange("p (c b t) -> p c b t", c=2, b=16); o += 2048
        self.Vs = carve(o, 2048, BF16).rearrange("p (b n) -> p b n", b=16); o += 2048
        self.pTs = carve(o, 128, BF16).rearrange("p (b s) -> p b s", b=16); o += 128
        self.bias_s = carve(o, 128); o += 128
        self.sink_s = carve(o, 2); o += 2
        assert o <= W, (o, W)
        self.NSLOT = 5
        self.wsl = [sb("wsl%d" % i, [128, 4096], BF16) for i in range(self.NSLOT)]
        self.NS = 14
        self.scr = sb("scr", [128, self.NS, 512], F32)
        self.NBS = 10
        self.bscr = sb("bscr", [128, self.NBS, 512], BF16)
        self.G = sb("G", [128, 513], F32)
        self.KT = sb("KT", [128, 2, 128 + T], BF16)
        self.V = sb("V", [128, 1 + T // 128, 256], BF16)
        self.bias = sb("bias", [128, 16, 256], F32)
        self.Sst = sb("Sst", [128, 16, 128], F32)
        self.coefP = sb("coefP", [128, NCOEF], F32)
        self.vecs = sb("vecs", [128, NV], F32)
        self.lbs = sb("lbs", [128, 2, 8], F32)
        self.sinks_bc = sb("sinks_bc", [128, 32], F32)
        self.ident = sb("ident", [128, 128], F32)
        self.identb = sb("identb", [128, 128], BF16)
        self.onesb = sb("onesb", [128, 128], BF16)
        self.ones64 = sb("ones64", [128, 512], F32)
        self.trimask = sb("trimask", [128, 64], F32)
        self.small = sb("small", [128, 64], F32)
        self.epsc = sb("epsc", [128, 2], F32)
        self.bmask = sb("bmask", [128, self.NBLK], F32)
        self.hmask = sb("hmask", [128, 1], F32)
        self.attm = sb("attm", [128, 4, 64], BF16)
        self.ps = [kb.psum("ps%d" % i, [128, 512], F32) for i in range(8)]
        self.ws_i = 0

    def S(self, i):
        return self.scr[:, i, :]

    def B(self, i):
        return self.bscr[:, i, :]

    def act(self, out, in_, func, reads, writes, **kw):
        self.kb.emit("act", lambda e: e.activation(out=out, in_=in_, func=func, **kw), reads, writes)

    def tt(self, eng, out, a, b, op, reads, writes):
        self.kb.emit(eng, lambda e: e.tensor_tensor(out=out, in0=a, in1=b, op=op), reads, writes)

    def ts(self, eng, out, a, s1, s2, op0, op1, reads, writes):
        self.kb.emit(eng, lambda e: e.tensor_scalar(out=out, in0=a, scalar1=s1, scalar2=s2, op0=op0, op1=op1),
                     reads, writes)

    def stt(self, out, a, s, b, op0, op1, reads, writes):
        self.kb.emit("dve", lambda e: e.scalar_tensor_tensor(out=out, in0=a, scalar=s, in1=b, op0=op0, op1=op1),
                     reads, writes)

    def cp(self, eng, out, in_, reads, writes):
        if eng == "act":
            self.act(out, in_, AF.Copy, reads, writes)
        else:
            self.kb.emit(eng, lambda e: e.tensor_copy(out=out, in_=in_), reads, writes)

    def mm(self, out, pairs, reads, writes):
        n = len(pairs)
        for i, (l, r) in enumerate(pairs):
            self.kb.emit("pe", lambda e, l=l, r=r, i=i: e.matmul(out, lhsT=l, rhs=r, start=(i == 0), stop=(i == n - 1)),
                         reads, writes, inc=(i == n - 1))

    def ld(self, out, in_, writes, key, q="sp", reads=()):
        self.kb.dma(q, key, out, in_, list(reads), writes)

    def slab(self, dram_ap, kind):
        i = self.ws_i % self.NSLOT
        self.ws_i += 1
        t = self.wsl[i]
        res = ("w", i)
        if kind == "K8":
            ncol = dram_ap.shape[1]
            view = t[:, :].rearrange("p (k n) -> p k n", k=8)
            self.kb.dma("pool", "w%d" % i, view[:, :, 0:ncol], dram_ap.rearrange("(k p) n -> p k n", p=128), [], [res])
        else:
            nr = dram_ap.shape[0] // 128
            view = t[:, :].rearrange("p (c n) -> p c n", c=4)
            self.kb.dma("pool", "w%d" % i, view[:, 0:nr, :], dram_ap.rearrange("(c p) n -> p c n", p=128), [], [res])
        return view, res

    def stream(self, items):
        q = []
        it = iter(items)
        for _ in range(2):
            x = next(it, None)
            if x is not None:
                q.append(self.slab(*x))
        while q:
            cur = q.pop(0)
            x = next(it, None)
            if x is not None:
                q.append(self.slab(*x))
            yield cur

    def setup(self):
        kb = self.kb
        self.ld(self.vecs[:, :], self.vecs_d[:, :], ["vecs"], "c11")
        self.ld(self.sinks_bc[:, :], self.sinks_bc_d[:, :], ["sinks"], "c12")
        self.ld(self.ident[:, :], self.ident_d[:, :], ["ident"], "c13")
        self.ld(self.trimask[:, :], self.trimask_d[:, :], ["trimask"], "c14")
        kb.emit("dve", lambda e: e.tensor_copy(out=self.identb[:, :], in_=self.ident[:, :]), ["ident"], ["identb"])
        self.ld(self.bmask[:, :], self.bmask_d[:, :], ["bmask"], "c30")
        if self.rmode:
            pk = self.types.index("prekv")
            self.ts("dve", self.hmask[:, 0:1], self.bmask[:, pk:pk + 1], 1.0, -MASKV, ALU.subtract, ALU.mult, ["bmask"], ["hmask"])
        kb.emit("dve", lambda e: e.memset(self.onesb[:, :], 1.0), [], ["onesb"])
        kb.emit("dve", lambda e: e.memset(self.ones64[:, :], 1.0), [], ["ones64"])
        kb.emit("dve", lambda e: e.memset(self.G[:, 0:1], 0.0), [], ["G0"])
        kb.emit("dve", lambda e: e.memset(self.attm[:, :, :], 0.0), [], ["attm"])
        kb.emit("dve", lambda e: e.memset(self.epsc[:, 0:1], EPS), [], ["epsc"])
        kb.emit("dve", lambda e: e.memset(self.epsc[:, 1:2], 1.0), [], ["epsc"])
        kb.emit("dve", lambda e: e.memset(self.Sst[:, :, :], 0.0), [], ["Sst"])
        kb.emit("dve", lambda e: e.memset(self.lbs[:, 0, :], 0.0), [], ["lbs"])
        t = self.small[:, 0:8]
        self.tt("dve", t, self.vecs[:, V_LB:V_LB + 8], self.vecs[:, V_LB + 8:V_LB + 16], ALU.subtract, ["vecs"], ["small"])
        self.act(t, t, AF.Exp, ["small"], ["small"])
        self.ts("dve", t, t, 1.0, None, ALU.add, ALU.bypass, ["small"], ["small"])
        kb.emit("dve", lambda e: e.reciprocal(out=self.lbs[:, 1, :], in_=t), ["small"], ["lbs"])
        rb = self.S(0)[0:33, 0:16]
        el = self.S(1)[0:33, 0:384]
        self.ld(rb, self.rb_ext_d[:, :], ["s0"], "c15")
        self.ld(el, self.eline_d[:, :], ["s1"], "c16")
        pl = self.ps[0][0:16, 0:384]
        self.mm(pl, [(rb, el)], ["s0", "s1"], ["ps0"])
        ln = self.S(2)[0:16, 0:384]
        self.cp("dve", ln, pl, ["ps0"], ["s2"])
        src = bass.AP(ln.tensor, ln.offset, [list(ln.ap[0]), [0, 128], [1, 384]])
        self.ld(self.lines_d[:, :, :], src, ["lines"], "c1", reads=["s2"])
        tv = bass.AP(self.lines_d, 127, [[383, 128], [128 * 384, 16], [1, 256]])
        self.ld(self.bias[:, :, :], tv, ["bias"], "c1", reads=["lines"])

    def coef(self, l, which):
        if l == 4:
            return 192, 200, None
        base = l * 48 + which * 24
        return base, base + 8, base + 16

    def modulate(self, l, which):
        a0, b0, _ = self.coef(l, which)
        for sb in range(self.NB):
            tok = slice(sb * 512, (sb + 1) * 512)
            hs = ("h", sb)
            sq = self.bscr[:, 0:8, :]
            sqr = ["b%d" % k for k in range(8)]
            self.act(sq, self.hT[:, :, tok], AF.Square, [hs], sqr)
            pss = self.ps[0]
            self.mm(pss[:, :], [(self.onesb[:, :], self.bscr[:, k, :]) for k in range(8)], ["onesb"] + sqr, ["ps0"])
            rstd = self.S(0)
            self.act(rstd, pss[:, :], AF.Ln, ["ps0"], ["s0"], scale=1.0 / D, bias=self.epsc[:, 0:1])
            self.act(rstd, rstd, AF.Exp, ["s0"], ["s0"], scale=-0.5)
            for k in range(8):
                t = self.S(1 + (k % 2))
                r = "s%d" % (1 + (k % 2))
                self.stt(t, self.hT[:, k, tok], self.coefP[:, a0 + k:a0 + k + 1], rstd, ALU.mult, ALU.mult,
                         [hs, "s0", "coefP"], [r])
                self.act(self.xn[:, k, tok], t, AF.Identity, [r, "coefP"], [("xn", sb)],
                         bias=self.coefP[:, b0 + k:b0 + k + 1], scale=1.0)

    def proj_residual(self, slabs, src, src_res, gcol, nk):
        dout = 0
        for view, wres in self.stream(slabs):
            for dd in range(4):
                for sb in range(self.NB):
                    tok = slice(sb * 512, (sb + 1) * 512)
                    pb = 4 + ((dout * self.NB + sb) % 4)
                    p = self.ps[pb]
                    self.mm(p[:, :], [(view[:, k, dd * 128:(dd + 1) * 128], src[:, k, tok]) for k in range(nk)],
                            [wres] + src_res(sb), ["ps%d" % pb])
                    self.stt(self.hT[:, dout, tok], p[:, :], self.coefP[:, gcol + dout:gcol + dout + 1],
                             self.hT[:, dout, tok], ALU.mult, ALU.add, ["ps%d" % pb, ("h", sb), "coefP"], [("h", sb)])
                dout += 1

    def ffn(self, l):
        self.modulate(l, 1)
        _, _, g2 = self.coef(l, 1)
        items = []
        for (c0, n) in FFG:
            items.append((self.w_ffn_in[l, :, c0 * 128:(c0 + n) * 128], "K8"))
            items.append((self.w_ffn_in[l, :, DFF + c0 * 128:DFF + (c0 + n) * 128], "K8"))
            items.append((self.w_ffn_out[l, c0 * 128:(c0 + n) * 128, :], "R4"))
        st = self.stream(items)
        a = self.og
        for (c0, n) in FFG:
            wg, rg = next(st)
            wu, ru = next(st)
            wo, ro = next(st)
            for sb in range(self.NB):
                tok = slice(sb * 512, (sb + 1) * 512)
                for c in range(n):
                    pg, pu = (0, 1) if (c % 2 == 0) else (2, 3)
                    self.mm(self.ps[pg][:, :], [(wg[:, k, c * 128:(c + 1) * 128], self.xn[:, k, tok]) for k in range(8)],
                            [rg, ("xn", sb)], ["ps%d" % pg])
                    self.mm(self.ps[pu][:, :], [(wu[:, k, c * 128:(c + 1) * 128], self.xn[:, k, tok]) for k in range(8)],
                            [ru, ("xn", sb)], ["ps%d" % pu])
                    sg = self.S(c % 2)
                    self.act(sg, self.ps[pg][:, :], AF.Silu, ["ps%d" % pg], ["s%d" % (c % 2)])
                    self.tt("dve", a[:, c, tok], sg, self.ps[pu][:, :], ALU.mult, ["s%d" % (c % 2), "ps%d" % pu],
                            [("og", c, sb)])
            for sb in range(self.NB):
                tok = slice(sb * 512, (sb + 1) * 512)
                for dout in range(8):
                    pb = 4 + (dout % 4)
                    p = self.ps[pb]
                    self.mm(p[:, :], [(wo[:, c, dout * 128:(dout + 1) * 128], a[:, c, tok]) for c in range(n)],
                            [ro] + [("og", c, sb) for c in range(n)], ["ps%d" % pb])
                    self.stt(self.hT[:, dout, tok], p[:, :], self.coefP[:, g2 + dout:g2 + dout + 1],
                             self.hT[:, dout, tok], ALU.mult, ALU.add, ["ps%d" % pb, ("h", sb), "coefP"], [("h", sb)])

    def headA_proj(self, l, j, sb, w, wres, state_only=False):
        ps = self.ps
        tok = slice(sb * 512, (sb + 1) * 512)
        xr = ("xn", sb)
        xk = lambda k: self.xn[:, k, tok]
        if not state_only:
            self.mm(ps[0][:, :], [(w[:, k, 0:128], xk(k)) for k in range(8)], [wres, xr], ["ps0"])
        self.mm(ps[1][:, :], [(w[:, k, 128:256], xk(k)) for k in range(8)], [wres, xr], ["ps1"])
        if not state_only:
            self.mm(ps[2][:, :], [(w[:, k, 384:512], xk(k)) for k in range(8)], [wres, xr], ["ps2"])

    def headA_evac(self, state_only=False):
        ps, S, Bb = self.ps, self.S, self.B
        if not state_only:
            self.act(S(0), ps[0][:, :], AF.Silu, ["ps0"], ["s0"])
            self.act(Bb(0), ps[2][:, :], AF.Silu, ["ps2"], ["b0"])
        self.act(S(1), ps[1][:, :], AF.Exp, ["ps1"], ["s1"], scale=-1.0)

    def headA_sub(self, l, j, sb, w, wres, blk=0, state_only=False):
        kb, ps = self.kb, self.ps
        tok = slice(sb * 512, (sb + 1) * 512)
        xr = ("xn", sb)
        S, Bb = self.S, self.B
        xk = lambda k: self.xn[:, k, tok]
        so = state_only
        lbc = self.lbs[:, l, j:j + 1]
        self.act(S(2), S(1), AF.Ln, ["s1", "lbs"], ["s2"], scale=lbc, bias=self.epsc[:, 1:2])
        self.act(S(3), S(1), AF.Ln, ["s1"], ["s3"], scale=1.0, bias=self.epsc[:, 1:2])
        self.tt("dve", S(2), S(2), S(3), ALU.subtract, ["s2", "s3"], ["s2"])
        Gc = self.G[:, 1:513]
        kb.emit("dve", lambda e: e.tensor_tensor_scan(out=Gc, data0=self.ones64[:, :], data1=S(2), initial=0.0,
                                                      op0=ALU.mult, op1=ALU.add), ["s2", "ones64"], ["G"])
        self.act(S(4), S(2), AF.Exp, ["s2"], ["s4"])
        self.ts("dve", S(3), S(4), -1.0, 1.0, ALU.mult, ALU.add, ["s4"], ["s3"])
        G3 = Gc.rearrange("p (c t) -> p c t", c=8)
        bc = lambda a: a.unsqueeze(2).broadcast_to([128, 8, 64])
        v3 = lambda i: self.scr[:, i, :].rearrange("p (c t) -> p c t", c=8)
        if not so:
            self.tt("dve", v3(5), G3, bc(self.G[:, 32:513:64]), ALU.subtract, ["G", "G0"], ["s5"])
        self.tt("dve", v3(8), G3, bc(self.G[:, 0:512:64]), ALU.subtract, ["G", "G0"], ["s8"])
        self.tt("dve", v3(9), G3, bc(self.G[:, 64:513:64]), ALU.subtract, ["G", "G0"], ["s9"])
        if not so:
            self.act(S(6), S(5), AF.Exp, ["s5"], ["s6"])
            self.act(S(7), S(5), AF.Exp, ["s5"], ["s7"], scale=-1.0)
        self.act(S(8), S(8), AF.Exp, ["s8"], ["s8"])
        self.act(S(9), S(9), AF.Exp, ["s9"], ["s9"], scale=-1.0)
        if not so:
            self.tt("dve", Bb(1), S(0), S(6), ALU.mult, ["s0", "s6"], ["b1"])
            self.tt("dve", S(10), S(0), S(8), ALU.mult, ["s0", "s8"], ["s10"])
            self.tt("dve", Bb(2), S(3), S(7), ALU.mult, ["s3", "s7"], ["b2"])
        self.tt("dve", Bb(3), S(3), S(9), ALU.mult, ["s3", "s9"], ["b3"])
        vtok = self.bscr[:, 6, :].rearrange("p (a d) -> p a d", d=128)
        for pp in range(4):
            t0 = sb * 512 + pp * 128
            self.mm(ps[3][:, pp * 128:(pp + 1) * 128],
                    [(self.xn[:, k, t0:t0 + 128], w[:, k, 256:384]) for k in range(8)], [wres, xr], ["ps3"])
        if self.rmode:
            self.act(self.bscr[:, 6, :], ps[3][:, :], AF.Identity, ["ps3", "bmask"], ["b6"],
                     scale=self.bmask[:, blk:blk + 1], bias=0.0)
        else:
            self.cp("act", self.bscr[:, 6, :], ps[3][:, :], ["ps3"], ["b6"])
        pk = ps[5][:, :].bitcast(BF16)
        for pp in range(4):
            kb.emit("pe", lambda e, pp=pp: e.transpose(out=pk[:, pp * 128:(pp + 1) * 128],
                                                       in_=Bb(3)[:, pp * 128:(pp + 1) * 128], identity=self.identb[:, :]),
                    ["b3", "identb"], ["ps5"], inc=(pp == 3))
        kh = self.bscr[:, 8, :].rearrange("p (a d) -> p a d", d=128)
        self.cp("act", self.bscr[:, 8, :], pk[:, 0:512], ["ps5"], ["b8"])
        if not so:
            for c in range(8):
                c0 = c * 64
                r0 = (c % 2) * 64
                q0 = (c // 2) * 64
                kb.emit("pe", lambda e, c0=c0, r0=r0, q0=q0: e.matmul(ps[4][r0:r0 + 64, q0 + 32:q0 + 64], lhsT=Bb(2)[:, c0:c0 + 64],
                                                                      rhs=Bb(1)[:, c0 + 32:c0 + 64], start=True, stop=True),
                        ["b1", "b2"], ["ps4"], inc=False)
                kb.emit("pe", lambda e, c0=c0, r0=r0, q0=q0: e.matmul(ps[4][r0:r0 + 32, q0:q0 + 32], lhsT=Bb(2)[:, c0:c0 + 32],
                                                                      rhs=Bb(1)[:, c0:c0 + 32], start=True, stop=True),
                        ["b1", "b2"], ["ps4"], inc=(c == 7))
            p43 = ps[4][:, 0:256].rearrange("p (c t) -> p c t", c=4)
            self.tt("dve", self.attm[:, :, 32:64], p43[:, :, 32:64],
                    self.trimask[:, 32:64].unsqueeze(1).broadcast_to([128, 4, 32]), ALU.mult, ["ps4", "trimask"], ["attm"])
            for r0 in (0, 64):
                self.tt("dve", self.attm[r0:r0 + 32, :, 0:32], p43[r0:r0 + 32, :, 0:32],
                        self.trimask[r0:r0 + 32, 0:32].unsqueeze(1).broadcast_to([32, 4, 32]), ALU.mult, ["ps4", "trimask"], ["attm"])
        Sj = self.Sst[:, l * 8 + j, :]
        sres = ("S", l, j)
        for c in range(8):
            cs = slice(c * 64, (c + 1) * 64)
            hp = slice((c % 2) * 64, (c % 2) * 64 + 64)
            pp = c // 2
            if not so:
                kb.emit("pe", lambda e, pp=pp, hp=hp, cs=cs: e.matmul(ps[7][:, cs], lhsT=vtok[hp, pp, :], rhs=self.attm[hp, pp, :], start=True, stop=False),
                        ["b6", "attm"], ["ps7"], inc=False)
                kb.emit("pe", lambda e, cs=cs: e.matmul(ps[7][:, cs], lhsT=Sj, rhs=S(10)[:, cs], start=False, stop=True),
                        [sres, "s10"], ["ps7"], inc=True)
            sl = c % 4
            pn = ps[6][:, sl * 128:(sl + 1) * 128]
            kb.emit("pe", lambda e, pp=pp, hp=hp, pn=pn: e.matmul(pn, lhsT=kh[hp, pp, :], rhs=vtok[hp, pp, :], start=True, stop=True),
                    ["b8", "b6"], ["ps6"], inc=True)
            dcol = self.scr[:, 8, c * 64 + 63:c * 64 + 64]
            self.stt(Sj, Sj, dcol, pn, ALU.mult, ALU.add, [sres, "s8", "ps6"], [sres])
        if so:
            return
        self.act(Bb(4), ps[7][:, :], AF.Square, ["ps7"], ["b4"])
        self.mm(ps[5][:, :], [(self.onesb[:, :], Bb(4))], ["onesb", "b4"], ["ps5"])
        self.act(S(11), ps[5][:, :], AF.Ln, ["ps5"], ["s11"], scale=1.0 / 128, bias=self.epsc[:, 0:1])
        self.act(S(11), S(11), AF.Exp, ["s11"], ["s11"], scale=-0.5)
        self.stt(S(12), ps[7][:, :], self.vecs[:, V_GN + l:V_GN + l + 1], S(11), ALU.mult, ALU.mult,
                 ["ps7", "s11", "vecs"], ["s12"])
        self.tt("dve", self.og[:, j, tok], S(12), Bb(0), ALU.mult, ["s12", "b0"], [("og", j, sb)])

    def layerA(self, blk, l, state_only=False):
        self.modulate(l, 0)
        items = [(self.w_in_h[l, j, :, :], "K8") for j in range(8)]
        st = self.stream(items)
        slabs = {}

        def slab_of(j):
            if j not in slabs:
                slabs[j] = next(st)
                slabs.pop(j - 2, None)
            return slabs[j]
        seq = [(j, sb) for j in range(8) for sb in range(self.NB)]
        w0, r0 = slab_of(0)
        self.headA_proj(l, 0, 0, w0, r0, state_only)
        for n, (j, sb) in enumerate(seq):
            w, wres = slab_of(j)
            self.headA_evac(state_only)
            if n + 1 < len(seq):
                j2, sb2 = seq[n + 1]
                w2, r2 = slab_of(j2)
                self.headA_proj(l, j2, sb2, w2, r2, state_only)
            self.headA_sub(l, j, sb, w, wres, blk=blk, state_only=state_only)
        if state_only:
            return
        _, _, g1 = self.coef(l, 0)
        self.proj_residual([(self.w_o_a[l, :, 0:512], "K8"), (self.w_o_a[l, :, 512:1024], "K8")], self.og,
                           lambda sb: [("og", j, sb) for j in range(8)], g1, 8)

    def kvproj(self, blk):
        T = self.T
        self.modulate(4, 0)
        (w, wres), = list(self.stream([(self.w_kv[:, :], "K8")]))
        last = (blk == self.NBLK - 1)
        for c in range(2):
            for sb in range(self.NB):
                tok = slice(sb * 512, (sb + 1) * 512)
                pb = c * 2 + (sb % 2)
                self.mm(self.ps[pb][:, :], [(w[:, k, c * 128:(c + 1) * 128], self.xn[:, k, tok]) for k in range(8)],
                        [wres, ("xn", sb)], ["ps%d" % pb])
                self.cp("act", self.KT[:, c, 128 + sb * 512:128 + (sb + 1) * 512], self.ps[pb][:, :], ["ps%d" % pb], ["KT"])
                if last and sb == self.NB - 1:
                    self.cp("dve", self.scr[:, 0, c * 128:(c + 1) * 128], self.ps[pb][:, 384:512], ["ps%d" % pb], ["s0"])
        if last:
            self.ld(self.kT_p[:, :, :], self.scr[:, 0, 0:256].rearrange("p (c t) -> p c t", c=2), ["kT_p"], "ok", reads=["s0"])
            self.outs.append("kT_p")
        for tt_ in range(T // 128):
            pb = 4 + (tt_ % 4)
            self.mm(self.ps[pb][:, 0:256], [(self.xn[:, k, tt_ * 128:(tt_ + 1) * 128], w[:, k, 256:512]) for k in range(8)],
                    [wres, ("xn", tt_ // 4)], ["ps%d" % pb])
            self.cp("act", self.V[:, 1 + tt_, :], self.ps[pb][:, 0:256], ["ps%d" % pb], ["V"])
            if last and tt_ == T // 128 - 1:
                self.cp("dve", self.scr[:, 1, 0:256], self.ps[pb][:, 0:256], ["ps%d" % pb], ["s1"])
                self.ld(self.v_p[:, :], self.scr[:, 1, 0:256], ["v_p"], "ov", reads=["s1"])
                self.outs.append("v_p")

    def layerB(self, blk, jb):
        l = 2 + jb
        kb, ps, T = self.kb, self.ps, self.T
        self.modulate(l, 0)
        items = [(self.w_q_p[jb, :, 0:512], "K8"), (self.w_q_p[jb, :, 512:1024], "K8")]
        i = 0
        for w, wres in self.stream(items):
            for dd in range(4):
                for sb in range(self.NB):
                    tok = slice(sb * 512, (sb + 1) * 512)
                    pb = 6 + ((i * self.NB + sb) % 2)
                    self.mm(ps[pb][:, :], [(w[:, k, dd * 128:(dd + 1) * 128], self.xn[:, k, tok]) for k in range(8)],
                            [wres, ("xn", sb)], ["ps%d" % pb])
                    self.act(self.qT[:, i, tok], ps[pb][:, :], AF.Copy, ["ps%d" % pb], [("qT", i)], scale=0.125)
                i += 1
        it = 0
        for i in range(8):
            c = i // 4
            for qt in range(T // 128):
                qtok = slice(qt * 128, (qt + 1) * 128)
                first = (qt == 0 and not any(t in ("prekv", "own") for t in self.types[:blk]))
                halo_m = (self.rmode and qt == 0 and blk == self.own_blks[0])
                nk = 128 if first else 256
                nkb = nk // 128
                koff = qt * 128 + (128 if first else 0)
                par = it % 2
                it += 1
                s3 = self.scr[:, 2 * par:2 * par + 2, 0:nk]
                sr = ["s%d" % (2 * par), "s%d" % (2 * par + 1)]
                for half in range(2):
                    hp = slice(half * 64, (half + 1) * 64)
                    pl = ps[2 * par + half]
                    plr = "ps%d" % (2 * par + half)
                    kb.emit("pe", lambda e, pl=pl, hp=hp, i=i, qtok=qtok, c=c, koff=koff, nk=nk: e.matmul(
                        pl[:, 0:nk], lhsT=self.qT[hp, i, qtok], rhs=self.KT[hp, c, koff:koff + nk], start=True, stop=True),
                        [("qT", i), "KT"], [plr])
                    self.tt("dve", s3[:, half, :], pl[:, 0:nk], self.bias[:, 2 * i + half, 256 - nk:256], ALU.add,
                            [plr, "bias"], [sr[half]])
                if halo_m:
                    self.ts("dve", s3[:, :, 0:128], s3[:, :, 0:128], self.hmask[:, 0:1], None, ALU.add, ALU.bypass, sr + ["hmask"], sr)
                sm = self.small[:, par * 16:par * 16 + 16]
                smr = "sm%d" % par
                mx, ng, rs, es, dn = sm[:, 0:2], sm[:, 2:4], sm[:, 4:6], sm[:, 6:8], sm[:, 8:10]
                kb.emit("dve", lambda e, mx=mx, s3=s3: e.tensor_reduce(out=mx, in_=s3, axis=AX.X, op=ALU.max), sr, [smr])
                sk = self.sinks_bc[:, jb * 16 + 2 * i:jb * 16 + 2 * i + 2]
                self.tt("dve", mx, mx, sk, ALU.max, [smr, "sinks"], [smr])
                self.tt("dve", s3, s3, mx.unsqueeze(2).broadcast_to([128, 2, nk]), ALU.subtract, sr + [smr], sr)
                self.act(s3, s3, AF.Exp, sr, sr)
                kb.emit("dve", lambda e, rs=rs, s3=s3: e.tensor_reduce(out=rs, in_=s3, axis=AX.X, op=ALU.add), sr, [smr])
                self.tt("dve", ng, sk, mx, ALU.subtract, [smr, "sinks"], [smr])
                self.act(es, ng, AF.Exp, [smr], [smr])
                self.tt("dve", dn, rs, es, ALU.add, [smr], [smr])
                kb.emit("dve", lambda e, dn=dn: e.reciprocal(out=dn, in_=dn), [smr], [smr])
                pn = self.bscr[:, 2 * par:2 * par + 2, 0:nk]
                pnr = ["b%d" % (2 * par), "b%d" % (2 * par + 1)]
                self.tt("dve", pn, s3, dn.unsqueeze(2).broadcast_to([128, 2, nk]), ALU.mult, sr + [smr], pnr)
                ptb = ps[4 + par][:, :].bitcast(BF16)
                ptr = "ps%d" % (4 + par)
                nt = 2 * nkb
                for half in range(2):
                    for kb2 in range(nkb):
                        o0 = (half * nkb + kb2) * 128
                        kb.emit("pe", lambda e, ptb=ptb, pn=pn, half=half, kb2=kb2, o0=o0: e.transpose(
                            out=ptb[:, o0:o0 + 128], in_=pn[:, half, kb2 * 128:(kb2 + 1) * 128], identity=self.identb[:, :]),
                            pnr + ["identb"], [ptr], inc=(half == 1 and kb2 == nkb - 1))
                pT = self.bscr[:, 4 + 2 * par:6 + 2 * par, :].rearrange("p a n -> p (a n)")[:, 0:nt * 128]
                pTr = ["b%d" % (4 + 2 * par), "b%d" % (5 + 2 * par)]
                self.cp("act", pT, ptb[:, 0:nt * 128], [ptr], pTr)
                pob = ps[6 + par]
                por = "ps%d" % (6 + par)
                for half in range(2):
                    hp = slice(half * 64, (half + 1) * 64)
                    vcol = (2 * c + half) * 64
                    for kb2 in range(nkb):
                        vt = (qt + kb2) if not first else 1
                        o0 = (half * nkb + kb2) * 128
                        kb.emit("pe", lambda e, pob=pob, hp=hp, vt=vt, vcol=vcol, pT=pT, kb2=kb2, nkb=nkb, o0=o0: e.matmul(
                            pob[hp, 0:128], lhsT=self.V[:, vt, vcol:vcol + 64], rhs=pT[:, o0:o0 + 128],
                            start=(kb2 == 0), stop=(kb2 == nkb - 1)), ["V"] + pTr, [por], inc=(half == 1 and kb2 == nkb - 1))
                self.cp("dve", self.og[:, i, qtok], pob[:, 0:128], [por], [("og", i, qt // 4)])
        _, _, g1 = self.coef(l, 0)
        self.proj_residual([(self.w_o_p[jb, :, 0:512], "K8"), (self.w_o_p[jb, :, 512:1024], "K8")], self.og,
                           lambda sb: [("og", j, sb) for j in range(8)], g1, 8)

    def final(self, blk):
        T = self.T
        for sb in range(self.NB):
            tok = slice(sb * 512, (sb + 1) * 512)
            hs = ("h", sb)
            sqr = ["b%d" % k for k in range(8)]
            self.act(self.bscr[:, 0:8, :], self.hT[:, :, tok], AF.Square, [hs], sqr)
            self.mm(self.ps[0][:, :], [(self.onesb[:, :], self.bscr[:, k, :]) for k in range(8)], ["onesb"] + sqr, ["ps0"])
            rstd = self.S(0)
            self.act(rstd, self.ps[0][:, :], AF.Ln, ["ps0"], ["s0"], scale=1.0 / D, bias=self.epsc[:, 0:1])
            self.act(rstd, rstd, AF.Exp, ["s0"], ["s0"], scale=-0.5)
            for k in range(8):
                self.stt(self.scr[:, 4 + k, :], self.hT[:, k, tok], self.vecs[:, V_FIN + k:V_FIN + k + 1], rstd,
                         ALU.mult, ALU.mult, [hs, "s0", "vecs"], ["s%d" % (4 + k)])
            ob = self.own_blks.index(blk)
            self.ld(self.yT[:, :, ob * T + sb * 512:ob * T + (sb + 1) * 512], self.scr[:, 4:12, :], ["yT"], "y%d" % sb,
                    reads=["s%d" % (4 + k) for k in range(8)])
        if "yT" not in self.outs:
            self.outs.append("yT")

    def block(self, blk):
        T = self.T
        for sb in range(self.NB):
            self.ld(self.hT[:, :, sb * 512:(sb + 1) * 512], self.xT[:, :, blk * T + sb * 512:blk * T + (sb + 1) * 512],
                    [("h", sb)], "x%d" % sb)
        typ = self.types[blk]
        for l in range(self.nlayers):
            if typ == "pre" and l >= 1:
                if l == 1:
                    self.layerA(blk, 1, state_only=True)
                continue
            if typ == "prekv" and l >= 2:
                continue
            if l < 2:
                self.layerA(blk, l)
            else:
                self.layerB(blk, l - 2)
            if self.dbg == l + 10 and blk == 0:
                self.ld(self.dbg_d[:, :, :], self.hT[:, :, :], ["dbg"], "o2", reads=[("h", sb) for sb in range(self.NB)])
                self.outs.append("dbg")
            self.ffn(l)
            if l == 1:
                self.kvproj(blk)
            if self.dbg == l and blk == 0:
                self.ld(self.dbg_d[:, :, :], self.hT[:, :, :], ["dbg"], "o2", reads=[("h", sb) for sb in range(self.NB)])
                self.outs.append("dbg")
        if self.nlayers == 4 and typ == "own":
            self.final(blk)
        if blk < self.NBLK - 1 and self.nlayers > 1 and typ != "pre":
            self.cp("dve", self.KT[:, :, 0:128], self.KT[:, :, T:T + 128], ["KT"], ["KT"])
            self.cp("dve", self.V[:, 0, :], self.V[:, T // 128, :], ["V"], ["V"])
        if blk == self.NBLK - 1:
            self.ld(self.st_p.ap().rearrange("l j k v -> k (l j) v"), self.Sst[:, :, :], ["st_p"], "ost", reads=[("S", l, j) for l in range(2) for j in range(8)])
            self.outs.append("st_p")

    def ada_phase(self):
        kb, ps = self.kb, self.ps
        cs = self.scr[:, 0, 0:136].rearrange("p (k n) -> p k n", k=8)
        self.ld(cs, self.cT[:, :, :], ["s0"], "c17")
        csb = self.csb
        self.act(csb[:, :, :], cs, AF.Silu, ["s0"], ["csb"])
        items = []
        for l in range(4):
            for s in range(12):
                items.append((self.w_ada[l, :, s * 512:(s + 1) * 512], "K8"))
        for s in range(4):
            items.append((self.w_ada_kv[:, s * 512:(s + 1) * 512], "K8"))
        n = 0
        for w, wres in self.stream(items):
            l, s = (n // 12, n % 12) if n < 48 else (4, n - 48)
            n += 1
            for dd in range(4):
                ch = s * 4 + dd
                gidx = l * 48 + ch
                bcol = (V_BADA + l * 48 + ch) if l < 4 else (V_BADAKV + ch)
                pb = gidx % 4
                self.mm(ps[pb][:, 0:17], [(w[:, k, dd * 128:(dd + 1) * 128], csb[:, k, :]) for k in range(8)],
                        [wres, "csb"], ["ps%d" % pb])
                self.ts("dve", self.adaT[:, gidx, :], ps[pb][:, 0:17], self.vecs[:, bcol:bcol + 1], None, ALU.add, ALU.bypass,
                        ["ps%d" % pb, "vecs"], ["adaT"])
        aT = self.adaT
        for l in range(5):
            for which in range(2 if l < 4 else 1):
                base = l * 48 + which * 24
                nw = self.vecs[:, V_NORM + (l * 2 + which) * 8:V_NORM + (l * 2 + which) * 8 + 8] if l < 4 else self.vecs[:, V_KVN:V_KVN + 8]
                self.stt(self.coefP[:, base:base + 8], aT[:, base + 8:base + 16, 0], 1.0, nw, ALU.add, ALU.mult,
                         ["adaT", "vecs"], ["coefP"])
                self.cp("dve", self.coefP[:, base + 8:base + 16], aT[:, base:base + 8, 0], ["adaT"], ["coefP"])
                if l < 4:
                    self.cp("dve", self.coefP[:, base + 16:base + 24], aT[:, base + 16:base + 24, 0], ["adaT"], ["coefP"])
                if self.sample:
                    self.stt(self.coefS[:, base:base + 8, :], aT[:, base + 8:base + 16, 1:17], 1.0,
                             nw.unsqueeze(2).broadcast_to([128, 8, 16]), ALU.add, ALU.mult, ["adaT", "vecs"], ["coefS"])
                    self.cp("dve", self.coefS[:, base + 8:base + 16, :], aT[:, base:base + 8, 1:17], ["adaT"], ["coefS"])
                    if l < 4:
                        self.cp("dve", self.coefS[:, base + 16:base + 24, :], aT[:, base + 16:base + 24, 1:17], ["adaT"], ["coefS"])

    def build(self):
        self.setup()
        self.ada_phase()
        if self.sample:
            self.sample_phase()
        self.kb.barrier()
        for blk in range(self.NBLK):
            self.block(blk)
        self.kb.finish(self.outs)
        return self.nc


def _fm(v):
    v = np.asarray(v, np.float32)
    return np.ascontiguousarray(v.reshape(-1, 128).T)


def _consts():
    eline = np.zeros((33, 384), np.float32)
    eline[32, :] = MASKV
    for ip in range(128, 256):
        eline[int(t5_bucket_np(255 - ip)), ip] = 1.0
        eline[32, ip] = 0.0
    eline_s = np.zeros((33, 128), np.float32)
    for r in range(128):
        eline_s[int(t5_bucket_np(127 - r)), r] = 1.0
    ident = np.eye(128, dtype=np.float32)
    tri = (np.arange(64)[:, None] <= np.arange(64)[None, :]).astype(np.float32)
    tri = np.concatenate([tri, tri], 0)
    hm = np.zeros((128, 2), np.float32)
    hm[:64, 0] = 1.0
    hm[64:, 1] = 1.0
    return dict(eline=eline, eline_s=eline_s, ident=ident, trimask=tri, halfmask=hm)


def prep_shared(inp):
    f32 = lambda a: np.ascontiguousarray(np.asarray(a, np.float32))
    sh = slot_heads()
    w_in_a = f32(inp["w_in_a"])
    d = {}
    d["w_in_h"] = np.ascontiguousarray(w_in_a.reshape(2, 1024, 4, 8, 128).transpose(0, 3, 1, 2, 4).reshape(2, 8, 1024, 512))
    d["w_o_a"] = f32(inp["w_o_a"])
    d["w_kv"] = f32(inp["w_kv"])
    d["w_ada"] = f32(inp["w_ada"])
    d["w_ada_kv"] = f32(inp["w_ada_kv"])
    wq = f32(inp["w_q_b"]).reshape(2, 1024, 16, 64)
    d["w_q_p"] = np.ascontiguousarray(wq[:, :, sh, :].reshape(2, 1024, 1024))
    wo = f32(inp["w_o_b"]).reshape(2, 16, 64, 1024)
    d["w_o_p"] = np.ascontiguousarray(wo[:, sh, :, :].reshape(2, 1024, 1024))
    d["w_ffn_in"] = f32(inp["w_ffn_in"])
    d["w_ffn_out"] = f32(inp["w_ffn_out"])
    sk = f32(inp["sinks_b"])[:, sh]
    d["sinks_bc"] = np.ascontiguousarray(np.broadcast_to(sk.reshape(1, 32), (128, 32)))
    d["sink_s"] = np.ascontiguousarray(sk.T)
    rb = f32(inp["rel_bias"])[:, sh]
    d["rb_ext"] = np.ascontiguousarray(np.concatenate([rb, np.ones((1, 16), np.float32)], 0))
    vecs = np.zeros((128, NV), np.float32)
    nw = f32(inp["norm_w"])
    for l in range(4):
        for wh in range(2):
            vecs[:, V_NORM + (l * 2 + wh) * 8:V_NORM + (l * 2 + wh) * 8 + 8] = _fm(nw[l, wh])
    vecs[:, V_KVN:V_KVN + 8] = _fm(inp["kv_norm_w"])
    vecs[:, V_FIN:V_FIN + 8] = _fm(inp["final_norm_w"])
    lb = f32(inp["lb_a"])
    vecs[:, V_LB:V_LB + 8] = _fm(lb[0])
    vecs[:, V_LB + 8:V_LB + 16] = _fm(lb[1])
    ba = f32(inp["b_ada"])
    for l in range(4):
        vecs[:, V_BADA + l * 48:V_BADA + (l + 1) * 48] = _fm(ba[l])
    vecs[:, V_BADAKV:V_BADAKV + 16] = _fm(inp["b_ada_kv"])
    gn = f32(inp["gnorm_a"])
    vecs[:, V_GN] = gn[0]
    vecs[:, V_GN + 1] = gn[1]
    d["vecs"] = vecs
    d.update(_consts())
    return d


def prep_core(inp, shared, core, T, NBLK, seq=None, win=None):
    f32 = lambda a: np.ascontiguousarray(np.asarray(a, np.float32))
    NTOK = T * NBLK
    d = dict(shared)
    if win is None:
        seq = core % 2 if seq is None else seq
        x = f32(inp["x_prompt"])[seq, :NTOK]
        bm = np.ones((128, NBLK), np.float32)
    else:
        seq, start, end = win
        assert end - start == NTOK
        x = np.zeros((NTOK, 1024), np.float32)
        v0 = max(start, 0)
        x[v0 - start:] = f32(inp["x_prompt"])[seq, v0:end]
        bm = np.zeros((128, NBLK), np.float32)
        for b in range(NBLK):
            bm[:, b] = 1.0 if start + b * T >= 0 else 0.0
    d["bmask"] = bm
    d["xT"] = np.ascontiguousarray(x.T.reshape(8, 128, NTOK).transpose(1, 0, 2))
    bs = slice(core * 16, (core + 1) * 16)
    c17 = np.concatenate([f32(inp["c_prompt"])[seq][None], f32(inp["c_sample"])[bs]], 0)
    d["cT"] = np.ascontiguousarray(c17.T.reshape(8, 128, 17).transpose(1, 0, 2))
    xs = f32(inp["x_sample"])[bs, 0]
    d["xsT"] = np.ascontiguousarray(xs.T.reshape(8, 128, 16).transpose(1, 0, 2))
    d["state"] = f32(inp["state_hgrn"])[bs]
    ck = f32(inp["cache_swa_k"])[bs].reshape(16, 128, 256)
    d["ck"] = ck
    d["cv"] = f32(inp["cache_swa_v"])[bs].reshape(16, 128, 256)
    d["ckT"] = np.ascontiguousarray(ck[:, 1:128, :].transpose(0, 2, 1))
    return d


def _sample_methods():
    def bc3(a, n):
        return a.unsqueeze(2).broadcast_to([a.shape[0], a.shape[1], n])

    def sample_modulate(self, l, which):
        a0, b0, _ = self.coef(l, which)
        S, ps = self.S, self.ps
        sq = self.B(0)[:, 0:128].rearrange("p (k n) -> p k n", k=8)
        self.act(sq, self.hs[:, :, :], AF.Square, ["hs"], ["b0"])
        self.mm(ps[0][:, 0:16], [(self.onesb[:, :], sq[:, k, :]) for k in range(8)], ["onesb", "b0"], ["ps0"])
        rstd = S(0)[:, 0:16]
        self.act(rstd, ps[0][:, 0:16], AF.Ln, ["ps0"], ["s0"], scale=1.0 / D, bias=self.epsc[:, 0:1])
        self.act(rstd, rstd, AF.Exp, ["s0"], ["s0"], scale=-0.5)
        t = S(1)[:, 0:128].rearrange("p (k n) -> p k n", k=8)
        self.tt("dve", t, self.hs[:, :, :], rstd.unsqueeze(1).broadcast_to([128, 8, 16]), ALU.mult, ["hs", "s0"], ["s1"])
        self.tt("dve", t, t, self.coefS[:, a0:a0 + 8, :], ALU.mult, ["s1", "coefS"], ["s1"])
        self.tt("dve", self.xns[:, :, :], t, self.coefS[:, b0:b0 + 8, :], ALU.add, ["s1", "coefS"], ["xns"])

    def sample_proj(self, slabs, src, src_res, gcol, nk):
        dout = 0
        for view, wres in self.stream(slabs):
            for dd in range(4):
                pb = 4 + (dout % 4)
                p = self.ps[pb]
                self.mm(p[:, 0:16], [(view[:, k, dd * 128:(dd + 1) * 128], src[:, k, :]) for k in range(nk)],
                        [wres, src_res], ["ps%d" % pb])
                tmp = self.S(3)[:, 0:16]
                self.tt("dve", tmp, p[:, 0:16], self.coefS[:, gcol + dout, :], ALU.mult, ["ps%d" % pb, "coefS"], ["s3"])
                self.tt("dve", self.hs[:, dout, :], self.hs[:, dout, :], tmp, ALU.add, ["hs", "s3"], ["hs"])
                dout += 1

    def sample_ffn(self, l):
        self.sample_modulate(l, 1)
        _, _, g2 = self.coef(l, 1)
        items = []
        for (c0, n) in FFG:
            items.append((self.w_ffn_in[l, :, c0 * 128:(c0 + n) * 128], "K8"))
            items.append((self.w_ffn_in[l, :, DFF + c0 * 128:DFF + (c0 + n) * 128], "K8"))
            items.append((self.w_ffn_out[l, c0 * 128:(c0 + n) * 128, :], "R4"))
        st = self.stream(items)
        ps = self.ps
        for (c0, n) in FFG:
            wg, rg = next(st)
            wu, ru = next(st)
            wo, ro = next(st)
            for c in range(n):
                self.mm(ps[0][:, 0:16], [(wg[:, k, c * 128:(c + 1) * 128], self.xns[:, k, :]) for k in range(8)], [rg, "xns"], ["ps0"])
                self.mm(ps[1][:, 0:16], [(wu[:, k, c * 128:(c + 1) * 128], self.xns[:, k, :]) for k in range(8)], [ru, "xns"], ["ps1"])
                sg = self.S(0)[:, 0:16]
                self.act(sg, ps[0][:, 0:16], AF.Silu, ["ps0"], ["s0"])
                self.tt("dve", self.as_[:, c, :], sg, ps[1][:, 0:16], ALU.mult, ["s0", "ps1"], ["as"])
            for dout in range(8):
                pb = 4 + (dout % 4)
                self.mm(ps[pb][:, 0:16], [(wo[:, c, dout * 128:(dout + 1) * 128], self.as_[:, c, :]) for c in range(n)],
                        [ro, "as"], ["ps%d" % pb])
                tmp = self.S(3)[:, 0:16]
                self.tt("dve", tmp, ps[pb][:, 0:16], self.coefS[:, g2 + dout, :], ALU.mult, ["ps%d" % pb, "coefS"], ["s3"])
                self.tt("dve", self.hs[:, dout, :], self.hs[:, dout, :], tmp, ALU.add, ["hs", "s3"], ["hs"])

    def sampleA(self, l):
        kb, ps, S = self.kb, self.ps, self.S
        self.sample_modulate(l, 0)
        items = [(self.w_in_h[l, j, :, :], "K8") for j in range(8)]
        for j, (w, wres) in enumerate(self.stream(items)):
            self.ld(self.Sin[:, :, :], self.state_d[:, l, j].rearrange("b k v -> k b v"), ["Sin"], "si")
            pp = ps[0]
            for part in range(4):
                self.mm(pp[:, part * 16:(part + 1) * 16],
                        [(w[:, k, part * 128:(part + 1) * 128], self.xns[:, k, :]) for k in range(8)], [wres, "xns"], ["ps0"])
            q, gate = S(0)[:, 0:16], S(0)[:, 16:32]
            self.act(q, pp[:, 0:16], AF.Silu, ["ps0"], ["s0"])
            self.act(gate, pp[:, 48:64], AF.Silu, ["ps0"], ["s0"])
            e, L1, L2, lg, f, kk, v = [S(1)[:, i * 16:(i + 1) * 16] for i in range(7)]
            self.act(e, pp[:, 16:32], AF.Exp, ["ps0"], ["s1"], scale=-1.0)
            self.act(L1, e, AF.Ln, ["s1", "lbs"], ["s1"], scale=self.lbs[:, l, j:j + 1], bias=self.epsc[:, 1:2])
            self.act(L2, e, AF.Ln, ["s1"], ["s1"], scale=1.0, bias=self.epsc[:, 1:2])
            self.tt("dve", lg, L1, L2, ALU.subtract, ["s1"], ["s1"])
            self.act(f, lg, AF.Exp, ["s1"], ["s1"])
            self.ts("dve", kk, f, -1.0, 1.0, ALU.mult, ALU.add, ["s1"], ["s1"])
            self.cp("dve", v, pp[:, 32:48], ["ps0"], ["s1"])
            rd = self.scr[:, 4:8, :].rearrange("p a (b v) -> p (a b) v", v=128)
            self.tt("dve", rd, self.ident[:, :].unsqueeze(1).broadcast_to([128, 16, 128]), bc3(v, 128), ALU.mult,
                    ["ident", "s1"], ["s4", "s5", "s6", "s7"])
            for qd in range(4):
                self.mm(ps[4 + qd][:, :], [(self.ones64[:, 0:128], self.scr[:, 4 + qd, :])], ["ones64", "s%d" % (4 + qd)],
                        ["ps%d" % (4 + qd)])
                self.tt("dve", self.scr[:, 8 + qd, :].rearrange("p (b v) -> p b v", v=128),
                        ps[4 + qd][:, :].rearrange("p (b v) -> p b v", v=128), bc3(kk[:, 4 * qd:4 * qd + 4], 128), ALU.mult,
                        ["ps%d" % (4 + qd), "s1"], ["s%d" % (8 + qd)])
            self.tt("dve", self.Sout[:, :, :], self.Sin[:, :, :], bc3(f, 128), ALU.mult, ["Sin", "s1"], ["Sout"])
            self.tt("dve", self.Sout[:, :, :], self.Sout[:, :, :], self.scr[:, 8:12, :].rearrange("p a (b v) -> p (a b) v", v=128),
                    ALU.add, ["Sout", "s8", "s9", "s10", "s11"], ["Sout"])
            self.ld(self.st_s[:, l, j].rearrange("b k v -> k b v"), self.Sout[:, :, :], ["st_s"], "so", reads=["Sout"])
            po2 = ps[1]
            q2 = S(0)[:, 32:64].rearrange("p (b t) -> p b t", t=2)
            self.cp("dve", q2, bc3(q, 2), ["s0"], ["s0"])
            for b in range(16):
                kb.emit("pe", lambda e, b=b: e.matmul(po2[:, 2 * b:2 * b + 2], lhsT=self.Sout[:, b, :], rhs=q2[:, b, :], start=True, stop=True),
                        ["Sout", "s0"], ["ps1"], inc=(b == 15))
            po = po2[:, 0:32:2]
            osq = self.B(1)[:, 0:16]
            self.act(osq, po, AF.Square, ["ps1"], ["b1"])
            self.mm(ps[2][:, 0:16], [(self.onesb[:, :], osq)], ["onesb", "b1"], ["ps2"])
            rstd = S(2)[:, 0:16]
            self.act(rstd, ps[2][:, 0:16], AF.Ln, ["ps2"], ["s2"], scale=1.0 / 128, bias=self.epsc[:, 0:1])
            self.act(rstd, rstd, AF.Exp, ["s2"], ["s2"], scale=-0.5)
            t = S(2)[:, 16:32]
            self.stt(t, po, self.vecs[:, V_GN + l:V_GN + l + 1], rstd, ALU.mult, ALU.mult, ["ps1", "s2", "vecs"], ["s2"])
            self.tt("dve", self.ogs[:, j, :], t, gate, ALU.mult, ["s2", "s0"], ["ogs"])
        if "st_s" not in self.outs:
            self.outs.append("st_s")
        _, _, g1 = self.coef(l, 0)
        self.sample_proj([(self.w_o_a[l, :, 0:512], "K8"), (self.w_o_a[l, :, 512:1024], "K8")], self.ogs, "ogs", g1, 8)

    def sample_kv(self):
        ps, S = self.ps, self.S
        self.sample_modulate(4, 0)
        self.kb.dma("pool", "cv", self.Vs[0:112, :, :], self.cv_d[:, 1:113, :].rearrange("b s n -> s b n"), [], ["Vs"])
        self.kb.dma("pool", "cv2", self.Vs[112:127, :, :], self.cv_d[:, 113:128, :].rearrange("b s n -> s b n"), [], ["Vs"])
        ckv = self.ckT_d.ap().rearrange("b (c p) t -> p c b t", p=128)
        for c in range(2):
            self.kb.dma("pool", "ck%d" % c, self.KTs[:, c, :, 0:127], ckv[:, c, :, :], [], ["KTs"])
        self.ld(self.ck_s[:, 0:127, :], self.ck_d[:, 1:128, :], ["ck_s"], "ock")
        self.ld(self.cv_s[:, 0:127, :], self.cv_d[:, 1:128, :], ["cv_s"], "ocv")
        (w, wres), = list(self.stream([(self.w_kv[:, :], "K8")]))
        for c in range(2):
            self.mm(ps[0][:, c * 16:(c + 1) * 16], [(w[:, k, c * 128:(c + 1) * 128], self.xns[:, k, :]) for k in range(8)],
                    [wres, "xns"], ["ps0"])
        self.cp("dve", self.KTs[:, :, :, 127], ps[0][:, 0:32].rearrange("p (c b) -> p c b", c=2), ["ps0"], ["KTs"])
        self.mm(ps[1][0:16, :], [(self.xns[:, k, :], w[:, k, :]) for k in range(8)], [wres, "xns"], ["ps1"])
        rowf = S(0)[0:16, :]
        self.cp("dve", rowf, ps[1][0:16, :], ["ps1"], ["s0"])
        self.ld(self.ck_s[:, 127, :], rowf[:, 0:256], ["ck_s"], "ock2", reads=["s0"])
        self.ld(self.cv_s[:, 127, :], rowf[:, 256:512], ["cv_s"], "ocv2", reads=["s0"])
        rowb = self.B(0)[0:16, 0:256]
        self.cp("dve", rowb, ps[1][0:16, 256:512], ["ps1"], ["b0"])
        self.ld(self.rowk_d[:, :], rowb, ["rowk"], "rk", reads=["b0"])
        self.ld(self.Vs[127:128, :, :], self.rowk_d.ap().rearrange("(o b) n -> o b n", o=1), ["Vs"], "rk2", reads=["rowk"])
        self.outs += ["ck_s", "cv_s"]

    def sampleB(self, jb):
        kb, ps, S = self.kb, self.ps, self.S
        l = 2 + jb
        self.sample_modulate(l, 0)
        i = 0
        for w, wres in self.stream([(self.w_q_p[jb, :, 0:512], "K8"), (self.w_q_p[jb, :, 512:1024], "K8")]):
            for dd in range(4):
                pb = i % 4
                self.mm(ps[pb][:, 0:16], [(w[:, k, dd * 128:(dd + 1) * 128], self.xns[:, k, :]) for k in range(8)],
                        [wres, "xns"], ["ps%d" % pb])
                self.act(self.qTs[:, i, :], ps[pb][:, 0:16], AF.Copy, ["ps%d" % pb], ["qTs"], scale=0.125)
                i += 1
        kb.emit("dve", lambda e: e.memset(self.Qb2[:, :, :, :], 0.0), [], ["Qb2"])
        for c in range(2):
            for half in range(2):
                hp = slice(half * 64, (half + 1) * 64)
                self.cp("dve", self.Qb2[hp, :, c, c * 8 + half:c * 8 + 8:2],
                        self.qTs[hp, 4 * c:4 * c + 4, :].rearrange("p i b -> p b i"), ["qTs"], ["Qb2"])
        for b in range(16):
            pb = 4 + b // 4
            self.mm(ps[pb][0:16, (b % 4) * 128:(b % 4 + 1) * 128],
                    [(self.Qb2[:, b, c, :], self.KTs[:, c, b, :]) for c in range(2)], ["Qb2", "KTs"], ["ps%d" % pb])
        sv = lambda qd: self.scr[0:16, 4 + qd, :].rearrange("p (b t) -> p b t", t=128)
        for qd in range(4):
            self.tt("dve", sv(qd), ps[4 + qd][0:16, :].rearrange("p (b t) -> p b t", t=128),
                    self.bias_s[0:16, :].unsqueeze(1).broadcast_to([16, 4, 128]), ALU.add, ["ps%d" % (4 + qd), "bias_s"], ["s%d" % (4 + qd)])
        s_all = self.scr[0:16, 4:8, :].rearrange("p a (b t) -> p (a b) t", t=128)
        sr = ["s4", "s5", "s6", "s7"]
        sm = self.small
        mx, rs, es, dn = sm[0:16, 0:16], sm[0:16, 16:32], sm[0:16, 32:48], sm[0:16, 48:64]
        kb.emit("dve", lambda e: e.tensor_reduce(out=mx, in_=s_all, axis=AX.X, op=ALU.max), sr, ["sm0"])
        skc = self.sink_s[0:16, jb:jb + 1]
        self.ts("dve", mx, mx, skc, None, ALU.max, ALU.bypass, ["sm0", "sink_s"], ["sm0"])
        self.tt("dve", s_all, s_all, bc3(mx, 128), ALU.subtract, sr + ["sm0"], sr)
        self.act(self.scr[0:16, 4:8, :], self.scr[0:16, 4:8, :], AF.Exp, sr, sr)
        kb.emit("dve", lambda e: e.tensor_reduce(out=rs, in_=s_all, axis=AX.X, op=ALU.add), sr, ["sm0"])
        self.act(es, mx, AF.Exp, ["sm0", "sink_s"], ["sm0"], scale=-1.0, bias=skc)
        self.tt("dve", dn, rs, es, ALU.add, ["sm0"], ["sm0"])
        kb.emit("dve", lambda e: e.reciprocal(out=dn, in_=dn), ["sm0"], ["sm0"])
        pn = self.bscr[0:16, 4:8, :].rearrange("p a (b t) -> p (a b) t", t=128)
        pnr = ["b4", "b5", "b6", "b7"]
        self.tt("dve", pn, s_all, bc3(dn, 128), ALU.mult, sr + ["sm0"], pnr)
        ptb = ps[0][:, :].bitcast(BF16)
        for b in range(16):
            kb.emit("pe", lambda e, b=b: e.transpose(out=ptb[:, b * 16:(b + 1) * 16], in_=pn[:, b, :], identity=self.identb[0:16, 0:16]),
                    pnr + ["identb"], ["ps0"], inc=(b == 15))
        self.cp("act", self.pTs[:, :, :].rearrange("p b s -> p (b s)"), ptb[:, 0:256], ["ps0"], ["pTs"])
        pv = ps[1]
        for b in range(16):
            for c in range(2):
                o0 = (b * 2 + c) * 16
                kb.emit("pe", lambda e, b=b, c=c, o0=o0: e.matmul(pv[:, o0:o0 + 16], lhsT=self.Vs[:, b, c * 128:(c + 1) * 128],
                                                                  rhs=self.pTs[:, b, :], start=True, stop=True),
                        ["Vs", "pTs"], ["ps1"], inc=(b == 15 and c == 1))
        pvv = pv[:, :].rearrange("p (b c s) -> p b c s", b=16, c=2)
        for c in range(2):
            for half in range(2):
                hp = slice(half * 64, (half + 1) * 64)
                self.cp("dve", self.ogs[hp, 4 * c:4 * c + 4, :].rearrange("p i b -> p b i"),
                        pvv[hp, :, c, c * 8 + half:c * 8 + 8:2], ["ps1"], ["ogs"])
        _, _, g1 = self.coef(l, 0)
        self.sample_proj([(self.w_o_p[jb, :, 0:512], "K8"), (self.w_o_p[jb, :, 512:1024], "K8")], self.ogs, "ogs", g1, 8)

    def sample_phase(self):
        S, ps = self.S, self.ps
        self.ld(self.hs[:, :, :], self.xsT[:, :, :], ["hs"], "c20")
        self.ld(self.sink_s[0:16, 0:2], self.sink_s_d[:, :], ["sink_s"], "c21")
        rb = S(12)[0:33, 0:16]
        el = S(13)[0:33, 0:128]
        self.ld(rb, self.rb_ext_d[:, :], ["s12"], "c22")
        self.ld(el, self.eline_s_d[:, :], ["s13"], "c23")
        self.mm(ps[3][0:16, 0:128], [(rb, el)], ["s12", "s13"], ["ps3"])
        self.cp("dve", self.bias_s[0:16, :], ps[3][0:16, 0:128], ["ps3"], ["bias_s"])
        for l in range(4):
            if l < 2:
                self.sampleA(l)
            else:
                self.sampleB(l - 2)
            self.sample_ffn(l)
            if l == 1:
                self.sample_kv()
        sq = self.B(0)[:, 0:128].rearrange("p (k n) -> p k n", k=8)
        self.act(sq, self.hs[:, :, :], AF.Square, ["hs"], ["b0"])
        self.mm(ps[0][:, 0:16], [(self.onesb[:, :], sq[:, k, :]) for k in range(8)], ["onesb", "b0"], ["ps0"])
        rstd = S(0)[:, 0:16]
        self.act(rstd, ps[0][:, 0:16], AF.Ln, ["ps0"], ["s0"], scale=1.0 / D, bias=self.epsc[:, 0:1])
        self.act(rstd, rstd, AF.Exp, ["s0"], ["s0"], scale=-0.5)
        t = S(1)[:, 0:128].rearrange("p (k n) -> p k n", k=8)
        self.tt("dve", t, self.hs[:, :, :], rstd.unsqueeze(1).broadcast_to([128, 8, 16]), ALU.mult, ["hs", "s0"], ["s1"])
        self.tt("dve", t, t, bc3(self.vecs[:, V_FIN:V_FIN + 8], 16), ALU.mult, ["s1", "vecs"], ["s1"])
        self.ld(self.y_s[:, :, :], t, ["y_s"], "oys", reads=["s1"])
        self.outs.append("y_s")

    for f in (sample_modulate, sample_proj, sample_ffn, sampleA, sample_kv, sampleB, sample_phase):
        setattr(Prog, f.__name__, f)


_sample_methods()


T_BLK = 1024
N_PRE = 5
N_OWN = 2
OWN_TOK = T_BLK * N_OWN
_CACHE = {}


def kernel(**inputs):
    if "nc" not in _CACHE:
        prog = Prog(T=T_BLK, sample=True, NPRE=N_PRE, NOWN=N_OWN)
        _CACHE["nc"] = prog.build()
    nc = _CACHE["nc"]
    NBLK = N_PRE + 1 + N_OWN
    NTOK = T_BLK * NBLK
    shared = prep_shared(inputs)
    in_maps = []
    for c in range(8):
        seq, p = c // 4, c % 4
        end = (p + 1) * OWN_TOK
        in_maps.append(prep_core(inputs, shared, c, T_BLK, NBLK, win=(seq, end - NTOK, end)))
    res = run_bass_kernel_spmd(nc, in_maps, core_ids=list(range(8)))
    r = res.results
    g = lambda c, n, shp: np.asarray(r[c][n], np.float32).reshape(shp)
    y_prompt = np.stack([np.concatenate([g(4 * s + p, "yT", (128, 8, OWN_TOK)).transpose(2, 1, 0).reshape(OWN_TOK, 1024)
                                         for p in range(4)], 0) for s in range(2)])
    y_sample = np.concatenate([g(c, "y_s", (128, 8, 16)).transpose(2, 1, 0).reshape(16, 1, 1024) for c in range(8)], 0)
    last = [3, 7]
    st_p = np.stack([g(c, "st_p", (2, 8, 128, 128)) for c in last])
    st_s = np.concatenate([g(c, "st_s", (16, 2, 8, 128, 128)) for c in range(8)], 0)
    k_p = np.stack([g(c, "kT_p", (2, 64, 2, 128)).transpose(3, 2, 0, 1).reshape(128, 4, 64) for c in last])
    v_p = np.stack([g(c, "v_p", (128, 4, 64)) for c in last])
    k_s = np.concatenate([g(c, "ck_s", (16, 128, 4, 64)) for c in range(8)], 0)
    v_s = np.concatenate([g(c, "cv_s", (16, 128, 4, 64)) for c in range(8)], 0)
    f = np.ascontiguousarray
    return (f(y_prompt), f(y_sample), f(st_p), f(st_s), f(k_p), f(v_p), f(k_s), f(v_s))
```
